# Optimizing a Trainium2 kernel written in Bass

```python
import math
import jax, jax.numpy as jnp
from jax import lax
import numpy as np

D_MODEL = 1024
BATCH = 8
SEQ = 4096
DEPTH = 2

GRID_W = 64
CTX_LEN = 256

GDN_HEADS = 4
GDN_DK = 128
GDN_DV = 128
GDN_QK = GDN_HEADS * GDN_DK
GDN_V = GDN_HEADS * GDN_DV
GDN_CHUNK = 64
GDN_SHORT_CONV = 5
POOL_WINDOWS = (2, 4, 8, 16)
N_POOL = len(POOL_WINDOWS)
POOL_DIM = D_MODEL // 2
POOL_GROUP = POOL_DIM // N_POOL
E_K = GDN_QK
E_V = 2 * GDN_QK
E_GATE = E_V + GDN_V
E_POOL = E_GATE + GDN_V
E_SCAL = E_POOL + POOL_DIM
EVEN_IN = E_SCAL + 4 * GDN_HEADS
EVEN_MIX = GDN_V + POOL_DIM

SC_DIM = D_MODEL // 2
SC_WIDTH = 3
CF_DIM = D_MODEL // 2
CF_WIDTH = 31
ODD_IN = 3 * SC_DIM + 2 * CF_DIM
ODD_MIX = SC_DIM + CF_DIM

D_FF = ((8 * D_MODEL // 3 + 127) // 128) * 128
FFN_CONV = 3

ALPHA = (2 * DEPTH) ** 0.25
BETA = (8 * DEPTH) ** -0.25
LN_EPS = 1e-5
RMS_EPS = 1e-6

kernel_name = 'hybrid_gdn_pool_conv_diffusion_block'


def _layernorm(x, g, b):
    xf = x.astype(jnp.float32)
    mu = jnp.mean(xf, -1, keepdims=True)
    var = jnp.mean(jnp.square(xf - mu), -1, keepdims=True)
    return ((xf - mu) * lax.rsqrt(var + LN_EPS)).astype(x.dtype) * g + b


def _modulate(h, shift, scale):
    return h * (1 + scale) + shift


def _dwconv1d(x, w):
    width, ch = w.shape
    return lax.conv_general_dilated(x, w[:, None, :], window_strides=(1,), padding=[(width // 2, width // 2)], dimension_numbers=('NWC', 'WIO', 'NWC'), feature_group_count=ch)


def _dwconv2d_grid(x, w):
    b, t, ch = x.shape
    rows = t // GRID_W
    xg = x.reshape(b, rows, GRID_W, ch)
    y = lax.conv_general_dilated(xg, w[:, :, None, :], window_strides=(1, 1), padding=[(FFN_CONV // 2, FFN_CONV // 2)] * 2, dimension_numbers=('NHWC', 'HWIO', 'NHWC'), feature_group_count=ch)
    return y.reshape(b, t, ch)


def _heads(t):
    b, n, hd = t.shape
    return t.reshape(b, n, GDN_HEADS, hd // GDN_HEADS).transpose(0, 2, 1, 3).astype(jnp.float32)


def _l2norm(t):
    return t * lax.rsqrt(jnp.sum(t * t, -1, keepdims=True) + RMS_EPS)


def _gdn_gates(s, a_log, dt_bias):
    b, n, _ = s.shape
    s = s.astype(jnp.float32).reshape(b, n, 4, GDN_HEADS).transpose(2, 0, 3, 1)
    beta = jax.nn.sigmoid(s[:2])
    g = -jnp.exp(a_log.astype(jnp.float32))[:, None, :, None] * jax.nn.softplus(s[2:] + dt_bias.astype(jnp.float32)[:, None, :, None])
    return beta, g


def _chunk_masks():
    idx = jnp.arange(GDN_CHUNK)
    return idx[:, None] >= idx[None, :], idx[:, None] > idx[None, :]


def _gdn_chunk_terms(k, v, g, beta):
    b, h, n, _ = k.shape
    nc = n // GDN_CHUNK
    k = k.reshape(b, h, nc, GDN_CHUNK, GDN_DK)
    v = v.reshape(b, h, nc, GDN_CHUNK, GDN_DV)
    beta = beta.reshape(b, h, nc, GDN_CHUNK, 1)
    gc = jnp.cumsum(g.reshape(b, h, nc, GDN_CHUNK), axis=-1)
    lower, strict = _chunk_masks()
    decay = jnp.exp(jnp.where(lower, gc[..., :, None] - gc[..., None, :], -jnp.inf))
    kb = k * beta
    a_mat = jnp.where(strict, jnp.einsum('bhnck,bhnsk->bhncs', kb, k) * decay, 0.0)
    rhs = jnp.concatenate([v * beta, kb * jnp.exp(gc)[..., None]], axis=-1)
    sol = lax.linalg.triangular_solve(a_mat + jnp.eye(GDN_CHUNK, dtype=a_mat.dtype), rhs, left_side=True, lower=True, unit_diagonal=True)
    u, w = sol[..., :GDN_DV], sol[..., GDN_DV:]
    k_dec = k * jnp.exp(gc[..., -1:] - gc)[..., None]
    g_last = jnp.exp(gc[..., -1])
    return gc, decay, u, w, k_dec, g_last


def _state_step(s, u_i, w_i, kd_i, gl_i):
    v_new = u_i - jnp.einsum('bhck,bhkv->bhcv', w_i, s)
    s_next = s * gl_i[..., None, None] + jnp.einsum('bhck,bhcv->bhkv', kd_i, v_new)
    return s_next, v_new


def _chunks_first(*arrays):
    return tuple(jnp.moveaxis(a, 2, 0) for a in arrays)


def _gdn_final_state(k, v, g, beta, s0):
    _, _, u, w, k_dec, g_last = _gdn_chunk_terms(k, v, g, beta)

    def step(s, xs):
        return _state_step(s, *xs)[0], None

    s, _ = lax.scan(step, s0, _chunks_first(u, w, k_dec, g_last))
    return s


def _gdn_attend(q, k, v, g, beta, s0):
    b, h, n, _ = q.shape
    gc, decay, u, w, k_dec, g_last = _gdn_chunk_terms(k, v, g, beta)
    qc = q.reshape(b, h, n // GDN_CHUNK, GDN_CHUNK, GDN_DK)
    kc = k.reshape(b, h, n // GDN_CHUNK, GDN_CHUNK, GDN_DK)
    lower, _ = _chunk_masks()
    attn = jnp.where(lower, jnp.einsum('bhnck,bhnsk->bhncs', qc, kc) * decay, 0.0)
    q_dec = qc * jnp.exp(gc)[..., None]

    def step(s, xs):
        u_i, w_i, kd_i, gl_i, qd_i, at_i = xs
        s_next, v_new = _state_step(s, u_i, w_i, kd_i, gl_i)
        o_i = jnp.einsum('bhck,bhkv->bhcv', qd_i, s) + jnp.einsum('bhcs,bhsv->bhcv', at_i, v_new)
        return s_next, o_i

    _, o = lax.scan(step, s0, _chunks_first(u, w, k_dec, g_last, q_dec, attn))
    return jnp.moveaxis(o, 0, 2).reshape(b, h, n, GDN_DV)


def _flip(a):
    return jnp.flip(a, axis=2)


def _multiscale_pool(p, pool_w, pool_scale):
    b, t, _ = p.shape
    pg = p.astype(jnp.float32).reshape(b, t, N_POOL, POOL_GROUP)
    cs = jnp.concatenate([jnp.zeros((b, 1, N_POOL, POOL_GROUP), jnp.float32), jnp.cumsum(pg, axis=1)], axis=1)
    pos = jnp.arange(t)
    groups = []
    for gi, win in enumerate(POOL_WINDOWS):
        lo = jnp.clip(pos - win // 2, 0, t)
        hi = jnp.clip(pos - win // 2 + win, 0, t)
        csg = cs[:, :, gi]
        mean = (csg[:, hi] - csg[:, lo]) / (hi - lo).astype(jnp.float32)[None, :, None]
        groups.append(mean - pg[:, :, gi])
    pooled = jnp.stack(groups, axis=2).astype(p.dtype)
    y = jnp.einsum('btgc,gcd->btgd', pooled, pool_w)
    return y.reshape(b, t, POOL_DIM) * pool_scale


def _even_mixer(u, ctx_u, w_in, w_out, conv_w, a_log, dt_bias, norm_w, pool_w, pool_scale):
    b, n, _ = u.shape
    p = u @ w_in
    qkv = jax.nn.silu(_dwconv1d(p[..., :E_GATE], conv_w))
    q = _l2norm(_heads(qkv[..., :E_K])) * GDN_DK ** -0.5
    k = _l2norm(_heads(qkv[..., E_K:E_V]))
    v = _heads(qkv[..., E_V:])
    beta, g = _gdn_gates(p[..., E_SCAL:], a_log, dt_bias)
    pc = ctx_u @ jnp.concatenate([w_in[:, E_K:E_GATE], w_in[:, E_SCAL:]], axis=1)
    kv_c = jax.nn.silu(_dwconv1d(pc[..., :E_GATE - E_K], conv_w[:, E_K:E_GATE]))
    k_c = _l2norm(_heads(kv_c[..., :GDN_QK]))
    v_c = _heads(kv_c[..., GDN_QK:])
    beta_c, g_c = _gdn_gates(pc[..., E_GATE - E_K:], a_log, dt_bias)
    s0 = jnp.zeros((b, GDN_HEADS, GDN_DK, GDN_DV), jnp.float32)
    s_fwd = _gdn_final_state(k_c, v_c, g_c[0], beta_c[0], s0)
    s_bwd = _gdn_final_state(_flip(k_c), _flip(v_c), _flip(g_c[1]), _flip(beta_c[1]), s0)
    o = _gdn_attend(q, k, v, g[0], beta[0], s_fwd) + _flip(_gdn_attend(_flip(q), _flip(k), _flip(v), _flip(g[1]), _flip(beta[1]), s_bwd))
    o = o.transpose(0, 2, 1, 3)
    o = o * lax.rsqrt(jnp.mean(o * o, -1, keepdims=True) + RMS_EPS) * norm_w.astype(jnp.float32)
    o = o.reshape(b, n, GDN_V).astype(u.dtype) * jax.nn.silu(p[..., E_GATE:E_POOL])
    y_pool = _multiscale_pool(p[..., E_POOL:E_SCAL], pool_w, pool_scale)
    return jnp.concatenate([o, y_pool], axis=-1) @ w_out


def _odd_mixer(u, w_in, w_out, sconv_w, conf_conv_w, conf_ln_g, conf_ln_b):
    p = u @ w_in
    g_b, g_c, h, glu_a, glu_b = jnp.split(p, [SC_DIM, 2 * SC_DIM, 3 * SC_DIM, 3 * SC_DIM + CF_DIM], axis=-1)
    y_sc = g_b * _dwconv1d(g_c * h, sconv_w)
    z = _dwconv1d(glu_a * jax.nn.sigmoid(glu_b), conf_conv_w)
    z = jax.nn.silu(_layernorm(z, conf_ln_g, conf_ln_b))
    return jnp.concatenate([y_sc, z], axis=-1) @ w_out


def _conv_ffn(u, w_up, conv_w, w_down):
    a, gate = jnp.split(u @ w_up, 2, axis=-1)
    return (jax.nn.silu(_dwconv2d_grid(a, conv_w)) * gate) @ w_down


def setup_inputs(seed: int = 0) -> dict:
    key = jax.random.key(seed)
    ks = jax.random.split(key, 26)
    D = D_MODEL

    def nrm(k, shape, scale=1.0):
        return jax.random.normal(k, shape, jnp.float32) * scale

    dt = jnp.exp(jax.random.uniform(ks[10], (2, GDN_HEADS), jnp.float32, math.log(1e-3), math.log(1e-1)))
    return {
        'x': nrm(ks[0], (BATCH, SEQ, D)),
        'c': nrm(ks[1], (BATCH, D)),
        'ctx': nrm(ks[2], (BATCH, CTX_LEN, D)),
        'c_ctx': nrm(ks[3], (D,)),
        'ada_w': nrm(ks[4], (DEPTH, D, 6 * D), 0.5 * D ** -0.5),
        'ada_b': nrm(ks[5], (DEPTH, 6 * D), 0.02),
        'ln_g': 1.0 + nrm(ks[6], (DEPTH, 2, D), 0.02),
        'ln_b': nrm(ks[7], (DEPTH, 2, D), 0.02),
        'even_w_in': nrm(ks[8], (D, EVEN_IN), D ** -0.5),
        'even_w_out': nrm(ks[9], (EVEN_MIX, D), BETA * EVEN_MIX ** -0.5),
        'gdn_conv_w': nrm(ks[11], (GDN_SHORT_CONV, E_GATE), GDN_SHORT_CONV ** -0.5),
        'gdn_a_log': jnp.log(jax.random.uniform(ks[12], (2, GDN_HEADS), jnp.float32, 1.0, 16.0)),
        'gdn_dt_bias': dt + jnp.log(-jnp.expm1(-dt)),
        'gdn_norm_w': 1.0 + nrm(ks[13], (GDN_DV,), 0.02),
        'pool_w': nrm(ks[14], (N_POOL, POOL_GROUP, POOL_GROUP), POOL_GROUP ** -0.5),
        'pool_scale': 1.0 + nrm(ks[15], (POOL_DIM,), 0.1),
        'odd_w_in': nrm(ks[16], (D, ODD_IN), D ** -0.5),
        'odd_w_out': nrm(ks[17], (ODD_MIX, D), BETA * ODD_MIX ** -0.5),
        'sconv_w': nrm(ks[18], (SC_WIDTH, SC_DIM), SC_WIDTH ** -0.5),
        'conf_conv_w': nrm(ks[19], (CF_WIDTH, CF_DIM), CF_WIDTH ** -0.5),
        'conf_ln_g': 1.0 + nrm(ks[20], (CF_DIM,), 0.02),
        'conf_ln_b': nrm(ks[21], (CF_DIM,), 0.02),
        'ffn_w_up': nrm(ks[22], (DEPTH, D, 2 * D_FF), D ** -0.5),
        'ffn_conv_w': nrm(ks[23], (DEPTH, FFN_CONV, FFN_CONV, D_FF), 1.0 / FFN_CONV),
        'ffn_w_down': nrm(ks[24], (DEPTH, D_FF, D), BETA * D_FF ** -0.5),
    }


def reference(x, c, ctx, c_ctx, ada_w, ada_b, ln_g, ln_b, even_w_in, even_w_out, gdn_conv_w, gdn_a_log, gdn_dt_bias, gdn_norm_w, pool_w, pool_scale, odd_w_in, odd_w_out, sconv_w, conf_conv_w, conf_ln_g, conf_ln_b, ffn_w_up, ffn_conv_w, ffn_w_down):
    D = D_MODEL
    silu_c = jax.nn.silu(c)
    silu_cc = jax.nn.silu(c_ctx)
    for layer in range(DEPTH):
        mod = silu_c @ ada_w[layer] + ada_b[layer]
        sh_m, sc_m, gt_m, sh_f, sc_f, gt_f = [m[:, None, :] for m in jnp.split(mod, 6, axis=-1)]
        u = _modulate(x, sh_m, sc_m)
        if layer % 2 == 0:
            mod_c = silu_cc @ ada_w[layer][:, :2 * D] + ada_b[layer][:2 * D]
            ctx_u = _modulate(ctx, mod_c[:D], mod_c[D:])
            y = _even_mixer(u, ctx_u, even_w_in, even_w_out, gdn_conv_w, gdn_a_log, gdn_dt_bias, gdn_norm_w, pool_w, pool_scale)
        else:
            y = _odd_mixer(u, odd_w_in, odd_w_out, sconv_w, conf_conv_w, conf_ln_g, conf_ln_b)
        x = _layernorm(ALPHA * x + gt_m * y, ln_g[layer, 0], ln_b[layer, 0])
        u = _modulate(x, sh_f, sc_f)
        y = _conv_ffn(u, ffn_w_up[layer], ffn_conv_w[layer], ffn_w_down[layer])
        x = _layernorm(ALPHA * x + gt_f * y, ln_g[layer, 1], ln_b[layer, 1])
    return x
```

```python
import numpy as np
from contextlib import ExitStack
import concourse.bass as bass
import concourse.mybir as mybir
from concourse.bass_utils import run_bass_kernel_spmd

F32 = mybir.dt.float32
BF16 = mybir.dt.bfloat16
AF = mybir.ActivationFunctionType
ALU = mybir.AluOpType

D = 1024
T = 4096
TC = 256
NCORES = 8
DFF = 2816
ALPHA = 4 ** 0.25
LN_EPS = 1e-5
RMS_EPS = 1e-6
BIG = 30000.0
ENG = ('pe', 'act', 'dve', 'pool', 'sp')
NDS = 24

DEBUG_OUT = []


class Res:
    __slots__ = ('w', 'r', 'excl')

    def __init__(self):
        self.w = None
        self.r = []
        self.excl = False


class TileT:
    def __init__(self, t):
        self.t = t
        self.res = Res()

    def __getitem__(self, k):
        return self.t[k]


class Sched:
    def __init__(self, nc, stack):
        self.nc = nc
        self.sem = {e: stack.enter_context(nc.semaphore('s_' + e)) for e in ENG}
        self.cnt = {e: 0 for e in ENG}
        self.known = {e: {} for e in ENG}
        self.q = {e: [] for e in ENG}
        self.dsem = [stack.enter_context(nc.semaphore('dq%d' % i)) for i in range(NDS)]
        self.dcnt = [0] * NDS
        self.dpool = {'sp': list(range(0, 12)), 'act': list(range(12, 20)), 'pool': list(range(20, 24))}
        self.dnext = {'sp': 0, 'act': 0, 'pool': 0}
        self.unsig = {e: False for e in ENG}

    def semof(self, key):
        return self.sem[key] if isinstance(key, str) else self.dsem[key]

    def _collect(self, eng, reads, writes):
        toks = []
        for r in reads:
            if r.w is not None:
                toks.append(r.w)
        for w in writes:
            if w.w is not None and (w.w[0] != eng or eng != 'pe'):
                toks.append(w.w)
            for t in w.r:
                if t[0] != eng or eng != 'pe':
                    toks.append(t)
        waits = {}
        kn = self.known[eng]
        for key, val in toks:
            if kn.get(key, 0) < val:
                waits[key] = max(waits.get(key, 0), val)
        for key, val in waits.items():
            kn[key] = val
        return list(waits.items())

    def _update(self, tok, reads, writes):
        for r in reads:
            r.r.append(tok)
        for w in writes:
            w.w = tok
            w.r = []

    def emit(self, eng, fn, reads=(), writes=(), sig=True):
        reads = [getattr(x, "res", x) for x in reads]
        writes = [getattr(x, "res", x) for x in writes]
        writes = writes + [r for r in reads if r.excl and eng != 'pe']
        waits = self._collect(eng, reads, writes)
        if sig:
            self.cnt[eng] += 1
            tok = (eng, self.cnt[eng])
            self.unsig[eng] = False
        else:
            tok = (eng, self.cnt[eng] + 1)
            self.unsig[eng] = True
        self.q[eng].append((waits, fn, sig, None))
        self._update(tok, reads, writes)

    def pe(self, fn, reads=(), writes=(), sig=True):
        self.emit('pe', fn, reads, writes, sig)

    def act(self, fn, reads=(), writes=()):
        self.emit('act', fn, reads, writes)

    def dve(self, fn, reads=(), writes=()):
        self.emit('dve', fn, reads, writes)

    def pool(self, fn, reads=(), writes=()):
        self.emit('pool', fn, reads, writes)

    def dma(self, out, in_, reads=(), writes=(), q='sp', **kw):
        reads = [getattr(x, "res", x) for x in reads]
        writes = [getattr(x, "res", x) for x in writes]
        pl = self.dpool[q]
        j = pl[self.dnext[q] % len(pl)]
        self.dnext[q] += 1
        waits = dict(self._collect(q, reads, writes))
        if self.dcnt[j] > 0 and self.known[q].get(j, 0) < self.dcnt[j]:
            waits[j] = self.dcnt[j]
            self.known[q][j] = self.dcnt[j]
        self.dcnt[j] += 16
        tok = (j, self.dcnt[j])
        self.q[q].append((list(waits.items()), lambda e: e.dma_start(out=out, in_=in_, **kw), False, j))
        self._update(tok, reads, writes)

    def barrier(self):
        for e in ENG:
            assert not self.unsig[e], e
        for e in ENG:
            waits = []
            for f in ENG:
                if f != e and self.known[e].get(f, 0) < self.cnt[f]:
                    waits.append((f, self.cnt[f]))
                    self.known[e][f] = self.cnt[f]
            for j in range(NDS):
                if self.dcnt[j] > 0 and self.known[e].get(j, 0) < self.dcnt[j]:
                    waits.append((j, self.dcnt[j]))
                    self.known[e][j] = self.dcnt[j]
            if waits:
                self.q[e].append((waits, None, False, None))

    def flush(self):
        nc = self.nc
        q = self.q
        self.q = {e: [] for e in ENG}

        def replay(eng, e):
            for waits, fn, sig, dj in q[eng]:
                for key, val in waits:
                    e.wait_ge(self.semof(key), val)
                if fn is None:
                    continue
                ins = fn(e)
                if dj is not None:
                    ins.then_inc(self.dsem[dj], 16)
                elif sig:
                    ins.then_inc(self.sem[eng], 1)

        with nc.Block() as block:
            @block.tensor
            def _(e):
                replay('pe', e)

            @block.scalar
            def _(e):
                replay('act', e)

            @block.vector
            def _(e):
                replay('dve', e)

            @block.gpsimd
            def _(e):
                replay('pool', e)

            @block.sync
            def _(e):
                replay('sp', e)


class Ctx:
    pass


_UID = [0]


def _uname(name):
    _UID[0] += 1
    return '%s_u%d' % (name, _UID[0])


def sb(nc, stack, name, shape, dt):
    return TileT(stack.enter_context(nc.sbuf_tensor(_uname(name), list(shape), dt)))


def ps(nc, stack, name, shape, dt):
    t = TileT(stack.enter_context(nc.psum_tensor(_uname(name), list(shape), dt)))
    t.res.excl = True
    return t


def build(debug_out=()):
    nc = bass.Bass("TRN2", target_bir_lowering=False)
    top = ExitStack()
    S = Sched(nc, top)
    C = Ctx()
    C.nc, C.S = nc, S
    din = {}

    def inp(name, shape):
        din[name] = TileT(nc.dram_tensor(name, list(shape), F32, kind="ExternalInput").ap())
        return din[name]

    inp('x', [T, D]); inp('c', [1, D]); inp('ctx', [TC, D]); inp('c_ctx', [1, D])
    inp('ada_w', [2, D, 6 * D]); inp('ada_b', [2, 6 * D])
    inp('ln_g', [4, D]); inp('ln_b', [4, D])
    inp('even_w_in', [D, 2576]); inp('even_w_out', [D, D])
    inp('gdn_conv_w', [5, 1536]); inp('gdn_a_log', [1, 8]); inp('gdn_dt_bias', [1, 8])
    inp('gdn_norm_w', [1, 128]); inp('pool_w', [4, 128, 128]); inp('pool_scale', [1, 512])
    inp('odd_w_in', [D, 2560]); inp('odd_w_out', [D, D])
    inp('sconv_w', [3, 512]); inp('conf_conv_w', [31, 512])
    inp('conf_ln_g', [1, 512]); inp('conf_ln_b', [1, 512])
    inp('ffn_w_up', [2, D, 2 * DFF]); inp('ffn_conv_w', [2, 9, DFF]); inp('ffn_w_down', [2, DFF, D])
    C.din = din
    C.out = TileT(nc.dram_tensor('out', [T, D], F32, kind="ExternalOutput").ap())

    def scratch(name, shape, dt):
        kind = "ExternalOutput" if name in debug_out else "Internal"
        t = TileT(nc.dram_tensor(name, list(shape), dt, kind=kind).ap())
        return t
    C.scratch = scratch

    C.ident_f = sb(nc, top, 'ident_f', [128, 128], F32)
    C.ident_b = sb(nc, top, 'ident_b', [128, 128], BF16)
    C.ones_f = sb(nc, top, 'ones_f', [128, 512], F32)
    C.ones_b = sb(nc, top, 'ones_b', [128, 128], BF16)
    C.zeros_b = sb(nc, top, 'zeros_b', [128, 512], BF16)
    C.modrow = sb(nc, top, 'modrow', [128, 6 * D], F32)
    st01 = ExitStack()
    C.modc = sb(nc, st01, 'modc', [128, 2 * D], F32)

    S.pool(lambda e: e.memset(C.ones_f[:], 1.0), writes=[C.ones_f])
    S.pool(lambda e: e.memset(C.ones_b[:], 1.0), writes=[C.ones_b])
    S.pool(lambda e: e.memset(C.zeros_b[:], 0.0), writes=[C.zeros_b])
    S.pool(lambda e: e.affine_select(out=C.ident_f[:], in_=C.ones_f[:, 0:128], pattern=[[-1, 128]],
                                     compare_op=ALU.is_equal, fill=0.0, base=0, channel_multiplier=1),
           reads=[C.ones_f], writes=[C.ident_f])
    S.pool(lambda e: e.tensor_copy(out=C.ident_b[:], in_=C.ident_f[:]), reads=[C.ident_f], writes=[C.ident_b])

    C.debug_out = debug_out
    phase_mod(C, 0)
    dbg_dump(C, 'dbg_mod0', C.modrow, C.modrow[0:1, :], [1, 6 * D], F32)
    dbg_dump(C, 'dbg_modc', C.modc, C.modc[0:1, :], [1, 2 * D], F32)
    phase_inproj0(C)
    st01.close()
    phase_qkv0(C)
    import os
    if os.environ.get('NOGDN') != '1':
        phase_gdn(C)
    else:
        C.OACC = C.scratch('OACC', [T, 512], F32)
        for r0 in range(0, T, 128):
            S.dma(C.OACC[r0:r0 + 128, :], C.ones_f[:, :], reads=[C.ones_f], writes=[C.OACC])
    PSTOP = int(os.environ.get('PSTOP', '99'))
    X1 = C.scratch('X1', [T, D], F32)
    X2 = C.scratch('X2', [T, D], F32)
    X3 = C.scratch('X3', [T, D], F32)
    if PSTOP >= 1:
        phase_mix0_out(C, X1)
    if PSTOP >= 2:
        phase_ffn_up(C, 0, X1)
    if PSTOP >= 3:
        phase_ffn_down(C, 0, X1, X2)
    if PSTOP >= 4:
        phase_mod(C, 1)
        phase_inproj1(C, X2)
    if PSTOP >= 5:
        phase_mix1_out(C, X2, X3)
    if PSTOP >= 6:
        phase_ffn_up(C, 1, X3)
        phase_ffn_down(C, 1, X3, C.out)

    S.barrier()
    S.flush()
    top.close()
    return nc


def dbg_dump(C, name, tile, ap, shape, dt):
    if name not in C.debug_out:
        return
    d = TileT(C.nc.dram_tensor(name, list(shape), dt, kind="ExternalOutput").ap())
    C.S.dma(d[:], ap, reads=[tile], writes=[d])


def to_col(C, st, psb, dram, R, ncol, name):
    nc, S = C.nc, C.S
    BL = 4
    tmp = sb(nc, st, name + '_row', [R, BL * 128], F32)
    outt = sb(nc, st, name + '_col', [128, ncol, R], F32)
    for c0 in range(0, ncol, BL):
        c1 = min(ncol, c0 + BL)
        S.dma(tmp[:, 0:(c1 - c0) * 128], dram[:, c0 * 128:c1 * 128], writes=[tmp])
        for c in range(c0, c1):
            S.pe(lambda e, c=c, c0=c0: e.transpose(out=psb[:, 0:R], in_=tmp[0:R, (c - c0) * 128:(c - c0 + 1) * 128],
                                                   identity=C.ident_f[0:R, 0:R]),
                 reads=[tmp, C.ident_f], writes=[psb])
            S.dve(lambda e, c=c: e.tensor_copy(out=outt[:, c, :], in_=psb[:, 0:R]), reads=[psb], writes=[outt])
    return outt


def phase_mod(C, layer):
    nc, S = C.nc, C.S
    st = ExitStack()
    pst = ps(nc, st, 'pm_t', [128, 512], F32)
    pacc = [ps(nc, st, 'pm_a%d' % i, [128, 512], F32) for i in range(2)]
    paccc = [ps(nc, st, 'pm_c%d' % i, [128, 512], F32) for i in range(2)]
    ccol = to_col(C, st, pst, C.din['c'][:, :], 1, 8, 'c')
    S.act(lambda e: e.activation(out=ccol[:], in_=ccol[:], func=AF.Silu), reads=[ccol], writes=[ccol])
    rep = sb(nc, st, 'c_rep', [128, 8, 128], F32)
    for k in range(8):
        S.dve(lambda e, k=k: e.tensor_scalar(out=rep[:, k, :], in0=C.ones_f[:, 0:128], scalar1=ccol[:, k, 0:1],
                                             scalar2=None, op0=ALU.mult), reads=[C.ones_f, ccol], writes=[rep])
    brow = sb(nc, st, 'adab_row', [1, 6 * D], F32)
    S.dma(brow[:], C.din['ada_b'][layer:layer + 1, :], writes=[brow])
    if layer == 0:
        cccol = to_col(C, st, pst, C.din['c_ctx'][:, :], 1, 8, 'cc')
        S.act(lambda e: e.activation(out=cccol[:], in_=cccol[:], func=AF.Silu), reads=[cccol], writes=[cccol])
        repc = sb(nc, st, 'cc_rep', [128, 8, 128], F32)
        for k in range(8):
            S.dve(lambda e, k=k: e.tensor_scalar(out=repc[:, k, :], in0=C.ones_f[:, 0:128],
                                                 scalar1=cccol[:, k, 0:1], scalar2=None, op0=ALU.mult),
                  reads=[C.ones_f, cccol], writes=[repc])
    wt = [sb(nc, st, 'adaw%d' % i, [128, 8, 512], F32) for i in range(2)]
    aw = C.din['ada_w']
    for n in range(12):
        w = wt[n % 2]
        S.dma(w[:], aw[layer, :, n * 512:(n + 1) * 512].rearrange("(k p) n -> p k n", p=128), writes=[w])
        pa = pacc[n % 2]
        for k in range(8):
            S.pe(lambda e, k=k, w=w, pa=pa: e.matmul(pa[:], lhsT=rep[:, k, :], rhs=w[:, k, :], start=(k == 0), stop=False),
                 reads=[rep, w], writes=[pa], sig=False)
        S.pe(lambda e, pa=pa, n=n: e.matmul(pa[:], lhsT=C.ones_f[0:1, 0:128], rhs=brow[0:1, n * 512:(n + 1) * 512],
                                            start=False, stop=True), reads=[C.ones_f, brow], writes=[pa])
        S.act(lambda e, pa=pa, n=n: e.activation(out=C.modrow[:, n * 512:(n + 1) * 512], in_=pa[:], func=AF.Copy),
              reads=[pa], writes=[C.modrow])
        if layer == 0 and n < 4:
            pc = paccc[n % 2]
            for k in range(8):
                S.pe(lambda e, k=k, w=w, pc=pc: e.matmul(pc[:], lhsT=repc[:, k, :], rhs=w[:, k, :], start=(k == 0), stop=False),
                     reads=[repc, w], writes=[pc], sig=False)
            S.pe(lambda e, pc=pc, n=n: e.matmul(pc[:], lhsT=C.ones_f[0:1, 0:128], rhs=brow[0:1, n * 512:(n + 1) * 512],
                                                start=False, stop=True), reads=[C.ones_f, brow], writes=[pc])
            S.dve(lambda e, pc=pc, n=n: e.tensor_copy(out=C.modc[:, n * 512:(n + 1) * 512], in_=pc[:]),
                  reads=[pc], writes=[C.modc])
    S.dve(lambda e: e.tensor_scalar_add(out=C.modrow[:, D:2 * D], in0=C.modrow[:, D:2 * D], scalar1=1.0),
          reads=[C.modrow], writes=[C.modrow])
    S.dve(lambda e: e.tensor_scalar_add(out=C.modrow[:, 4 * D:5 * D], in0=C.modrow[:, 4 * D:5 * D], scalar1=1.0),
          reads=[C.modrow], writes=[C.modrow])
    if layer == 0:
        S.dve(lambda e: e.tensor_scalar_add(out=C.modc[:, D:2 * D], in0=C.modc[:, D:2 * D], scalar1=1.0),
              reads=[C.modc], writes=[C.modc])
    S.barrier()
    S.flush()
    st.close()


def modulate_transpose(C, xt, nsub, shift, scale1, ub, uT, pT, evac_i):
    S = C.S
    for s in range(nsub):
        S.pool(lambda e, s=s: e.tensor_tensor(out=xt[:, s, :], in0=xt[:, s, :], in1=scale1, op=ALU.mult),
               reads=[xt, C.modrow, C.modc], writes=[xt])
        S.dve(lambda e, s=s: e.tensor_tensor(out=ub[:, s, :], in0=xt[:, s, :], in1=shift, op=ALU.add),
              reads=[xt, C.modrow, C.modc], writes=[ub])
    for k in range(8):
        p = pT[k % len(pT)]
        for s in range(nsub):
            S.pe(lambda e, s=s, k=k, p=p: e.transpose(out=p[:, s * 128:(s + 1) * 128], in_=ub[:, s, k * 128:(k + 1) * 128],
                                                      identity=C.ident_b[:]),
                 reads=[ub, C.ident_b], writes=[p], sig=(s == nsub - 1))
        if (k + evac_i) % 2 == 0:
            S.act(lambda e, k=k, p=p: e.activation(out=uT[:, k, 0:nsub * 128], in_=p[:, 0:nsub * 128], func=AF.Copy),
                  reads=[p], writes=[uT])
        else:
            S.dve(lambda e, k=k, p=p: e.tensor_copy(out=uT[:, k, 0:nsub * 128], in_=p[:, 0:nsub * 128]),
                  reads=[p], writes=[uT])


def load_w_bf16(C, st, name, dram, kchunks, ncols):
    nc, S = C.nc, C.S
    w = sb(nc, st, name, [128, kchunks, ncols], BF16)
    step = max(1, 4096 // ncols)
    for k0 in range(0, kchunks, step):
        k1 = min(kchunks, k0 + step)
        S.dma(w[:, k0:k1, :], dram[k0 * 128:k1 * 128, :].rearrange("(k p) n -> p k n", p=128), writes=[w], q='pool')
    return w


def phase_inproj0(C):
    nc, S = C.nc, C.S
    st = ExitStack()
    P0T = C.scratch('P0T', [2048, T + 16], BF16); C.P0T = P0T
    G0 = C.scratch('G0', [T, 512], BF16); C.G0 = G0
    SG = C.scratch('SG', [T + TC, 16], F32); C.SG = SG
    PCT = C.scratch('PCT', [1024, TC + 16], BF16); C.PCT = PCT
    w = load_w_bf16(C, st, 'w_in0', C.din['even_w_in'][:, :], 8, 2576)
    for r0 in range(0, 2048, 128):
        S.dma(P0T[r0:r0 + 128, 0:8], C.zeros_b[:, 0:8], reads=[C.zeros_b], writes=[P0T])
        S.dma(P0T[r0:r0 + 128, T + 8:T + 16], C.zeros_b[:, 0:8], reads=[C.zeros_b], writes=[P0T])
    for r0 in range(0, 1024, 128):
        S.dma(PCT[r0:r0 + 128, 0:8], C.zeros_b[:, 0:8], reads=[C.zeros_b], writes=[PCT])
        S.dma(PCT[r0:r0 + 128, TC + 8:TC + 16], C.zeros_b[:, 0:8], reads=[C.zeros_b], writes=[PCT])
    xt = [sb(nc, st, 'xt%d' % i, [128, 4, D], F32) for i in range(2)]
    ub = [sb(nc, st, 'ub%d' % i, [128, 4, D], BF16) for i in range(2)]
    uT = [sb(nc, st, 'uT%d' % i, [128, 8, 512], BF16) for i in range(2)]
    pstg = [sb(nc, st, 'pstg%d' % i, [128, 4, 512], BF16) for i in range(2)]
    gstg = [sb(nc, st, 'gstg%d' % i, [128, 4, 512], BF16) for i in range(2)]
    sstg = [sb(nc, st, 'sstg%d' % i, [128, 4, 16], F32) for i in range(2)]
    pT = [ps(nc, st, 'pT%d' % i, [128, 512], BF16) for i in range(2)]
    pm = [ps(nc, st, 'pm%d' % i, [128, 512], F32) for i in range(4)]
    pss = ps(nc, st, 'pss', [128, 4, 16], F32)
    x = C.din['x']
    tiles = [('ctx', 0)] + [('lat', i) for i in range(8)]

    def load(i):
        kind, t = tiles[i]
        b = xt[i % 2]
        if kind == 'ctx':
            S.dma(b[:, 0:2, :], C.din['ctx'][:, :].rearrange("(s p) d -> p s d", p=128), writes=[b])
        else:
            S.dma(b[:, :, :], x[t * 512:(t + 1) * 512, :].rearrange("(s p) d -> p s d", p=128), writes=[b])

    load(0)
    pmi = 0
    for i, (kind, t) in enumerate(tiles):
        if i + 1 < len(tiles):
            load(i + 1)
        b, u, ut = xt[i % 2], ub[i % 2], uT[i % 2]
        isctx = kind == 'ctx'
        nsub = 2 if isctx else 4
        ntok = nsub * 128
        if isctx:
            modulate_transpose(C, b, nsub, C.modc[:, 0:D], C.modc[:, D:2 * D], u, ut, pT, i)
            fchunks = list(range(4, 12))
        else:
            modulate_transpose(C, b, nsub, C.modrow[:, 0:D], C.modrow[:, D:2 * D], u, ut, pT, i)
            fchunks = list(range(0, 12)) + list(range(16, 20))
        for gi in range(0, len(fchunks), 4):
            grp = fchunks[gi:gi + 4]
            stg = pstg[(gi // 4) % 2]
            for j, fc in enumerate(grp):
                p = pm[pmi % 4]; pmi += 1
                for k in range(8):
                    S.pe(lambda e, p=p, k=k, fc=fc, ut=ut, ntok=ntok: e.matmul(
                        p[:, 0:ntok], lhsT=w[:, k, fc * 128:(fc + 1) * 128], rhs=ut[:, k, 0:ntok],
                        start=(k == 0), stop=(k == 7)), reads=[w, ut], writes=[p], sig=(k == 7))
                if j % 2 == 0:
                    S.act(lambda e, p=p, j=j, stg=stg, ntok=ntok: e.activation(out=stg[:, j, 0:ntok], in_=p[:, 0:ntok], func=AF.Copy),
                          reads=[p], writes=[stg])
                else:
                    S.dve(lambda e, p=p, j=j, stg=stg, ntok=ntok: e.tensor_copy(out=stg[:, j, 0:ntok], in_=p[:, 0:ntok]),
                          reads=[p], writes=[stg])
            if isctx:
                r0 = (grp[0] - 4) * 128
                dst = PCT[r0:r0 + 512, 8:8 + ntok].rearrange("(j p) n -> p j n", p=128)
                S.dma(dst, stg[:, :, 0:ntok], reads=[stg], writes=[PCT], q='act')
            else:
                fc0 = grp[0]
                r0 = fc0 * 128 if fc0 < 12 else (fc0 - 4) * 128
                dst = P0T[r0:r0 + 512, 8 + t * 512:8 + (t + 1) * 512].rearrange("(j p) n -> p j n", p=128)
                S.dma(dst, stg[:, :, :], reads=[stg], writes=[P0T], q='act')
        gs = gstg[i % 2]
        ss = sstg[i % 2]
        for s in range(nsub):
            if not isctx:
                p = pm[pmi % 4]; pmi += 1
                for k in range(8):
                    S.pe(lambda e, p=p, k=k, s=s, ut=ut: e.matmul(p[:], lhsT=ut[:, k, s * 128:(s + 1) * 128], rhs=w[:, k, 1536:2048],
                                                            start=(k == 0), stop=(k == 7)), reads=[w, ut], writes=[p], sig=(k == 7))
                S.act(lambda e, p=p, s=s, gs=gs: e.activation(out=gs[:, s, :], in_=p[:], func=AF.Silu), reads=[p], writes=[gs])
            for k in range(8):
                S.pe(lambda e, k=k, s=s, ut=ut: e.matmul(pss[:, s, :], lhsT=ut[:, k, s * 128:(s + 1) * 128], rhs=w[:, k, 2560:2576],
                                                       start=(k == 0), stop=(k == 7)), reads=[w, ut], writes=[pss], sig=(k == 7))
        S.dve(lambda e, ss=ss, nsub=nsub: e.tensor_copy(out=ss[:, 0:nsub, :], in_=pss[:, 0:nsub, :]), reads=[pss], writes=[ss])
        if isctx:
            S.dma(SG[T:T + TC, :].rearrange("(s p) c -> p s c", p=128), ss[:, 0:2, :], reads=[ss], writes=[SG], q='act')
        else:
            S.dma(SG[t * 512:(t + 1) * 512, :].rearrange("(s p) c -> p s c", p=128), ss[:, :, :], reads=[ss], writes=[SG], q='act')
            S.dma(G0[t * 512:(t + 1) * 512, :].rearrange("(s p) c -> p s c", p=128), gs[:, :, :], reads=[gs], writes=[G0], q='act')
    S.barrier()
    S.flush()
    st.close()


def build_diag(C, st, psb, dram, R, ncol, name):
    nc, S = C.nc, C.S
    cw = to_col(C, st, psb, dram, R, ncol, name)
    dg = sb(nc, st, name + '_dg', [128, ncol, R, 128], BF16)
    i = 0
    for c in range(ncol):
        for r in range(R):
            fn = lambda e, c=c, r=r: e.tensor_scalar(out=dg[:, c, r, :], in0=C.ident_b[:], scalar1=cw[:, c, r:r + 1],
                                                     scalar2=None, op0=ALU.mult)
            if i % 2 == 0:
                S.dve(fn, reads=[C.ident_b, cw], writes=[dg])
            else:
                S.pool(fn, reads=[C.ident_b, cw], writes=[dg])
            i += 1
    return dg


def phase_qkv0(C):
    nc, S = C.nc, C.S
    st = ExitStack()
    QT = C.scratch('QT', [512, T], BF16); C.QT = QT
    KT = C.scratch('KT', [512, T + TC], BF16); C.KT = KT
    QTOK = C.scratch('QTOK', [T, 512], BF16); C.QTOK = QTOK
    KTOK = C.scratch('KTOK', [T + TC, 512], BF16); C.KTOK = KTOK
    VTOK = C.scratch('VTOK', [T + TC, 512], BF16); C.VTOK = VTOK
    YPT = C.scratch('YPT', [512, T], BF16); C.YPT = YPT
    pconv = [ps(nc, st, 'pconv%d' % i, [128, 512], F32) for i in range(2)]
    pssq = [ps(nc, st, 'pssq%d' % i, [128, 512], F32) for i in range(2)]
    pT = [ps(nc, st, 'pTq%d' % i, [128, 512], BF16) for i in range(2)]
    ppool = ps(nc, st, 'ppool', [128, 512], F32)
    dg = build_diag(C, st, pconv[0], C.din['gdn_conv_w'][:, :], 5, 12, 'cw5')
    pscale = to_col(C, st, pconv[1], C.din['pool_scale'][:, :], 1, 4, 'pscale')
    poolw = sb(nc, st, 'poolw', [128, 4, 128], BF16)
    S.dma(poolw[:], C.din['pool_w'][:, :, :].rearrange("g c d -> c g d"), writes=[poolw], q='pool')
    corrF = sb(nc, st, 'corrF', [128, 4, 8], F32)
    corrL = sb(nc, st, 'corrL', [128, 4, 8], F32)
    S.pool(lambda e: e.memset(corrF[:], 1.0), writes=[corrF])
    S.pool(lambda e: e.memset(corrL[:], 1.0), writes=[corrL])
    for g in range(4):
        hw = 1 << g
        for j in range(hw):
            S.pool(lambda e, g=g, j=j, hw=hw: e.memset(corrF[:, g, j:j + 1], 2.0 * hw / (j + hw)), writes=[corrF])
        for m in range(hw - 1):
            S.pool(lambda e, g=g, m=m, hw=hw: e.memset(corrL[:, g, 7 - m:8 - m], 2.0 * hw / (1 + m + hw)), writes=[corrL])
    pin = [sb(nc, st, 'pin%d' % i, [128, 12, 516], BF16) for i in range(2)]
    pp = [sb(nc, st, 'pp%d' % i, [128, 4, 528], BF16) for i in range(2)]
    xs8 = sb(nc, st, 'xs8', [128, 8, 512], F32)
    ss8 = sb(nc, st, 'ss8', [128, 8, 512], F32)
    epsb = sb(nc, st, 'epsb', [128, 1], F32)
    S.pool(lambda e: e.memset(epsb[:], RMS_EPS), writes=[epsb])
    sqb = [sb(nc, st, 'sqb%d' % i, [128, 512], BF16) for i in range(2)]
    qkn = [sb(nc, st, 'qkn0', [128, 12, 512], BF16)] * 2
    tokst = [[sb(nc, st, 'tok%d_%d' % (g, i), [128, 4, 512], BF16) for i in range(2)] for g in range(3)]
    wa = [sb(nc, st, 'wa%d' % i, [128, 528], F32) for i in range(2)]
    wb = [sb(nc, st, 'wb%d' % i, [128, 528], F32) for i in range(2)]
    pld = [sb(nc, st, 'pld%d' % i, [128, 4, 512], BF16) for i in range(2)]
    ypst = [sb(nc, st, 'ypst%d' % i, [128, 4, 512], BF16) for i in range(2)]
    tiles = [('ctx', 0)] + [('lat', i) for i in range(8)]

    def load(i):
        kind, t = tiles[i]
        b = pin[i % 2]
        if kind == 'ctx':
            S.dma(b[:, 4:12, 0:260], C.PCT[:, 6:266].rearrange("(f p) n -> p f n", p=128), reads=[C.PCT], writes=[b])
        else:
            S.dma(b[:, :, :], C.P0T[0:1536, 6 + t * 512:6 + t * 512 + 516].rearrange("(f p) n -> p f n", p=128),
                  reads=[C.P0T], writes=[b])
            S.dma(pp[i % 2][:, :, :], C.P0T[1536:2048, t * 512:t * 512 + 528].rearrange("(f p) n -> p f n", p=128),
                  reads=[C.P0T], writes=[pp[i % 2]])

    load(0)
    ci = 0
    for i, (kind, t) in enumerate(tiles):
        if i + 1 < len(tiles):
            load(i + 1)
        isctx = kind == 'ctx'
        ntok = 256 if isctx else 512
        nsub = ntok // 128
        b = pin[i % 2]
        qk = qkn[i % 2]
        for fc in (range(4, 12) if isctx else range(12)):
            pc = pconv[ci % 2]
            sq_ = sqb[ci % 2]; pq = pssq[ci % 2]
            ci += 1
            for tap in range(5):
                S.pe(lambda e, pc=pc, fc=fc, tap=tap, b=b, ntok=ntok: e.matmul(
                    pc[:, 0:ntok], lhsT=dg[:, fc, tap, :], rhs=b[:, fc, tap:tap + ntok], start=(tap == 0), stop=(tap == 4)),
                    reads=[dg, b], writes=[pc], sig=(tap == 4))
            if fc >= 8:
                S.act(lambda e, pc=pc, fc=fc, qk=qk, ntok=ntok: e.activation(out=qk[:, fc, 0:ntok], in_=pc[:, 0:ntok], func=AF.Silu),
                      reads=[pc], writes=[qk])
                continue
            S.act(lambda e, pc=pc, fc=fc, ntok=ntok: e.activation(out=xs8[:, fc, 0:ntok], in_=pc[:, 0:ntok], func=AF.Silu),
                  reads=[pc], writes=[xs8])
            S.pool(lambda e, fc=fc, sq_=sq_, ntok=ntok: e.tensor_tensor(out=sq_[:, 0:ntok], in0=xs8[:, fc, 0:ntok], in1=xs8[:, fc, 0:ntok], op=ALU.mult),
                   reads=[xs8], writes=[sq_])
            S.pe(lambda e, pq=pq, sq_=sq_, ntok=ntok: e.matmul(pq[:, 0:ntok], lhsT=C.ones_b[:], rhs=sq_[:, 0:ntok], start=True, stop=True),
                 reads=[C.ones_b, sq_], writes=[pq])
            S.dve(lambda e, pq=pq, fc=fc, ntok=ntok: e.tensor_copy(out=ss8[:, fc, 0:ntok], in_=pq[:, 0:ntok]), reads=[pq], writes=[ss8])
        f0 = 4 if isctx else 0
        S.act(lambda e, f0=f0, ntok=ntok: e.activation(out=ss8[:, f0:8, 0:ntok], in_=ss8[:, f0:8, 0:ntok], func=AF.Ln, bias=epsb[:, 0:1]),
              reads=[ss8, epsb], writes=[ss8])
        S.act(lambda e, f0=f0, ntok=ntok: e.activation(out=ss8[:, f0:8, 0:ntok], in_=ss8[:, f0:8, 0:ntok], func=AF.Exp, scale=-0.5),
              reads=[ss8], writes=[ss8])
        for fc in range(f0, 8):
            sc = (128.0 ** -0.5) if fc < 4 else 1.0
            fn = lambda e, qk=qk, fc=fc, sc=sc, ntok=ntok: e.scalar_tensor_tensor(
                out=qk[:, fc, 0:ntok], in0=xs8[:, fc, 0:ntok], scalar=sc, in1=ss8[:, fc, 0:ntok], op0=ALU.mult, op1=ALU.mult)
            if fc < 4:
                S.dve(fn, reads=[xs8, ss8], writes=[qk])
            else:
                S.pool(lambda e, qk=qk, fc=fc, ntok=ntok: e.tensor_tensor(out=qk[:, fc, 0:ntok], in0=xs8[:, fc, 0:ntok],
                                                                       in1=ss8[:, fc, 0:ntok], op=ALU.mult),
                       reads=[xs8, ss8], writes=[qk])
        ti = 0
        for g in ((1, 2) if isctx else (0, 1, 2)):
            tk = tokst[g][i % 2]
            for s_ in range(nsub):
                p = pT[ti % 2]; ti += 1
                for h in range(4):
                    S.pe(lambda e, p=p, h=h, g=g, s_=s_, qk=qk: e.transpose(out=p[:, h * 128:(h + 1) * 128],
                                                                       in_=qk[:, g * 4 + h, s_ * 128:(s_ + 1) * 128], identity=C.ident_b[:]),
                         reads=[qk, C.ident_b], writes=[p], sig=(h == 3))
                if ti % 2 == 0:
                    S.act(lambda e, p=p, tk=tk, s_=s_: e.activation(out=tk[:, s_, :], in_=p[:], func=AF.Copy), reads=[p], writes=[tk])
                else:
                    S.dve(lambda e, p=p, tk=tk, s_=s_: e.tensor_copy(out=tk[:, s_, :], in_=p[:]), reads=[p], writes=[tk])
        c0 = T if isctx else t * 512
        if not isctx:
            S.dma(QT[:, c0:c0 + 512].rearrange("(f p) n -> p f n", p=128), qk[:, 0:4, :], reads=[qk], writes=[QT], q='act')
            S.dma(QTOK[c0:c0 + 512, :].rearrange("(s p) c -> p s c", p=128), tokst[0][i % 2][:, :, :], reads=[tokst[0][i % 2]], writes=[QTOK], q='act')
        S.dma(KT[:, c0:c0 + ntok].rearrange("(f p) n -> p f n", p=128), qk[:, 4:8, 0:ntok], reads=[qk], writes=[KT], q='act')
        S.dma(KTOK[c0:c0 + ntok, :].rearrange("(s p) c -> p s c", p=128), tokst[1][i % 2][:, 0:nsub, :], reads=[tokst[1][i % 2]], writes=[KTOK], q='act')
        S.dma(VTOK[c0:c0 + ntok, :].rearrange("(s p) c -> p s c", p=128), tokst[2][i % 2][:, 0:nsub, :], reads=[tokst[2][i % 2]], writes=[VTOK], q='act')
        if isctx:
            continue
        ppb = pp[i % 2]
        pl = pld[i % 2]
        yp = ypst[i % 2]
        for g in range(4):
            a_, b_ = wa[g % 2], wb[g % 2]
            S.pool(lambda e, a_=a_, g=g, ppb=ppb: e.tensor_tensor(out=a_[:, 1:527], in0=ppb[:, g, 0:526], in1=ppb[:, g, 1:527], op=ALU.add),
                   reads=[ppb], writes=[a_])
            cur, oth = a_, b_
            lo, hi = 1, 527
            for lvl in range(g):
                sh = 1 << lvl
                lo, hi = lo + sh, hi - sh
                S.pool(lambda e, cur=cur, oth=oth, lo=lo, hi=hi, sh=sh: e.tensor_tensor(
                    out=oth[:, lo:hi], in0=cur[:, lo - sh:hi - sh], in1=cur[:, lo + sh:hi + sh], op=ALU.add),
                    reads=[cur], writes=[oth])
                cur, oth = oth, cur
            S.dve(lambda e, cur=cur, g=g: e.tensor_scalar(out=cur[:, 8:520], in0=cur[:, 8:520], scalar1=1.0 / (2 << g), scalar2=None, op0=ALU.mult),
                  reads=[cur], writes=[cur])
            if t == 0:
                S.dve(lambda e, cur=cur, g=g: e.tensor_tensor(out=cur[:, 8:16], in0=cur[:, 8:16], in1=corrF[:, g, :], op=ALU.mult),
                      reads=[cur, corrF], writes=[cur])
            if t == 7:
                S.dve(lambda e, cur=cur, g=g: e.tensor_tensor(out=cur[:, 512:520], in0=cur[:, 512:520], in1=corrL[:, g, :], op=ALU.mult),
                      reads=[cur, corrL], writes=[cur])
            S.dve(lambda e, cur=cur, g=g, pl=pl, ppb=ppb: e.tensor_tensor(out=pl[:, g, :], in0=cur[:, 8:520], in1=ppb[:, g, 8:520], op=ALU.subtract),
                  reads=[cur, ppb], writes=[pl])
            S.pe(lambda e, g=g, pl=pl: e.matmul(ppool[:], lhsT=poolw[:, g, :], rhs=pl[:, g, :], start=True, stop=True),
                 reads=[poolw, pl], writes=[ppool])
            S.act(lambda e, g=g, yp=yp: e.activation(out=yp[:, g, :], in_=ppool[:], func=AF.Identity, scale=pscale[:, g, 0:1]),
                  reads=[ppool, pscale], writes=[yp])
        S.dma(YPT[:, c0:c0 + 512].rearrange("(f p) n -> p f n", p=128), yp[:, :, :], reads=[yp], writes=[YPT], q='act')
    S.barrier()
    S.flush()
    st.close()


class Slot:
    def __init__(self, bank, k):
        self.f = bank.t[:, k * 128:(k + 1) * 128]
        self.b = bank.t[:, :].bitcast(BF16)[:, k * 256:k * 256 + 128]
        self.res = bank.res


def run_interleaved(gens):
    gens = list(gens)
    while gens:
        nxt = []
        for g in gens:
            try:
                next(g)
                nxt.append(g)
            except StopIteration:
                pass
        gens = nxt


def phase_gdn(C):
    nc, S = C.nc, C.S
    st = ExitStack()
    NT = 34
    import os
    OACC = C.scratch('OACC', [T, 512], F32); C.OACC = OACC
    banks = [ps(nc, st, 'gbank%d' % i, [128, 512], F32) for i in range(8)]
    for b_ in banks:
        b_.res.excl = True
    slots = [[Slot(banks[c], k) for k in range(4)] for c in range(8)]
    sall = sb(nc, st, 'sall', [128, NT, 16], F32)
    for n0 in ([] if os.environ.get('NOSALL') == '1' else range(0, NT, 6)):
        n1 = min(NT, n0 + 6)
        S.dma(sall[:, n0:n1, :], C.SG[n0 * 128:n1 * 128, :].rearrange("(n p) c -> p n c", p=128), reads=[C.SG], writes=[sall])
    adb = sb(nc, st, 'adb', [128, 16], F32)
    S.dma(adb[:, 0:8], C.din['gdn_a_log'][0:1, :].to_broadcast([128, 8]), writes=[adb])
    S.dma(adb[:, 8:16], C.din['gdn_dt_bias'][0:1, :].to_broadcast([128, 8]), writes=[adb])
    S.act(lambda e: e.activation(out=adb[:, 0:8], in_=adb[:, 0:8], func=AF.Exp), reads=[adb], writes=[adb])
    S.dve(lambda e: e.tensor_scalar(out=adb[:, 0:8], in0=adb[:, 0:8], scalar1=-1.0, scalar2=None, op0=ALU.mult),
          reads=[adb], writes=[adb])

    GCUT = int(os.environ.get('GCUT', '0'))

    def fin():
        S.barrier(); S.flush(); st.close()
    if GCUT == 1:
        return fin()

    def gt(name):
        return sb(nc, st, name, [128, NT, 8], F32)
    beta, g_, gc, eg, be, kds, gl = gt('g_beta'), gt('g_g'), gt('g_gc'), gt('g_eg'), gt('g_be'), gt('g_kds'), gt('g_gl')
    S.act(lambda e: e.activation(out=beta[:], in_=sall[:, :, 0:8], func=AF.Sigmoid), reads=[sall], writes=[beta])
    S.dve(lambda e: e.tensor_tensor(out=g_[:], in0=sall[:, :, 8:16], in1=adb[:, 8:16].unsqueeze(1).to_broadcast([128, NT, 8]), op=ALU.add),
          reads=[sall, adb], writes=[g_])
    S.act(lambda e: e.activation(out=g_[:], in_=g_[:], func=AF.Exp), reads=[g_], writes=[g_])
    S.act(lambda e: e.activation(out=g_[:], in_=g_[:], func=AF.Ln, bias=1.0), reads=[g_], writes=[g_])
    S.dve(lambda e: e.tensor_tensor(out=g_[:], in0=g_[:], in1=adb[:, 0:8].unsqueeze(1).to_broadcast([128, NT, 8]), op=ALU.mult),
          reads=[g_, adb], writes=[g_])
    if GCUT == 2:
        return fin()
    Lt = sb(nc, st, 'Lt', [128, 128], F32)
    Ut = sb(nc, st, 'Ut', [128, 128], F32)
    bigm = [sb(nc, st, 'bigm%d' % i, [128, 128], F32) for i in range(2)]
    strict = [sb(nc, st, 'strict%d' % i, [128, 128], F32) for i in range(2)]
    bigfull = sb(nc, st, 'bigfull', [128, 128], F32)
    S.pool(lambda e: e.memset(bigfull[:], BIG), writes=[bigfull])
    one = C.ones_f[:, 0:128]
    S.pool(lambda e: e.affine_select(out=Lt[:], in_=one, pattern=[[1, 128]], compare_op=ALU.is_ge, fill=0.0, base=0, channel_multiplier=-1),
           reads=[C.ones_f], writes=[Lt])
    S.pool(lambda e: e.affine_select(out=Ut[:], in_=one, pattern=[[-1, 128]], compare_op=ALU.is_ge, fill=0.0, base=0, channel_multiplier=1),
           reads=[C.ones_f], writes=[Ut])
    S.pool(lambda e: e.affine_select(out=bigm[0][:], in_=bigfull[:], pattern=[[1, 128]], compare_op=ALU.is_gt, fill=0.0, base=0, channel_multiplier=-1),
           reads=[bigfull], writes=[bigm[0]])
    S.pool(lambda e: e.affine_select(out=bigm[1][:], in_=bigfull[:], pattern=[[-1, 128]], compare_op=ALU.is_gt, fill=0.0, base=0, channel_multiplier=1),
           reads=[bigfull], writes=[bigm[1]])
    S.pool(lambda e: e.affine_select(out=strict[0][:], in_=one, pattern=[[-1, 128]], compare_op=ALU.is_gt, fill=0.0, base=0, channel_multiplier=1),
           reads=[C.ones_f], writes=[strict[0]])
    S.pool(lambda e: e.affine_select(out=strict[1][:], in_=one, pattern=[[1, 128]], compare_op=ALU.is_gt, fill=0.0, base=0, channel_multiplier=-1),
           reads=[C.ones_f], writes=[strict[1]])
    if GCUT == 3:
        return fin()
    Bm = {}
    for s_ in (16, 32, 64):
        G = 128 // s_
        E = sb(nc, st, 'E%d' % s_, [G, 128], F32)
        S.pool(lambda e, E=E, G=G, s_=s_: e.affine_select(out=E[:], in_=C.ones_f[0:G, 0:128], pattern=[[1, 128]], compare_op=ALU.is_ge,
                                                         fill=0.0, base=0, channel_multiplier=-s_), reads=[C.ones_f], writes=[E])
        S.pool(lambda e, E=E, G=G, s_=s_: e.affine_select(out=E[:], in_=E[:], pattern=[[-1, 128]], compare_op=ALU.is_gt,
                                                         fill=0.0, base=s_, channel_multiplier=s_), reads=[E], writes=[E])
        pb_ = banks[4]
        S.pe(lambda e, E=E, pb_=pb_: e.matmul(pb_[:, 0:128], lhsT=E[:], rhs=E[:], start=True, stop=True), reads=[E], writes=[pb_])
        Bm[s_] = sb(nc, st, 'Bm%d' % s_, [128, 128], F32)
        S.dve(lambda e, s_=s_, pb_=pb_: e.tensor_copy(out=Bm[s_][:], in_=pb_[:, 0:128]), reads=[pb_], writes=[Bm[s_]])
    Md = [sb(nc, st, 'Md%d' % d, [128, 128], F32) for d in range(2)]
    Mo = [[sb(nc, st, 'Mo%d_%d' % (d, l), [128, 128], F32) for l in range(3)] for d in range(2)]
    for d in range(2):
        S.dve(lambda e, d=d: e.tensor_tensor(out=Md[d][:], in0=strict[d][:], in1=Bm[16][:], op=ALU.mult), reads=[strict[d], Bm[16]], writes=[Md[d]])
        for l, (big_, small_) in enumerate(((32, 16), (64, 32), (None, 64))):
            t_ = Mo[d][l]
            if big_ is None:
                S.dve(lambda e, t_=t_, small_=small_: e.tensor_scalar(out=t_[:], in0=Bm[small_][:], scalar1=-1.0, scalar2=1.0, op0=ALU.mult, op1=ALU.add),
                      reads=[Bm[small_]], writes=[t_])
            else:
                S.dve(lambda e, t_=t_, big_=big_, small_=small_: e.tensor_tensor(out=t_[:], in0=Bm[big_][:], in1=Bm[small_][:], op=ALU.subtract),
                      reads=[Bm[big_], Bm[small_]], writes=[t_])
            S.dve(lambda e, t_=t_, d=d: e.tensor_tensor(out=t_[:], in0=t_[:], in1=strict[d][:], op=ALU.mult), reads=[t_, strict[d]], writes=[t_])
    pgc = banks[1]
    S.pe(lambda e: e.matmul(pgc[:, 0:NT * 8], lhsT=Lt[:], rhs=g_[:, :, :], start=True, stop=True), reads=[Lt, g_], writes=[pgc])
    S.dve(lambda e: e.tensor_copy(out=gc[:, :, 0:4], in_=pgc[:, 0:NT * 8].rearrange("p (n c) -> p n c", c=8)[:, :, 0:4]), reads=[pgc], writes=[gc])
    pgc2 = banks[2]
    S.pe(lambda e: e.matmul(pgc2[:, 0:NT * 8], lhsT=Ut[:], rhs=g_[:, :, :], start=True, stop=True), reads=[Ut, g_], writes=[pgc2])
    S.dve(lambda e: e.tensor_copy(out=gc[:, :, 4:8], in_=pgc2[:, 0:NT * 8].rearrange("p (n c) -> p n c", c=8)[:, :, 4:8]), reads=[pgc2], writes=[gc])
    if GCUT == 5:
        return fin()
    pgt = banks[3]
    S.pe(lambda e: e.matmul(pgt[:, 0:NT * 8], lhsT=C.ones_f[:, 0:128], rhs=g_[:, :, :], start=True, stop=True), reads=[C.ones_f, g_], writes=[pgt])
    S.act(lambda e: e.activation(out=gl[:], in_=pgt[:, 0:NT * 8].rearrange("p (n c) -> p n c", c=8), func=AF.Exp), reads=[pgt], writes=[gl])
    S.dve(lambda e: e.tensor_tensor(out=kds[:], in0=pgt[:, 0:NT * 8].rearrange("p (n c) -> p n c", c=8), in1=gc[:], op=ALU.subtract),
          reads=[pgt, gc], writes=[kds])
    if GCUT == 6:
        return fin()
    S.act(lambda e: e.activation(out=kds[:], in_=kds[:], func=AF.Exp), reads=[kds], writes=[kds])
    S.act(lambda e: e.activation(out=eg[:], in_=gc[:], func=AF.Exp), reads=[gc], writes=[eg])
    S.dve(lambda e: e.tensor_tensor(out=be[:], in0=beta[:], in1=eg[:], op=ALU.mult), reads=[beta, eg], writes=[be])
    if GCUT == 4:
        return fin()
    dbg_dump(C, 'dbg_gc', gc, gc[:, :, :], [128, NT, 8], F32)
    dbg_dump(C, 'dbg_beta', beta, beta[:, :, :], [128, NT, 8], F32)
    dbg_dump(C, 'dbg_g', g_, g_[:, :, :], [128, NT, 8], F32)

    S.barrier()
    import os
    GSTOP = int(os.environ.get('GSTOP', '99'))
    def tile_of(d, n):
        if n < 2:
            return 32 + n if d == 0 else 33 - n
        return n - 2 if d == 0 else 33 - n
    opnd = [[{k: sb(nc, st, 'op_%s_%d_%d' % (k, d, i), [128, 4, 128], BF16) for k in ('kT', 'qT', 'ktok', 'qtok', 'vtok')}
             for i in range(2)] for d in range(2)]

    def load_tile(d, n):
        nt = tile_of(d, n)
        o = opnd[d][n % 2]
        c0 = T + (nt - 32) * 128 if nt >= 32 else nt * 128
        S.dma(o['kT'][:, :, :], C.KT[:, c0:c0 + 128].rearrange("(h p) n -> p h n", p=128), reads=[C.KT], writes=[o['kT']])
        S.dma(o['ktok'][:, :, :], C.KTOK[c0:c0 + 128, :].rearrange("p (h d) -> p h d", d=128), reads=[C.KTOK], writes=[o['ktok']])
        S.dma(o['vtok'][:, :, :], C.VTOK[c0:c0 + 128, :].rearrange("p (h d) -> p h d", d=128), reads=[C.VTOK], writes=[o['vtok']])
        if nt < 32:
            S.dma(o['qT'][:, :, :], C.QT[:, c0:c0 + 128].rearrange("(h p) n -> p h n", p=128), reads=[C.QT], writes=[o['qT']])
            S.dma(o['qtok'][:, :, :], C.QTOK[c0:c0 + 128, :].rearrange("p (h d) -> p h d", d=128), reads=[C.QTOK], writes=[o['qtok']])

    def cb(name, dt, n=1, shape=(128, 128)):
        return [[sb(nc, st, '%s_%d_%d' % (name, c, i), list(shape), dt) for i in range(n)] for c in range(8)]
    dgc = cb('dgc', F32); Dm = dgc; Ai = cb('Ai', F32)
    Pb = cb('Pb', BF16, 2); PTb = cb('PTb', BF16, 2); Yb = cb('Yb', BF16, 2)
    bv = cb('bv', BF16); kbe = cb('kbe', BF16); qe = cb('qe', BF16); AOb = cb('AOb', BF16, 3)
    attnT = cb('attnT', BF16, 2); u_ = cb('u_', F32, 2); wT = cb('wT', BF16, 2); kd = cb('kd', BF16, 2); qdT = cb('qdT', BF16, 2)
    S32 = cb('S32', F32); Sbf = cb('Sbf', BF16, 2); vn = cb('vn', BF16)
    for c in range(8):
        S.pool(lambda e, c=c: e.memset(S32[c][0][:], 0.0), writes=[S32[c][0]])
        S.pool(lambda e, c=c: e.memset(Sbf[c][0][:], 0.0), writes=[Sbf[c][0]])
    oacc = sb(nc, st, 'oacc', [128, 32, 512], F32)
    ores = [[Res() for h in range(4)] for nt in range(32)]
    ofirst = [[True] * 4 for nt in range(32)]

    def precompute(c, n):
        d, h = c // 4, c % 4
        nt = tile_of(d, n)
        lat = nt < 32
        o = opnd[d][n % 2]
        r = n % 2
        sl = slots[c]
        gcol = gc[:, nt, c:c + 1]
        S.pool(lambda e: e.tensor_scalar(out=dgc[c][0][:], in0=C.ident_f[:], scalar1=gcol, scalar2=None, op0=ALU.mult),
               reads=[C.ident_f, gc], writes=[dgc[c][0]])
        S.pe(lambda e: e.matmul(sl[1].f, lhsT=o['kT'][:, h, :], rhs=o['kT'][:, h, :], start=True, stop=True),
             reads=[o['kT']], writes=[sl[1]])
        if lat:
            S.pe(lambda e: e.matmul(sl[2].f, lhsT=o['qT'][:, h, :], rhs=o['kT'][:, h, :], start=True, stop=True),
                 reads=[o['kT'], o['qT']], writes=[sl[2]])
        yield
        S.pe(lambda e: e.matmul(sl[0].f, lhsT=C.ones_f[:, 0:128], rhs=dgc[c][0][:], start=True, stop=False),
             reads=[C.ones_f, dgc[c][0]], writes=[sl[0]], sig=False)
        S.pe(lambda e: e.matmul(sl[0].f, lhsT=C.ident_f[:], rhs=bigm[d][:], start=False, stop=True),
             reads=[C.ident_f, bigm[d]], writes=[sl[0]])
        yield
        S.act(lambda e: e.activation(out=Dm[c][0][:], in_=sl[0].f, func=AF.Exp, bias=gcol, scale=-1.0),
              reads=[sl[0], gc], writes=[Dm[c][0]])
        yield
        S.dve(lambda e: e.scalar_tensor_tensor(out=Ai[c][0][:], in0=sl[1].f, scalar=beta[:, nt, c:c + 1], in1=Dm[c][0][:],
                                               op0=ALU.mult, op1=ALU.mult), reads=[sl[1], beta, Dm[c][0]], writes=[Ai[c][0]])
        if lat:
            S.dve(lambda e: e.tensor_tensor(out=qe[c][0][:], in0=sl[2].f, in1=Dm[c][0][:], op=ALU.mult),
                  reads=[sl[2], Dm[c][0]], writes=[qe[c][0]])
        yield
        A = Pb[c][0]
        S.pool(lambda e: e.tensor_tensor(out=A[:], in0=Ai[c][0][:], in1=Md[d][:], op=ALU.mult),
               reads=[Ai[c][0], Md[d]], writes=[A])
        for li in range(3):
            fn = lambda e, li=li: e.tensor_tensor(out=AOb[c][li][:], in0=Ai[c][0][:], in1=Mo[d][li][:], op=ALU.mult)
            if li == 1:
                S.dve(fn, reads=[Ai[c][0], Mo[d][li]], writes=[AOb[c][li]])
            else:
                S.pool(fn, reads=[Ai[c][0], Mo[d][li]], writes=[AOb[c][li]])
        yield
        S.pe(lambda e: e.transpose(out=sl[0].b, in_=A[:], identity=C.ident_b[:]), reads=[A, C.ident_b], writes=[sl[0]])
        if lat:
            S.pe(lambda e: e.transpose(out=sl[1].b, in_=qe[c][0][:], identity=C.ident_b[:]), reads=[qe[c][0], C.ident_b], writes=[sl[1]])
        yield
        AT = PTb[c][0]
        Y = Yb[c][0]
        S.act(lambda e: e.activation(out=AT[:], in_=sl[0].b, func=AF.Copy), reads=[sl[0]], writes=[AT])
        S.dve(lambda e: e.scalar_tensor_tensor(out=Y[:], in0=sl[0].b, scalar=-1.0, in1=C.ident_b[:], op0=ALU.mult, op1=ALU.add),
              reads=[sl[0], C.ident_b], writes=[Y])
        if lat:
            S.act(lambda e: e.activation(out=attnT[c][r][:], in_=sl[1].b, func=AF.Copy), reads=[sl[1]], writes=[attnT[c][r]])
        yield
        S.pool(lambda e: e.tensor_scalar(out=bv[c][0][:], in0=o['vtok'][:, h, :], scalar1=beta[:, nt, c:c + 1], scalar2=None, op0=ALU.mult),
               reads=[o['vtok'], beta], writes=[bv[c][0]])
        S.pool(lambda e: e.tensor_scalar(out=kbe[c][0][:], in0=o['ktok'][:, h, :], scalar1=be[:, nt, c:c + 1], scalar2=None, op0=ALU.mult),
               reads=[o['ktok'], be], writes=[kbe[c][0]])
        S.pool(lambda e: e.tensor_scalar(out=kd[c][r][:], in0=o['ktok'][:, h, :], scalar1=kds[:, nt, c:c + 1], scalar2=None, op0=ALU.mult),
               reads=[o['ktok'], kds], writes=[kd[c][r]])
        if lat:
            S.pool(lambda e: e.tensor_scalar(out=qe[c][0][:], in0=o['qtok'][:, h, :], scalar1=eg[:, nt, c:c + 1], scalar2=None, op0=ALU.mult),
                   reads=[o['qtok'], eg], writes=[qe[c][0]])
        cur = 0
        for lvl in range(1, 4):
            P, PT, Yc = Pb[c][cur], PTb[c][cur], Yb[c][cur]
            Pn, PTn, Yn = Pb[c][1 - cur], PTb[c][1 - cur], Yb[c][1 - cur]
            S.pe(lambda e, P=P, PT=PT: e.matmul(sl[0].f, lhsT=PT[:], rhs=P[:], start=True, stop=True), reads=[P, PT], writes=[sl[0]])
            if lvl < 3:
                S.pe(lambda e, P=P, PT=PT: e.matmul(sl[1].f, lhsT=P[:], rhs=PT[:], start=True, stop=True), reads=[P, PT], writes=[sl[1]])
            yield
            S.act(lambda e, Pn=Pn: e.activation(out=Pn[:], in_=sl[0].f, func=AF.Copy), reads=[sl[0]], writes=[Pn])
            if lvl < 3:
                S.dve(lambda e, PTn=PTn: e.tensor_copy(out=PTn[:], in_=sl[1].f), reads=[sl[1]], writes=[PTn])
            yield
            S.pe(lambda e, Pn=Pn, Yc=Yc: e.matmul(sl[2].f, lhsT=Pn[:], rhs=Yc[:], start=True, stop=True), reads=[Pn, Yc], writes=[sl[2]])
            yield
            S.dve(lambda e, Yc=Yc, Yn=Yn: e.tensor_tensor(out=Yn[:], in0=sl[2].f, in1=Yc[:], op=ALU.add), reads=[sl[2], Yc], writes=[Yn])
            yield
            cur = 1 - cur
        for li in range(3):
            Yc, Yn = Yb[c][cur], Yb[c][1 - cur]
            Tt, N1 = Pb[c][0], PTb[c][0]
            S.pe(lambda e, Yc=Yc: e.transpose(out=sl[0].b, in_=Yc[:], identity=C.ident_b[:]), reads=[Yc, C.ident_b], writes=[sl[0]])
            S.pe(lambda e, Yc=Yc, li=li: e.matmul(sl[1].f, lhsT=AOb[c][li][:], rhs=Yc[:], start=True, stop=True),
                 reads=[AOb[c][li], Yc], writes=[sl[1]])
            yield
            S.act(lambda e, Tt=Tt: e.activation(out=Tt[:], in_=sl[0].b, func=AF.Copy), reads=[sl[0]], writes=[Tt])
            S.dve(lambda e, N1=N1: e.tensor_copy(out=N1[:], in_=sl[1].f), reads=[sl[1]], writes=[N1])
            yield
            S.pe(lambda e, Tt=Tt, N1=N1: e.matmul(sl[2].f, lhsT=Tt[:], rhs=N1[:], start=True, stop=True), reads=[Tt, N1], writes=[sl[2]])
            yield
            S.dve(lambda e, Yc=Yc, Yn=Yn: e.scalar_tensor_tensor(out=Yn[:], in0=sl[2].f, scalar=-1.0, in1=Yc[:], op0=ALU.mult, op1=ALU.add),
                  reads=[sl[2], Yc], writes=[Yn])
            yield
            cur = 1 - cur
        Y = Yb[c][cur]
        S.pe(lambda e: e.matmul(sl[0].f, lhsT=Y[:], rhs=bv[c][0][:], start=True, stop=True), reads=[Y, bv[c][0]], writes=[sl[0]])
        S.pe(lambda e: e.matmul(sl[1].f, lhsT=kbe[c][0][:], rhs=Y[:], start=True, stop=True), reads=[Y, kbe[c][0]], writes=[sl[1]])
        if lat:
            S.pe(lambda e: e.transpose(out=sl[2].b, in_=qe[c][0][:], identity=C.ident_b[:]), reads=[qe[c][0], C.ident_b], writes=[sl[2]])
        yield
        S.act(lambda e: e.activation(out=u_[c][r][:], in_=sl[0].f, func=AF.Copy), reads=[sl[0]], writes=[u_[c][r]])
        S.dve(lambda e: e.tensor_copy(out=wT[c][r][:], in_=sl[1].f), reads=[sl[1]], writes=[wT[c][r]])
        if lat:
            S.act(lambda e: e.activation(out=qdT[c][r][:], in_=sl[2].b, func=AF.Copy), reads=[sl[2]], writes=[qdT[c][r]])
        yield

    def scan(c, n):
        d, h = c // 4, c % 4
        nt = tile_of(d, n)
        lat = nt < 32
        r = n % 2
        sl = slots[c][3]
        Sold, Snew = Sbf[c][n % 2], Sbf[c][1 - n % 2]
        S.pe(lambda e: e.matmul(sl.f, lhsT=wT[c][r][:], rhs=Sold[:], start=True, stop=True), reads=[wT[c][r], Sold], writes=[sl])
        yield
        S.dve(lambda e: e.scalar_tensor_tensor(out=vn[c][0][:], in0=sl.f, scalar=-1.0, in1=u_[c][r][:], op0=ALU.mult, op1=ALU.add),
              reads=[sl, u_[c][r]], writes=[vn[c][0]])
        yield
        S.pe(lambda e: e.matmul(sl.f, lhsT=kd[c][r][:], rhs=vn[c][0][:], start=True, stop=True), reads=[kd[c][r], vn[c][0]], writes=[sl])
        yield
        glc = gl[:, nt, c:c + 1]
        S.dve(lambda e: e.scalar_tensor_tensor(out=Snew[:], in0=S32[c][0][:], scalar=glc, in1=sl.f, op0=ALU.mult, op1=ALU.add),
              reads=[S32[c][0], gl, sl], writes=[Snew])
        S.dve(lambda e: e.scalar_tensor_tensor(out=S32[c][0][:], in0=S32[c][0][:], scalar=glc, in1=sl.f, op0=ALU.mult, op1=ALU.add),
              reads=[S32[c][0], gl, sl], writes=[S32[c][0]])
        yield
        if lat:
            S.pe(lambda e: e.matmul(sl.f, lhsT=qdT[c][r][:], rhs=Sold[:], start=True, stop=False), reads=[qdT[c][r], Sold], writes=[sl], sig=False)
            S.pe(lambda e: e.matmul(sl.f, lhsT=attnT[c][r][:], rhs=vn[c][0][:], start=False, stop=True), reads=[attnT[c][r], vn[c][0]], writes=[sl])
            yield
            orr = ores[nt][h]
            if ofirst[nt][h]:
                ofirst[nt][h] = False
                S.act(lambda e: e.activation(out=oacc[:, nt, h * 128:(h + 1) * 128], in_=sl.f, func=AF.Copy), reads=[sl], writes=[orr])
            else:
                S.dve(lambda e: e.tensor_tensor(out=oacc[:, nt, h * 128:(h + 1) * 128], in0=sl.f, in1=oacc[:, nt, h * 128:(h + 1) * 128], op=ALU.add),
                      reads=[sl, orr], writes=[orr])
            yield

    NR = min(34, GSTOP)
    if GSTOP >= 0:
        for d in range(2):
            load_tile(d, 0)
        run_interleaved([precompute(c, 0) for c in range(8)])
    for n in range(NR):
        gens = [scan(c, n) for c in range(8)]
        if n + 1 < NR:
            for d in range(2):
                load_tile(d, n + 1)
            gens += [precompute(c, n + 1) for c in range(8)]
        run_interleaved(gens)
    for nt in (range(32) if NR == 34 else []):
        S.dma(OACC[nt * 128:(nt + 1) * 128, :], oacc[:, nt, :], reads=ores[nt], writes=[OACC], q='sp')
    for c in range(8):
        dbg_dump(C, 'dbg_S%d' % c, S32[c][0], S32[c][0][:], [128, 128], F32)
    S.barrier()
    S.flush()
    st.close()


def load_rows_bcast(C, st, name, dram_row, n):
    t = sb(C.nc, st, name, [128, n], F32)
    C.S.dma(t[:], dram_row.to_broadcast([128, n]), writes=[t])
    return t


class Epi:
    def __init__(self, C, st, ln_idx, gate_ap, nsub):
        nc = C.nc
        self.C, self.nsub, self.gate = C, nsub, gate_ap
        self.g = load_rows_bcast(C, st, 'ln_g%d' % ln_idx, C.din['ln_g'][ln_idx:ln_idx + 1, :], D)
        self.b = load_rows_bcast(C, st, 'ln_b%d' % ln_idx, C.din['ln_b'][ln_idx:ln_idx + 1, :], D)
        self.t2 = sb(nc, st, 'ep_t2', [128, nsub, D], F32)
        self.junk = sb(nc, st, 'ep_junk', [128, D], BF16)
        self.st = sb(nc, st, 'ep_st', [128, 6, nsub], F32)
        self.eps = sb(nc, st, 'ep_eps', [128, 1], F32)
        self.xo = [self.t2] * 2
        C.S.pool(lambda e: e.memset(self.eps[:], LN_EPS), writes=[self.eps])
        self.i = 0

    def sub(self, s_, ypair, xt):
        S, t2, stt = self.C.S, self.t2, self.st
        for hf in range(2):
            S.dve(lambda e, hf=hf: e.tensor_tensor(out=t2[:, s_, hf * 512:(hf + 1) * 512], in0=ypair[hf][:],
                                                   in1=self.gate[:, hf * 512:(hf + 1) * 512], op=ALU.mult),
                  reads=[ypair[hf], self.C.modrow], writes=[t2])
        S.dve(lambda e: e.scalar_tensor_tensor(out=t2[:, s_, :], in0=xt[:, s_, :], scalar=ALPHA, in1=t2[:, s_, :], op0=ALU.mult, op1=ALU.add),
              reads=[xt, t2], writes=[t2])
        S.act(lambda e: e.activation(out=self.junk[:], in_=t2[:, s_, :], func=AF.Copy, accum_out=stt[:, 0, s_:s_ + 1]),
              reads=[t2], writes=[self.junk, stt])
        S.act(lambda e: e.activation(out=self.junk[:], in_=t2[:, s_, :], func=AF.Square, accum_out=stt[:, 1, s_:s_ + 1]),
              reads=[t2], writes=[self.junk, stt])

    def finish(self, dst_rows, xt_unused=None):
        S, t2, stt, n = self.C.S, self.t2, self.st, self.nsub
        dst, r0 = dst_rows
        xo = self.xo[self.i % 2]
        self.i += 1
        S.dve(lambda e: e.tensor_scalar(out=stt[:, 2, :], in0=stt[:, 0, :], scalar1=1.0 / D, scalar2=None, op0=ALU.mult), reads=[stt], writes=[stt])
        S.dve(lambda e: e.tensor_tensor(out=stt[:, 4, :], in0=stt[:, 2, :], in1=stt[:, 2, :], op=ALU.mult), reads=[stt], writes=[stt])
        S.dve(lambda e: e.scalar_tensor_tensor(out=stt[:, 3, :], in0=stt[:, 1, :], scalar=1.0 / D, in1=stt[:, 4, :], op0=ALU.mult, op1=ALU.subtract),
              reads=[stt], writes=[stt])
        S.act(lambda e: e.activation(out=stt[:, 3, :], in_=stt[:, 3, :], func=AF.Ln, bias=self.eps[:, 0:1]), reads=[stt, self.eps], writes=[stt])
        S.act(lambda e: e.activation(out=stt[:, 3, :], in_=stt[:, 3, :], func=AF.Exp, scale=-0.5), reads=[stt], writes=[stt])
        S.dve(lambda e: e.scalar_tensor_tensor(out=stt[:, 5, :], in0=stt[:, 2, :], scalar=-1.0, in1=stt[:, 3, :], op0=ALU.mult, op1=ALU.mult),
              reads=[stt], writes=[stt])
        for s_ in range(n):
            S.act(lambda e, s_=s_: e.activation(out=t2[:, s_, :], in_=t2[:, s_, :], func=AF.Identity, scale=stt[:, 3, s_:s_ + 1], bias=stt[:, 5, s_:s_ + 1]),
                  reads=[t2, stt], writes=[t2])
            S.pool(lambda e, s_=s_: e.tensor_tensor(out=xo[:, s_, :], in0=t2[:, s_, :], in1=self.g[:], op=ALU.mult), reads=[t2, self.g], writes=[xo])
            S.pool(lambda e, s_=s_: e.tensor_tensor(out=xo[:, s_, :], in0=xo[:, s_, :], in1=self.b[:], op=ALU.add), reads=[xo, self.b], writes=[xo])
        S.dma(dst[r0:r0 + n * 128, :].rearrange("(s p) d -> p s d", p=128), xo[:, :, :], reads=[xo], writes=[dst], q='sp')


def out_proj(C, mixT, wout, s_, ypair, nk):
    S = C.S
    for hf in range(2):
        for k in range(nk):
            S.pe(lambda e, hf=hf, k=k: e.matmul(ypair[hf][:], lhsT=mixT[:, k, s_ * 128:(s_ + 1) * 128], rhs=wout[:, k, hf * 512:(hf + 1) * 512],
                                                start=(k == 0), stop=(k == nk - 1)), reads=[mixT, wout], writes=[ypair[hf]], sig=(k == nk - 1))


def phase_mix0_out(C, X1):
    nc, S = C.nc, C.S
    st = ExitStack()
    wout = load_w_bf16(C, st, 'wout0', C.din['even_w_out'][:, :], 8, D)
    normw = load_rows_bcast(C, st, 'normw', C.din['gdn_norm_w'][0:1, :], 128)
    epi = Epi(C, st, 0, C.modrow[:, 2 * D:3 * D], 4)
    yps = [[ps(nc, st, 'yps%d_%d' % (i, h), [128, 512], F32) for h in range(2)] for i in range(2)]
    pT = [ps(nc, st, 'pTm%d' % i, [128, 512], BF16) for i in range(2)]
    ot = [sb(nc, st, 'ot%d' % i, [128, 4, 512], F32) for i in range(2)]
    gt_ = [sb(nc, st, 'gt%d' % i, [128, 4, 512], BF16) for i in range(2)]
    xt = [sb(nc, st, 'xtm%d' % i, [128, 4, D], F32) for i in range(2)]
    mixT = [sb(nc, st, 'mixT%d' % i, [128, 8, 512], BF16) for i in range(2)]
    osq = sb(nc, st, 'osq', [128, 4, 512], F32)
    ss = sb(nc, st, 'oss', [128, 16], F32)
    og = sb(nc, st, 'og', [128, 4, 512], BF16)
    epsr = sb(nc, st, 'epsr', [128, 1], F32)
    S.pool(lambda e: e.memset(epsr[:], RMS_EPS), writes=[epsr])

    def load(t):
        i = t % 2
        S.dma(ot[i][:, :, :], C.OACC[t * 512:(t + 1) * 512, :].rearrange("(s p) d -> p s d", p=128), reads=[C.OACC], writes=[ot[i]])
        S.dma(gt_[i][:, :, :], C.G0[t * 512:(t + 1) * 512, :].rearrange("(s p) d -> p s d", p=128), reads=[C.G0], writes=[gt_[i]])
        S.dma(xt[i][:, :, :], C.din['x'][t * 512:(t + 1) * 512, :].rearrange("(s p) d -> p s d", p=128), writes=[xt[i]])
        S.dma(mixT[i][:, 4:8, :], C.YPT[:, t * 512:(t + 1) * 512].rearrange("(f p) n -> p f n", p=128), reads=[C.YPT], writes=[mixT[i]])

    load(0)
    yi = 0
    for t in range(8):
        if t + 1 < 8:
            load(t + 1)
        i = t % 2
        o_, g_, x_, m_ = ot[i], gt_[i], xt[i], mixT[i]
        S.pool(lambda e, o_=o_: e.tensor_tensor(out=osq[:], in0=o_[:], in1=o_[:], op=ALU.mult), reads=[o_], writes=[osq])
        S.dve(lambda e: e.tensor_reduce(out=ss[:], in_=osq[:].rearrange("p s (h d) -> p (s h) d", d=128), axis=mybir.AxisListType.X, op=ALU.add),
              reads=[osq], writes=[ss])
        S.act(lambda e: e.activation(out=ss[:], in_=ss[:], func=AF.Ln, scale=1.0 / 128, bias=epsr[:, 0:1]), reads=[ss, epsr], writes=[ss])
        S.act(lambda e: e.activation(out=ss[:], in_=ss[:], func=AF.Exp, scale=-0.5), reads=[ss], writes=[ss])
        S.dve(lambda e, o_=o_: e.tensor_tensor(out=osq[:].rearrange("p s (h d) -> p (s h) d", d=128), in0=o_[:].rearrange("p s (h d) -> p (s h) d", d=128),
                                              in1=ss[:].unsqueeze(2).to_broadcast([128, 16, 128]), op=ALU.mult), reads=[o_, ss], writes=[osq])
        S.pool(lambda e: e.tensor_tensor(out=osq[:].rearrange("p s (h d) -> p (s h) d", d=128), in0=osq[:].rearrange("p s (h d) -> p (s h) d", d=128),
                                         in1=normw[:].unsqueeze(1).to_broadcast([128, 16, 128]), op=ALU.mult), reads=[osq, normw], writes=[osq])
        S.dve(lambda e, g_=g_: e.tensor_tensor(out=og[:], in0=osq[:], in1=g_[:], op=ALU.mult), reads=[osq, g_], writes=[og])
        for h in range(4):
            p = pT[h % 2]
            for s_ in range(4):
                S.pe(lambda e, p=p, h=h, s_=s_: e.transpose(out=p[:, s_ * 128:(s_ + 1) * 128], in_=og[:, s_, h * 128:(h + 1) * 128], identity=C.ident_b[:]),
                     reads=[og, C.ident_b], writes=[p], sig=(s_ == 3))
            if h % 2 == 0:
                S.act(lambda e, p=p, h=h, m_=m_: e.activation(out=m_[:, h, :], in_=p[:], func=AF.Copy), reads=[p], writes=[m_])
            else:
                S.dve(lambda e, p=p, h=h, m_=m_: e.tensor_copy(out=m_[:, h, :], in_=p[:]), reads=[p], writes=[m_])
        for s_ in range(4):
            yp = yps[yi % 2]; yi += 1
            out_proj(C, m_, wout, s_, yp, 8)
            epi.sub(s_, yp, x_)
        epi.finish((X1, t * 512))
    dbg_dump(C, 'dbg_x1', X1, X1[0:128, :], [128, D], F32)
    S.barrier()
    S.flush()
    st.close()


def phase_ffn_up(C, layer, Xin):
    nc, S = C.nc, C.S
    st = ExitStack()
    if layer == 0:
        C.AT = C.scratch('AT', [DFF, T + 128], BF16)
        C.GTt = C.scratch('GTt', [DFF, T], BF16)
        for r0 in range(0, DFF, 128):
            S.dma(C.AT[r0:r0 + 128, 0:64], C.zeros_b[:, 0:64], reads=[C.zeros_b], writes=[C.AT])
            S.dma(C.AT[r0:r0 + 128, T + 64:T + 128], C.zeros_b[:, 0:64], reads=[C.zeros_b], writes=[C.AT])
    AT, GTt = C.AT, C.GTt
    w = load_w_bf16(C, st, 'wup', C.din['ffn_w_up'][layer, :, :], 8, 2 * DFF)
    xt = sb(nc, st, 'xtu', [128, 4, D], F32)
    ub = sb(nc, st, 'ubu', [128, 4, D], BF16)
    uT = [sb(nc, st, 'uTu%d' % i, [128, 8, 512], BF16) for i in range(2)]
    stg = [sb(nc, st, 'stgu%d' % i, [128, 4, 512], BF16) for i in range(2)]
    pT = [ps(nc, st, 'pTu%d' % i, [128, 512], BF16) for i in range(2)]
    pm = [ps(nc, st, 'pmu%d' % i, [128, 512], F32) for i in range(4)]
    pmi = 0
    for t in range(8):
        S.dma(xt[:, :, :], Xin[t * 512:(t + 1) * 512, :].rearrange("(s p) d -> p s d", p=128), reads=[Xin], writes=[xt])
        ut = uT[t % 2]
        modulate_transpose(C, xt, 4, C.modrow[:, 3 * D:4 * D], C.modrow[:, 4 * D:5 * D], ub, ut, pT, t)
        gi = 0
        for f0 in range(0, 44, 4):
            sg = stg[gi % 2]; gi += 1
            nf = min(4, 44 - f0)
            for j in range(nf):
                fc = f0 + j
                p = pm[pmi % 4]; pmi += 1
                for k in range(8):
                    S.pe(lambda e, p=p, k=k, fc=fc, ut=ut: e.matmul(p[:], lhsT=w[:, k, fc * 128:(fc + 1) * 128], rhs=ut[:, k, :],
                                                                 start=(k == 0), stop=(k == 7)), reads=[w, ut], writes=[p], sig=(k == 7))
                if pmi % 2 == 0:
                    S.act(lambda e, p=p, j=j, sg=sg: e.activation(out=sg[:, j, :], in_=p[:], func=AF.Copy), reads=[p], writes=[sg])
                else:
                    S.dve(lambda e, p=p, j=j, sg=sg: e.tensor_copy(out=sg[:, j, :], in_=p[:]), reads=[p], writes=[sg])
            if f0 < 22:
                na = min(nf, 22 - f0)
                S.dma(AT[f0 * 128:(f0 + na) * 128, 64 + t * 512:64 + (t + 1) * 512].rearrange("(j p) n -> p j n", p=128), sg[:, 0:na, :],
                      reads=[sg], writes=[AT], q='act')
                if na < nf:
                    S.dma(GTt[0:(nf - na) * 128, t * 512:(t + 1) * 512].rearrange("(j p) n -> p j n", p=128), sg[:, na:nf, :],
                          reads=[sg], writes=[GTt], q='act')
            else:
                g0 = f0 - 22
                S.dma(GTt[g0 * 128:(g0 + nf) * 128, t * 512:(t + 1) * 512].rearrange("(j p) n -> p j n", p=128), sg[:, 0:nf, :],
                      reads=[sg], writes=[GTt], q='act')
    S.barrier()
    S.flush()
    st.close()


def phase_ffn_down(C, layer, Xin, Xout):
    nc, S = C.nc, C.S
    st = ExitStack()
    AT, GTt = C.AT, C.GTt
    NTK = 256
    pconv = [ps(nc, st, 'pcv%d' % i, [128, 512], F32) for i in range(2)]
    yps = [[ps(nc, st, 'ypd%d_%d' % (i, h), [128, 512], F32) for h in range(2)] for i in range(2)]
    dg = build_diag(C, st, pconv[0], C.din['ffn_conv_w'][layer, :, :], 9, 22, 'cw9')
    wd = load_w_bf16(C, st, 'wdn', C.din['ffn_w_down'][layer, :, :], 22, D)
    epi = Epi(C, st, layer * 2 + 1, C.modrow[:, 5 * D:6 * D], 2)
    ad = [sb(nc, st, 'ad0', [128, 22, 384], BF16)] * 2
    gt_ = [sb(nc, st, 'gd0', [128, 22, 256], BF16)] * 2
    xt = [sb(nc, st, 'xtd0', [128, 2, D], F32)] * 2
    apad = sb(nc, st, 'apad', [128, 22, 6, 66], BF16)
    hs = [sb(nc, st, 'hs%d' % i, [128, 256], BF16) for i in range(2)]
    hg = sb(nc, st, 'hg', [128, 22, 256], BF16)
    S.pool(lambda e: e.memset(apad[:], 0.0), writes=[apad])
    ntile = T // NTK

    def load(t):
        i = t % 2
        S.dma(ad[i][:, :, :], AT[:, t * NTK:t * NTK + 384].rearrange("(f p) n -> p f n", p=128), reads=[AT], writes=[ad[i]])
        S.dma(gt_[i][:, :, :], GTt[:, t * NTK:(t + 1) * NTK].rearrange("(f p) n -> p f n", p=128), reads=[GTt], writes=[gt_[i]])
        S.dma(xt[i][:, :, :], Xin[t * NTK:(t + 1) * NTK, :].rearrange("(s p) d -> p s d", p=128), reads=[Xin], writes=[xt[i]])

    ci = 0
    yi = 0
    for t in range(ntile):
        load(t)
        i = t % 2
        a_, g_, x_ = ad[i], gt_[i], xt[i]
        for fc in range(22):
            S.pool(lambda e, fc=fc, a_=a_: e.tensor_copy(out=apad[:, fc, :, 1:65], in_=a_[:, fc, :].rearrange("p (r c) -> p r c", c=64)),
                   reads=[a_], writes=[apad])
        for fc in range(22):
            pc = pconv[ci % 2]; h_ = hs[ci % 2]; ci += 1
            for tap in range(9):
                dy, dx = tap // 3 - 1, tap % 3 - 1
                S.pe(lambda e, pc=pc, fc=fc, tap=tap, dy=dy, dx=dx: e.matmul(
                    pc[:, 0:256], lhsT=dg[:, fc, tap, :], rhs=apad[:, fc, 1 + dy:5 + dy, 1 + dx:65 + dx], start=(tap == 0), stop=(tap == 8)),
                    reads=[dg, apad], writes=[pc], sig=(tap == 8))
            S.act(lambda e, pc=pc, h_=h_: e.activation(out=h_[:], in_=pc[:, 0:256], func=AF.Silu), reads=[pc], writes=[h_])
            fn = lambda e, fc=fc, h_=h_, g_=g_: e.tensor_tensor(out=hg[:, fc, :], in0=h_[:], in1=g_[:, fc, :], op=ALU.mult)
            if fc % 2 == 0:
                S.pool(fn, reads=[h_, g_], writes=[hg])
            else:
                S.dve(fn, reads=[h_, g_], writes=[hg])
        if t == 1 and layer == 0:
            dbg_dump(C, 'dbg_hg', hg, hg[:, :, :], [128, 22, 256], BF16)
            dbg_dump(C, 'dbg_apad', apad, apad[:, :, :, :], [128, 22, 6, 66], BF16)
        for s_ in range(2):
            yp = yps[yi % 2]; yi += 1
            out_proj(C, hg, wd, s_, yp, 22)
            epi.sub(s_, yp, x_)
        epi.finish((Xout, t * NTK))
    S.barrier()
    S.flush()
    st.close()


def phase_inproj1(C, Xin):
    nc, S = C.nc, C.S
    st = ExitStack()
    GBT = C.scratch('GBT', [512, T], BF16); C.GBT = GBT
    M1T = C.scratch('M1T', [512, T + 32], BF16); C.M1T = M1T
    M2T = C.scratch('M2T', [512, T + 32], BF16); C.M2T = M2T
    for M in (M1T, M2T):
        for r0 in range(0, 512, 128):
            S.dma(M[r0:r0 + 128, 0:16], C.zeros_b[:, 0:16], reads=[C.zeros_b], writes=[M])
            S.dma(M[r0:r0 + 128, T + 16:T + 32], C.zeros_b[:, 0:16], reads=[C.zeros_b], writes=[M])
    w = load_w_bf16(C, st, 'w_in1', C.din['odd_w_in'][:, :], 8, 2560)
    xt = sb(nc, st, 'xt1', [128, 4, D], F32)
    ub = sb(nc, st, 'ub1', [128, 4, D], BF16)
    uT = [sb(nc, st, 'uT1%d' % i, [128, 8, 512], BF16) for i in range(2)]
    stg = [[sb(nc, st, 'stg1_%d_%d' % (g, i), [128, 4, 512], BF16) for i in range(2)] for g in range(3)]
    tmp = [sb(nc, st, 'tmp1_%d' % i, [128, 512], F32) for i in range(2)]
    pT = [ps(nc, st, 'pT1%d' % i, [128, 512], BF16) for i in range(2)]
    pm = [ps(nc, st, 'pm1%d' % i, [128, 512], F32) for i in range(4)]
    pmi = 0
    ti = 0

    def mm(fc, ut):
        nonlocal pmi
        p = pm[pmi % 4]; pmi += 1
        for k in range(8):
            S.pe(lambda e, p=p, k=k: e.matmul(p[:], lhsT=w[:, k, fc * 128:(fc + 1) * 128], rhs=ut[:, k, :], start=(k == 0), stop=(k == 7)),
                 reads=[w, ut], writes=[p], sig=(k == 7))
        return p

    for t in range(8):
        S.dma(xt[:, :, :], Xin[t * 512:(t + 1) * 512, :].rearrange("(s p) d -> p s d", p=128), reads=[Xin], writes=[xt])
        ut = uT[t % 2]
        modulate_transpose(C, xt, 4, C.modrow[:, 0:D], C.modrow[:, D:2 * D], ub, ut, pT, t)
        sgb, sm1, sm2 = stg[0][t % 2], stg[1][t % 2], stg[2][t % 2]
        for j in range(4):
            p = mm(j, ut)
            S.act(lambda e, p=p, j=j, sgb=sgb: e.activation(out=sgb[:, j, :], in_=p[:], func=AF.Copy), reads=[p], writes=[sgb])
            tm = tmp[ti % 2]; ti += 1
            p = mm(4 + j, ut)
            S.act(lambda e, p=p, tm=tm: e.activation(out=tm[:], in_=p[:], func=AF.Copy), reads=[p], writes=[tm])
            p = mm(8 + j, ut)
            S.dve(lambda e, p=p, tm=tm, j=j, sm1=sm1: e.tensor_tensor(out=sm1[:, j, :], in0=p[:], in1=tm[:], op=ALU.mult), reads=[p, tm], writes=[sm1])
            tm = tmp[ti % 2]; ti += 1
            p = mm(16 + j, ut)
            S.act(lambda e, p=p, tm=tm: e.activation(out=tm[:], in_=p[:], func=AF.Sigmoid), reads=[p], writes=[tm])
            p = mm(12 + j, ut)
            S.dve(lambda e, p=p, tm=tm, j=j, sm2=sm2: e.tensor_tensor(out=sm2[:, j, :], in0=p[:], in1=tm[:], op=ALU.mult), reads=[p, tm], writes=[sm2])
        c0 = t * 512
        S.dma(GBT[:, c0:c0 + 512].rearrange("(j p) n -> p j n", p=128), sgb[:, :, :], reads=[sgb], writes=[GBT], q='act')
        S.dma(M1T[:, 16 + c0:16 + c0 + 512].rearrange("(j p) n -> p j n", p=128), sm1[:, :, :], reads=[sm1], writes=[M1T], q='act')
        S.dma(M2T[:, 16 + c0:16 + c0 + 512].rearrange("(j p) n -> p j n", p=128), sm2[:, :, :], reads=[sm2], writes=[M2T], q='act')
    S.barrier()
    S.flush()
    st.close()


def phase_mix1_out(C, Xin, Xout):
    nc, S = C.nc, C.S
    st = ExitStack()
    pconv = [ps(nc, st, 'pc1%d' % i, [128, 512], F32) for i in range(2)]
    pstat = [ps(nc, st, 'pst1%d' % i, [128, 512], F32) for i in range(2)]
    yps = [[ps(nc, st, 'yp1%d_%d' % (i, h), [128, 512], F32) for h in range(2)] for i in range(2)]
    dg3 = build_diag(C, st, pconv[0], C.din['sconv_w'][:, :], 3, 4, 'cw3')
    dg31 = build_diag(C, st, pconv[1], C.din['conf_conv_w'][:, :], 31, 4, 'cw31')
    lng = to_col(C, st, pconv[0], C.din['conf_ln_g'][:, :], 1, 4, 'clng')
    lnb = to_col(C, st, pconv[1], C.din['conf_ln_b'][:, :], 1, 4, 'clnb')
    wout = load_w_bf16(C, st, 'wout1', C.din['odd_w_out'][:, :], 8, D)
    epi = Epi(C, st, 2, C.modrow[:, 2 * D:3 * D], 4)
    m1 = [sb(nc, st, 'm1_%d' % i, [128, 4, 514], BF16) for i in range(2)]
    m2 = [sb(nc, st, 'm2_%d' % i, [128, 4, 542], BF16) for i in range(2)]
    gb = [sb(nc, st, 'gb_%d' % i, [128, 4, 512], BF16) for i in range(2)]
    xt = [sb(nc, st, 'xt1o%d' % i, [128, 4, D], F32) for i in range(2)]
    mixT = sb(nc, st, 'mixT1', [128, 8, 512], BF16)
    z = sb(nc, st, 'z1', [128, 4, 512], F32)
    zsq = sb(nc, st, 'zsq1', [128, 4, 512], F32)
    mean = sb(nc, st, 'mean1', [128, 512], F32)
    rstd = sb(nc, st, 'rstd1', [128, 512], F32)
    msq = sb(nc, st, 'msq1', [128, 512], F32)
    epsl = sb(nc, st, 'epsl1', [128, 1], F32)
    S.pool(lambda e: e.memset(epsl[:], LN_EPS), writes=[epsl])

    def load(t):
        i = t % 2
        c0 = t * 512
        S.dma(m1[i][:, :, :], C.M1T[:, 15 + c0:15 + c0 + 514].rearrange("(f p) n -> p f n", p=128), reads=[C.M1T], writes=[m1[i]])
        S.dma(m2[i][:, :, :], C.M2T[:, 1 + c0:1 + c0 + 542].rearrange("(f p) n -> p f n", p=128), reads=[C.M2T], writes=[m2[i]])
        S.dma(gb[i][:, :, :], C.GBT[:, c0:c0 + 512].rearrange("(f p) n -> p f n", p=128), reads=[C.GBT], writes=[gb[i]])
        S.dma(xt[i][:, :, :], Xin[c0:c0 + 512, :].rearrange("(s p) d -> p s d", p=128), reads=[Xin], writes=[xt[i]])

    load(0)
    ci = 0
    yi = 0
    for t in range(8):
        if t + 1 < 8:
            load(t + 1)
        i = t % 2
        a1, a2, g_, x_ = m1[i], m2[i], gb[i], xt[i]
        for j in range(4):
            pc = pconv[ci % 2]; ci += 1
            for tap in range(3):
                S.pe(lambda e, pc=pc, j=j, tap=tap, a1=a1: e.matmul(pc[:], lhsT=dg3[:, j, tap, :], rhs=a1[:, j, tap:tap + 512], start=(tap == 0), stop=(tap == 2)),
                     reads=[dg3, a1], writes=[pc], sig=(tap == 2))
            S.dve(lambda e, pc=pc, j=j, g_=g_: e.tensor_tensor(out=mixT[:, j, :], in0=pc[:], in1=g_[:, j, :], op=ALU.mult), reads=[pc, g_], writes=[mixT])
        for j in range(4):
            pc = pconv[ci % 2]; ci += 1
            for tap in range(31):
                S.pe(lambda e, pc=pc, j=j, tap=tap, a2=a2: e.matmul(pc[:], lhsT=dg31[:, j, tap, :], rhs=a2[:, j, tap:tap + 512], start=(tap == 0), stop=(tap == 30)),
                     reads=[dg31, a2], writes=[pc], sig=(tap == 30))
            S.act(lambda e, pc=pc, j=j: e.activation(out=z[:, j, :], in_=pc[:], func=AF.Copy), reads=[pc], writes=[z])
            S.pool(lambda e, j=j: e.tensor_tensor(out=zsq[:, j, :], in0=z[:, j, :], in1=z[:, j, :], op=ALU.mult), reads=[z], writes=[zsq])
        for j in range(4):
            S.pe(lambda e, j=j: e.matmul(pstat[0][:], lhsT=C.ones_f[:, 0:128], rhs=z[:, j, :], start=(j == 0), stop=(j == 3)),
                 reads=[C.ones_f, z], writes=[pstat[0]], sig=(j == 3))
        for j in range(4):
            S.pe(lambda e, j=j: e.matmul(pstat[1][:], lhsT=C.ones_f[:, 0:128], rhs=zsq[:, j, :], start=(j == 0), stop=(j == 3)),
                 reads=[C.ones_f, zsq], writes=[pstat[1]], sig=(j == 3))
        S.dve(lambda e: e.tensor_scalar(out=mean[:], in0=pstat[0][:], scalar1=1.0 / 512, scalar2=None, op0=ALU.mult), reads=[pstat[0]], writes=[mean])
        S.dve(lambda e: e.tensor_tensor(out=msq[:], in0=mean[:], in1=mean[:], op=ALU.mult), reads=[mean], writes=[msq])
        S.dve(lambda e: e.scalar_tensor_tensor(out=rstd[:], in0=pstat[1][:], scalar=1.0 / 512, in1=msq[:], op0=ALU.mult, op1=ALU.subtract),
              reads=[pstat[1], msq], writes=[rstd])
        S.act(lambda e: e.activation(out=rstd[:], in_=rstd[:], func=AF.Ln, bias=epsl[:, 0:1]), reads=[rstd, epsl], writes=[rstd])
        S.act(lambda e: e.activation(out=rstd[:], in_=rstd[:], func=AF.Exp, scale=-0.5), reads=[rstd], writes=[rstd])
        for j in range(4):
            S.dve(lambda e, j=j: e.tensor_tensor(out=z[:, j, :], in0=z[:, j, :], in1=mean[:], op=ALU.subtract), reads=[z, mean], writes=[z])
            S.pool(lambda e, j=j: e.tensor_tensor(out=z[:, j, :], in0=z[:, j, :], in1=rstd[:], op=ALU.mult), reads=[z, rstd], writes=[z])
            S.act(lambda e, j=j: e.activation(out=mixT[:, 4 + j, :], in_=z[:, j, :], func=AF.Silu, scale=lng[:, j, 0:1], bias=lnb[:, j, 0:1]),
                  reads=[z, lng, lnb], writes=[mixT])
        for s_ in range(4):
            yp = yps[yi % 2]; yi += 1
            out_proj(C, mixT, wout, s_, yp, 8)
            epi.sub(s_, yp, x_)
        epi.finish((Xout, t * 512))
    S.barrier()
    S.flush()
    st.close()


_NC_CACHE = {}


def kernel(**inputs):
    if 'nc' not in _NC_CACHE:
        _NC_CACHE['nc'] = build()
    nc = _NC_CACHE['nc']
    f = lambda a: np.ascontiguousarray(np.asarray(a, dtype=np.float32))
    shared = {
        'c_ctx': f(inputs['c_ctx']).reshape(1, D),
        'ada_w': f(inputs['ada_w']), 'ada_b': f(inputs['ada_b']),
        'ln_g': f(inputs['ln_g']).reshape(4, D), 'ln_b': f(inputs['ln_b']).reshape(4, D),
        'even_w_in': f(inputs['even_w_in']), 'even_w_out': f(inputs['even_w_out']),
        'gdn_conv_w': f(inputs['gdn_conv_w']), 'gdn_a_log': f(inputs['gdn_a_log']).reshape(1, 8),
        'gdn_dt_bias': f(inputs['gdn_dt_bias']).reshape(1, 8), 'gdn_norm_w': f(inputs['gdn_norm_w']).reshape(1, 128),
        'pool_w': f(inputs['pool_w']), 'pool_scale': f(inputs['pool_scale']).reshape(1, 512),
        'odd_w_in': f(inputs['odd_w_in']), 'odd_w_out': f(inputs['odd_w_out']),
        'sconv_w': f(inputs['sconv_w']), 'conf_conv_w': f(inputs['conf_conv_w']),
        'conf_ln_g': f(inputs['conf_ln_g']).reshape(1, 512), 'conf_ln_b': f(inputs['conf_ln_b']).reshape(1, 512),
        'ffn_w_up': f(inputs['ffn_w_up']), 'ffn_conv_w': f(inputs['ffn_conv_w']).reshape(2, 9, DFF),
        'ffn_w_down': f(inputs['ffn_w_down']),
    }
    x = f(inputs['x']); c = f(inputs['c']); ctx = f(inputs['ctx'])
    in_maps = []
    for b in range(NCORES):
        m = dict(shared)
        m['x'] = x[b]; m['c'] = c[b:b + 1]; m['ctx'] = ctx[b]
        in_maps.append(m)
    res = run_bass_kernel_spmd(nc, in_maps, core_ids=list(range(NCORES)))
    return np.stack([r['out'] for r in res.results], axis=0)
```

```python
import numpy as np
from contextlib import ExitStack
import concourse.bass as bass
import concourse.mybir as mybir
from concourse.bass_utils import run_bass_kernel_spmd

F32 = mybir.dt.float32
BF16 = mybir.dt.bfloat16
AF = mybir.ActivationFunctionType
ALU = mybir.AluOpType

D = 1024
T = 4096
TC = 256
NCORES = 8
DFF = 2816
ALPHA = 4 ** 0.25
LN_EPS = 1e-5
RMS_EPS = 1e-6
BIG = 30000.0
ENG = ('pe', 'act', 'dve', 'pool', 'sp')
NDS = 24

DEBUG_OUT = []


class Res:
    __slots__ = ('w', 'r', 'excl')

    def __init__(self):
        self.w = None
        self.r = []
        self.excl = False


class TileT:
    def __init__(self, t):
        self.t = t
        self.res = Res()

    def __getitem__(self, k):
        return self.t[k]


class Sched:
    def __init__(self, nc, stack):
        self.nc = nc
        self.sem = {e: stack.enter_context(nc.semaphore('s_' + e)) for e in ENG}
        self.cnt = {e: 0 for e in ENG}
        self.known = {e: {} for e in ENG}
        self.q = {e: [] for e in ENG}
        self.dsem = [stack.enter_context(nc.semaphore('dq%d' % i)) for i in range(NDS)]
        self.dcnt = [0] * NDS
        self.dpool = {'sp': list(range(0, 12)), 'act': list(range(12, 20)), 'pool': list(range(20, 24))}
        self.dnext = {'sp': 0, 'act': 0, 'pool': 0}
        self.unsig = {e: False for e in ENG}

    def semof(self, key):
        return self.sem[key] if isinstance(key, str) else self.dsem[key]

    def _collect(self, eng, reads, writes):
        toks = []
        for r in reads:
            if r.w is not None:
                toks.append(r.w)
        for w in writes:
            if w.w is not None and (w.w[0] != eng or eng != 'pe'):
                toks.append(w.w)
            for t in w.r:
                if t[0] != eng or eng != 'pe':
                    toks.append(t)
        waits = {}
        kn = self.known[eng]
        for key, val in toks:
            if kn.get(key, 0) < val:
                waits[key] = max(waits.get(key, 0), val)
        for key, val in waits.items():
            kn[key] = val
        return list(waits.items())

    def _update(self, tok, reads, writes):
        for r in reads:
            r.r.append(tok)
        for w in writes:
            w.w = tok
            w.r = []

    def emit(self, eng, fn, reads=(), writes=(), sig=True):
        reads = [getattr(x, "res", x) for x in reads]
        writes = [getattr(x, "res", x) for x in writes]
        writes = writes + [r for r in reads if r.excl and eng != 'pe']
        waits = self._collect(eng, reads, writes)
        if sig:
            self.cnt[eng] += 1
            tok = (eng, self.cnt[eng])
            self.unsig[eng] = False
        else:
            tok = (eng, self.cnt[eng] + 1)
            self.unsig[eng] = True
        self.q[eng].append((waits, fn, sig, None))
        self._update(tok, reads, writes)

    def pe(self, fn, reads=(), writes=(), sig=True):
        self.emit('pe', fn, reads, writes, sig)

    def act(self, fn, reads=(), writes=()):
        self.emit('act', fn, reads, writes)

    def dve(self, fn, reads=(), writes=()):
        self.emit('dve', fn, reads, writes)

    def pool(self, fn, reads=(), writes=()):
        self.emit('pool', fn, reads, writes)

    def dma(self, out, in_, reads=(), writes=(), q='sp', **kw):
        reads = [getattr(x, "res", x) for x in reads]
        writes = [getattr(x, "res", x) for x in writes]
        pl = self.dpool[q]
        j = pl[self.dnext[q] % len(pl)]
        self.dnext[q] += 1
        waits = dict(self._collect(q, reads, writes))
        if self.dcnt[j] > 0 and self.known[q].get(j, 0) < self.dcnt[j]:
            waits[j] = self.dcnt[j]
            self.known[q][j] = self.dcnt[j]
        self.dcnt[j] += 16
        tok = (j, self.dcnt[j])
        self.q[q].append((list(waits.items()), lambda e: e.dma_start(out=out, in_=in_, **kw), False, j))
        self._update(tok, reads, writes)

    def barrier(self):
        for e in ENG:
            assert not self.unsig[e], e
        for e in ENG:
            waits = []
            for f in ENG:
                if f != e and self.known[e].get(f, 0) < self.cnt[f]:
                    waits.append((f, self.cnt[f]))
                    self.known[e][f] = self.cnt[f]
            for j in range(NDS):
                if self.dcnt[j] > 0 and self.known[e].get(j, 0) < self.dcnt[j]:
                    waits.append((j, self.dcnt[j]))
                    self.known[e][j] = self.dcnt[j]
            if waits:
                self.q[e].append((waits, None, False, None))

    def flush(self):
        nc = self.nc
        q = self.q
        self.q = {e: [] for e in ENG}

        def replay(eng, e):
            for waits, fn, sig, dj in q[eng]:
                for key, val in waits:
                    e.wait_ge(self.semof(key), val)
                if fn is None:
                    continue
                ins = fn(e)
                if dj is not None:
                    ins.then_inc(self.dsem[dj], 16)
                elif sig:
                    ins.then_inc(self.sem[eng], 1)

        with nc.Block() as block:
            @block.tensor
            def _(e):
                replay('pe', e)

            @block.scalar
            def _(e):
                replay('act', e)

            @block.vector
            def _(e):
                replay('dve', e)

            @block.gpsimd
            def _(e):
                replay('pool', e)

            @block.sync
            def _(e):
                replay('sp', e)


class Ctx:
    pass


_UID = [0]


def _uname(name):
    _UID[0] += 1
    return '%s_u%d' % (name, _UID[0])


def sb(nc, stack, name, shape, dt):
    return TileT(stack.enter_context(nc.sbuf_tensor(_uname(name), list(shape), dt)))


def ps(nc, stack, name, shape, dt):
    t = TileT(stack.enter_context(nc.psum_tensor(_uname(name), list(shape), dt)))
    t.res.excl = True
    return t


def build(debug_out=()):
    nc = bass.Bass("TRN2", target_bir_lowering=False)
    top = ExitStack()
    S = Sched(nc, top)
    C = Ctx()
    C.nc, C.S = nc, S
    din = {}

    def inp(name, shape):
        din[name] = TileT(nc.dram_tensor(name, list(shape), F32, kind="ExternalInput").ap())
        return din[name]

    inp('x', [T, D]); inp('c', [1, D]); inp('ctx', [TC, D]); inp('c_ctx', [1, D])
    inp('ada_w', [2, D, 6 * D]); inp('ada_b', [2, 6 * D])
    inp('ln_g', [4, D]); inp('ln_b', [4, D])
    inp('even_w_in', [D, 2576]); inp('even_w_out', [D, D])
    inp('gdn_conv_w', [5, 1536]); inp('gdn_a_log', [1, 8]); inp('gdn_dt_bias', [1, 8])
    inp('gdn_norm_w', [1, 128]); inp('pool_w', [4, 128, 128]); inp('pool_scale', [1, 512])
    inp('odd_w_in', [D, 2560]); inp('odd_w_out', [D, D])
    inp('sconv_w', [3, 512]); inp('conf_conv_w', [31, 512])
    inp('conf_ln_g', [1, 512]); inp('conf_ln_b', [1, 512])
    inp('ffn_w_up', [2, D, 2 * DFF]); inp('ffn_conv_w', [2, 9, DFF]); inp('ffn_w_down', [2, DFF, D])
    C.din = din
    C.out = TileT(nc.dram_tensor('out', [T, D], F32, kind="ExternalOutput").ap())

    def scratch(name, shape, dt):
        kind = "ExternalOutput" if name in debug_out else "Internal"
        t = TileT(nc.dram_tensor(name, list(shape), dt, kind=kind).ap())
        return t
    C.scratch = scratch

    C.ident_f = sb(nc, top, 'ident_f', [128, 128], F32)
    C.ident_b = sb(nc, top, 'ident_b', [128, 128], BF16)
    C.ones_f = sb(nc, top, 'ones_f', [128, 512], F32)
    C.ones_b = sb(nc, top, 'ones_b', [128, 128], BF16)
    C.zeros_b = sb(nc, top, 'zeros_b', [128, 512], BF16)
    C.modrow = sb(nc, top, 'modrow', [128, 6 * D], F32)
    st01 = ExitStack()
    C.modc = sb(nc, st01, 'modc', [128, 2 * D], F32)

    S.pool(lambda e: e.memset(C.ones_f[:], 1.0), writes=[C.ones_f])
    S.pool(lambda e: e.memset(C.ones_b[:], 1.0), writes=[C.ones_b])
    S.pool(lambda e: e.memset(C.zeros_b[:], 0.0), writes=[C.zeros_b])
    S.pool(lambda e: e.affine_select(out=C.ident_f[:], in_=C.ones_f[:, 0:128], pattern=[[-1, 128]],
                                     compare_op=ALU.is_equal, fill=0.0, base=0, channel_multiplier=1),
           reads=[C.ones_f], writes=[C.ident_f])
    S.pool(lambda e: e.tensor_copy(out=C.ident_b[:], in_=C.ident_f[:]), reads=[C.ident_f], writes=[C.ident_b])

    C.debug_out = debug_out
    phase_mod(C, 0)
    dbg_dump(C, 'dbg_mod0', C.modrow, C.modrow[0:1, :], [1, 6 * D], F32)
    dbg_dump(C, 'dbg_modc', C.modc, C.modc[0:1, :], [1, 2 * D], F32)
    phase_inproj0(C)
    st01.close()
    phase_qkv0(C)
    import os
    if os.environ.get('NOGDN') != '1':
        phase_gdn(C)
    else:
        C.OACC = C.scratch('OACC', [T, 512], F32)
        for r0 in range(0, T, 128):
            S.dma(C.OACC[r0:r0 + 128, :], C.ones_f[:, :], reads=[C.ones_f], writes=[C.OACC])
    PSTOP = int(os.environ.get('PSTOP', '99'))
    X1 = C.scratch('X1', [T, D], F32)
    X2 = C.scratch('X2', [T, D], F32)
    X3 = C.scratch('X3', [T, D], F32)
    if PSTOP >= 1:
        phase_mix0_out(C, X1)
    if PSTOP >= 2:
        phase_ffn_up(C, 0, X1)
    if PSTOP >= 3:
        phase_ffn_down(C, 0, X1, X2)
    if PSTOP >= 4:
        phase_mod(C, 1)
        phase_inproj1(C, X2)
    if PSTOP >= 5:
        phase_mix1_out(C, X2, X3)
    if PSTOP >= 6:
        phase_ffn_up(C, 1, X3)
        phase_ffn_down(C, 1, X3, C.out)

    S.barrier()
    S.flush()
    top.close()
    return nc


def dbg_dump(C, name, tile, ap, shape, dt):
    if name not in C.debug_out:
        return
    d = TileT(C.nc.dram_tensor(name, list(shape), dt, kind="ExternalOutput").ap())
    C.S.dma(d[:], ap, reads=[tile], writes=[d])


def to_col(C, st, psb, dram, R, ncol, name):
    nc, S = C.nc, C.S
    BL = 4
    tmp = sb(nc, st, name + '_row', [R, BL * 128], F32)
    outt = sb(nc, st, name + '_col', [128, ncol, R], F32)
    for c0 in range(0, ncol, BL):
        c1 = min(ncol, c0 + BL)
        S.dma(tmp[:, 0:(c1 - c0) * 128], dram[:, c0 * 128:c1 * 128], writes=[tmp])
        for c in range(c0, c1):
            S.pe(lambda e, c=c, c0=c0: e.transpose(out=psb[:, 0:R], in_=tmp[0:R, (c - c0) * 128:(c - c0 + 1) * 128],
                                                   identity=C.ident_f[0:R, 0:R]),
                 reads=[tmp, C.ident_f], writes=[psb])
            S.dve(lambda e, c=c: e.tensor_copy(out=outt[:, c, :], in_=psb[:, 0:R]), reads=[psb], writes=[outt])
    return outt


def phase_mod(C, layer):
    nc, S = C.nc, C.S
    st = ExitStack()
    pst = ps(nc, st, 'pm_t', [128, 512], F32)
    pacc = [ps(nc, st, 'pm_a%d' % i, [128, 512], F32) for i in range(2)]
    paccc = [ps(nc, st, 'pm_c%d' % i, [128, 512], F32) for i in range(2)]
    ccol = to_col(C, st, pst, C.din['c'][:, :], 1, 8, 'c')
    S.act(lambda e: e.activation(out=ccol[:], in_=ccol[:], func=AF.Silu), reads=[ccol], writes=[ccol])
    rep = sb(nc, st, 'c_rep', [128, 8, 128], F32)
    for k in range(8):
        S.dve(lambda e, k=k: e.tensor_scalar(out=rep[:, k, :], in0=C.ones_f[:, 0:128], scalar1=ccol[:, k, 0:1],
                                             scalar2=None, op0=ALU.mult), reads=[C.ones_f, ccol], writes=[rep])
    brow = sb(nc, st, 'adab_row', [1, 6 * D], F32)
    S.dma(brow[:], C.din['ada_b'][layer:layer + 1, :], writes=[brow])
    if layer == 0:
        cccol = to_col(C, st, pst, C.din['c_ctx'][:, :], 1, 8, 'cc')
        S.act(lambda e: e.activation(out=cccol[:], in_=cccol[:], func=AF.Silu), reads=[cccol], writes=[cccol])
        repc = sb(nc, st, 'cc_rep', [128, 8, 128], F32)
        for k in range(8):
            S.dve(lambda e, k=k: e.tensor_scalar(out=repc[:, k, :], in0=C.ones_f[:, 0:128],
                                                 scalar1=cccol[:, k, 0:1], scalar2=None, op0=ALU.mult),
                  reads=[C.ones_f, cccol], writes=[repc])
    wt = [sb(nc, st, 'adaw%d' % i, [128, 8, 512], F32) for i in range(2)]
    aw = C.din['ada_w']
    for n in range(12):
        w = wt[n % 2]
        S.dma(w[:], aw[layer, :, n * 512:(n + 1) * 512].rearrange("(k p) n -> p k n", p=128), writes=[w])
        pa = pacc[n % 2]
        for k in range(8):
            S.pe(lambda e, k=k, w=w, pa=pa: e.matmul(pa[:], lhsT=rep[:, k, :], rhs=w[:, k, :], start=(k == 0), stop=False),
                 reads=[rep, w], writes=[pa], sig=False)
        S.pe(lambda e, pa=pa, n=n: e.matmul(pa[:], lhsT=C.ones_f[0:1, 0:128], rhs=brow[0:1, n * 512:(n + 1) * 512],
                                            start=False, stop=True), reads=[C.ones_f, brow], writes=[pa])
        S.act(lambda e, pa=pa, n=n: e.activation(out=C.modrow[:, n * 512:(n + 1) * 512], in_=pa[:], func=AF.Copy),
              reads=[pa], writes=[C.modrow])
        if layer == 0 and n < 4:
            pc = paccc[n % 2]
            for k in range(8):
                S.pe(lambda e, k=k, w=w, pc=pc: e.matmul(pc[:], lhsT=repc[:, k, :], rhs=w[:, k, :], start=(k == 0), stop=False),
                     reads=[repc, w], writes=[pc], sig=False)
            S.pe(lambda e, pc=pc, n=n: e.matmul(pc[:], lhsT=C.ones_f[0:1, 0:128], rhs=brow[0:1, n * 512:(n + 1) * 512],
                                                start=False, stop=True), reads=[C.ones_f, brow], writes=[pc])
            S.dve(lambda e, pc=pc, n=n: e.tensor_copy(out=C.modc[:, n * 512:(n + 1) * 512], in_=pc[:]),
                  reads=[pc], writes=[C.modc])
    S.dve(lambda e: e.tensor_scalar_add(out=C.modrow[:, D:2 * D], in0=C.modrow[:, D:2 * D], scalar1=1.0),
          reads=[C.modrow], writes=[C.modrow])
    S.dve(lambda e: e.tensor_scalar_add(out=C.modrow[:, 4 * D:5 * D], in0=C.modrow[:, 4 * D:5 * D], scalar1=1.0),
          reads=[C.modrow], writes=[C.modrow])
    if layer == 0:
        S.dve(lambda e: e.tensor_scalar_add(out=C.modc[:, D:2 * D], in0=C.modc[:, D:2 * D], scalar1=1.0),
              reads=[C.modc], writes=[C.modc])
    S.barrier()
    S.flush()
    st.close()


def modulate_transpose(C, xt, nsub, shift, scale1, ub, uT, pT, evac_i):
    S = C.S
    for s in range(nsub):
        S.pool(lambda e, s=s: e.tensor_tensor(out=xt[:, s, :], in0=xt[:, s, :], in1=scale1, op=ALU.mult),
               reads=[xt, C.modrow, C.modc], writes=[xt])
        S.dve(lambda e, s=s: e.tensor_tensor(out=ub[:, s, :], in0=xt[:, s, :], in1=shift, op=ALU.add),
              reads=[xt, C.modrow, C.modc], writes=[ub])
    for k in range(8):
        p = pT[k % len(pT)]
        for s in range(nsub):
            S.pe(lambda e, s=s, k=k, p=p: e.transpose(out=p[:, s * 128:(s + 1) * 128], in_=ub[:, s, k * 128:(k + 1) * 128],
                                                      identity=C.ident_b[:]),
                 reads=[ub, C.ident_b], writes=[p], sig=(s == nsub - 1))
        if (k + evac_i) % 2 == 0:
            S.act(lambda e, k=k, p=p: e.activation(out=uT[:, k, 0:nsub * 128], in_=p[:, 0:nsub * 128], func=AF.Copy),
                  reads=[p], writes=[uT])
        else:
            S.dve(lambda e, k=k, p=p: e.tensor_copy(out=uT[:, k, 0:nsub * 128], in_=p[:, 0:nsub * 128]),
                  reads=[p], writes=[uT])


def load_w_bf16(C, st, name, dram, kchunks, ncols):
    nc, S = C.nc, C.S
    w = sb(nc, st, name, [128, kchunks, ncols], BF16)
    step = max(1, 4096 // ncols)
    for k0 in range(0, kchunks, step):
        k1 = min(kchunks, k0 + step)
        S.dma(w[:, k0:k1, :], dram[k0 * 128:k1 * 128, :].rearrange("(k p) n -> p k n", p=128), writes=[w], q='pool')
    return w


def phase_inproj0(C):
    nc, S = C.nc, C.S
    st = ExitStack()
    P0T = C.scratch('P0T', [2048, T + 16], BF16); C.P0T = P0T
    G0 = C.scratch('G0', [T, 512], BF16); C.G0 = G0
    SG = C.scratch('SG', [T + TC, 16], F32); C.SG = SG
    PCT = C.scratch('PCT', [1024, TC + 16], BF16); C.PCT = PCT
    w = load_w_bf16(C, st, 'w_in0', C.din['even_w_in'][:, :], 8, 2576)
    for r0 in range(0, 2048, 128):
        S.dma(P0T[r0:r0 + 128, 0:8], C.zeros_b[:, 0:8], reads=[C.zeros_b], writes=[P0T])
        S.dma(P0T[r0:r0 + 128, T + 8:T + 16], C.zeros_b[:, 0:8], reads=[C.zeros_b], writes=[P0T])
    for r0 in range(0, 1024, 128):
        S.dma(PCT[r0:r0 + 128, 0:8], C.zeros_b[:, 0:8], reads=[C.zeros_b], writes=[PCT])
        S.dma(PCT[r0:r0 + 128, TC + 8:TC + 16], C.zeros_b[:, 0:8], reads=[C.zeros_b], writes=[PCT])
    xt = [sb(nc, st, 'xt%d' % i, [128, 4, D], F32) for i in range(2)]
    ub = [sb(nc, st, 'ub%d' % i, [128, 4, D], BF16) for i in range(2)]
    uT = [sb(nc, st, 'uT%d' % i, [128, 8, 512], BF16) for i in range(2)]
    pstg = [sb(nc, st, 'pstg%d' % i, [128, 4, 512], BF16) for i in range(2)]
    gstg = [sb(nc, st, 'gstg%d' % i, [128, 4, 512], BF16) for i in range(2)]
    sstg = [sb(nc, st, 'sstg%d' % i, [128, 4, 16], F32) for i in range(2)]
    pT = [ps(nc, st, 'pT%d' % i, [128, 512], BF16) for i in range(2)]
    pm = [ps(nc, st, 'pm%d' % i, [128, 512], F32) for i in range(4)]
    pss = ps(nc, st, 'pss', [128, 4, 16], F32)
    x = C.din['x']
    tiles = [('ctx', 0)] + [('lat', i) for i in range(8)]

    def load(i):
        kind, t = tiles[i]
        b = xt[i % 2]
        if kind == 'ctx':
            S.dma(b[:, 0:2, :], C.din['ctx'][:, :].rearrange("(s p) d -> p s d", p=128), writes=[b])
        else:
            S.dma(b[:, :, :], x[t * 512:(t + 1) * 512, :].rearrange("(s p) d -> p s d", p=128), writes=[b])

    load(0)
    pmi = 0
    for i, (kind, t) in enumerate(tiles):
        if i + 1 < len(tiles):
            load(i + 1)
        b, u, ut = xt[i % 2], ub[i % 2], uT[i % 2]
        isctx = kind == 'ctx'
        nsub = 2 if isctx else 4
        ntok = nsub * 128
        if isctx:
            modulate_transpose(C, b, nsub, C.modc[:, 0:D], C.modc[:, D:2 * D], u, ut, pT, i)
            fchunks = list(range(4, 12))
        else:
            modulate_transpose(C, b, nsub, C.modrow[:, 0:D], C.modrow[:, D:2 * D], u, ut, pT, i)
            fchunks = list(range(0, 12)) + list(range(16, 20))
        for gi in range(0, len(fchunks), 4):
            grp = fchunks[gi:gi + 4]
            stg = pstg[(gi // 4) % 2]
            for j, fc in enumerate(grp):
                p = pm[pmi % 4]; pmi += 1
                for k in range(8):
                    S.pe(lambda e, p=p, k=k, fc=fc, ut=ut, ntok=ntok: e.matmul(
                        p[:, 0:ntok], lhsT=w[:, k, fc * 128:(fc + 1) * 128], rhs=ut[:, k, 0:ntok],
                        start=(k == 0), stop=(k == 7)), reads=[w, ut], writes=[p], sig=(k == 7))
                if j % 2 == 0:
                    S.act(lambda e, p=p, j=j, stg=stg, ntok=ntok: e.activation(out=stg[:, j, 0:ntok], in_=p[:, 0:ntok], func=AF.Copy),
                          reads=[p], writes=[stg])
                else:
                    S.dve(lambda e, p=p, j=j, stg=stg, ntok=ntok: e.tensor_copy(out=stg[:, j, 0:ntok], in_=p[:, 0:ntok]),
                          reads=[p], writes=[stg])
            if isctx:
                r0 = (grp[0] - 4) * 128
                dst = PCT[r0:r0 + 512, 8:8 + ntok].rearrange("(j p) n -> p j n", p=128)
                S.dma(dst, stg[:, :, 0:ntok], reads=[stg], writes=[PCT], q='act')
            else:
                fc0 = grp[0]
                r0 = fc0 * 128 if fc0 < 12 else (fc0 - 4) * 128
                dst = P0T[r0:r0 + 512, 8 + t * 512:8 + (t + 1) * 512].rearrange("(j p) n -> p j n", p=128)
                S.dma(dst, stg[:, :, :], reads=[stg], writes=[P0T], q='act')
        gs = gstg[i % 2]
        ss = sstg[i % 2]
        for s in range(nsub):
            if not isctx:
                p = pm[pmi % 4]; pmi += 1
                for k in range(8):
                    S.pe(lambda e, p=p, k=k, s=s, ut=ut: e.matmul(p[:], lhsT=ut[:, k, s * 128:(s + 1) * 128], rhs=w[:, k, 1536:2048],
                                                            start=(k == 0), stop=(k == 7)), reads=[w, ut], writes=[p], sig=(k == 7))
                S.act(lambda e, p=p, s=s, gs=gs: e.activation(out=gs[:, s, :], in_=p[:], func=AF.Silu), reads=[p], writes=[gs])
            for k in range(8):
                S.pe(lambda e, k=k, s=s, ut=ut: e.matmul(pss[:, s, :], lhsT=ut[:, k, s * 128:(s + 1) * 128], rhs=w[:, k, 2560:2576],
                                                       start=(k == 0), stop=(k == 7)), reads=[w, ut], writes=[pss], sig=(k == 7))
        S.dve(lambda e, ss=ss, nsub=nsub: e.tensor_copy(out=ss[:, 0:nsub, :], in_=pss[:, 0:nsub, :]), reads=[pss], writes=[ss])
        if isctx:
            S.dma(SG[T:T + TC, :].rearrange("(s p) c -> p s c", p=128), ss[:, 0:2, :], reads=[ss], writes=[SG], q='act')
        else:
            S.dma(SG[t * 512:(t + 1) * 512, :].rearrange("(s p) c -> p s c", p=128), ss[:, :, :], reads=[ss], writes=[SG], q='act')
            S.dma(G0[t * 512:(t + 1) * 512, :].rearrange("(s p) c -> p s c", p=128), gs[:, :, :], reads=[gs], writes=[G0], q='act')
    S.barrier()
    S.flush()
    st.close()


def build_diag(C, st, psb, dram, R, ncol, name):
    nc, S = C.nc, C.S
    cw = to_col(C, st, psb, dram, R, ncol, name)
    dg = sb(nc, st, name + '_dg', [128, ncol, R, 128], BF16)
    i = 0
    for c in range(ncol):
        for r in range(R):
            fn = lambda e, c=c, r=r: e.tensor_scalar(out=dg[:, c, r, :], in0=C.ident_b[:], scalar1=cw[:, c, r:r + 1],
                                                     scalar2=None, op0=ALU.mult)
            if i % 2 == 0:
                S.dve(fn, reads=[C.ident_b, cw], writes=[dg])
            else:
                S.pool(fn, reads=[C.ident_b, cw], writes=[dg])
            i += 1
    return dg


def phase_qkv0(C):
    nc, S = C.nc, C.S
    st = ExitStack()
    QT = C.scratch('QT', [512, T], BF16); C.QT = QT
    KT = C.scratch('KT', [512, T + TC], BF16); C.KT = KT
    QTOK = C.scratch('QTOK', [T, 512], BF16); C.QTOK = QTOK
    KTOK = C.scratch('KTOK', [T + TC, 512], BF16); C.KTOK = KTOK
    VTOK = C.scratch('VTOK', [T + TC, 512], BF16); C.VTOK = VTOK
    YPT = C.scratch('YPT', [512, T], BF16); C.YPT = YPT
    pconv = [ps(nc, st, 'pconv%d' % i, [128, 512], F32) for i in range(2)]
    pssq = [ps(nc, st, 'pssq%d' % i, [128, 512], F32) for i in range(2)]
    pT = [ps(nc, st, 'pTq%d' % i, [128, 512], BF16) for i in range(2)]
    ppool = ps(nc, st, 'ppool', [128, 512], F32)
    dg = build_diag(C, st, pconv[0], C.din['gdn_conv_w'][:, :], 5, 12, 'cw5')
    pscale = to_col(C, st, pconv[1], C.din['pool_scale'][:, :], 1, 4, 'pscale')
    poolw = sb(nc, st, 'poolw', [128, 4, 128], BF16)
    S.dma(poolw[:], C.din['pool_w'][:, :, :].rearrange("g c d -> c g d"), writes=[poolw], q='pool')
    corrF = sb(nc, st, 'corrF', [128, 4, 8], F32)
    corrL = sb(nc, st, 'corrL', [128, 4, 8], F32)
    S.pool(lambda e: e.memset(corrF[:], 1.0), writes=[corrF])
    S.pool(lambda e: e.memset(corrL[:], 1.0), writes=[corrL])
    for g in range(4):
        hw = 1 << g
        for j in range(hw):
            S.pool(lambda e, g=g, j=j, hw=hw: e.memset(corrF[:, g, j:j + 1], 2.0 * hw / (j + hw)), writes=[corrF])
        for m in range(hw - 1):
            S.pool(lambda e, g=g, m=m, hw=hw: e.memset(corrL[:, g, 7 - m:8 - m], 2.0 * hw / (1 + m + hw)), writes=[corrL])
    pin = [sb(nc, st, 'pin%d' % i, [128, 12, 516], BF16) for i in range(2)]
    pp = [sb(nc, st, 'pp%d' % i, [128, 4, 528], BF16) for i in range(2)]
    xs8 = sb(nc, st, 'xs8', [128, 8, 512], F32)
    ss8 = sb(nc, st, 'ss8', [128, 8, 512], F32)
    epsb = sb(nc, st, 'epsb', [128, 1], F32)
    S.pool(lambda e: e.memset(epsb[:], RMS_EPS), writes=[epsb])
    sqb = [sb(nc, st, 'sqb%d' % i, [128, 512], BF16) for i in range(2)]
    qkn = [sb(nc, st, 'qkn0', [128, 12, 512], BF16)] * 2
    tokst = [[sb(nc, st, 'tok%d_%d' % (g, i), [128, 4, 512], BF16) for i in range(2)] for g in range(3)]
    wa = [sb(nc, st, 'wa%d' % i, [128, 528], F32) for i in range(2)]
    wb = [sb(nc, st, 'wb%d' % i, [128, 528], F32) for i in range(2)]
    pld = [sb(nc, st, 'pld%d' % i, [128, 4, 512], BF16) for i in range(2)]
    ypst = [sb(nc, st, 'ypst%d' % i, [128, 4, 512], BF16) for i in range(2)]
    tiles = [('ctx', 0)] + [('lat', i) for i in range(8)]

    def load(i):
        kind, t = tiles[i]
        b = pin[i % 2]
        if kind == 'ctx':
            S.dma(b[:, 4:12, 0:260], C.PCT[:, 6:266].rearrange("(f p) n -> p f n", p=128), reads=[C.PCT], writes=[b])
        else:
            S.dma(b[:, :, :], C.P0T[0:1536, 6 + t * 512:6 + t * 512 + 516].rearrange("(f p) n -> p f n", p=128),
                  reads=[C.P0T], writes=[b])
            S.dma(pp[i % 2][:, :, :], C.P0T[1536:2048, t * 512:t * 512 + 528].rearrange("(f p) n -> p f n", p=128),
                  reads=[C.P0T], writes=[pp[i % 2]])

    load(0)
    ci = 0
    for i, (kind, t) in enumerate(tiles):
        if i + 1 < len(tiles):
            load(i + 1)
        isctx = kind == 'ctx'
        ntok = 256 if isctx else 512
        nsub = ntok // 128
        b = pin[i % 2]
        qk = qkn[i % 2]
        for fc in (range(4, 12) if isctx else range(12)):
            pc = pconv[ci % 2]
            sq_ = sqb[ci % 2]; pq = pssq[ci % 2]
            ci += 1
            for tap in range(5):
                S.pe(lambda e, pc=pc, fc=fc, tap=tap, b=b, ntok=ntok: e.matmul(
                    pc[:, 0:ntok], lhsT=dg[:, fc, tap, :], rhs=b[:, fc, tap:tap + ntok], start=(tap == 0), stop=(tap == 4)),
                    reads=[dg, b], writes=[pc], sig=(tap == 4))
            if fc >= 8:
                S.act(lambda e, pc=pc, fc=fc, qk=qk, ntok=ntok: e.activation(out=qk[:, fc, 0:ntok], in_=pc[:, 0:ntok], func=AF.Silu),
                      reads=[pc], writes=[qk])
                continue
            S.act(lambda e, pc=pc, fc=fc, ntok=ntok: e.activation(out=xs8[:, fc, 0:ntok], in_=pc[:, 0:ntok], func=AF.Silu),
                  reads=[pc], writes=[xs8])
            S.pool(lambda e, fc=fc, sq_=sq_, ntok=ntok: e.tensor_tensor(out=sq_[:, 0:ntok], in0=xs8[:, fc, 0:ntok], in1=xs8[:, fc, 0:ntok], op=ALU.mult),
                   reads=[xs8], writes=[sq_])
            S.pe(lambda e, pq=pq, sq_=sq_, ntok=ntok: e.matmul(pq[:, 0:ntok], lhsT=C.ones_b[:], rhs=sq_[:, 0:ntok], start=True, stop=True),
                 reads=[C.ones_b, sq_], writes=[pq])
            S.dve(lambda e, pq=pq, fc=fc, ntok=ntok: e.tensor_copy(out=ss8[:, fc, 0:ntok], in_=pq[:, 0:ntok]), reads=[pq], writes=[ss8])
        f0 = 4 if isctx else 0
        S.act(lambda e, f0=f0, ntok=ntok: e.activation(out=ss8[:, f0:8, 0:ntok], in_=ss8[:, f0:8, 0:ntok], func=AF.Ln, bias=epsb[:, 0:1]),
              reads=[ss8, epsb], writes=[ss8])
        S.act(lambda e, f0=f0, ntok=ntok: e.activation(out=ss8[:, f0:8, 0:ntok], in_=ss8[:, f0:8, 0:ntok], func=AF.Exp, scale=-0.5),
              reads=[ss8], writes=[ss8])
        for fc in range(f0, 8):
            sc = (128.0 ** -0.5) if fc < 4 else 1.0
            fn = lambda e, qk=qk, fc=fc, sc=sc, ntok=ntok: e.scalar_tensor_tensor(
                out=qk[:, fc, 0:ntok], in0=xs8[:, fc, 0:ntok], scalar=sc, in1=ss8[:, fc, 0:ntok], op0=ALU.mult, op1=ALU.mult)
            if fc < 4:
                S.dve(fn, reads=[xs8, ss8], writes=[qk])
            else:
                S.pool(lambda e, qk=qk, fc=fc, ntok=ntok: e.tensor_tensor(out=qk[:, fc, 0:ntok], in0=xs8[:, fc, 0:ntok],
                                                                       in1=ss8[:, fc, 0:ntok], op=ALU.mult),
                       reads=[xs8, ss8], writes=[qk])
        ti = 0
        for g in ((1, 2) if isctx else (0, 1, 2)):
            tk = tokst[g][i % 2]
            for s_ in range(nsub):
                p = pT[ti % 2]; ti += 1
                for h in range(4):
                    S.pe(lambda e, p=p, h=h, g=g, s_=s_, qk=qk: e.transpose(out=p[:, h * 128:(h + 1) * 128],
                                                                       in_=qk[:, g * 4 + h, s_ * 128:(s_ + 1) * 128], identity=C.ident_b[:]),
                         reads=[qk, C.ident_b], writes=[p], sig=(h == 3))
                if ti % 2 == 0:
                    S.act(lambda e, p=p, tk=tk, s_=s_: e.activation(out=tk[:, s_, :], in_=p[:], func=AF.Copy), reads=[p], writes=[tk])
                else:
                    S.dve(lambda e, p=p, tk=tk, s_=s_: e.tensor_copy(out=tk[:, s_, :], in_=p[:]), reads=[p], writes=[tk])
        c0 = T if isctx else t * 512
        if not isctx:
            S.dma(QT[:, c0:c0 + 512].rearrange("(f p) n -> p f n", p=128), qk[:, 0:4, :], reads=[qk], writes=[QT], q='act')
            S.dma(QTOK[c0:c0 + 512, :].rearrange("(s p) c -> p s c", p=128), tokst[0][i % 2][:, :, :], reads=[tokst[0][i % 2]], writes=[QTOK], q='act')
        S.dma(KT[:, c0:c0 + ntok].rearrange("(f p) n -> p f n", p=128), qk[:, 4:8, 0:ntok], reads=[qk], writes=[KT], q='act')
        S.dma(KTOK[c0:c0 + ntok, :].rearrange("(s p) c -> p s c", p=128), tokst[1][i % 2][:, 0:nsub, :], reads=[tokst[1][i % 2]], writes=[KTOK], q='act')
        S.dma(VTOK[c0:c0 + ntok, :].rearrange("(s p) c -> p s c", p=128), tokst[2][i % 2][:, 0:nsub, :], reads=[tokst[2][i % 2]], writes=[VTOK], q='act')
        if isctx:
            continue
        ppb = pp[i % 2]
        pl = pld[i % 2]
        yp = ypst[i % 2]
        for g in range(4):
            a_, b_ = wa[g % 2], wb[g % 2]
            S.pool(lambda e, a_=a_, g=g, ppb=ppb: e.tensor_tensor(out=a_[:, 1:527], in0=ppb[:, g, 0:526], in1=ppb[:, g, 1:527], op=ALU.add),
                   reads=[ppb], writes=[a_])
            cur, oth = a_, b_
            lo, hi = 1, 527
            for lvl in range(g):
                sh = 1 << lvl
                lo, hi = lo + sh, hi - sh
                S.pool(lambda e, cur=cur, oth=oth, lo=lo, hi=hi, sh=sh: e.tensor_tensor(
                    out=oth[:, lo:hi], in0=cur[:, lo - sh:hi - sh], in1=cur[:, lo + sh:hi + sh], op=ALU.add),
                    reads=[cur], writes=[oth])
                cur, oth = oth, cur
            S.dve(lambda e, cur=cur, g=g: e.tensor_scalar(out=cur[:, 8:520], in0=cur[:, 8:520], scalar1=1.0 / (2 << g), scalar2=None, op0=ALU.mult),
                  reads=[cur], writes=[cur])
            if t == 0:
                S.dve(lambda e, cur=cur, g=g: e.tensor_tensor(out=cur[:, 8:16], in0=cur[:, 8:16], in1=corrF[:, g, :], op=ALU.mult),
                      reads=[cur, corrF], writes=[cur])
            if t == 7:
                S.dve(lambda e, cur=cur, g=g: e.tensor_tensor(out=cur[:, 512:520], in0=cur[:, 512:520], in1=corrL[:, g, :], op=ALU.mult),
                      reads=[cur, corrL], writes=[cur])
            S.dve(lambda e, cur=cur, g=g, pl=pl, ppb=ppb: e.tensor_tensor(out=pl[:, g, :], in0=cur[:, 8:520], in1=ppb[:, g, 8:520], op=ALU.subtract),
                  reads=[cur, ppb], writes=[pl])
            S.pe(lambda e, g=g, pl=pl: e.matmul(ppool[:], lhsT=poolw[:, g, :], rhs=pl[:, g, :], start=True, stop=True),
                 reads=[poolw, pl], writes=[ppool])
            S.act(lambda e, g=g, yp=yp: e.activation(out=yp[:, g, :], in_=ppool[:], func=AF.Identity, scale=pscale[:, g, 0:1]),
                  reads=[ppool, pscale], writes=[yp])
        S.dma(YPT[:, c0:c0 + 512].rearrange("(f p) n -> p f n", p=128), yp[:, :, :], reads=[yp], writes=[YPT], q='act')
    S.barrier()
    S.flush()
    st.close()


class Slot:
    def __init__(self, bank, k):
        self.f = bank.t[:, k * 128:(k + 1) * 128]
        self.b = bank.t[:, :].bitcast(BF16)[:, k * 256:k * 256 + 128]
        self.res = bank.res


def run_interleaved(gens):
    gens = list(gens)
    while gens:
        nxt = []
        for g in gens:
            try:
                next(g)
                nxt.append(g)
            except StopIteration:
                pass
        gens = nxt


def phase_gdn(C):
    nc, S = C.nc, C.S
    st = ExitStack()
    NT = 34
    import os
    OACC = C.scratch('OACC', [T, 512], F32); C.OACC = OACC
    banks = [ps(nc, st, 'gbank%d' % i, [128, 512], F32) for i in range(8)]
    for b_ in banks:
        b_.res.excl = True
    slots = [[Slot(banks[c], k) for k in range(4)] for c in range(8)]
    sall = sb(nc, st, 'sall', [128, NT, 16], F32)
    for n0 in ([] if os.environ.get('NOSALL') == '1' else range(0, NT, 6)):
        n1 = min(NT, n0 + 6)
        S.dma(sall[:, n0:n1, :], C.SG[n0 * 128:n1 * 128, :].rearrange("(n p) c -> p n c", p=128), reads=[C.SG], writes=[sall])
    adb = sb(nc, st, 'adb', [128, 16], F32)
    S.dma(adb[:, 0:8], C.din['gdn_a_log'][0:1, :].to_broadcast([128, 8]), writes=[adb])
    S.dma(adb[:, 8:16], C.din['gdn_dt_bias'][0:1, :].to_broadcast([128, 8]), writes=[adb])
    S.act(lambda e: e.activation(out=adb[:, 0:8], in_=adb[:, 0:8], func=AF.Exp), reads=[adb], writes=[adb])
    S.dve(lambda e: e.tensor_scalar(out=adb[:, 0:8], in0=adb[:, 0:8], scalar1=-1.0, scalar2=None, op0=ALU.mult),
          reads=[adb], writes=[adb])

    GCUT = int(os.environ.get('GCUT', '0'))

    def fin():
        S.barrier(); S.flush(); st.close()
    if GCUT == 1:
        return fin()

    def gt(name):
        return sb(nc, st, name, [128, NT, 8], F32)
    beta, g_, gc, eg, be, kds, gl = gt('g_beta'), gt('g_g'), gt('g_gc'), gt('g_eg'), gt('g_be'), gt('g_kds'), gt('g_gl')
    S.act(lambda e: e.activation(out=beta[:], in_=sall[:, :, 0:8], func=AF.Sigmoid), reads=[sall], writes=[beta])
    S.dve(lambda e: e.tensor_tensor(out=g_[:], in0=sall[:, :, 8:16], in1=adb[:, 8:16].unsqueeze(1).to_broadcast([128, NT, 8]), op=ALU.add),
          reads=[sall, adb], writes=[g_])
    S.act(lambda e: e.activation(out=g_[:], in_=g_[:], func=AF.Exp), reads=[g_], writes=[g_])
    S.act(lambda e: e.activation(out=g_[:], in_=g_[:], func=AF.Ln, bias=1.0), reads=[g_], writes=[g_])
    S.dve(lambda e: e.tensor_tensor(out=g_[:], in0=g_[:], in1=adb[:, 0:8].unsqueeze(1).to_broadcast([128, NT, 8]), op=ALU.mult),
          reads=[g_, adb], writes=[g_])
    if GCUT == 2:
        return fin()
    Lt = sb(nc, st, 'Lt', [128, 128], F32)
    Ut = sb(nc, st, 'Ut', [128, 128], F32)
    bigm = [sb(nc, st, 'bigm%d' % i, [128, 128], F32) for i in range(2)]
    strict = [sb(nc, st, 'strict%d' % i, [128, 128], F32) for i in range(2)]
    bigfull = sb(nc, st, 'bigfull', [128, 128], F32)
    S.pool(lambda e: e.memset(bigfull[:], BIG), writes=[bigfull])
    one = C.ones_f[:, 0:128]
    S.pool(lambda e: e.affine_select(out=Lt[:], in_=one, pattern=[[1, 128]], compare_op=ALU.is_ge, fill=0.0, base=0, channel_multiplier=-1),
           reads=[C.ones_f], writes=[Lt])
    S.pool(lambda e: e.affine_select(out=Ut[:], in_=one, pattern=[[-1, 128]], compare_op=ALU.is_ge, fill=0.0, base=0, channel_multiplier=1),
           reads=[C.ones_f], writes=[Ut])
    S.pool(lambda e: e.affine_select(out=bigm[0][:], in_=bigfull[:], pattern=[[1, 128]], compare_op=ALU.is_gt, fill=0.0, base=0, channel_multiplier=-1),
           reads=[bigfull], writes=[bigm[0]])
    S.pool(lambda e: e.affine_select(out=bigm[1][:], in_=bigfull[:], pattern=[[-1, 128]], compare_op=ALU.is_gt, fill=0.0, base=0, channel_multiplier=1),
           reads=[bigfull], writes=[bigm[1]])
    S.pool(lambda e: e.affine_select(out=strict[0][:], in_=one, pattern=[[-1, 128]], compare_op=ALU.is_gt, fill=0.0, base=0, channel_multiplier=1),
           reads=[C.ones_f], writes=[strict[0]])
    S.pool(lambda e: e.affine_select(out=strict[1][:], in_=one, pattern=[[1, 128]], compare_op=ALU.is_gt, fill=0.0, base=0, channel_multiplier=-1),
           reads=[C.ones_f], writes=[strict[1]])
    if GCUT == 3:
        return fin()
    Bm = {}
    for s_ in (16, 32, 64):
        G = 128 // s_
        E = sb(nc, st, 'E%d' % s_, [G, 128], F32)
        S.pool(lambda e, E=E, G=G, s_=s_: e.affine_select(out=E[:], in_=C.ones_f[0:G, 0:128], pattern=[[1, 128]], compare_op=ALU.is_ge,
                                                         fill=0.0, base=0, channel_multiplier=-s_), reads=[C.ones_f], writes=[E])
        S.pool(lambda e, E=E, G=G, s_=s_: e.affine_select(out=E[:], in_=E[:], pattern=[[-1, 128]], compare_op=ALU.is_gt,
                                                         fill=0.0, base=s_, channel_multiplier=s_), reads=[E], writes=[E])
        pb_ = banks[4]
        S.pe(lambda e, E=E, pb_=pb_: e.matmul(pb_[:, 0:128], lhsT=E[:], rhs=E[:], start=True, stop=True), reads=[E], writes=[pb_])
        Bm[s_] = sb(nc, st, 'Bm%d' % s_, [128, 128], F32)
        S.dve(lambda e, s_=s_, pb_=pb_: e.tensor_copy(out=Bm[s_][:], in_=pb_[:, 0:128]), reads=[pb_], writes=[Bm[s_]])
    Md = [sb(nc, st, 'Md%d' % d, [128, 128], F32) for d in range(2)]
    Mo = [[sb(nc, st, 'Mo%d_%d' % (d, l), [128, 128], F32) for l in range(3)] for d in range(2)]
    for d in range(2):
        S.dve(lambda e, d=d: e.tensor_tensor(out=Md[d][:], in0=strict[d][:], in1=Bm[16][:], op=ALU.mult), reads=[strict[d], Bm[16]], writes=[Md[d]])
        for l, (big_, small_) in enumerate(((32, 16), (64, 32), (None, 64))):
            t_ = Mo[d][l]
            if big_ is None:
                S.dve(lambda e, t_=t_, small_=small_: e.tensor_scalar(out=t_[:], in0=Bm[small_][:], scalar1=-1.0, scalar2=1.0, op0=ALU.mult, op1=ALU.add),
                      reads=[Bm[small_]], writes=[t_])
            else:
                S.dve(lambda e, t_=t_, big_=big_, small_=small_: e.tensor_tensor(out=t_[:], in0=Bm[big_][:], in1=Bm[small_][:], op=ALU.subtract),
                      reads=[Bm[big_], Bm[small_]], writes=[t_])
            S.dve(lambda e, t_=t_, d=d: e.tensor_tensor(out=t_[:], in0=t_[:], in1=strict[d][:], op=ALU.mult), reads=[t_, strict[d]], writes=[t_])
    pgc = banks[1]
    S.pe(lambda e: e.matmul(pgc[:, 0:NT * 8], lhsT=Lt[:], rhs=g_[:, :, :], start=True, stop=True), reads=[Lt, g_], writes=[pgc])
    S.dve(lambda e: e.tensor_copy(out=gc[:, :, 0:4], in_=pgc[:, 0:NT * 8].rearrange("p (n c) -> p n c", c=8)[:, :, 0:4]), reads=[pgc], writes=[gc])
    pgc2 = banks[2]
    S.pe(lambda e: e.matmul(pgc2[:, 0:NT * 8], lhsT=Ut[:], rhs=g_[:, :, :], start=True, stop=True), reads=[Ut, g_], writes=[pgc2])
    S.dve(lambda e: e.tensor_copy(out=gc[:, :, 4:8], in_=pgc2[:, 0:NT * 8].rearrange("p (n c) -> p n c", c=8)[:, :, 4:8]), reads=[pgc2], writes=[gc])
    if GCUT == 5:
        return fin()
    pgt = banks[3]
    S.pe(lambda e: e.matmul(pgt[:, 0:NT * 8], lhsT=C.ones_f[:, 0:128], rhs=g_[:, :, :], start=True, stop=True), reads=[C.ones_f, g_], writes=[pgt])
    S.act(lambda e: e.activation(out=gl[:], in_=pgt[:, 0:NT * 8].rearrange("p (n c) -> p n c", c=8), func=AF.Exp), reads=[pgt], writes=[gl])
    S.dve(lambda e: e.tensor_tensor(out=kds[:], in0=pgt[:, 0:NT * 8].rearrange("p (n c) -> p n c", c=8), in1=gc[:], op=ALU.subtract),
          reads=[pgt, gc], writes=[kds])
    if GCUT == 6:
        return fin()
    S.act(lambda e: e.activation(out=kds[:], in_=kds[:], func=AF.Exp), reads=[kds], writes=[kds])
    S.act(lambda e: e.activation(out=eg[:], in_=gc[:], func=AF.Exp), reads=[gc], writes=[eg])
    S.dve(lambda e: e.tensor_tensor(out=be[:], in0=beta[:], in1=eg[:], op=ALU.mult), reads=[beta, eg], writes=[be])
    if GCUT == 4:
        return fin()
    dbg_dump(C, 'dbg_gc', gc, gc[:, :, :], [128, NT, 8], F32)
    dbg_dump(C, 'dbg_beta', beta, beta[:, :, :], [128, NT, 8], F32)
    dbg_dump(C, 'dbg_g', g_, g_[:, :, :], [128, NT, 8], F32)

    S.barrier()
    import os
    GSTOP = int(os.environ.get('GSTOP', '99'))
    def tile_of(d, n):
        if n < 2:
            return 32 + n if d == 0 else 33 - n
        return n - 2 if d == 0 else 33 - n
    opnd = [[{k: sb(nc, st, 'op_%s_%d_%d' % (k, d, i), [128, 4, 128], BF16) for k in ('kT', 'qT', 'ktok', 'qtok', 'vtok')}
             for i in range(2)] for d in range(2)]

    def load_tile(d, n):
        nt = tile_of(d, n)
        o = opnd[d][n % 2]
        c0 = T + (nt - 32) * 128 if nt >= 32 else nt * 128
        S.dma(o['kT'][:, :, :], C.KT[:, c0:c0 + 128].rearrange("(h p) n -> p h n", p=128), reads=[C.KT], writes=[o['kT']])
        S.dma(o['ktok'][:, :, :], C.KTOK[c0:c0 + 128, :].rearrange("p (h d) -> p h d", d=128), reads=[C.KTOK], writes=[o['ktok']])
        S.dma(o['vtok'][:, :, :], C.VTOK[c0:c0 + 128, :].rearrange("p (h d) -> p h d", d=128), reads=[C.VTOK], writes=[o['vtok']])
        if nt < 32:
            S.dma(o['qT'][:, :, :], C.QT[:, c0:c0 + 128].rearrange("(h p) n -> p h n", p=128), reads=[C.QT], writes=[o['qT']])
            S.dma(o['qtok'][:, :, :], C.QTOK[c0:c0 + 128, :].rearrange("p (h d) -> p h d", d=128), reads=[C.QTOK], writes=[o['qtok']])

    def cb(name, dt, n=1, shape=(128, 128)):
        return [[sb(nc, st, '%s_%d_%d' % (name, c, i), list(shape), dt) for i in range(n)] for c in range(8)]
    dgc = cb('dgc', F32); Dm = dgc; Ai = cb('Ai', F32)
    Pb = cb('Pb', BF16, 2); PTb = cb('PTb', BF16, 2); Yb = cb('Yb', BF16, 2)
    bv = cb('bv', BF16); kbe = cb('kbe', BF16); qe = cb('qe', BF16); AOb = cb('AOb', BF16, 3)
    attnT = cb('attnT', BF16, 2); u_ = cb('u_', F32, 2); wT = cb('wT', BF16, 2); kd = cb('kd', BF16, 2); qdT = cb('qdT', BF16, 2)
    S32 = cb('S32', F32); Sbf = cb('Sbf', BF16, 2); vn = cb('vn', BF16)
    for c in range(8):
        S.pool(lambda e, c=c: e.memset(S32[c][0][:], 0.0), writes=[S32[c][0]])
        S.pool(lambda e, c=c: e.memset(Sbf[c][0][:], 0.0), writes=[Sbf[c][0]])
    oacc = sb(nc, st, 'oacc', [128, 32, 512], F32)
    ores = [[Res() for h in range(4)] for nt in range(32)]
    ofirst = [[True] * 4 for nt in range(32)]

    def precompute(c, n):
        d, h = c // 4, c % 4
        nt = tile_of(d, n)
        lat = nt < 32
        o = opnd[d][n % 2]
        r = n % 2
        sl = slots[c]
        gcol = gc[:, nt, c:c + 1]
        S.act(lambda e: e.activation(out=dgc[c][0][:], in_=C.ident_f[:], func=AF.Copy, scale=gcol),
              reads=[C.ident_f, gc], writes=[dgc[c][0]])
        S.pe(lambda e: e.matmul(sl[1].f, lhsT=o['kT'][:, h, :], rhs=o['kT'][:, h, :], start=True, stop=True),
             reads=[o['kT']], writes=[sl[1]])
        if lat:
            S.pe(lambda e: e.matmul(sl[2].f, lhsT=o['qT'][:, h, :], rhs=o['kT'][:, h, :], start=True, stop=True),
                 reads=[o['kT'], o['qT']], writes=[sl[2]])
        yield
        S.pe(lambda e: e.matmul(sl[0].f, lhsT=C.ones_f[:, 0:128], rhs=dgc[c][0][:], start=True, stop=False),
             reads=[C.ones_f, dgc[c][0]], writes=[sl[0]], sig=False)
        S.pe(lambda e: e.matmul(sl[0].f, lhsT=C.ident_f[:], rhs=bigm[d][:], start=False, stop=True),
             reads=[C.ident_f, bigm[d]], writes=[sl[0]])
        yield
        S.act(lambda e: e.activation(out=Dm[c][0][:], in_=sl[0].f, func=AF.Exp, bias=gcol, scale=-1.0),
              reads=[sl[0], gc], writes=[Dm[c][0]])
        yield
        S.dve(lambda e: e.scalar_tensor_tensor(out=Ai[c][0][:], in0=sl[1].f, scalar=beta[:, nt, c:c + 1], in1=Dm[c][0][:],
                                               op0=ALU.mult, op1=ALU.mult), reads=[sl[1], beta, Dm[c][0]], writes=[Ai[c][0]])
        if lat:
            S.dve(lambda e: e.tensor_tensor(out=qe[c][0][:], in0=sl[2].f, in1=Dm[c][0][:], op=ALU.mult),
                  reads=[sl[2], Dm[c][0]], writes=[qe[c][0]])
        yield
        A = Pb[c][0]
        S.dve(lambda e: e.tensor_tensor(out=A[:], in0=Ai[c][0][:], in1=Md[d][:], op=ALU.mult),
              reads=[Ai[c][0], Md[d]], writes=[A])
        for li in range(3):
            fn = lambda e, li=li: e.tensor_tensor(out=AOb[c][li][:], in0=Ai[c][0][:], in1=Mo[d][li][:], op=ALU.mult)
            if li < 2:
                S.dve(fn, reads=[Ai[c][0], Mo[d][li]], writes=[AOb[c][li]])
            else:
                S.pool(fn, reads=[Ai[c][0], Mo[d][li]], writes=[AOb[c][li]])
        yield
        S.pe(lambda e: e.transpose(out=sl[0].b, in_=A[:], identity=C.ident_b[:]), reads=[A, C.ident_b], writes=[sl[0]])
        if lat:
            S.pe(lambda e: e.transpose(out=sl[1].b, in_=qe[c][0][:], identity=C.ident_b[:]), reads=[qe[c][0], C.ident_b], writes=[sl[1]])
        yield
        AT = PTb[c][0]
        Y = Yb[c][0]
        S.act(lambda e: e.activation(out=AT[:], in_=sl[0].b, func=AF.Copy), reads=[sl[0]], writes=[AT])
        S.dve(lambda e: e.scalar_tensor_tensor(out=Y[:], in0=sl[0].b, scalar=-1.0, in1=C.ident_b[:], op0=ALU.mult, op1=ALU.add),
              reads=[sl[0], C.ident_b], writes=[Y])
        if lat:
            S.act(lambda e: e.activation(out=attnT[c][r][:], in_=sl[1].b, func=AF.Copy), reads=[sl[1]], writes=[attnT[c][r]])
        yield
        S.act(lambda e: e.activation(out=bv[c][0][:], in_=o['vtok'][:, h, :], func=AF.Copy, scale=beta[:, nt, c:c + 1]),
              reads=[o['vtok'], beta], writes=[bv[c][0]])
        S.act(lambda e: e.activation(out=kbe[c][0][:], in_=o['ktok'][:, h, :], func=AF.Copy, scale=be[:, nt, c:c + 1]),
              reads=[o['ktok'], be], writes=[kbe[c][0]])
        S.pool(lambda e: e.tensor_scalar(out=kd[c][r][:], in0=o['ktok'][:, h, :], scalar1=kds[:, nt, c:c + 1], scalar2=None, op0=ALU.mult),
               reads=[o['ktok'], kds], writes=[kd[c][r]])
        if lat:
            S.act(lambda e: e.activation(out=qe[c][0][:], in_=o['qtok'][:, h, :], func=AF.Copy, scale=eg[:, nt, c:c + 1]),
                  reads=[o['qtok'], eg], writes=[qe[c][0]])
        cur = 0
        for lvl in range(1, 4):
            P, PT, Yc = Pb[c][cur], PTb[c][cur], Yb[c][cur]
            Pn, PTn, Yn = Pb[c][1 - cur], PTb[c][1 - cur], Yb[c][1 - cur]
            S.pe(lambda e, P=P, PT=PT: e.matmul(sl[0].f, lhsT=PT[:], rhs=P[:], start=True, stop=True), reads=[P, PT], writes=[sl[0]])
            if lvl < 3:
                S.pe(lambda e, P=P, PT=PT: e.matmul(sl[1].f, lhsT=P[:], rhs=PT[:], start=True, stop=True), reads=[P, PT], writes=[sl[1]])
            yield
            S.act(lambda e, Pn=Pn: e.activation(out=Pn[:], in_=sl[0].f, func=AF.Copy), reads=[sl[0]], writes=[Pn])
            if lvl < 3:
                S.dve(lambda e, PTn=PTn: e.tensor_copy(out=PTn[:], in_=sl[1].f), reads=[sl[1]], writes=[PTn])
            yield
            S.pe(lambda e, Pn=Pn, Yc=Yc: e.matmul(sl[2].f, lhsT=Pn[:], rhs=Yc[:], start=True, stop=True), reads=[Pn, Yc], writes=[sl[2]])
            yield
            S.dve(lambda e, Yc=Yc, Yn=Yn: e.tensor_tensor(out=Yn[:], in0=sl[2].f, in1=Yc[:], op=ALU.add), reads=[sl[2], Yc], writes=[Yn])
            yield
            cur = 1 - cur
        for li in range(3):
            Yc, Yn = Yb[c][cur], Yb[c][1 - cur]
            Tt, N1 = Pb[c][0], PTb[c][0]
            S.pe(lambda e, Yc=Yc: e.transpose(out=sl[0].b, in_=Yc[:], identity=C.ident_b[:]), reads=[Yc, C.ident_b], writes=[sl[0]])
            S.pe(lambda e, Yc=Yc, li=li: e.matmul(sl[1].f, lhsT=AOb[c][li][:], rhs=Yc[:], start=True, stop=True),
                 reads=[AOb[c][li], Yc], writes=[sl[1]])
            yield
            S.act(lambda e, Tt=Tt: e.activation(out=Tt[:], in_=sl[0].b, func=AF.Copy), reads=[sl[0]], writes=[Tt])
            S.dve(lambda e, N1=N1: e.tensor_copy(out=N1[:], in_=sl[1].f), reads=[sl[1]], writes=[N1])
            yield
            S.pe(lambda e, Tt=Tt, N1=N1: e.matmul(sl[2].f, lhsT=Tt[:], rhs=N1[:], start=True, stop=True), reads=[Tt, N1], writes=[sl[2]])
            yield
            S.dve(lambda e, Yc=Yc, Yn=Yn: e.scalar_tensor_tensor(out=Yn[:], in0=sl[2].f, scalar=-1.0, in1=Yc[:], op0=ALU.mult, op1=ALU.add),
                  reads=[sl[2], Yc], writes=[Yn])
            yield
            cur = 1 - cur
        Y = Yb[c][cur]
        S.pe(lambda e: e.matmul(sl[0].f, lhsT=Y[:], rhs=bv[c][0][:], start=True, stop=True), reads=[Y, bv[c][0]], writes=[sl[0]])
        S.pe(lambda e: e.matmul(sl[1].f, lhsT=kbe[c][0][:], rhs=Y[:], start=True, stop=True), reads=[Y, kbe[c][0]], writes=[sl[1]])
        if lat:
            S.pe(lambda e: e.transpose(out=sl[2].b, in_=qe[c][0][:], identity=C.ident_b[:]), reads=[qe[c][0], C.ident_b], writes=[sl[2]])
        yield
        S.act(lambda e: e.activation(out=u_[c][r][:], in_=sl[0].f, func=AF.Copy), reads=[sl[0]], writes=[u_[c][r]])
        S.dve(lambda e: e.tensor_copy(out=wT[c][r][:], in_=sl[1].f), reads=[sl[1]], writes=[wT[c][r]])
        if lat:
            S.act(lambda e: e.activation(out=qdT[c][r][:], in_=sl[2].b, func=AF.Copy), reads=[sl[2]], writes=[qdT[c][r]])
        yield

    def scan(c, n):
        d, h = c // 4, c % 4
        nt = tile_of(d, n)
        lat = nt < 32
        r = n % 2
        sl = slots[c][3]
        Sold, Snew = Sbf[c][n % 2], Sbf[c][1 - n % 2]
        S.pe(lambda e: e.matmul(sl.f, lhsT=wT[c][r][:], rhs=Sold[:], start=True, stop=True), reads=[wT[c][r], Sold], writes=[sl])
        yield
        S.dve(lambda e: e.scalar_tensor_tensor(out=vn[c][0][:], in0=sl.f, scalar=-1.0, in1=u_[c][r][:], op0=ALU.mult, op1=ALU.add),
              reads=[sl, u_[c][r]], writes=[vn[c][0]])
        yield
        S.pe(lambda e: e.matmul(sl.f, lhsT=kd[c][r][:], rhs=vn[c][0][:], start=True, stop=True), reads=[kd[c][r], vn[c][0]], writes=[sl])
        yield
        glc = gl[:, nt, c:c + 1]
        S.dve(lambda e: e.scalar_tensor_tensor(out=Snew[:], in0=S32[c][0][:], scalar=glc, in1=sl.f, op0=ALU.mult, op1=ALU.add),
              reads=[S32[c][0], gl, sl], writes=[Snew])
        S.dve(lambda e: e.scalar_tensor_tensor(out=S32[c][0][:], in0=S32[c][0][:], scalar=glc, in1=sl.f, op0=ALU.mult, op1=ALU.add),
              reads=[S32[c][0], gl, sl], writes=[S32[c][0]])
        yield
        if lat:
            S.pe(lambda e: e.matmul(sl.f, lhsT=qdT[c][r][:], rhs=Sold[:], start=True, stop=False), reads=[qdT[c][r], Sold], writes=[sl], sig=False)
            S.pe(lambda e: e.matmul(sl.f, lhsT=attnT[c][r][:], rhs=vn[c][0][:], start=False, stop=True), reads=[attnT[c][r], vn[c][0]], writes=[sl])
            yield
            orr = ores[nt][h]
            if ofirst[nt][h]:
                ofirst[nt][h] = False
                S.act(lambda e: e.activation(out=oacc[:, nt, h * 128:(h + 1) * 128], in_=sl.f, func=AF.Copy), reads=[sl], writes=[orr])
            else:
                S.dve(lambda e: e.tensor_tensor(out=oacc[:, nt, h * 128:(h + 1) * 128], in0=sl.f, in1=oacc[:, nt, h * 128:(h + 1) * 128], op=ALU.add),
                      reads=[sl, orr], writes=[orr])
            yield

    NR = min(34, GSTOP)
    if GSTOP >= 0:
        for d in range(2):
            load_tile(d, 0)
        run_interleaved([precompute(c, 0) for c in range(8)])
    for n in range(NR):
        gens = [scan(c, n) for c in range(8)]
        if n + 1 < NR:
            for d in range(2):
                load_tile(d, n + 1)
            gens += [precompute(c, n + 1) for c in range(8)]
        run_interleaved(gens)
    for nt in (range(32) if NR == 34 else []):
        S.dma(OACC[nt * 128:(nt + 1) * 128, :], oacc[:, nt, :], reads=ores[nt], writes=[OACC], q='sp')
    for c in range(8):
        dbg_dump(C, 'dbg_S%d' % c, S32[c][0], S32[c][0][:], [128, 128], F32)
    S.barrier()
    S.flush()
    st.close()


def load_rows_bcast(C, st, name, dram_row, n):
    t = sb(C.nc, st, name, [128, n], F32)
    C.S.dma(t[:], dram_row.to_broadcast([128, n]), writes=[t])
    return t


class Epi:
    def __init__(self, C, st, ln_idx, gate_ap, nsub):
        nc = C.nc
        self.C, self.nsub, self.gate = C, nsub, gate_ap
        self.g = load_rows_bcast(C, st, 'ln_g%d' % ln_idx, C.din['ln_g'][ln_idx:ln_idx + 1, :], D)
        self.b = load_rows_bcast(C, st, 'ln_b%d' % ln_idx, C.din['ln_b'][ln_idx:ln_idx + 1, :], D)
        self.t2 = sb(nc, st, 'ep_t2', [128, nsub, D], F32)
        self.junk = sb(nc, st, 'ep_junk', [128, D], BF16)
        self.st = sb(nc, st, 'ep_st', [128, 6, nsub], F32)
        self.eps = sb(nc, st, 'ep_eps', [128, 1], F32)
        self.xo = [self.t2] * 2
        C.S.pool(lambda e: e.memset(self.eps[:], LN_EPS), writes=[self.eps])
        self.i = 0

    def sub(self, s_, ypair, xt):
        S, t2, stt = self.C.S, self.t2, self.st
        for hf in range(2):
            S.dve(lambda e, hf=hf: e.tensor_tensor(out=t2[:, s_, hf * 512:(hf + 1) * 512], in0=ypair[hf][:],
                                                   in1=self.gate[:, hf * 512:(hf + 1) * 512], op=ALU.mult),
                  reads=[ypair[hf], self.C.modrow], writes=[t2])
        S.dve(lambda e: e.scalar_tensor_tensor(out=t2[:, s_, :], in0=xt[:, s_, :], scalar=ALPHA, in1=t2[:, s_, :], op0=ALU.mult, op1=ALU.add),
              reads=[xt, t2], writes=[t2])
        S.act(lambda e: e.activation(out=self.junk[:], in_=t2[:, s_, :], func=AF.Copy, accum_out=stt[:, 0, s_:s_ + 1]),
              reads=[t2], writes=[self.junk, stt])
        S.act(lambda e: e.activation(out=self.junk[:], in_=t2[:, s_, :], func=AF.Square, accum_out=stt[:, 1, s_:s_ + 1]),
              reads=[t2], writes=[self.junk, stt])

    def finish(self, dst_rows, xt_unused=None):
        S, t2, stt, n = self.C.S, self.t2, self.st, self.nsub
        dst, r0 = dst_rows
        xo = self.xo[self.i % 2]
        self.i += 1
        S.dve(lambda e: e.tensor_scalar(out=stt[:, 2, :], in0=stt[:, 0, :], scalar1=1.0 / D, scalar2=None, op0=ALU.mult), reads=[stt], writes=[stt])
        S.dve(lambda e: e.tensor_tensor(out=stt[:, 4, :], in0=stt[:, 2, :], in1=stt[:, 2, :], op=ALU.mult), reads=[stt], writes=[stt])
        S.dve(lambda e: e.scalar_tensor_tensor(out=stt[:, 3, :], in0=stt[:, 1, :], scalar=1.0 / D, in1=stt[:, 4, :], op0=ALU.mult, op1=ALU.subtract),
              reads=[stt], writes=[stt])
        S.act(lambda e: e.activation(out=stt[:, 3, :], in_=stt[:, 3, :], func=AF.Ln, bias=self.eps[:, 0:1]), reads=[stt, self.eps], writes=[stt])
        S.act(lambda e: e.activation(out=stt[:, 3, :], in_=stt[:, 3, :], func=AF.Exp, scale=-0.5), reads=[stt], writes=[stt])
        S.dve(lambda e: e.scalar_tensor_tensor(out=stt[:, 5, :], in0=stt[:, 2, :], scalar=-1.0, in1=stt[:, 3, :], op0=ALU.mult, op1=ALU.mult),
              reads=[stt], writes=[stt])
        for s_ in range(n):
            S.act(lambda e, s_=s_: e.activation(out=t2[:, s_, :], in_=t2[:, s_, :], func=AF.Identity, scale=stt[:, 3, s_:s_ + 1], bias=stt[:, 5, s_:s_ + 1]),
                  reads=[t2, stt], writes=[t2])
            S.pool(lambda e, s_=s_: e.tensor_tensor(out=xo[:, s_, :], in0=t2[:, s_, :], in1=self.g[:], op=ALU.mult), reads=[t2, self.g], writes=[xo])
            S.pool(lambda e, s_=s_: e.tensor_tensor(out=xo[:, s_, :], in0=xo[:, s_, :], in1=self.b[:], op=ALU.add), reads=[xo, self.b], writes=[xo])
        S.dma(dst[r0:r0 + n * 128, :].rearrange("(s p) d -> p s d", p=128), xo[:, :, :], reads=[xo], writes=[dst], q='sp')


def out_proj(C, mixT, wout, s_, ypair, nk):
    S = C.S
    for hf in range(2):
        for k in range(nk):
            S.pe(lambda e, hf=hf, k=k: e.matmul(ypair[hf][:], lhsT=mixT[:, k, s_ * 128:(s_ + 1) * 128], rhs=wout[:, k, hf * 512:(hf + 1) * 512],
                                                start=(k == 0), stop=(k == nk - 1)), reads=[mixT, wout], writes=[ypair[hf]], sig=(k == nk - 1))


def phase_mix0_out(C, X1):
    nc, S = C.nc, C.S
    st = ExitStack()
    wout = load_w_bf16(C, st, 'wout0', C.din['even_w_out'][:, :], 8, D)
    normw = load_rows_bcast(C, st, 'normw', C.din['gdn_norm_w'][0:1, :], 128)
    epi = Epi(C, st, 0, C.modrow[:, 2 * D:3 * D], 4)
    yps = [[ps(nc, st, 'yps%d_%d' % (i, h), [128, 512], F32) for h in range(2)] for i in range(2)]
    pT = [ps(nc, st, 'pTm%d' % i, [128, 512], BF16) for i in range(2)]
    ot = [sb(nc, st, 'ot%d' % i, [128, 4, 512], F32) for i in range(2)]
    gt_ = [sb(nc, st, 'gt%d' % i, [128, 4, 512], BF16) for i in range(2)]
    xt = [sb(nc, st, 'xtm%d' % i, [128, 4, D], F32) for i in range(2)]
    mixT = [sb(nc, st, 'mixT%d' % i, [128, 8, 512], BF16) for i in range(2)]
    osq = sb(nc, st, 'osq', [128, 4, 512], F32)
    ss = sb(nc, st, 'oss', [128, 16], F32)
    og = sb(nc, st, 'og', [128, 4, 512], BF16)
    epsr = sb(nc, st, 'epsr', [128, 1], F32)
    S.pool(lambda e: e.memset(epsr[:], RMS_EPS), writes=[epsr])

    def load(t):
        i = t % 2
        S.dma(ot[i][:, :, :], C.OACC[t * 512:(t + 1) * 512, :].rearrange("(s p) d -> p s d", p=128), reads=[C.OACC], writes=[ot[i]])
        S.dma(gt_[i][:, :, :], C.G0[t * 512:(t + 1) * 512, :].rearrange("(s p) d -> p s d", p=128), reads=[C.G0], writes=[gt_[i]])
        S.dma(xt[i][:, :, :], C.din['x'][t * 512:(t + 1) * 512, :].rearrange("(s p) d -> p s d", p=128), writes=[xt[i]])
        S.dma(mixT[i][:, 4:8, :], C.YPT[:, t * 512:(t + 1) * 512].rearrange("(f p) n -> p f n", p=128), reads=[C.YPT], writes=[mixT[i]])

    load(0)
    yi = 0
    for t in range(8):
        if t + 1 < 8:
            load(t + 1)
        i = t % 2
        o_, g_, x_, m_ = ot[i], gt_[i], xt[i], mixT[i]
        S.pool(lambda e, o_=o_: e.tensor_tensor(out=osq[:], in0=o_[:], in1=o_[:], op=ALU.mult), reads=[o_], writes=[osq])
        S.dve(lambda e: e.tensor_reduce(out=ss[:], in_=osq[:].rearrange("p s (h d) -> p (s h) d", d=128), axis=mybir.AxisListType.X, op=ALU.add),
              reads=[osq], writes=[ss])
        S.act(lambda e: e.activation(out=ss[:], in_=ss[:], func=AF.Ln, scale=1.0 / 128, bias=epsr[:, 0:1]), reads=[ss, epsr], writes=[ss])
        S.act(lambda e: e.activation(out=ss[:], in_=ss[:], func=AF.Exp, scale=-0.5), reads=[ss], writes=[ss])
        S.dve(lambda e, o_=o_: e.tensor_tensor(out=osq[:].rearrange("p s (h d) -> p (s h) d", d=128), in0=o_[:].rearrange("p s (h d) -> p (s h) d", d=128),
                                              in1=ss[:].unsqueeze(2).to_broadcast([128, 16, 128]), op=ALU.mult), reads=[o_, ss], writes=[osq])
        S.pool(lambda e: e.tensor_tensor(out=osq[:].rearrange("p s (h d) -> p (s h) d", d=128), in0=osq[:].rearrange("p s (h d) -> p (s h) d", d=128),
                                         in1=normw[:].unsqueeze(1).to_broadcast([128, 16, 128]), op=ALU.mult), reads=[osq, normw], writes=[osq])
        S.dve(lambda e, g_=g_: e.tensor_tensor(out=og[:], in0=osq[:], in1=g_[:], op=ALU.mult), reads=[osq, g_], writes=[og])
        for h in range(4):
            p = pT[h % 2]
            for s_ in range(4):
                S.pe(lambda e, p=p, h=h, s_=s_: e.transpose(out=p[:, s_ * 128:(s_ + 1) * 128], in_=og[:, s_, h * 128:(h + 1) * 128], identity=C.ident_b[:]),
                     reads=[og, C.ident_b], writes=[p], sig=(s_ == 3))
            if h % 2 == 0:
                S.act(lambda e, p=p, h=h, m_=m_: e.activation(out=m_[:, h, :], in_=p[:], func=AF.Copy), reads=[p], writes=[m_])
            else:
                S.dve(lambda e, p=p, h=h, m_=m_: e.tensor_copy(out=m_[:, h, :], in_=p[:]), reads=[p], writes=[m_])
        for s_ in range(4):
            yp = yps[yi % 2]; yi += 1
            out_proj(C, m_, wout, s_, yp, 8)
            epi.sub(s_, yp, x_)
        epi.finish((X1, t * 512))
    dbg_dump(C, 'dbg_x1', X1, X1[0:128, :], [128, D], F32)
    S.barrier()
    S.flush()
    st.close()


def phase_ffn_up(C, layer, Xin):
    nc, S = C.nc, C.S
    st = ExitStack()
    if layer == 0:
        C.AT = C.scratch('AT', [DFF, T + 128], BF16)
        C.GTt = C.scratch('GTt', [DFF, T], BF16)
        for r0 in range(0, DFF, 128):
            S.dma(C.AT[r0:r0 + 128, 0:64], C.zeros_b[:, 0:64], reads=[C.zeros_b], writes=[C.AT])
            S.dma(C.AT[r0:r0 + 128, T + 64:T + 128], C.zeros_b[:, 0:64], reads=[C.zeros_b], writes=[C.AT])
    AT, GTt = C.AT, C.GTt
    w = load_w_bf16(C, st, 'wup', C.din['ffn_w_up'][layer, :, :], 8, 2 * DFF)
    xt = sb(nc, st, 'xtu', [128, 4, D], F32)
    ub = sb(nc, st, 'ubu', [128, 4, D], BF16)
    uT = [sb(nc, st, 'uTu%d' % i, [128, 8, 512], BF16) for i in range(2)]
    stg = [sb(nc, st, 'stgu%d' % i, [128, 4, 512], BF16) for i in range(2)]
    pT = [ps(nc, st, 'pTu%d' % i, [128, 512], BF16) for i in range(2)]
    pm = [ps(nc, st, 'pmu%d' % i, [128, 512], F32) for i in range(4)]
    pmi = 0
    for t in range(8):
        S.dma(xt[:, :, :], Xin[t * 512:(t + 1) * 512, :].rearrange("(s p) d -> p s d", p=128), reads=[Xin], writes=[xt])
        ut = uT[t % 2]
        modulate_transpose(C, xt, 4, C.modrow[:, 3 * D:4 * D], C.modrow[:, 4 * D:5 * D], ub, ut, pT, t)
        gi = 0
        for f0 in range(0, 44, 4):
            sg = stg[gi % 2]; gi += 1
            nf = min(4, 44 - f0)
            for j in range(nf):
                fc = f0 + j
                p = pm[pmi % 4]; pmi += 1
                for k in range(8):
                    S.pe(lambda e, p=p, k=k, fc=fc, ut=ut: e.matmul(p[:], lhsT=w[:, k, fc * 128:(fc + 1) * 128], rhs=ut[:, k, :],
                                                                 start=(k == 0), stop=(k == 7)), reads=[w, ut], writes=[p], sig=(k == 7))
                if pmi % 2 == 0:
                    S.act(lambda e, p=p, j=j, sg=sg: e.activation(out=sg[:, j, :], in_=p[:], func=AF.Copy), reads=[p], writes=[sg])
                else:
                    S.dve(lambda e, p=p, j=j, sg=sg: e.tensor_copy(out=sg[:, j, :], in_=p[:]), reads=[p], writes=[sg])
            if f0 < 22:
                na = min(nf, 22 - f0)
                S.dma(AT[f0 * 128:(f0 + na) * 128, 64 + t * 512:64 + (t + 1) * 512].rearrange("(j p) n -> p j n", p=128), sg[:, 0:na, :],
                      reads=[sg], writes=[AT], q='act')
                if na < nf:
                    S.dma(GTt[0:(nf - na) * 128, t * 512:(t + 1) * 512].rearrange("(j p) n -> p j n", p=128), sg[:, na:nf, :],
                          reads=[sg], writes=[GTt], q='act')
            else:
                g0 = f0 - 22
                S.dma(GTt[g0 * 128:(g0 + nf) * 128, t * 512:(t + 1) * 512].rearrange("(j p) n -> p j n", p=128), sg[:, 0:nf, :],
                      reads=[sg], writes=[GTt], q='act')
    S.barrier()
    S.flush()
    st.close()


def phase_ffn_down(C, layer, Xin, Xout):
    nc, S = C.nc, C.S
    st = ExitStack()
    AT, GTt = C.AT, C.GTt
    NTK = 256
    pconv = [ps(nc, st, 'pcv%d' % i, [128, 512], F32) for i in range(2)]
    yps = [[ps(nc, st, 'ypd%d_%d' % (i, h), [128, 512], F32) for h in range(2)] for i in range(2)]
    dg = build_diag(C, st, pconv[0], C.din['ffn_conv_w'][layer, :, :], 9, 22, 'cw9')
    wd = load_w_bf16(C, st, 'wdn', C.din['ffn_w_down'][layer, :, :], 22, D)
    epi = Epi(C, st, layer * 2 + 1, C.modrow[:, 5 * D:6 * D], 2)
    ad = [sb(nc, st, 'ad0', [128, 22, 384], BF16)] * 2
    gt_ = [sb(nc, st, 'gd0', [128, 22, 256], BF16)] * 2
    xt = [sb(nc, st, 'xtd0', [128, 2, D], F32)] * 2
    apad = sb(nc, st, 'apad', [128, 22, 6, 66], BF16)
    hs = [sb(nc, st, 'hs%d' % i, [128, 256], BF16) for i in range(2)]
    hg = sb(nc, st, 'hg', [128, 22, 256], BF16)
    S.pool(lambda e: e.memset(apad[:], 0.0), writes=[apad])
    ntile = T // NTK

    def load(t):
        i = t % 2
        S.dma(ad[i][:, :, :], AT[:, t * NTK:t * NTK + 384].rearrange("(f p) n -> p f n", p=128), reads=[AT], writes=[ad[i]])
        S.dma(gt_[i][:, :, :], GTt[:, t * NTK:(t + 1) * NTK].rearrange("(f p) n -> p f n", p=128), reads=[GTt], writes=[gt_[i]])
        S.dma(xt[i][:, :, :], Xin[t * NTK:(t + 1) * NTK, :].rearrange("(s p) d -> p s d", p=128), reads=[Xin], writes=[xt[i]])

    ci = 0
    yi = 0
    for t in range(ntile):
        load(t)
        i = t % 2
        a_, g_, x_ = ad[i], gt_[i], xt[i]
        for fc in range(22):
            S.pool(lambda e, fc=fc, a_=a_: e.tensor_copy(out=apad[:, fc, :, 1:65], in_=a_[:, fc, :].rearrange("p (r c) -> p r c", c=64)),
                   reads=[a_], writes=[apad])
        for fc in range(22):
            pc = pconv[ci % 2]; h_ = hs[ci % 2]; ci += 1
            for tap in range(9):
                dy, dx = tap // 3 - 1, tap % 3 - 1
                S.pe(lambda e, pc=pc, fc=fc, tap=tap, dy=dy, dx=dx: e.matmul(
                    pc[:, 0:256], lhsT=dg[:, fc, tap, :], rhs=apad[:, fc, 1 + dy:5 + dy, 1 + dx:65 + dx], start=(tap == 0), stop=(tap == 8)),
                    reads=[dg, apad], writes=[pc], sig=(tap == 8))
            S.act(lambda e, pc=pc, h_=h_: e.activation(out=h_[:], in_=pc[:, 0:256], func=AF.Silu), reads=[pc], writes=[h_])
            fn = lambda e, fc=fc, h_=h_, g_=g_: e.tensor_tensor(out=hg[:, fc, :], in0=h_[:], in1=g_[:, fc, :], op=ALU.mult)
            if fc % 2 == 0:
                S.pool(fn, reads=[h_, g_], writes=[hg])
            else:
                S.dve(fn, reads=[h_, g_], writes=[hg])
        if t == 1 and layer == 0:
            dbg_dump(C, 'dbg_hg', hg, hg[:, :, :], [128, 22, 256], BF16)
            dbg_dump(C, 'dbg_apad', apad, apad[:, :, :, :], [128, 22, 6, 66], BF16)
        for s_ in range(2):
            yp = yps[yi % 2]; yi += 1
            out_proj(C, hg, wd, s_, yp, 22)
            epi.sub(s_, yp, x_)
        epi.finish((Xout, t * NTK))
    S.barrier()
    S.flush()
    st.close()


def phase_inproj1(C, Xin):
    nc, S = C.nc, C.S
    st = ExitStack()
    GBT = C.scratch('GBT', [512, T], BF16); C.GBT = GBT
    M1T = C.scratch('M1T', [512, T + 32], BF16); C.M1T = M1T
    M2T = C.scratch('M2T', [512, T + 32], BF16); C.M2T = M2T
    for M in (M1T, M2T):
        for r0 in range(0, 512, 128):
            S.dma(M[r0:r0 + 128, 0:16], C.zeros_b[:, 0:16], reads=[C.zeros_b], writes=[M])
            S.dma(M[r0:r0 + 128, T + 16:T + 32], C.zeros_b[:, 0:16], reads=[C.zeros_b], writes=[M])
    w = load_w_bf16(C, st, 'w_in1', C.din['odd_w_in'][:, :], 8, 2560)
    xt = sb(nc, st, 'xt1', [128, 4, D], F32)
    ub = sb(nc, st, 'ub1', [128, 4, D], BF16)
    uT = [sb(nc, st, 'uT1%d' % i, [128, 8, 512], BF16) for i in range(2)]
    stg = [[sb(nc, st, 'stg1_%d_%d' % (g, i), [128, 4, 512], BF16) for i in range(2)] for g in range(3)]
    tmp = [sb(nc, st, 'tmp1_%d' % i, [128, 512], F32) for i in range(2)]
    pT = [ps(nc, st, 'pT1%d' % i, [128, 512], BF16) for i in range(2)]
    pm = [ps(nc, st, 'pm1%d' % i, [128, 512], F32) for i in range(4)]
    pmi = 0
    ti = 0

    def mm(fc, ut):
        nonlocal pmi
        p = pm[pmi % 4]; pmi += 1
        for k in range(8):
            S.pe(lambda e, p=p, k=k: e.matmul(p[:], lhsT=w[:, k, fc * 128:(fc + 1) * 128], rhs=ut[:, k, :], start=(k == 0), stop=(k == 7)),
                 reads=[w, ut], writes=[p], sig=(k == 7))
        return p

    for t in range(8):
        S.dma(xt[:, :, :], Xin[t * 512:(t + 1) * 512, :].rearrange("(s p) d -> p s d", p=128), reads=[Xin], writes=[xt])
        ut = uT[t % 2]
        modulate_transpose(C, xt, 4, C.modrow[:, 0:D], C.modrow[:, D:2 * D], ub, ut, pT, t)
        sgb, sm1, sm2 = stg[0][t % 2], stg[1][t % 2], stg[2][t % 2]
        for j in range(4):
            p = mm(j, ut)
            S.act(lambda e, p=p, j=j, sgb=sgb: e.activation(out=sgb[:, j, :], in_=p[:], func=AF.Copy), reads=[p], writes=[sgb])
            tm = tmp[ti % 2]; ti += 1
            p = mm(4 + j, ut)
            S.act(lambda e, p=p, tm=tm: e.activation(out=tm[:], in_=p[:], func=AF.Copy), reads=[p], writes=[tm])
            p = mm(8 + j, ut)
            S.dve(lambda e, p=p, tm=tm, j=j, sm1=sm1: e.tensor_tensor(out=sm1[:, j, :], in0=p[:], in1=tm[:], op=ALU.mult), reads=[p, tm], writes=[sm1])
            tm = tmp[ti % 2]; ti += 1
            p = mm(16 + j, ut)
            S.act(lambda e, p=p, tm=tm: e.activation(out=tm[:], in_=p[:], func=AF.Sigmoid), reads=[p], writes=[tm])
            p = mm(12 + j, ut)
            S.dve(lambda e, p=p, tm=tm, j=j, sm2=sm2: e.tensor_tensor(out=sm2[:, j, :], in0=p[:], in1=tm[:], op=ALU.mult), reads=[p, tm], writes=[sm2])
        c0 = t * 512
        S.dma(GBT[:, c0:c0 + 512].rearrange("(j p) n -> p j n", p=128), sgb[:, :, :], reads=[sgb], writes=[GBT], q='act')
        S.dma(M1T[:, 16 + c0:16 + c0 + 512].rearrange("(j p) n -> p j n", p=128), sm1[:, :, :], reads=[sm1], writes=[M1T], q='act')
        S.dma(M2T[:, 16 + c0:16 + c0 + 512].rearrange("(j p) n -> p j n", p=128), sm2[:, :, :], reads=[sm2], writes=[M2T], q='act')
    S.barrier()
    S.flush()
    st.close()


def phase_mix1_out(C, Xin, Xout):
    nc, S = C.nc, C.S
    st = ExitStack()
    pconv = [ps(nc, st, 'pc1%d' % i, [128, 512], F32) for i in range(2)]
    pstat = [ps(nc, st, 'pst1%d' % i, [128, 512], F32) for i in range(2)]
    yps = [[ps(nc, st, 'yp1%d_%d' % (i, h), [128, 512], F32) for h in range(2)] for i in range(2)]
    dg3 = build_diag(C, st, pconv[0], C.din['sconv_w'][:, :], 3, 4, 'cw3')
    dg31 = build_diag(C, st, pconv[1], C.din['conf_conv_w'][:, :], 31, 4, 'cw31')
    lng = to_col(C, st, pconv[0], C.din['conf_ln_g'][:, :], 1, 4, 'clng')
    lnb = to_col(C, st, pconv[1], C.din['conf_ln_b'][:, :], 1, 4, 'clnb')
    wout = load_w_bf16(C, st, 'wout1', C.din['odd_w_out'][:, :], 8, D)
    epi = Epi(C, st, 2, C.modrow[:, 2 * D:3 * D], 4)
    m1 = [sb(nc, st, 'm1_%d' % i, [128, 4, 514], BF16) for i in range(2)]
    m2 = [sb(nc, st, 'm2_%d' % i, [128, 4, 542], BF16) for i in range(2)]
    gb = [sb(nc, st, 'gb_%d' % i, [128, 4, 512], BF16) for i in range(2)]
    xt = [sb(nc, st, 'xt1o%d' % i, [128, 4, D], F32) for i in range(2)]
    mixT = sb(nc, st, 'mixT1', [128, 8, 512], BF16)
    z = sb(nc, st, 'z1', [128, 4, 512], F32)
    zsq = sb(nc, st, 'zsq1', [128, 4, 512], F32)
    mean = sb(nc, st, 'mean1', [128, 512], F32)
    rstd = sb(nc, st, 'rstd1', [128, 512], F32)
    msq = sb(nc, st, 'msq1', [128, 512], F32)
    epsl = sb(nc, st, 'epsl1', [128, 1], F32)
    S.pool(lambda e: e.memset(epsl[:], LN_EPS), writes=[epsl])

    def load(t):
        i = t % 2
        c0 = t * 512
        S.dma(m1[i][:, :, :], C.M1T[:, 15 + c0:15 + c0 + 514].rearrange("(f p) n -> p f n", p=128), reads=[C.M1T], writes=[m1[i]])
        S.dma(m2[i][:, :, :], C.M2T[:, 1 + c0:1 + c0 + 542].rearrange("(f p) n -> p f n", p=128), reads=[C.M2T], writes=[m2[i]])
        S.dma(gb[i][:, :, :], C.GBT[:, c0:c0 + 512].rearrange("(f p) n -> p f n", p=128), reads=[C.GBT], writes=[gb[i]])
        S.dma(xt[i][:, :, :], Xin[c0:c0 + 512, :].rearrange("(s p) d -> p s d", p=128), reads=[Xin], writes=[xt[i]])

    load(0)
    ci = 0
    yi = 0
    for t in range(8):
        if t + 1 < 8:
            load(t + 1)
        i = t % 2
        a1, a2, g_, x_ = m1[i], m2[i], gb[i], xt[i]
        for j in range(4):
            pc = pconv[ci % 2]; ci += 1
            for tap in range(3):
                S.pe(lambda e, pc=pc, j=j, tap=tap, a1=a1: e.matmul(pc[:], lhsT=dg3[:, j, tap, :], rhs=a1[:, j, tap:tap + 512], start=(tap == 0), stop=(tap == 2)),
                     reads=[dg3, a1], writes=[pc], sig=(tap == 2))
            S.dve(lambda e, pc=pc, j=j, g_=g_: e.tensor_tensor(out=mixT[:, j, :], in0=pc[:], in1=g_[:, j, :], op=ALU.mult), reads=[pc, g_], writes=[mixT])
        for j in range(4):
            pc = pconv[ci % 2]; ci += 1
            for tap in range(31):
                S.pe(lambda e, pc=pc, j=j, tap=tap, a2=a2: e.matmul(pc[:], lhsT=dg31[:, j, tap, :], rhs=a2[:, j, tap:tap + 512], start=(tap == 0), stop=(tap == 30)),
                     reads=[dg31, a2], writes=[pc], sig=(tap == 30))
            S.act(lambda e, pc=pc, j=j: e.activation(out=z[:, j, :], in_=pc[:], func=AF.Copy), reads=[pc], writes=[z])
            S.pool(lambda e, j=j: e.tensor_tensor(out=zsq[:, j, :], in0=z[:, j, :], in1=z[:, j, :], op=ALU.mult), reads=[z], writes=[zsq])
        for j in range(4):
            S.pe(lambda e, j=j: e.matmul(pstat[0][:], lhsT=C.ones_f[:, 0:128], rhs=z[:, j, :], start=(j == 0), stop=(j == 3)),
                 reads=[C.ones_f, z], writes=[pstat[0]], sig=(j == 3))
        for j in range(4):
            S.pe(lambda e, j=j: e.matmul(pstat[1][:], lhsT=C.ones_f[:, 0:128], rhs=zsq[:, j, :], start=(j == 0), stop=(j == 3)),
                 reads=[C.ones_f, zsq], writes=[pstat[1]], sig=(j == 3))
        S.dve(lambda e: e.tensor_scalar(out=mean[:], in0=pstat[0][:], scalar1=1.0 / 512, scalar2=None, op0=ALU.mult), reads=[pstat[0]], writes=[mean])
        S.dve(lambda e: e.tensor_tensor(out=msq[:], in0=mean[:], in1=mean[:], op=ALU.mult), reads=[mean], writes=[msq])
        S.dve(lambda e: e.scalar_tensor_tensor(out=rstd[:], in0=pstat[1][:], scalar=1.0 / 512, in1=msq[:], op0=ALU.mult, op1=ALU.subtract),
              reads=[pstat[1], msq], writes=[rstd])
        S.act(lambda e: e.activation(out=rstd[:], in_=rstd[:], func=AF.Ln, bias=epsl[:, 0:1]), reads=[rstd, epsl], writes=[rstd])
        S.act(lambda e: e.activation(out=rstd[:], in_=rstd[:], func=AF.Exp, scale=-0.5), reads=[rstd], writes=[rstd])
        for j in range(4):
            S.dve(lambda e, j=j: e.tensor_tensor(out=z[:, j, :], in0=z[:, j, :], in1=mean[:], op=ALU.subtract), reads=[z, mean], writes=[z])
            S.pool(lambda e, j=j: e.tensor_tensor(out=z[:, j, :], in0=z[:, j, :], in1=rstd[:], op=ALU.mult), reads=[z, rstd], writes=[z])
            S.act(lambda e, j=j: e.activation(out=mixT[:, 4 + j, :], in_=z[:, j, :], func=AF.Silu, scale=lng[:, j, 0:1], bias=lnb[:, j, 0:1]),
                  reads=[z, lng, lnb], writes=[mixT])
        for s_ in range(4):
            yp = yps[yi % 2]; yi += 1
            out_proj(C, mixT, wout, s_, yp, 8)
            epi.sub(s_, yp, x_)
        epi.finish((Xout, t * 512))
    S.barrier()
    S.flush()
    st.close()


_NC_CACHE = {}


def kernel(**inputs):
    if 'nc' not in _NC_CACHE:
        _NC_CACHE['nc'] = build()
    nc = _NC_CACHE['nc']
    f = lambda a: np.ascontiguousarray(np.asarray(a, dtype=np.float32))
    shared = {
        'c_ctx': f(inputs['c_ctx']).reshape(1, D),
        'ada_w': f(inputs['ada_w']), 'ada_b': f(inputs['ada_b']),
        'ln_g': f(inputs['ln_g']).reshape(4, D), 'ln_b': f(inputs['ln_b']).reshape(4, D),
        'even_w_in': f(inputs['even_w_in']), 'even_w_out': f(inputs['even_w_out']),
        'gdn_conv_w': f(inputs['gdn_conv_w']), 'gdn_a_log': f(inputs['gdn_a_log']).reshape(1, 8),
        'gdn_dt_bias': f(inputs['gdn_dt_bias']).reshape(1, 8), 'gdn_norm_w': f(inputs['gdn_norm_w']).reshape(1, 128),
        'pool_w': f(inputs['pool_w']), 'pool_scale': f(inputs['pool_scale']).reshape(1, 512),
        'odd_w_in': f(inputs['odd_w_in']), 'odd_w_out': f(inputs['odd_w_out']),
        'sconv_w': f(inputs['sconv_w']), 'conf_conv_w': f(inputs['conf_conv_w']),
        'conf_ln_g': f(inputs['conf_ln_g']).reshape(1, 512), 'conf_ln_b': f(inputs['conf_ln_b']).reshape(1, 512),
        'ffn_w_up': f(inputs['ffn_w_up']), 'ffn_conv_w': f(inputs['ffn_conv_w']).reshape(2, 9, DFF),
        'ffn_w_down': f(inputs['ffn_w_down']),
    }
    x = f(inputs['x']); c = f(inputs['c']); ctx = f(inputs['ctx'])
    in_maps = []
    for b in range(NCORES):
        m = dict(shared)
        m['x'] = x[b]; m['c'] = c[b:b + 1]; m['ctx'] = ctx[b]
        in_maps.append(m)
    res = run_bass_kernel_spmd(nc, in_maps, core_ids=list(range(NCORES)))
    return np.stack([r['out'] for r in res.results], axis=0)
```

```python
import numpy as np
from contextlib import ExitStack
import concourse.bass as bass
import concourse.mybir as mybir
from concourse.bass_utils import run_bass_kernel_spmd

F32 = mybir.dt.float32
BF16 = mybir.dt.bfloat16
AF = mybir.ActivationFunctionType
ALU = mybir.AluOpType

D = 1024
T = 4096
TC = 256
NCORES = 8
DFF = 2816
ALPHA = 4 ** 0.25
LN_EPS = 1e-5
RMS_EPS = 1e-6
BIG = 30000.0
ENG = ('pe', 'act', 'dve', 'pool', 'sp')
NDS = 24

DEBUG_OUT = []


class Res:
    __slots__ = ('w', 'r', 'excl')

    def __init__(self):
        self.w = None
        self.r = []
        self.excl = False


class TileT:
    def __init__(self, t):
        self.t = t
        self.res = Res()

    def __getitem__(self, k):
        return self.t[k]


class Sched:
    def __init__(self, nc, stack):
        self.nc = nc
        self.sem = {e: stack.enter_context(nc.semaphore('s_' + e)) for e in ENG}
        self.cnt = {e: 0 for e in ENG}
        self.known = {e: {} for e in ENG}
        self.q = {e: [] for e in ENG}
        self.dsem = [stack.enter_context(nc.semaphore('dq%d' % i)) for i in range(NDS)]
        self.dcnt = [0] * NDS
        self.dpool = {'sp': list(range(0, 12)), 'act': list(range(12, 20)), 'pool': list(range(20, 24))}
        self.dnext = {'sp': 0, 'act': 0, 'pool': 0}
        self.unsig = {e: False for e in ENG}

    def semof(self, key):
        return self.sem[key] if isinstance(key, str) else self.dsem[key]

    def _collect(self, eng, reads, writes):
        toks = []
        for r in reads:
            if r.w is not None:
                toks.append(r.w)
        for w in writes:
            if w.w is not None and (w.w[0] != eng or eng != 'pe'):
                toks.append(w.w)
            for t in w.r:
                if t[0] != eng or eng != 'pe':
                    toks.append(t)
        waits = {}
        kn = self.known[eng]
        for key, val in toks:
            if kn.get(key, 0) < val:
                waits[key] = max(waits.get(key, 0), val)
        for key, val in waits.items():
            kn[key] = val
        return list(waits.items())

    def _update(self, tok, reads, writes):
        for r in reads:
            r.r.append(tok)
        for w in writes:
            w.w = tok
            w.r = []

    def emit(self, eng, fn, reads=(), writes=(), sig=True):
        reads = [getattr(x, "res", x) for x in reads]
        writes = [getattr(x, "res", x) for x in writes]
        writes = writes + [r for r in reads if r.excl and eng != 'pe']
        waits = self._collect(eng, reads, writes)
        if sig:
            self.cnt[eng] += 1
            tok = (eng, self.cnt[eng])
            self.unsig[eng] = False
        else:
            tok = (eng, self.cnt[eng] + 1)
            self.unsig[eng] = True
        self.q[eng].append((waits, fn, sig, None))
        self._update(tok, reads, writes)

    def pe(self, fn, reads=(), writes=(), sig=True):
        self.emit('pe', fn, reads, writes, sig)

    def act(self, fn, reads=(), writes=()):
        self.emit('act', fn, reads, writes)

    def dve(self, fn, reads=(), writes=()):
        self.emit('dve', fn, reads, writes)

    def pool(self, fn, reads=(), writes=()):
        self.emit('pool', fn, reads, writes)

    def dma(self, out, in_, reads=(), writes=(), q='sp', **kw):
        reads = [getattr(x, "res", x) for x in reads]
        writes = [getattr(x, "res", x) for x in writes]
        pl = self.dpool[q]
        j = pl[self.dnext[q] % len(pl)]
        self.dnext[q] += 1
        waits = dict(self._collect(q, reads, writes))
        if self.dcnt[j] > 0 and self.known[q].get(j, 0) < self.dcnt[j]:
            waits[j] = self.dcnt[j]
            self.known[q][j] = self.dcnt[j]
        self.dcnt[j] += 16
        tok = (j, self.dcnt[j])
        self.q[q].append((list(waits.items()), lambda e: e.dma_start(out=out, in_=in_, **kw), False, j))
        self._update(tok, reads, writes)

    def barrier(self):
        for e in ENG:
            assert not self.unsig[e], e
        for e in ENG:
            waits = []
            for f in ENG:
                if f != e and self.known[e].get(f, 0) < self.cnt[f]:
                    waits.append((f, self.cnt[f]))
                    self.known[e][f] = self.cnt[f]
            for j in range(NDS):
                if self.dcnt[j] > 0 and self.known[e].get(j, 0) < self.dcnt[j]:
                    waits.append((j, self.dcnt[j]))
                    self.known[e][j] = self.dcnt[j]
            if waits:
                self.q[e].append((waits, None, False, None))

    def flush(self):
        nc = self.nc
        q = self.q
        self.q = {e: [] for e in ENG}

        def replay(eng, e):
            for waits, fn, sig, dj in q[eng]:
                for key, val in waits:
                    e.wait_ge(self.semof(key), val)
                if fn is None:
                    continue
                ins = fn(e)
                if dj is not None:
                    ins.then_inc(self.dsem[dj], 16)
                elif sig:
                    ins.then_inc(self.sem[eng], 1)

        with nc.Block() as block:
            @block.tensor
            def _(e):
                replay('pe', e)

            @block.scalar
            def _(e):
                replay('act', e)

            @block.vector
            def _(e):
                replay('dve', e)

            @block.gpsimd
            def _(e):
                replay('pool', e)

            @block.sync
            def _(e):
                replay('sp', e)


class Ctx:
    pass


_UID = [0]


def _uname(name):
    _UID[0] += 1
    return '%s_u%d' % (name, _UID[0])


def sb(nc, stack, name, shape, dt):
    return TileT(stack.enter_context(nc.sbuf_tensor(_uname(name), list(shape), dt)))


def ps(nc, stack, name, shape, dt):
    t = TileT(stack.enter_context(nc.psum_tensor(_uname(name), list(shape), dt)))
    t.res.excl = True
    return t


def build(debug_out=()):
    nc = bass.Bass("TRN2", target_bir_lowering=False)
    top = ExitStack()
    S = Sched(nc, top)
    C = Ctx()
    C.nc, C.S = nc, S
    din = {}

    def inp(name, shape):
        din[name] = TileT(nc.dram_tensor(name, list(shape), F32, kind="ExternalInput").ap())
        return din[name]

    inp('x', [T, D]); inp('c', [1, D]); inp('ctx', [TC, D]); inp('c_ctx', [1, D])
    inp('ada_w', [2, D, 6 * D]); inp('ada_b', [2, 6 * D])
    inp('ln_g', [4, D]); inp('ln_b', [4, D])
    inp('even_w_in', [D, 2576]); inp('even_w_out', [D, D])
    inp('gdn_conv_w', [5, 1536]); inp('gdn_a_log', [1, 8]); inp('gdn_dt_bias', [1, 8])
    inp('gdn_norm_w', [1, 128]); inp('pool_w', [4, 128, 128]); inp('pool_scale', [1, 512])
    inp('odd_w_in', [D, 2560]); inp('odd_w_out', [D, D])
    inp('sconv_w', [3, 512]); inp('conf_conv_w', [31, 512])
    inp('conf_ln_g', [1, 512]); inp('conf_ln_b', [1, 512])
    inp('ffn_w_up', [2, D, 2 * DFF]); inp('ffn_conv_w', [2, 9, DFF]); inp('ffn_w_down', [2, DFF, D])
    C.din = din
    C.out = TileT(nc.dram_tensor('out', [T, D], F32, kind="ExternalOutput").ap())

    def scratch(name, shape, dt):
        kind = "ExternalOutput" if name in debug_out else "Internal"
        t = TileT(nc.dram_tensor(name, list(shape), dt, kind=kind).ap())
        return t
    C.scratch = scratch

    C.ident_f = sb(nc, top, 'ident_f', [128, 128], F32)
    C.ident_b = sb(nc, top, 'ident_b', [128, 128], BF16)
    C.ones_f = sb(nc, top, 'ones_f', [128, 512], F32)
    C.ones_b = sb(nc, top, 'ones_b', [128, 128], BF16)
    C.zeros_b = sb(nc, top, 'zeros_b', [128, 512], BF16)
    C.modrow = sb(nc, top, 'modrow', [128, 6 * D], F32)
    st01 = ExitStack()
    C.modc = sb(nc, st01, 'modc', [128, 2 * D], F32)

    S.pool(lambda e: e.memset(C.ones_f[:], 1.0), writes=[C.ones_f])
    S.pool(lambda e: e.memset(C.ones_b[:], 1.0), writes=[C.ones_b])
    S.pool(lambda e: e.memset(C.zeros_b[:], 0.0), writes=[C.zeros_b])
    S.pool(lambda e: e.affine_select(out=C.ident_f[:], in_=C.ones_f[:, 0:128], pattern=[[-1, 128]],
                                     compare_op=ALU.is_equal, fill=0.0, base=0, channel_multiplier=1),
           reads=[C.ones_f], writes=[C.ident_f])
    S.pool(lambda e: e.tensor_copy(out=C.ident_b[:], in_=C.ident_f[:]), reads=[C.ident_f], writes=[C.ident_b])

    C.debug_out = debug_out
    phase_mod(C, 0)
    dbg_dump(C, 'dbg_mod0', C.modrow, C.modrow[0:1, :], [1, 6 * D], F32)
    dbg_dump(C, 'dbg_modc', C.modc, C.modc[0:1, :], [1, 2 * D], F32)
    phase_inproj0(C)
    st01.close()
    phase_qkv0(C)
    import os
    if os.environ.get('NOGDN') != '1':
        phase_gdn(C)
    else:
        C.OACC = C.scratch('OACC', [T, 512], F32)
        for r0 in range(0, T, 128):
            S.dma(C.OACC[r0:r0 + 128, :], C.ones_f[:, :], reads=[C.ones_f], writes=[C.OACC])
    PSTOP = int(os.environ.get('PSTOP', '99'))
    X1 = C.scratch('X1', [T, D], F32)
    X2 = C.scratch('X2', [T, D], F32)
    X3 = C.scratch('X3', [T, D], F32)
    if PSTOP >= 1:
        phase_mix0_out(C, X1)
    if PSTOP >= 2:
        phase_ffn_up(C, 0, X1)
    if PSTOP >= 3:
        phase_ffn_down(C, 0, X1, X2)
    if PSTOP >= 4:
        phase_mod(C, 1)
        phase_inproj1(C, X2)
    if PSTOP >= 5:
        phase_mix1_out(C, X2, X3)
    if PSTOP >= 6:
        phase_ffn_up(C, 1, X3)
        phase_ffn_down(C, 1, X3, C.out)

    S.barrier()
    S.flush()
    top.close()
    return nc


def dbg_dump(C, name, tile, ap, shape, dt):
    if name not in C.debug_out:
        return
    d = TileT(C.nc.dram_tensor(name, list(shape), dt, kind="ExternalOutput").ap())
    C.S.dma(d[:], ap, reads=[tile], writes=[d])


def to_col(C, st, psb, dram, R, ncol, name):
    nc, S = C.nc, C.S
    BL = 4
    tmp = sb(nc, st, name + '_row', [R, BL * 128], F32)
    outt = sb(nc, st, name + '_col', [128, ncol, R], F32)
    for c0 in range(0, ncol, BL):
        c1 = min(ncol, c0 + BL)
        S.dma(tmp[:, 0:(c1 - c0) * 128], dram[:, c0 * 128:c1 * 128], writes=[tmp])
        for c in range(c0, c1):
            S.pe(lambda e, c=c, c0=c0: e.transpose(out=psb[:, 0:R], in_=tmp[0:R, (c - c0) * 128:(c - c0 + 1) * 128],
                                                   identity=C.ident_f[0:R, 0:R]),
                 reads=[tmp, C.ident_f], writes=[psb])
            S.dve(lambda e, c=c: e.tensor_copy(out=outt[:, c, :], in_=psb[:, 0:R]), reads=[psb], writes=[outt])
    return outt


def phase_mod(C, layer):
    nc, S = C.nc, C.S
    st = ExitStack()
    pst = ps(nc, st, 'pm_t', [128, 512], F32)
    pacc = [ps(nc, st, 'pm_a%d' % i, [128, 512], F32) for i in range(2)]
    paccc = [ps(nc, st, 'pm_c%d' % i, [128, 512], F32) for i in range(2)]
    ccol = to_col(C, st, pst, C.din['c'][:, :], 1, 8, 'c')
    S.act(lambda e: e.activation(out=ccol[:], in_=ccol[:], func=AF.Silu), reads=[ccol], writes=[ccol])
    rep = sb(nc, st, 'c_rep', [128, 8, 128], F32)
    for k in range(8):
        S.dve(lambda e, k=k: e.tensor_scalar(out=rep[:, k, :], in0=C.ones_f[:, 0:128], scalar1=ccol[:, k, 0:1],
                                             scalar2=None, op0=ALU.mult), reads=[C.ones_f, ccol], writes=[rep])
    brow = sb(nc, st, 'adab_row', [1, 6 * D], F32)
    S.dma(brow[:], C.din['ada_b'][layer:layer + 1, :], writes=[brow])
    if layer == 0:
        cccol = to_col(C, st, pst, C.din['c_ctx'][:, :], 1, 8, 'cc')
        S.act(lambda e: e.activation(out=cccol[:], in_=cccol[:], func=AF.Silu), reads=[cccol], writes=[cccol])
        repc = sb(nc, st, 'cc_rep', [128, 8, 128], F32)
        for k in range(8):
            S.dve(lambda e, k=k: e.tensor_scalar(out=repc[:, k, :], in0=C.ones_f[:, 0:128],
                                                 scalar1=cccol[:, k, 0:1], scalar2=None, op0=ALU.mult),
                  reads=[C.ones_f, cccol], writes=[repc])
    wt = [sb(nc, st, 'adaw%d' % i, [128, 8, 512], F32) for i in range(2)]
    aw = C.din['ada_w']
    for n in range(12):
        w = wt[n % 2]
        S.dma(w[:], aw[layer, :, n * 512:(n + 1) * 512].rearrange("(k p) n -> p k n", p=128), writes=[w])
        pa = pacc[n % 2]
        for k in range(8):
            S.pe(lambda e, k=k, w=w, pa=pa: e.matmul(pa[:], lhsT=rep[:, k, :], rhs=w[:, k, :], start=(k == 0), stop=False),
                 reads=[rep, w], writes=[pa], sig=False)
        S.pe(lambda e, pa=pa, n=n: e.matmul(pa[:], lhsT=C.ones_f[0:1, 0:128], rhs=brow[0:1, n * 512:(n + 1) * 512],
                                            start=False, stop=True), reads=[C.ones_f, brow], writes=[pa])
        S.act(lambda e, pa=pa, n=n: e.activation(out=C.modrow[:, n * 512:(n + 1) * 512], in_=pa[:], func=AF.Copy),
              reads=[pa], writes=[C.modrow])
        if layer == 0 and n < 4:
            pc = paccc[n % 2]
            for k in range(8):
                S.pe(lambda e, k=k, w=w, pc=pc: e.matmul(pc[:], lhsT=repc[:, k, :], rhs=w[:, k, :], start=(k == 0), stop=False),
                     reads=[repc, w], writes=[pc], sig=False)
            S.pe(lambda e, pc=pc, n=n: e.matmul(pc[:], lhsT=C.ones_f[0:1, 0:128], rhs=brow[0:1, n * 512:(n + 1) * 512],
                                                start=False, stop=True), reads=[C.ones_f, brow], writes=[pc])
            S.dve(lambda e, pc=pc, n=n: e.tensor_copy(out=C.modc[:, n * 512:(n + 1) * 512], in_=pc[:]),
                  reads=[pc], writes=[C.modc])
    S.dve(lambda e: e.tensor_scalar_add(out=C.modrow[:, D:2 * D], in0=C.modrow[:, D:2 * D], scalar1=1.0),
          reads=[C.modrow], writes=[C.modrow])
    S.dve(lambda e: e.tensor_scalar_add(out=C.modrow[:, 4 * D:5 * D], in0=C.modrow[:, 4 * D:5 * D], scalar1=1.0),
          reads=[C.modrow], writes=[C.modrow])
    if layer == 0:
        S.dve(lambda e: e.tensor_scalar_add(out=C.modc[:, D:2 * D], in0=C.modc[:, D:2 * D], scalar1=1.0),
              reads=[C.modc], writes=[C.modc])
    S.barrier()
    S.flush()
    st.close()


def modulate_transpose(C, xt, nsub, shift, scale1, ub, uT, pT, evac_i):
    S = C.S
    for s in range(nsub):
        S.pool(lambda e, s=s: e.tensor_tensor(out=xt[:, s, :], in0=xt[:, s, :], in1=scale1, op=ALU.mult),
               reads=[xt, C.modrow, C.modc], writes=[xt])
        S.dve(lambda e, s=s: e.tensor_tensor(out=ub[:, s, :], in0=xt[:, s, :], in1=shift, op=ALU.add),
              reads=[xt, C.modrow, C.modc], writes=[ub])
    for k in range(8):
        p = pT[k % len(pT)]
        for s in range(nsub):
            S.pe(lambda e, s=s, k=k, p=p: e.transpose(out=p[:, s * 128:(s + 1) * 128], in_=ub[:, s, k * 128:(k + 1) * 128],
                                                      identity=C.ident_b[:]),
                 reads=[ub, C.ident_b], writes=[p], sig=(s == nsub - 1))
        if (k + evac_i) % 2 == 0:
            S.act(lambda e, k=k, p=p: e.activation(out=uT[:, k, 0:nsub * 128], in_=p[:, 0:nsub * 128], func=AF.Copy),
                  reads=[p], writes=[uT])
        else:
            S.dve(lambda e, k=k, p=p: e.tensor_copy(out=uT[:, k, 0:nsub * 128], in_=p[:, 0:nsub * 128]),
                  reads=[p], writes=[uT])


def load_w_bf16(C, st, name, dram, kchunks, ncols):
    nc, S = C.nc, C.S
    w = sb(nc, st, name, [128, kchunks, ncols], BF16)
    SW = min(2048, ncols)
    stg = [sb(nc, st, name + '_stg%d' % i, [128, SW], F32) for i in range(2)]
    i = 0
    for k in range(kchunks):
        for c0 in range(0, ncols, SW):
            c1 = min(ncols, c0 + SW)
            b = stg[i % 2]
            S.dma(b[:, 0:c1 - c0], dram[k * 128:(k + 1) * 128, c0:c1], writes=[b], q=('sp' if i % 2 == 0 else 'act'))
            fn = lambda e, b=b, k=k, c0=c0, c1=c1: e.tensor_copy(out=w[:, k, c0:c1], in_=b[:, 0:c1 - c0])
            if i % 3 == 0:
                S.dve(fn, reads=[b], writes=[w])
            elif i % 3 == 1:
                S.act(lambda e, b=b, k=k, c0=c0, c1=c1: e.activation(out=w[:, k, c0:c1], in_=b[:, 0:c1 - c0], func=AF.Copy), reads=[b], writes=[w])
            else:
                S.pool(fn, reads=[b], writes=[w])
            i += 1
    return w


def phase_inproj0(C):
    nc, S = C.nc, C.S
    st = ExitStack()
    P0T = C.scratch('P0T', [2048, T + 16], BF16); C.P0T = P0T
    G0 = C.scratch('G0', [T, 512], BF16); C.G0 = G0
    SG = C.scratch('SG', [T + TC, 16], F32); C.SG = SG
    PCT = C.scratch('PCT', [1024, TC + 16], BF16); C.PCT = PCT
    w = load_w_bf16(C, st, 'w_in0', C.din['even_w_in'][:, :], 8, 2576)
    for r0 in range(0, 2048, 128):
        S.dma(P0T[r0:r0 + 128, 0:8], C.zeros_b[:, 0:8], reads=[C.zeros_b], writes=[P0T])
        S.dma(P0T[r0:r0 + 128, T + 8:T + 16], C.zeros_b[:, 0:8], reads=[C.zeros_b], writes=[P0T])
    for r0 in range(0, 1024, 128):
        S.dma(PCT[r0:r0 + 128, 0:8], C.zeros_b[:, 0:8], reads=[C.zeros_b], writes=[PCT])
        S.dma(PCT[r0:r0 + 128, TC + 8:TC + 16], C.zeros_b[:, 0:8], reads=[C.zeros_b], writes=[PCT])
    xt = [sb(nc, st, 'xt%d' % i, [128, 4, D], F32) for i in range(2)]
    ub = [sb(nc, st, 'ub%d' % i, [128, 4, D], BF16) for i in range(2)]
    uT = [sb(nc, st, 'uT%d' % i, [128, 8, 512], BF16) for i in range(2)]
    pstg = [sb(nc, st, 'pstg%d' % i, [128, 4, 512], BF16) for i in range(2)]
    gstg = [sb(nc, st, 'gstg%d' % i, [128, 4, 512], BF16) for i in range(2)]
    sstg = [sb(nc, st, 'sstg%d' % i, [128, 4, 16], F32) for i in range(2)]
    pT = [ps(nc, st, 'pT%d' % i, [128, 512], BF16) for i in range(2)]
    pm = [ps(nc, st, 'pm%d' % i, [128, 512], F32) for i in range(4)]
    pss = ps(nc, st, 'pss', [128, 4, 16], F32)
    x = C.din['x']
    tiles = [('ctx', 0)] + [('lat', i) for i in range(8)]

    def load(i):
        kind, t = tiles[i]
        b = xt[i % 2]
        if kind == 'ctx':
            S.dma(b[:, 0:2, :], C.din['ctx'][:, :].rearrange("(s p) d -> p s d", p=128), writes=[b])
        else:
            S.dma(b[:, :, :], x[t * 512:(t + 1) * 512, :].rearrange("(s p) d -> p s d", p=128), writes=[b])

    load(0)
    pmi = 0
    for i, (kind, t) in enumerate(tiles):
        if i + 1 < len(tiles):
            load(i + 1)
        b, u, ut = xt[i % 2], ub[i % 2], uT[i % 2]
        isctx = kind == 'ctx'
        nsub = 2 if isctx else 4
        ntok = nsub * 128
        if isctx:
            modulate_transpose(C, b, nsub, C.modc[:, 0:D], C.modc[:, D:2 * D], u, ut, pT, i)
            fchunks = list(range(4, 12))
        else:
            modulate_transpose(C, b, nsub, C.modrow[:, 0:D], C.modrow[:, D:2 * D], u, ut, pT, i)
            fchunks = list(range(0, 12)) + list(range(16, 20))
        for gi in range(0, len(fchunks), 4):
            grp = fchunks[gi:gi + 4]
            stg = pstg[(gi // 4) % 2]
            for j, fc in enumerate(grp):
                p = pm[pmi % 4]; pmi += 1
                for k in range(8):
                    S.pe(lambda e, p=p, k=k, fc=fc, ut=ut, ntok=ntok: e.matmul(
                        p[:, 0:ntok], lhsT=w[:, k, fc * 128:(fc + 1) * 128], rhs=ut[:, k, 0:ntok],
                        start=(k == 0), stop=(k == 7)), reads=[w, ut], writes=[p], sig=(k == 7))
                if j % 2 == 0:
                    S.act(lambda e, p=p, j=j, stg=stg, ntok=ntok: e.activation(out=stg[:, j, 0:ntok], in_=p[:, 0:ntok], func=AF.Copy),
                          reads=[p], writes=[stg])
                else:
                    S.dve(lambda e, p=p, j=j, stg=stg, ntok=ntok: e.tensor_copy(out=stg[:, j, 0:ntok], in_=p[:, 0:ntok]),
                          reads=[p], writes=[stg])
            if isctx:
                r0 = (grp[0] - 4) * 128
                dst = PCT[r0:r0 + 512, 8:8 + ntok].rearrange("(j p) n -> p j n", p=128)
                S.dma(dst, stg[:, :, 0:ntok], reads=[stg], writes=[PCT], q='act')
            else:
                fc0 = grp[0]
                r0 = fc0 * 128 if fc0 < 12 else (fc0 - 4) * 128
                dst = P0T[r0:r0 + 512, 8 + t * 512:8 + (t + 1) * 512].rearrange("(j p) n -> p j n", p=128)
                S.dma(dst, stg[:, :, :], reads=[stg], writes=[P0T], q='act')
        gs = gstg[i % 2]
        ss = sstg[i % 2]
        for s in range(nsub):
            if not isctx:
                p = pm[pmi % 4]; pmi += 1
                for k in range(8):
                    S.pe(lambda e, p=p, k=k, s=s, ut=ut: e.matmul(p[:], lhsT=ut[:, k, s * 128:(s + 1) * 128], rhs=w[:, k, 1536:2048],
                                                            start=(k == 0), stop=(k == 7)), reads=[w, ut], writes=[p], sig=(k == 7))
                S.act(lambda e, p=p, s=s, gs=gs: e.activation(out=gs[:, s, :], in_=p[:], func=AF.Silu), reads=[p], writes=[gs])
            for k in range(8):
                S.pe(lambda e, k=k, s=s, ut=ut: e.matmul(pss[:, s, :], lhsT=ut[:, k, s * 128:(s + 1) * 128], rhs=w[:, k, 2560:2576],
                                                       start=(k == 0), stop=(k == 7)), reads=[w, ut], writes=[pss], sig=(k == 7))
        S.dve(lambda e, ss=ss, nsub=nsub: e.tensor_copy(out=ss[:, 0:nsub, :], in_=pss[:, 0:nsub, :]), reads=[pss], writes=[ss])
        if isctx:
            S.dma(SG[T:T + TC, :].rearrange("(s p) c -> p s c", p=128), ss[:, 0:2, :], reads=[ss], writes=[SG], q='act')
        else:
            S.dma(SG[t * 512:(t + 1) * 512, :].rearrange("(s p) c -> p s c", p=128), ss[:, :, :], reads=[ss], writes=[SG], q='act')
            S.dma(G0[t * 512:(t + 1) * 512, :].rearrange("(s p) c -> p s c", p=128), gs[:, :, :], reads=[gs], writes=[G0], q='act')
    S.barrier()
    S.flush()
    st.close()


def build_diag(C, st, psb, dram, R, ncol, name):
    nc, S = C.nc, C.S
    cw = to_col(C, st, psb, dram, R, ncol, name)
    dg = sb(nc, st, name + '_dg', [128, ncol, R, 128], BF16)
    i = 0
    for c in range(ncol):
        for r in range(R):
            fn = lambda e, c=c, r=r: e.tensor_scalar(out=dg[:, c, r, :], in0=C.ident_b[:], scalar1=cw[:, c, r:r + 1],
                                                     scalar2=None, op0=ALU.mult)
            if i % 2 == 0:
                S.dve(fn, reads=[C.ident_b, cw], writes=[dg])
            else:
                S.pool(fn, reads=[C.ident_b, cw], writes=[dg])
            i += 1
    return dg


def phase_qkv0(C):
    nc, S = C.nc, C.S
    st = ExitStack()
    QT = C.scratch('QT', [512, T], BF16); C.QT = QT
    KT = C.scratch('KT', [512, T + TC], BF16); C.KT = KT
    QTOK = C.scratch('QTOK', [T, 512], BF16); C.QTOK = QTOK
    KTOK = C.scratch('KTOK', [T + TC, 512], BF16); C.KTOK = KTOK
    VTOK = C.scratch('VTOK', [T + TC, 512], BF16); C.VTOK = VTOK
    YPT = C.scratch('YPT', [512, T], BF16); C.YPT = YPT
    pconv = [ps(nc, st, 'pconv%d' % i, [128, 512], F32) for i in range(2)]
    pssq = [ps(nc, st, 'pssq%d' % i, [128, 512], F32) for i in range(2)]
    pT = [ps(nc, st, 'pTq%d' % i, [128, 512], BF16) for i in range(2)]
    ppool = ps(nc, st, 'ppool', [128, 512], F32)
    dg = build_diag(C, st, pconv[0], C.din['gdn_conv_w'][:, :], 5, 12, 'cw5')
    pscale = to_col(C, st, pconv[1], C.din['pool_scale'][:, :], 1, 4, 'pscale')
    poolw = sb(nc, st, 'poolw', [128, 4, 128], BF16)
    S.dma(poolw[:], C.din['pool_w'][:, :, :].rearrange("g c d -> c g d"), writes=[poolw], q='pool')
    corrF = sb(nc, st, 'corrF', [128, 4, 8], F32)
    corrL = sb(nc, st, 'corrL', [128, 4, 8], F32)
    S.pool(lambda e: e.memset(corrF[:], 1.0), writes=[corrF])
    S.pool(lambda e: e.memset(corrL[:], 1.0), writes=[corrL])
    for g in range(4):
        hw = 1 << g
        for j in range(hw):
            S.pool(lambda e, g=g, j=j, hw=hw: e.memset(corrF[:, g, j:j + 1], 2.0 * hw / (j + hw)), writes=[corrF])
        for m in range(hw - 1):
            S.pool(lambda e, g=g, m=m, hw=hw: e.memset(corrL[:, g, 7 - m:8 - m], 2.0 * hw / (1 + m + hw)), writes=[corrL])
    pin = [sb(nc, st, 'pin%d' % i, [128, 12, 516], BF16) for i in range(2)]
    pp = [sb(nc, st, 'pp%d' % i, [128, 4, 528], BF16) for i in range(2)]
    xs8 = sb(nc, st, 'xs8', [128, 8, 512], F32)
    ss8 = sb(nc, st, 'ss8', [128, 8, 512], F32)
    epsb = sb(nc, st, 'epsb', [128, 1], F32)
    S.pool(lambda e: e.memset(epsb[:], RMS_EPS), writes=[epsb])
    sqb = [sb(nc, st, 'sqb%d' % i, [128, 512], BF16) for i in range(2)]
    qkn = [sb(nc, st, 'qkn0', [128, 12, 512], BF16)] * 2
    tokst = [[sb(nc, st, 'tok%d_%d' % (g, i), [128, 4, 512], BF16) for i in range(2)] for g in range(3)]
    wa = [sb(nc, st, 'wa%d' % i, [128, 528], F32) for i in range(2)]
    wb = [sb(nc, st, 'wb%d' % i, [128, 528], F32) for i in range(2)]
    pld = [sb(nc, st, 'pld%d' % i, [128, 4, 512], BF16) for i in range(2)]
    ypst = [sb(nc, st, 'ypst%d' % i, [128, 4, 512], BF16) for i in range(2)]
    tiles = [('ctx', 0)] + [('lat', i) for i in range(8)]

    def load(i):
        kind, t = tiles[i]
        b = pin[i % 2]
        if kind == 'ctx':
            S.dma(b[:, 4:12, 0:260], C.PCT[:, 6:266].rearrange("(f p) n -> p f n", p=128), reads=[C.PCT], writes=[b])
        else:
            S.dma(b[:, :, :], C.P0T[0:1536, 6 + t * 512:6 + t * 512 + 516].rearrange("(f p) n -> p f n", p=128),
                  reads=[C.P0T], writes=[b])
            S.dma(pp[i % 2][:, :, :], C.P0T[1536:2048, t * 512:t * 512 + 528].rearrange("(f p) n -> p f n", p=128),
                  reads=[C.P0T], writes=[pp[i % 2]])

    load(0)
    ci = 0
    for i, (kind, t) in enumerate(tiles):
        if i + 1 < len(tiles):
            load(i + 1)
        isctx = kind == 'ctx'
        ntok = 256 if isctx else 512
        nsub = ntok // 128
        b = pin[i % 2]
        qk = qkn[i % 2]
        for fc in (range(4, 12) if isctx else range(12)):
            pc = pconv[ci % 2]
            sq_ = sqb[ci % 2]; pq = pssq[ci % 2]
            ci += 1
            for tap in range(5):
                S.pe(lambda e, pc=pc, fc=fc, tap=tap, b=b, ntok=ntok: e.matmul(
                    pc[:, 0:ntok], lhsT=dg[:, fc, tap, :], rhs=b[:, fc, tap:tap + ntok], start=(tap == 0), stop=(tap == 4)),
                    reads=[dg, b], writes=[pc], sig=(tap == 4))
            if fc >= 8:
                S.act(lambda e, pc=pc, fc=fc, qk=qk, ntok=ntok: e.activation(out=qk[:, fc, 0:ntok], in_=pc[:, 0:ntok], func=AF.Silu),
                      reads=[pc], writes=[qk])
                continue
            S.act(lambda e, pc=pc, fc=fc, ntok=ntok: e.activation(out=xs8[:, fc, 0:ntok], in_=pc[:, 0:ntok], func=AF.Silu),
                  reads=[pc], writes=[xs8])
            S.pool(lambda e, fc=fc, sq_=sq_, ntok=ntok: e.tensor_tensor(out=sq_[:, 0:ntok], in0=xs8[:, fc, 0:ntok], in1=xs8[:, fc, 0:ntok], op=ALU.mult),
                   reads=[xs8], writes=[sq_])
            S.pe(lambda e, pq=pq, sq_=sq_, ntok=ntok: e.matmul(pq[:, 0:ntok], lhsT=C.ones_b[:], rhs=sq_[:, 0:ntok], start=True, stop=True),
                 reads=[C.ones_b, sq_], writes=[pq])
            S.dve(lambda e, pq=pq, fc=fc, ntok=ntok: e.tensor_copy(out=ss8[:, fc, 0:ntok], in_=pq[:, 0:ntok]), reads=[pq], writes=[ss8])
        f0 = 4 if isctx else 0
        S.act(lambda e, f0=f0, ntok=ntok: e.activation(out=ss8[:, f0:8, 0:ntok], in_=ss8[:, f0:8, 0:ntok], func=AF.Ln, bias=epsb[:, 0:1]),
              reads=[ss8, epsb], writes=[ss8])
        S.act(lambda e, f0=f0, ntok=ntok: e.activation(out=ss8[:, f0:8, 0:ntok], in_=ss8[:, f0:8, 0:ntok], func=AF.Exp, scale=-0.5),
              reads=[ss8], writes=[ss8])
        for fc in range(f0, 8):
            sc = (128.0 ** -0.5) if fc < 4 else 1.0
            fn = lambda e, qk=qk, fc=fc, sc=sc, ntok=ntok: e.scalar_tensor_tensor(
                out=qk[:, fc, 0:ntok], in0=xs8[:, fc, 0:ntok], scalar=sc, in1=ss8[:, fc, 0:ntok], op0=ALU.mult, op1=ALU.mult)
            if fc < 4:
                S.dve(fn, reads=[xs8, ss8], writes=[qk])
            else:
                S.pool(lambda e, qk=qk, fc=fc, ntok=ntok: e.tensor_tensor(out=qk[:, fc, 0:ntok], in0=xs8[:, fc, 0:ntok],
                                                                       in1=ss8[:, fc, 0:ntok], op=ALU.mult),
                       reads=[xs8, ss8], writes=[qk])
        ti = 0
        for g in ((1, 2) if isctx else (0, 1, 2)):
            tk = tokst[g][i % 2]
            for s_ in range(nsub):
                p = pT[ti % 2]; ti += 1
                for h in range(4):
                    S.pe(lambda e, p=p, h=h, g=g, s_=s_, qk=qk: e.transpose(out=p[:, h * 128:(h + 1) * 128],
                                                                       in_=qk[:, g * 4 + h, s_ * 128:(s_ + 1) * 128], identity=C.ident_b[:]),
                         reads=[qk, C.ident_b], writes=[p], sig=(h == 3))
                if ti % 2 == 0:
                    S.act(lambda e, p=p, tk=tk, s_=s_: e.activation(out=tk[:, s_, :], in_=p[:], func=AF.Copy), reads=[p], writes=[tk])
                else:
                    S.dve(lambda e, p=p, tk=tk, s_=s_: e.tensor_copy(out=tk[:, s_, :], in_=p[:]), reads=[p], writes=[tk])
        c0 = T if isctx else t * 512
        if not isctx:
            S.dma(QT[:, c0:c0 + 512].rearrange("(f p) n -> p f n", p=128), qk[:, 0:4, :], reads=[qk], writes=[QT], q='act')
            S.dma(QTOK[c0:c0 + 512, :].rearrange("(s p) c -> p s c", p=128), tokst[0][i % 2][:, :, :], reads=[tokst[0][i % 2]], writes=[QTOK], q='act')
        S.dma(KT[:, c0:c0 + ntok].rearrange("(f p) n -> p f n", p=128), qk[:, 4:8, 0:ntok], reads=[qk], writes=[KT], q='act')
        S.dma(KTOK[c0:c0 + ntok, :].rearrange("(s p) c -> p s c", p=128), tokst[1][i % 2][:, 0:nsub, :], reads=[tokst[1][i % 2]], writes=[KTOK], q='act')
        S.dma(VTOK[c0:c0 + ntok, :].rearrange("(s p) c -> p s c", p=128), tokst[2][i % 2][:, 0:nsub, :], reads=[tokst[2][i % 2]], writes=[VTOK], q='act')
        if isctx:
            continue
        ppb = pp[i % 2]
        pl = pld[i % 2]
        yp = ypst[i % 2]
        for g in range(4):
            a_, b_ = wa[g % 2], wb[g % 2]
            S.pool(lambda e, a_=a_, g=g, ppb=ppb: e.tensor_tensor(out=a_[:, 1:527], in0=ppb[:, g, 0:526], in1=ppb[:, g, 1:527], op=ALU.add),
                   reads=[ppb], writes=[a_])
            cur, oth = a_, b_
            lo, hi = 1, 527
            for lvl in range(g):
                sh = 1 << lvl
                lo, hi = lo + sh, hi - sh
                S.pool(lambda e, cur=cur, oth=oth, lo=lo, hi=hi, sh=sh: e.tensor_tensor(
                    out=oth[:, lo:hi], in0=cur[:, lo - sh:hi - sh], in1=cur[:, lo + sh:hi + sh], op=ALU.add),
                    reads=[cur], writes=[oth])
                cur, oth = oth, cur
            S.dve(lambda e, cur=cur, g=g: e.tensor_scalar(out=cur[:, 8:520], in0=cur[:, 8:520], scalar1=1.0 / (2 << g), scalar2=None, op0=ALU.mult),
                  reads=[cur], writes=[cur])
            if t == 0:
                S.dve(lambda e, cur=cur, g=g: e.tensor_tensor(out=cur[:, 8:16], in0=cur[:, 8:16], in1=corrF[:, g, :], op=ALU.mult),
                      reads=[cur, corrF], writes=[cur])
            if t == 7:
                S.dve(lambda e, cur=cur, g=g: e.tensor_tensor(out=cur[:, 512:520], in0=cur[:, 512:520], in1=corrL[:, g, :], op=ALU.mult),
                      reads=[cur, corrL], writes=[cur])
            S.dve(lambda e, cur=cur, g=g, pl=pl, ppb=ppb: e.tensor_tensor(out=pl[:, g, :], in0=cur[:, 8:520], in1=ppb[:, g, 8:520], op=ALU.subtract),
                  reads=[cur, ppb], writes=[pl])
            S.pe(lambda e, g=g, pl=pl: e.matmul(ppool[:], lhsT=poolw[:, g, :], rhs=pl[:, g, :], start=True, stop=True),
                 reads=[poolw, pl], writes=[ppool])
            S.act(lambda e, g=g, yp=yp: e.activation(out=yp[:, g, :], in_=ppool[:], func=AF.Identity, scale=pscale[:, g, 0:1]),
                  reads=[ppool, pscale], writes=[yp])
        S.dma(YPT[:, c0:c0 + 512].rearrange("(f p) n -> p f n", p=128), yp[:, :, :], reads=[yp], writes=[YPT], q='act')
    S.barrier()
    S.flush()
    st.close()


class Slot:
    def __init__(self, bank, k):
        self.f = bank.t[:, k * 128:(k + 1) * 128]
        self.b = bank.t[:, :].bitcast(BF16)[:, k * 256:k * 256 + 128]
        self.res = bank.res


def run_interleaved(gens):
    gens = list(gens)
    while gens:
        nxt = []
        for g in gens:
            try:
                next(g)
                nxt.append(g)
            except StopIteration:
                pass
        gens = nxt


def phase_gdn(C):
    nc, S = C.nc, C.S
    st = ExitStack()
    NT = 34
    import os
    OACC = C.scratch('OACC', [T, 512], F32); C.OACC = OACC
    banks = [ps(nc, st, 'gbank%d' % i, [128, 512], F32) for i in range(8)]
    for b_ in banks:
        b_.res.excl = True
    slots = [[Slot(banks[c], k) for k in range(4)] for c in range(8)]
    sall = sb(nc, st, 'sall', [128, NT, 16], F32)
    for n0 in ([] if os.environ.get('NOSALL') == '1' else range(0, NT, 6)):
        n1 = min(NT, n0 + 6)
        S.dma(sall[:, n0:n1, :], C.SG[n0 * 128:n1 * 128, :].rearrange("(n p) c -> p n c", p=128), reads=[C.SG], writes=[sall])
    adb = sb(nc, st, 'adb', [128, 16], F32)
    S.dma(adb[:, 0:8], C.din['gdn_a_log'][0:1, :].to_broadcast([128, 8]), writes=[adb])
    S.dma(adb[:, 8:16], C.din['gdn_dt_bias'][0:1, :].to_broadcast([128, 8]), writes=[adb])
    S.act(lambda e: e.activation(out=adb[:, 0:8], in_=adb[:, 0:8], func=AF.Exp), reads=[adb], writes=[adb])
    S.dve(lambda e: e.tensor_scalar(out=adb[:, 0:8], in0=adb[:, 0:8], scalar1=-1.0, scalar2=None, op0=ALU.mult),
          reads=[adb], writes=[adb])

    GCUT = int(os.environ.get('GCUT', '0'))

    def fin():
        S.barrier(); S.flush(); st.close()
    if GCUT == 1:
        return fin()

    def gt(name):
        return sb(nc, st, name, [128, NT, 8], F32)
    beta, g_, gc, eg, be, kds, gl = gt('g_beta'), gt('g_g'), gt('g_gc'), gt('g_eg'), gt('g_be'), gt('g_kds'), gt('g_gl')
    S.act(lambda e: e.activation(out=beta[:], in_=sall[:, :, 0:8], func=AF.Sigmoid), reads=[sall], writes=[beta])
    S.dve(lambda e: e.tensor_tensor(out=g_[:], in0=sall[:, :, 8:16], in1=adb[:, 8:16].unsqueeze(1).to_broadcast([128, NT, 8]), op=ALU.add),
          reads=[sall, adb], writes=[g_])
    S.act(lambda e: e.activation(out=g_[:], in_=g_[:], func=AF.Exp), reads=[g_], writes=[g_])
    S.act(lambda e: e.activation(out=g_[:], in_=g_[:], func=AF.Ln, bias=1.0), reads=[g_], writes=[g_])
    S.dve(lambda e: e.tensor_tensor(out=g_[:], in0=g_[:], in1=adb[:, 0:8].unsqueeze(1).to_broadcast([128, NT, 8]), op=ALU.mult),
          reads=[g_, adb], writes=[g_])
    if GCUT == 2:
        return fin()
    Lt = sb(nc, st, 'Lt', [128, 128], F32)
    Ut = sb(nc, st, 'Ut', [128, 128], F32)
    bigm = [sb(nc, st, 'bigm%d' % i, [128, 128], F32) for i in range(2)]
    strict = [sb(nc, st, 'strict%d' % i, [128, 128], F32) for i in range(2)]
    bigfull = sb(nc, st, 'bigfull', [128, 128], F32)
    S.pool(lambda e: e.memset(bigfull[:], BIG), writes=[bigfull])
    one = C.ones_f[:, 0:128]
    S.pool(lambda e: e.affine_select(out=Lt[:], in_=one, pattern=[[1, 128]], compare_op=ALU.is_ge, fill=0.0, base=0, channel_multiplier=-1),
           reads=[C.ones_f], writes=[Lt])
    S.pool(lambda e: e.affine_select(out=Ut[:], in_=one, pattern=[[-1, 128]], compare_op=ALU.is_ge, fill=0.0, base=0, channel_multiplier=1),
           reads=[C.ones_f], writes=[Ut])
    S.pool(lambda e: e.affine_select(out=bigm[0][:], in_=bigfull[:], pattern=[[1, 128]], compare_op=ALU.is_gt, fill=0.0, base=0, channel_multiplier=-1),
           reads=[bigfull], writes=[bigm[0]])
    S.pool(lambda e: e.affine_select(out=bigm[1][:], in_=bigfull[:], pattern=[[-1, 128]], compare_op=ALU.is_gt, fill=0.0, base=0, channel_multiplier=1),
           reads=[bigfull], writes=[bigm[1]])
    S.pool(lambda e: e.affine_select(out=strict[0][:], in_=one, pattern=[[-1, 128]], compare_op=ALU.is_gt, fill=0.0, base=0, channel_multiplier=1),
           reads=[C.ones_f], writes=[strict[0]])
    S.pool(lambda e: e.affine_select(out=strict[1][:], in_=one, pattern=[[1, 128]], compare_op=ALU.is_gt, fill=0.0, base=0, channel_multiplier=-1),
           reads=[C.ones_f], writes=[strict[1]])
    if GCUT == 3:
        return fin()
    Bm = {}
    for s_ in (16, 32, 64):
        G = 128 // s_
        E = sb(nc, st, 'E%d' % s_, [G, 128], F32)
        S.pool(lambda e, E=E, G=G, s_=s_: e.affine_select(out=E[:], in_=C.ones_f[0:G, 0:128], pattern=[[1, 128]], compare_op=ALU.is_ge,
                                                         fill=0.0, base=0, channel_multiplier=-s_), reads=[C.ones_f], writes=[E])
        S.pool(lambda e, E=E, G=G, s_=s_: e.affine_select(out=E[:], in_=E[:], pattern=[[-1, 128]], compare_op=ALU.is_gt,
                                                         fill=0.0, base=s_, channel_multiplier=s_), reads=[E], writes=[E])
        pb_ = banks[4]
        S.pe(lambda e, E=E, pb_=pb_: e.matmul(pb_[:, 0:128], lhsT=E[:], rhs=E[:], start=True, stop=True), reads=[E], writes=[pb_])
        Bm[s_] = sb(nc, st, 'Bm%d' % s_, [128, 128], F32)
        S.dve(lambda e, s_=s_, pb_=pb_: e.tensor_copy(out=Bm[s_][:], in_=pb_[:, 0:128]), reads=[pb_], writes=[Bm[s_]])
    Md = [sb(nc, st, 'Md%d' % d, [128, 128], F32) for d in range(2)]
    Mo = [[sb(nc, st, 'Mo%d_%d' % (d, l), [128, 128], F32) for l in range(3)] for d in range(2)]
    for d in range(2):
        S.dve(lambda e, d=d: e.tensor_tensor(out=Md[d][:], in0=strict[d][:], in1=Bm[16][:], op=ALU.mult), reads=[strict[d], Bm[16]], writes=[Md[d]])
        for l, (big_, small_) in enumerate(((32, 16), (64, 32), (None, 64))):
            t_ = Mo[d][l]
            if big_ is None:
                S.dve(lambda e, t_=t_, small_=small_: e.tensor_scalar(out=t_[:], in0=Bm[small_][:], scalar1=-1.0, scalar2=1.0, op0=ALU.mult, op1=ALU.add),
                      reads=[Bm[small_]], writes=[t_])
            else:
                S.dve(lambda e, t_=t_, big_=big_, small_=small_: e.tensor_tensor(out=t_[:], in0=Bm[big_][:], in1=Bm[small_][:], op=ALU.subtract),
                      reads=[Bm[big_], Bm[small_]], writes=[t_])
            S.dve(lambda e, t_=t_, d=d: e.tensor_tensor(out=t_[:], in0=t_[:], in1=strict[d][:], op=ALU.mult), reads=[t_, strict[d]], writes=[t_])
    pgc = banks[1]
    S.pe(lambda e: e.matmul(pgc[:, 0:NT * 8], lhsT=Lt[:], rhs=g_[:, :, :], start=True, stop=True), reads=[Lt, g_], writes=[pgc])
    S.dve(lambda e: e.tensor_copy(out=gc[:, :, 0:4], in_=pgc[:, 0:NT * 8].rearrange("p (n c) -> p n c", c=8)[:, :, 0:4]), reads=[pgc], writes=[gc])
    pgc2 = banks[2]
    S.pe(lambda e: e.matmul(pgc2[:, 0:NT * 8], lhsT=Ut[:], rhs=g_[:, :, :], start=True, stop=True), reads=[Ut, g_], writes=[pgc2])
    S.dve(lambda e: e.tensor_copy(out=gc[:, :, 4:8], in_=pgc2[:, 0:NT * 8].rearrange("p (n c) -> p n c", c=8)[:, :, 4:8]), reads=[pgc2], writes=[gc])
    if GCUT == 5:
        return fin()
    pgt = banks[3]
    S.pe(lambda e: e.matmul(pgt[:, 0:NT * 8], lhsT=C.ones_f[:, 0:128], rhs=g_[:, :, :], start=True, stop=True), reads=[C.ones_f, g_], writes=[pgt])
    S.act(lambda e: e.activation(out=gl[:], in_=pgt[:, 0:NT * 8].rearrange("p (n c) -> p n c", c=8), func=AF.Exp), reads=[pgt], writes=[gl])
    S.dve(lambda e: e.tensor_tensor(out=kds[:], in0=pgt[:, 0:NT * 8].rearrange("p (n c) -> p n c", c=8), in1=gc[:], op=ALU.subtract),
          reads=[pgt, gc], writes=[kds])
    if GCUT == 6:
        return fin()
    S.act(lambda e: e.activation(out=kds[:], in_=kds[:], func=AF.Exp), reads=[kds], writes=[kds])
    S.act(lambda e: e.activation(out=eg[:], in_=gc[:], func=AF.Exp), reads=[gc], writes=[eg])
    S.dve(lambda e: e.tensor_tensor(out=be[:], in0=beta[:], in1=eg[:], op=ALU.mult), reads=[beta, eg], writes=[be])
    if GCUT == 4:
        return fin()
    dbg_dump(C, 'dbg_gc', gc, gc[:, :, :], [128, NT, 8], F32)
    dbg_dump(C, 'dbg_beta', beta, beta[:, :, :], [128, NT, 8], F32)
    dbg_dump(C, 'dbg_g', g_, g_[:, :, :], [128, NT, 8], F32)

    S.barrier()
    import os
    GSTOP = int(os.environ.get('GSTOP', '99'))
    def tile_of(d, n):
        if n < 2:
            return 32 + n if d == 0 else 33 - n
        return n - 2 if d == 0 else 33 - n
    opnd = [[{k: sb(nc, st, 'op_%s_%d_%d' % (k, d, i), [128, 4, 128], BF16) for k in ('kT', 'qT', 'ktok', 'qtok', 'vtok')}
             for i in range(2)] for d in range(2)]

    def load_tile(d, n):
        nt = tile_of(d, n)
        o = opnd[d][n % 2]
        c0 = T + (nt - 32) * 128 if nt >= 32 else nt * 128
        S.dma(o['kT'][:, :, :], C.KT[:, c0:c0 + 128].rearrange("(h p) n -> p h n", p=128), reads=[C.KT], writes=[o['kT']])
        S.dma(o['ktok'][:, :, :], C.KTOK[c0:c0 + 128, :].rearrange("p (h d) -> p h d", d=128), reads=[C.KTOK], writes=[o['ktok']])
        S.dma(o['vtok'][:, :, :], C.VTOK[c0:c0 + 128, :].rearrange("p (h d) -> p h d", d=128), reads=[C.VTOK], writes=[o['vtok']])
        if nt < 32:
            S.dma(o['qT'][:, :, :], C.QT[:, c0:c0 + 128].rearrange("(h p) n -> p h n", p=128), reads=[C.QT], writes=[o['qT']])
            S.dma(o['qtok'][:, :, :], C.QTOK[c0:c0 + 128, :].rearrange("p (h d) -> p h d", d=128), reads=[C.QTOK], writes=[o['qtok']])

    def cb(name, dt, n=1, shape=(128, 128)):
        return [[sb(nc, st, '%s_%d_%d' % (name, c, i), list(shape), dt) for i in range(n)] for c in range(8)]
    dgc = cb('dgc', F32); Dm = dgc; Ai = cb('Ai', F32)
    Pb = cb('Pb', BF16, 2); PTb = cb('PTb', BF16, 2); Yb = cb('Yb', BF16, 2)
    bv = cb('bv', BF16); kbe = cb('kbe', BF16); qe = cb('qe', BF16); AOb = cb('AOb', BF16, 3)
    attnT = cb('attnT', BF16, 2); u_ = cb('u_', F32, 2); wT = cb('wT', BF16, 2); kd = cb('kd', BF16, 2); qdT = cb('qdT', BF16, 2)
    S32 = cb('S32', F32); Sbf = cb('Sbf', BF16, 2); vn = cb('vn', BF16)
    for c in range(8):
        S.pool(lambda e, c=c: e.memset(S32[c][0][:], 0.0), writes=[S32[c][0]])
        S.pool(lambda e, c=c: e.memset(Sbf[c][0][:], 0.0), writes=[Sbf[c][0]])
    oacc = sb(nc, st, 'oacc', [128, 32, 512], F32)
    ores = [[Res() for h in range(4)] for nt in range(32)]
    ofirst = [[True] * 4 for nt in range(32)]

    def precompute(c, n):
        d, h = c // 4, c % 4
        nt = tile_of(d, n)
        lat = nt < 32
        o = opnd[d][n % 2]
        r = n % 2
        sl = slots[c]
        gcol = gc[:, nt, c:c + 1]
        S.act(lambda e: e.activation(out=dgc[c][0][:], in_=C.ident_f[:], func=AF.Copy, scale=gcol),
              reads=[C.ident_f, gc], writes=[dgc[c][0]])
        S.pe(lambda e: e.matmul(sl[1].f, lhsT=o['kT'][:, h, :], rhs=o['kT'][:, h, :], start=True, stop=True),
             reads=[o['kT']], writes=[sl[1]])
        if lat:
            S.pe(lambda e: e.matmul(sl[2].f, lhsT=o['qT'][:, h, :], rhs=o['kT'][:, h, :], start=True, stop=True),
                 reads=[o['kT'], o['qT']], writes=[sl[2]])
        yield
        S.pe(lambda e: e.matmul(sl[0].f, lhsT=C.ones_f[:, 0:128], rhs=dgc[c][0][:], start=True, stop=False),
             reads=[C.ones_f, dgc[c][0]], writes=[sl[0]], sig=False)
        S.pe(lambda e: e.matmul(sl[0].f, lhsT=C.ident_f[:], rhs=bigm[d][:], start=False, stop=True),
             reads=[C.ident_f, bigm[d]], writes=[sl[0]])
        yield
        S.act(lambda e: e.activation(out=Dm[c][0][:], in_=sl[0].f, func=AF.Exp, bias=gcol, scale=-1.0),
              reads=[sl[0], gc], writes=[Dm[c][0]])
        yield
        S.dve(lambda e: e.scalar_tensor_tensor(out=Ai[c][0][:], in0=sl[1].f, scalar=beta[:, nt, c:c + 1], in1=Dm[c][0][:],
                                               op0=ALU.mult, op1=ALU.mult), reads=[sl[1], beta, Dm[c][0]], writes=[Ai[c][0]])
        if lat:
            S.dve(lambda e: e.tensor_tensor(out=qe[c][0][:], in0=sl[2].f, in1=Dm[c][0][:], op=ALU.mult),
                  reads=[sl[2], Dm[c][0]], writes=[qe[c][0]])
        yield
        A = Pb[c][0]
        S.dve(lambda e: e.tensor_tensor(out=A[:], in0=Ai[c][0][:], in1=Md[d][:], op=ALU.mult),
              reads=[Ai[c][0], Md[d]], writes=[A])
        for li in range(3):
            fn = lambda e, li=li: e.tensor_tensor(out=AOb[c][li][:], in0=Ai[c][0][:], in1=Mo[d][li][:], op=ALU.mult)
            if li < 2:
                S.dve(fn, reads=[Ai[c][0], Mo[d][li]], writes=[AOb[c][li]])
            else:
                S.pool(fn, reads=[Ai[c][0], Mo[d][li]], writes=[AOb[c][li]])
        yield
        S.pe(lambda e: e.transpose(out=sl[0].b, in_=A[:], identity=C.ident_b[:]), reads=[A, C.ident_b], writes=[sl[0]])
        if lat:
            S.pe(lambda e: e.transpose(out=sl[1].b, in_=qe[c][0][:], identity=C.ident_b[:]), reads=[qe[c][0], C.ident_b], writes=[sl[1]])
        yield
        AT = PTb[c][0]
        Y = Yb[c][0]
        S.act(lambda e: e.activation(out=AT[:], in_=sl[0].b, func=AF.Copy), reads=[sl[0]], writes=[AT])
        S.dve(lambda e: e.scalar_tensor_tensor(out=Y[:], in0=sl[0].b, scalar=-1.0, in1=C.ident_b[:], op0=ALU.mult, op1=ALU.add),
              reads=[sl[0], C.ident_b], writes=[Y])
        if lat:
            S.act(lambda e: e.activation(out=attnT[c][r][:], in_=sl[1].b, func=AF.Copy), reads=[sl[1]], writes=[attnT[c][r]])
        yield
        S.act(lambda e: e.activation(out=bv[c][0][:], in_=o['vtok'][:, h, :], func=AF.Copy, scale=beta[:, nt, c:c + 1]),
              reads=[o['vtok'], beta], writes=[bv[c][0]])
        S.act(lambda e: e.activation(out=kbe[c][0][:], in_=o['ktok'][:, h, :], func=AF.Copy, scale=be[:, nt, c:c + 1]),
              reads=[o['ktok'], be], writes=[kbe[c][0]])
        S.pool(lambda e: e.tensor_scalar(out=kd[c][r][:], in0=o['ktok'][:, h, :], scalar1=kds[:, nt, c:c + 1], scalar2=None, op0=ALU.mult),
               reads=[o['ktok'], kds], writes=[kd[c][r]])
        if lat:
            S.act(lambda e: e.activation(out=qe[c][0][:], in_=o['qtok'][:, h, :], func=AF.Copy, scale=eg[:, nt, c:c + 1]),
                  reads=[o['qtok'], eg], writes=[qe[c][0]])
        cur = 0
        for lvl in range(1, 4):
            P, PT, Yc = Pb[c][cur], PTb[c][cur], Yb[c][cur]
            Pn, PTn, Yn = Pb[c][1 - cur], PTb[c][1 - cur], Yb[c][1 - cur]
            S.pe(lambda e, P=P, PT=PT: e.matmul(sl[0].f, lhsT=PT[:], rhs=P[:], start=True, stop=True), reads=[P, PT], writes=[sl[0]])
            if lvl < 3:
                S.pe(lambda e, P=P, PT=PT: e.matmul(sl[1].f, lhsT=P[:], rhs=PT[:], start=True, stop=True), reads=[P, PT], writes=[sl[1]])
            yield
            S.act(lambda e, Pn=Pn: e.activation(out=Pn[:], in_=sl[0].f, func=AF.Copy), reads=[sl[0]], writes=[Pn])
            if lvl < 3:
                S.dve(lambda e, PTn=PTn: e.tensor_copy(out=PTn[:], in_=sl[1].f), reads=[sl[1]], writes=[PTn])
            yield
            S.pe(lambda e, Pn=Pn, Yc=Yc: e.matmul(sl[2].f, lhsT=Pn[:], rhs=Yc[:], start=True, stop=True), reads=[Pn, Yc], writes=[sl[2]])
            yield
            S.dve(lambda e, Yc=Yc, Yn=Yn: e.tensor_tensor(out=Yn[:], in0=sl[2].f, in1=Yc[:], op=ALU.add), reads=[sl[2], Yc], writes=[Yn])
            yield
            cur = 1 - cur
        for li in range(3):
            Yc, Yn = Yb[c][cur], Yb[c][1 - cur]
            Tt, N1 = Pb[c][0], PTb[c][0]
            S.pe(lambda e, Yc=Yc: e.transpose(out=sl[0].b, in_=Yc[:], identity=C.ident_b[:]), reads=[Yc, C.ident_b], writes=[sl[0]])
            S.pe(lambda e, Yc=Yc, li=li: e.matmul(sl[1].f, lhsT=AOb[c][li][:], rhs=Yc[:], start=True, stop=True),
                 reads=[AOb[c][li], Yc], writes=[sl[1]])
            yield
            S.act(lambda e, Tt=Tt: e.activation(out=Tt[:], in_=sl[0].b, func=AF.Copy), reads=[sl[0]], writes=[Tt])
            S.dve(lambda e, N1=N1: e.tensor_copy(out=N1[:], in_=sl[1].f), reads=[sl[1]], writes=[N1])
            yield
            S.pe(lambda e, Tt=Tt, N1=N1: e.matmul(sl[2].f, lhsT=Tt[:], rhs=N1[:], start=True, stop=True), reads=[Tt, N1], writes=[sl[2]])
            yield
            S.dve(lambda e, Yc=Yc, Yn=Yn: e.scalar_tensor_tensor(out=Yn[:], in0=sl[2].f, scalar=-1.0, in1=Yc[:], op0=ALU.mult, op1=ALU.add),
                  reads=[sl[2], Yc], writes=[Yn])
            yield
            cur = 1 - cur
        Y = Yb[c][cur]
        S.pe(lambda e: e.matmul(sl[0].f, lhsT=Y[:], rhs=bv[c][0][:], start=True, stop=True), reads=[Y, bv[c][0]], writes=[sl[0]])
        S.pe(lambda e: e.matmul(sl[1].f, lhsT=kbe[c][0][:], rhs=Y[:], start=True, stop=True), reads=[Y, kbe[c][0]], writes=[sl[1]])
        if lat:
            S.pe(lambda e: e.transpose(out=sl[2].b, in_=qe[c][0][:], identity=C.ident_b[:]), reads=[qe[c][0], C.ident_b], writes=[sl[2]])
        yield
        S.act(lambda e: e.activation(out=u_[c][r][:], in_=sl[0].f, func=AF.Copy), reads=[sl[0]], writes=[u_[c][r]])
        S.dve(lambda e: e.tensor_copy(out=wT[c][r][:], in_=sl[1].f), reads=[sl[1]], writes=[wT[c][r]])
        if lat:
            S.act(lambda e: e.activation(out=qdT[c][r][:], in_=sl[2].b, func=AF.Copy), reads=[sl[2]], writes=[qdT[c][r]])
        yield

    def scan(c, n):
        d, h = c // 4, c % 4
        nt = tile_of(d, n)
        lat = nt < 32
        r = n % 2
        sl = slots[c][3]
        Sold, Snew = Sbf[c][n % 2], Sbf[c][1 - n % 2]
        S.pe(lambda e: e.matmul(sl.f, lhsT=wT[c][r][:], rhs=Sold[:], start=True, stop=True), reads=[wT[c][r], Sold], writes=[sl])
        yield
        S.dve(lambda e: e.scalar_tensor_tensor(out=vn[c][0][:], in0=sl.f, scalar=-1.0, in1=u_[c][r][:], op0=ALU.mult, op1=ALU.add),
              reads=[sl, u_[c][r]], writes=[vn[c][0]])
        yield
        S.pe(lambda e: e.matmul(sl.f, lhsT=kd[c][r][:], rhs=vn[c][0][:], start=True, stop=True), reads=[kd[c][r], vn[c][0]], writes=[sl])
        yield
        glc = gl[:, nt, c:c + 1]
        S.dve(lambda e: e.scalar_tensor_tensor(out=Snew[:], in0=S32[c][0][:], scalar=glc, in1=sl.f, op0=ALU.mult, op1=ALU.add),
              reads=[S32[c][0], gl, sl], writes=[Snew])
        S.dve(lambda e: e.scalar_tensor_tensor(out=S32[c][0][:], in0=S32[c][0][:], scalar=glc, in1=sl.f, op0=ALU.mult, op1=ALU.add),
              reads=[S32[c][0], gl, sl], writes=[S32[c][0]])
        yield
        if lat:
            S.pe(lambda e: e.matmul(sl.f, lhsT=qdT[c][r][:], rhs=Sold[:], start=True, stop=False), reads=[qdT[c][r], Sold], writes=[sl], sig=False)
            S.pe(lambda e: e.matmul(sl.f, lhsT=attnT[c][r][:], rhs=vn[c][0][:], start=False, stop=True), reads=[attnT[c][r], vn[c][0]], writes=[sl])
            yield
            orr = ores[nt][h]
            if ofirst[nt][h]:
                ofirst[nt][h] = False
                S.act(lambda e: e.activation(out=oacc[:, nt, h * 128:(h + 1) * 128], in_=sl.f, func=AF.Copy), reads=[sl], writes=[orr])
            else:
                S.dve(lambda e: e.tensor_tensor(out=oacc[:, nt, h * 128:(h + 1) * 128], in0=sl.f, in1=oacc[:, nt, h * 128:(h + 1) * 128], op=ALU.add),
                      reads=[sl, orr], writes=[orr])
            yield

    NR = min(34, GSTOP)
    if GSTOP >= 0:
        for d in range(2):
            load_tile(d, 0)
        run_interleaved([precompute(c, 0) for c in range(8)])
    for n in range(NR):
        gens = [scan(c, n) for c in range(8)]
        if n + 1 < NR:
            for d in range(2):
                load_tile(d, n + 1)
            gens += [precompute(c, n + 1) for c in range(8)]
        run_interleaved(gens)
    for nt in (range(32) if NR == 34 else []):
        S.dma(OACC[nt * 128:(nt + 1) * 128, :], oacc[:, nt, :], reads=ores[nt], writes=[OACC], q='sp')
    for c in range(8):
        dbg_dump(C, 'dbg_S%d' % c, S32[c][0], S32[c][0][:], [128, 128], F32)
    S.barrier()
    S.flush()
    st.close()


def load_rows_bcast(C, st, name, dram_row, n):
    t = sb(C.nc, st, name, [128, n], F32)
    C.S.dma(t[:], dram_row.to_broadcast([128, n]), writes=[t])
    return t


class Epi:
    def __init__(self, C, st, ln_idx, gate_ap, nsub):
        nc = C.nc
        self.C, self.nsub, self.gate = C, nsub, gate_ap
        self.g = load_rows_bcast(C, st, 'ln_g%d' % ln_idx, C.din['ln_g'][ln_idx:ln_idx + 1, :], D)
        self.b = load_rows_bcast(C, st, 'ln_b%d' % ln_idx, C.din['ln_b'][ln_idx:ln_idx + 1, :], D)
        self.t2 = sb(nc, st, 'ep_t2', [128, nsub, D], F32)
        self.junk = sb(nc, st, 'ep_junk', [128, D], BF16)
        self.st = sb(nc, st, 'ep_st', [128, 6, nsub], F32)
        self.eps = sb(nc, st, 'ep_eps', [128, 1], F32)
        self.xo = [self.t2] * 2
        C.S.pool(lambda e: e.memset(self.eps[:], LN_EPS), writes=[self.eps])
        self.i = 0

    def sub(self, s_, ypair, xt):
        S, t2, stt = self.C.S, self.t2, self.st
        for hf in range(2):
            S.dve(lambda e, hf=hf: e.tensor_tensor(out=t2[:, s_, hf * 512:(hf + 1) * 512], in0=ypair[hf][:],
                                                   in1=self.gate[:, hf * 512:(hf + 1) * 512], op=ALU.mult),
                  reads=[ypair[hf], self.C.modrow], writes=[t2])
        S.dve(lambda e: e.scalar_tensor_tensor(out=t2[:, s_, :], in0=xt[:, s_, :], scalar=ALPHA, in1=t2[:, s_, :], op0=ALU.mult, op1=ALU.add),
              reads=[xt, t2], writes=[t2])
        S.act(lambda e: e.activation(out=self.junk[:], in_=t2[:, s_, :], func=AF.Copy, accum_out=stt[:, 0, s_:s_ + 1]),
              reads=[t2], writes=[self.junk, stt])
        S.act(lambda e: e.activation(out=self.junk[:], in_=t2[:, s_, :], func=AF.Square, accum_out=stt[:, 1, s_:s_ + 1]),
              reads=[t2], writes=[self.junk, stt])

    def finish(self, dst_rows, xt_unused=None):
        S, t2, stt, n = self.C.S, self.t2, self.st, self.nsub
        dst, r0 = dst_rows
        xo = self.xo[self.i % 2]
        self.i += 1
        S.dve(lambda e: e.tensor_scalar(out=stt[:, 2, :], in0=stt[:, 0, :], scalar1=1.0 / D, scalar2=None, op0=ALU.mult), reads=[stt], writes=[stt])
        S.dve(lambda e: e.tensor_tensor(out=stt[:, 4, :], in0=stt[:, 2, :], in1=stt[:, 2, :], op=ALU.mult), reads=[stt], writes=[stt])
        S.dve(lambda e: e.scalar_tensor_tensor(out=stt[:, 3, :], in0=stt[:, 1, :], scalar=1.0 / D, in1=stt[:, 4, :], op0=ALU.mult, op1=ALU.subtract),
              reads=[stt], writes=[stt])
        S.act(lambda e: e.activation(out=stt[:, 3, :], in_=stt[:, 3, :], func=AF.Ln, bias=self.eps[:, 0:1]), reads=[stt, self.eps], writes=[stt])
        S.act(lambda e: e.activation(out=stt[:, 3, :], in_=stt[:, 3, :], func=AF.Exp, scale=-0.5), reads=[stt], writes=[stt])
        S.dve(lambda e: e.scalar_tensor_tensor(out=stt[:, 5, :], in0=stt[:, 2, :], scalar=-1.0, in1=stt[:, 3, :], op0=ALU.mult, op1=ALU.mult),
              reads=[stt], writes=[stt])
        for s_ in range(n):
            S.act(lambda e, s_=s_: e.activation(out=t2[:, s_, :], in_=t2[:, s_, :], func=AF.Identity, scale=stt[:, 3, s_:s_ + 1], bias=stt[:, 5, s_:s_ + 1]),
                  reads=[t2, stt], writes=[t2])
            S.pool(lambda e, s_=s_: e.tensor_tensor(out=xo[:, s_, :], in0=t2[:, s_, :], in1=self.g[:], op=ALU.mult), reads=[t2, self.g], writes=[xo])
            S.dve(lambda e, s_=s_: e.tensor_tensor(out=xo[:, s_, :], in0=xo[:, s_, :], in1=self.b[:], op=ALU.add), reads=[xo, self.b], writes=[xo])
        S.dma(dst[r0:r0 + n * 128, :].rearrange("(s p) d -> p s d", p=128), xo[:, :, :], reads=[xo], writes=[dst], q='sp')


def out_proj(C, mixT, wout, s_, ypair, nk):
    S = C.S
    for hf in range(2):
        for k in range(nk):
            S.pe(lambda e, hf=hf, k=k: e.matmul(ypair[hf][:], lhsT=mixT[:, k, s_ * 128:(s_ + 1) * 128], rhs=wout[:, k, hf * 512:(hf + 1) * 512],
                                                start=(k == 0), stop=(k == nk - 1)), reads=[mixT, wout], writes=[ypair[hf]], sig=(k == nk - 1))


def phase_mix0_out(C, X1):
    nc, S = C.nc, C.S
    st = ExitStack()
    wout = load_w_bf16(C, st, 'wout0', C.din['even_w_out'][:, :], 8, D)
    normw = load_rows_bcast(C, st, 'normw', C.din['gdn_norm_w'][0:1, :], 128)
    epi = Epi(C, st, 0, C.modrow[:, 2 * D:3 * D], 4)
    yps = [[ps(nc, st, 'yps%d_%d' % (i, h), [128, 512], F32) for h in range(2)] for i in range(2)]
    pT = [ps(nc, st, 'pTm%d' % i, [128, 512], BF16) for i in range(2)]
    ot = [sb(nc, st, 'ot%d' % i, [128, 4, 512], F32) for i in range(2)]
    gt_ = [sb(nc, st, 'gt%d' % i, [128, 4, 512], BF16) for i in range(2)]
    xt = [sb(nc, st, 'xtm%d' % i, [128, 4, D], F32) for i in range(2)]
    mixT = [sb(nc, st, 'mixT%d' % i, [128, 8, 512], BF16) for i in range(2)]
    osq = sb(nc, st, 'osq', [128, 4, 512], F32)
    ss = sb(nc, st, 'oss', [128, 16], F32)
    og = sb(nc, st, 'og', [128, 4, 512], BF16)
    epsr = sb(nc, st, 'epsr', [128, 1], F32)
    S.pool(lambda e: e.memset(epsr[:], RMS_EPS), writes=[epsr])

    def load(t):
        i = t % 2
        S.dma(ot[i][:, :, :], C.OACC[t * 512:(t + 1) * 512, :].rearrange("(s p) d -> p s d", p=128), reads=[C.OACC], writes=[ot[i]])
        S.dma(gt_[i][:, :, :], C.G0[t * 512:(t + 1) * 512, :].rearrange("(s p) d -> p s d", p=128), reads=[C.G0], writes=[gt_[i]])
        S.dma(xt[i][:, :, :], C.din['x'][t * 512:(t + 1) * 512, :].rearrange("(s p) d -> p s d", p=128), writes=[xt[i]])
        S.dma(mixT[i][:, 4:8, :], C.YPT[:, t * 512:(t + 1) * 512].rearrange("(f p) n -> p f n", p=128), reads=[C.YPT], writes=[mixT[i]])

    load(0)
    yi = 0
    for t in range(8):
        if t + 1 < 8:
            load(t + 1)
        i = t % 2
        o_, g_, x_, m_ = ot[i], gt_[i], xt[i], mixT[i]
        S.pool(lambda e, o_=o_: e.tensor_tensor(out=osq[:], in0=o_[:], in1=o_[:], op=ALU.mult), reads=[o_], writes=[osq])
        S.dve(lambda e: e.tensor_reduce(out=ss[:], in_=osq[:].rearrange("p s (h d) -> p (s h) d", d=128), axis=mybir.AxisListType.X, op=ALU.add),
              reads=[osq], writes=[ss])
        S.act(lambda e: e.activation(out=ss[:], in_=ss[:], func=AF.Ln, scale=1.0 / 128, bias=epsr[:, 0:1]), reads=[ss, epsr], writes=[ss])
        S.act(lambda e: e.activation(out=ss[:], in_=ss[:], func=AF.Exp, scale=-0.5), reads=[ss], writes=[ss])
        S.dve(lambda e, o_=o_: e.tensor_tensor(out=osq[:].rearrange("p s (h d) -> p (s h) d", d=128), in0=o_[:].rearrange("p s (h d) -> p (s h) d", d=128),
                                              in1=ss[:].unsqueeze(2).to_broadcast([128, 16, 128]), op=ALU.mult), reads=[o_, ss], writes=[osq])
        S.pool(lambda e: e.tensor_tensor(out=osq[:].rearrange("p s (h d) -> p (s h) d", d=128), in0=osq[:].rearrange("p s (h d) -> p (s h) d", d=128),
                                         in1=normw[:].unsqueeze(1).to_broadcast([128, 16, 128]), op=ALU.mult), reads=[osq, normw], writes=[osq])
        S.dve(lambda e, g_=g_: e.tensor_tensor(out=og[:], in0=osq[:], in1=g_[:], op=ALU.mult), reads=[osq, g_], writes=[og])
        for h in range(4):
            p = pT[h % 2]
            for s_ in range(4):
                S.pe(lambda e, p=p, h=h, s_=s_: e.transpose(out=p[:, s_ * 128:(s_ + 1) * 128], in_=og[:, s_, h * 128:(h + 1) * 128], identity=C.ident_b[:]),
                     reads=[og, C.ident_b], writes=[p], sig=(s_ == 3))
            if h % 2 == 0:
                S.act(lambda e, p=p, h=h, m_=m_: e.activation(out=m_[:, h, :], in_=p[:], func=AF.Copy), reads=[p], writes=[m_])
            else:
                S.dve(lambda e, p=p, h=h, m_=m_: e.tensor_copy(out=m_[:, h, :], in_=p[:]), reads=[p], writes=[m_])
        for s_ in range(4):
            yp = yps[yi % 2]; yi += 1
            out_proj(C, m_, wout, s_, yp, 8)
            epi.sub(s_, yp, x_)
        epi.finish((X1, t * 512))
    dbg_dump(C, 'dbg_x1', X1, X1[0:128, :], [128, D], F32)
    S.barrier()
    S.flush()
    st.close()


def phase_ffn_up(C, layer, Xin):
    nc, S = C.nc, C.S
    st = ExitStack()
    if layer == 0:
        C.AT = C.scratch('AT', [DFF, T + 128], BF16)
        C.GTt = C.scratch('GTt', [DFF, T], BF16)
        for r0 in range(0, DFF, 128):
            S.dma(C.AT[r0:r0 + 128, 0:64], C.zeros_b[:, 0:64], reads=[C.zeros_b], writes=[C.AT])
            S.dma(C.AT[r0:r0 + 128, T + 64:T + 128], C.zeros_b[:, 0:64], reads=[C.zeros_b], writes=[C.AT])
    AT, GTt = C.AT, C.GTt
    w = load_w_bf16(C, st, 'wup', C.din['ffn_w_up'][layer, :, :], 8, 2 * DFF)
    xt = sb(nc, st, 'xtu', [128, 4, D], F32)
    ub = sb(nc, st, 'ubu', [128, 4, D], BF16)
    uT = [sb(nc, st, 'uTu%d' % i, [128, 8, 512], BF16) for i in range(2)]
    stg = [sb(nc, st, 'stgu%d' % i, [128, 4, 512], BF16) for i in range(2)]
    pT = [ps(nc, st, 'pTu%d' % i, [128, 512], BF16) for i in range(2)]
    pm = [ps(nc, st, 'pmu%d' % i, [128, 512], F32) for i in range(4)]
    pmi = 0
    for t in range(8):
        S.dma(xt[:, :, :], Xin[t * 512:(t + 1) * 512, :].rearrange("(s p) d -> p s d", p=128), reads=[Xin], writes=[xt])
        ut = uT[t % 2]
        modulate_transpose(C, xt, 4, C.modrow[:, 3 * D:4 * D], C.modrow[:, 4 * D:5 * D], ub, ut, pT, t)
        gi = 0
        for f0 in range(0, 44, 4):
            sg = stg[gi % 2]; gi += 1
            nf = min(4, 44 - f0)
            for j in range(nf):
                fc = f0 + j
                p = pm[pmi % 4]; pmi += 1
                for k in range(8):
                    S.pe(lambda e, p=p, k=k, fc=fc, ut=ut: e.matmul(p[:], lhsT=w[:, k, fc * 128:(fc + 1) * 128], rhs=ut[:, k, :],
                                                                 start=(k == 0), stop=(k == 7)), reads=[w, ut], writes=[p], sig=(k == 7))
                if pmi % 2 == 0:
                    S.act(lambda e, p=p, j=j, sg=sg: e.activation(out=sg[:, j, :], in_=p[:], func=AF.Copy), reads=[p], writes=[sg])
                else:
                    S.dve(lambda e, p=p, j=j, sg=sg: e.tensor_copy(out=sg[:, j, :], in_=p[:]), reads=[p], writes=[sg])
            if f0 < 22:
                na = min(nf, 22 - f0)
                S.dma(AT[f0 * 128:(f0 + na) * 128, 64 + t * 512:64 + (t + 1) * 512].rearrange("(j p) n -> p j n", p=128), sg[:, 0:na, :],
                      reads=[sg], writes=[AT], q='act')
                if na < nf:
                    S.dma(GTt[0:(nf - na) * 128, t * 512:(t + 1) * 512].rearrange("(j p) n -> p j n", p=128), sg[:, na:nf, :],
                          reads=[sg], writes=[GTt], q='act')
            else:
                g0 = f0 - 22
                S.dma(GTt[g0 * 128:(g0 + nf) * 128, t * 512:(t + 1) * 512].rearrange("(j p) n -> p j n", p=128), sg[:, 0:nf, :],
                      reads=[sg], writes=[GTt], q='act')
    S.barrier()
    S.flush()
    st.close()


def phase_ffn_down(C, layer, Xin, Xout):
    phase_ffn_conv(C, layer)
    phase_ffn_proj(C, layer, Xin, Xout)


def phase_ffn_conv(C, layer):
    nc, S = C.nc, C.S
    st = ExitStack()
    AT, GTt = C.AT, C.GTt
    if layer == 0:
        C.HGT = C.scratch('HGT', [DFF, T], BF16)
    HGT = C.HGT
    NTK = 256
    pconv = [ps(nc, st, 'pcv%d' % i, [128, 512], F32) for i in range(4)]
    dg = build_diag(C, st, pconv[0], C.din['ffn_conv_w'][layer, :, :], 9, 22, 'cw9')
    ad = [sb(nc, st, 'ad%d' % i, [128, 22, 384], BF16) for i in range(2)]
    gt_ = [sb(nc, st, 'gd%d' % i, [128, 22, 256], BF16) for i in range(2)]
    apad = [sb(nc, st, 'apad%d' % i, [128, 22, 6, 66], BF16) for i in range(2)]
    hgs = [sb(nc, st, 'hgs%d' % i, [128, 22, 256], BF16) for i in range(2)]
    hs = [sb(nc, st, 'hs%d' % i, [128, 256], BF16) for i in range(4)]
    for i in range(2):
        S.pool(lambda e, i=i: e.memset(apad[i][:], 0.0), writes=[apad[i]])
    ntile = T // NTK

    def load(t):
        i = t % 2
        S.dma(ad[i][:, :, :], AT[:, t * NTK:t * NTK + 384].rearrange("(f p) n -> p f n", p=128), reads=[AT], writes=[ad[i]])
        S.dma(gt_[i][:, :, :], GTt[:, t * NTK:(t + 1) * NTK].rearrange("(f p) n -> p f n", p=128), reads=[GTt], writes=[gt_[i]])

    load(0)
    ci = 0
    for t in range(ntile):
        if t + 1 < ntile:
            load(t + 1)
        i = t % 2
        a_, g_, ap_, hg = ad[i], gt_[i], apad[i], hgs[i]
        for fc in range(22):
            fn = lambda e, fc=fc, a_=a_, ap_=ap_: e.tensor_copy(out=ap_[:, fc, :, 1:65], in_=a_[:, fc, :].rearrange("p (r c) -> p r c", c=64))
            if fc % 2 == 0:
                S.pool(fn, reads=[a_], writes=[ap_])
            else:
                S.dve(fn, reads=[a_], writes=[ap_])
        for fc in range(22):
            pc = pconv[ci % 4]; h_ = hs[ci % 4]; ci += 1
            for tap in range(9):
                dy, dx = tap // 3 - 1, tap % 3 - 1
                S.pe(lambda e, pc=pc, fc=fc, tap=tap, dy=dy, dx=dx, ap_=ap_: e.matmul(
                    pc[:, 0:256], lhsT=dg[:, fc, tap, :], rhs=ap_[:, fc, 1 + dy:5 + dy, 1 + dx:65 + dx], start=(tap == 0), stop=(tap == 8)),
                    reads=[dg, ap_], writes=[pc], sig=(tap == 8))
            S.act(lambda e, pc=pc, h_=h_: e.activation(out=h_[:], in_=pc[:, 0:256], func=AF.Silu), reads=[pc], writes=[h_])
            S.dve(lambda e, fc=fc, h_=h_, g_=g_, hg=hg: e.tensor_tensor(out=hg[:, fc, :], in0=h_[:], in1=g_[:, fc, :], op=ALU.mult),
                  reads=[h_, g_], writes=[hg])
        S.dma(HGT[:, t * NTK:(t + 1) * NTK].rearrange("(f p) n -> p f n", p=128), hg[:, :, :], reads=[hg], writes=[HGT], q='act')
    S.barrier()
    S.flush()
    st.close()


def phase_ffn_proj(C, layer, Xin, Xout):
    nc, S = C.nc, C.S
    st = ExitStack()
    HGT = C.HGT
    yps = [[ps(nc, st, 'ypd%d_%d' % (i, h), [128, 512], F32) for h in range(2)] for i in range(3)]
    wd = load_w_bf16(C, st, 'wdn', C.din['ffn_w_down'][layer, :, :], 22, D)
    epi = Epi(C, st, layer * 2 + 1, C.modrow[:, 5 * D:6 * D], 4)
    hg = [sb(nc, st, 'hgp%d' % i, [128, 22, 512], BF16) for i in range(2)]
    xt = [sb(nc, st, 'xtd%d' % i, [128, 4, D], F32) for i in range(2)]

    def load(t):
        i = t % 2
        S.dma(hg[i][:, :, :], HGT[:, t * 512:(t + 1) * 512].rearrange("(f p) n -> p f n", p=128), reads=[HGT], writes=[hg[i]])
        S.dma(xt[i][:, :, :], Xin[t * 512:(t + 1) * 512, :].rearrange("(s p) d -> p s d", p=128), reads=[Xin], writes=[xt[i]])

    load(0)
    yi = 0
    for t in range(8):
        if t + 1 < 8:
            load(t + 1)
        h_, x_ = hg[t % 2], xt[t % 2]
        for s_ in range(4):
            yp = yps[yi % 3]; yi += 1
            out_proj(C, h_, wd, s_, yp, 22)
            epi.sub(s_, yp, x_)
        epi.finish((Xout, t * 512))
    S.barrier()
    S.flush()
    st.close()


def phase_inproj1(C, Xin):
    nc, S = C.nc, C.S
    st = ExitStack()
    GBT = C.scratch('GBT', [512, T], BF16); C.GBT = GBT
    M1T = C.scratch('M1T', [512, T + 32], BF16); C.M1T = M1T
    M2T = C.scratch('M2T', [512, T + 32], BF16); C.M2T = M2T
    for M in (M1T, M2T):
        for r0 in range(0, 512, 128):
            S.dma(M[r0:r0 + 128, 0:16], C.zeros_b[:, 0:16], reads=[C.zeros_b], writes=[M])
            S.dma(M[r0:r0 + 128, T + 16:T + 32], C.zeros_b[:, 0:16], reads=[C.zeros_b], writes=[M])
    w = load_w_bf16(C, st, 'w_in1', C.din['odd_w_in'][:, :], 8, 2560)
    xt = sb(nc, st, 'xt1', [128, 4, D], F32)
    ub = sb(nc, st, 'ub1', [128, 4, D], BF16)
    uT = [sb(nc, st, 'uT1%d' % i, [128, 8, 512], BF16) for i in range(2)]
    stg = [[sb(nc, st, 'stg1_%d_%d' % (g, i), [128, 4, 512], BF16) for i in range(2)] for g in range(3)]
    tmp = [sb(nc, st, 'tmp1_%d' % i, [128, 512], F32) for i in range(2)]
    pT = [ps(nc, st, 'pT1%d' % i, [128, 512], BF16) for i in range(2)]
    pm = [ps(nc, st, 'pm1%d' % i, [128, 512], F32) for i in range(4)]
    pmi = 0
    ti = 0

    def mm(fc, ut):
        nonlocal pmi
        p = pm[pmi % 4]; pmi += 1
        for k in range(8):
            S.pe(lambda e, p=p, k=k: e.matmul(p[:], lhsT=w[:, k, fc * 128:(fc + 1) * 128], rhs=ut[:, k, :], start=(k == 0), stop=(k == 7)),
                 reads=[w, ut], writes=[p], sig=(k == 7))
        return p

    for t in range(8):
        S.dma(xt[:, :, :], Xin[t * 512:(t + 1) * 512, :].rearrange("(s p) d -> p s d", p=128), reads=[Xin], writes=[xt])
        ut = uT[t % 2]
        modulate_transpose(C, xt, 4, C.modrow[:, 0:D], C.modrow[:, D:2 * D], ub, ut, pT, t)
        sgb, sm1, sm2 = stg[0][t % 2], stg[1][t % 2], stg[2][t % 2]
        for j in range(4):
            p = mm(j, ut)
            S.act(lambda e, p=p, j=j, sgb=sgb: e.activation(out=sgb[:, j, :], in_=p[:], func=AF.Copy), reads=[p], writes=[sgb])
            tm = tmp[ti % 2]; ti += 1
            p = mm(4 + j, ut)
            S.act(lambda e, p=p, tm=tm: e.activation(out=tm[:], in_=p[:], func=AF.Copy), reads=[p], writes=[tm])
            p = mm(8 + j, ut)
            S.dve(lambda e, p=p, tm=tm, j=j, sm1=sm1: e.tensor_tensor(out=sm1[:, j, :], in0=p[:], in1=tm[:], op=ALU.mult), reads=[p, tm], writes=[sm1])
            tm = tmp[ti % 2]; ti += 1
            p = mm(16 + j, ut)
            S.act(lambda e, p=p, tm=tm: e.activation(out=tm[:], in_=p[:], func=AF.Sigmoid), reads=[p], writes=[tm])
            p = mm(12 + j, ut)
            S.dve(lambda e, p=p, tm=tm, j=j, sm2=sm2: e.tensor_tensor(out=sm2[:, j, :], in0=p[:], in1=tm[:], op=ALU.mult), reads=[p, tm], writes=[sm2])
        c0 = t * 512
        S.dma(GBT[:, c0:c0 + 512].rearrange("(j p) n -> p j n", p=128), sgb[:, :, :], reads=[sgb], writes=[GBT], q='act')
        S.dma(M1T[:, 16 + c0:16 + c0 + 512].rearrange("(j p) n -> p j n", p=128), sm1[:, :, :], reads=[sm1], writes=[M1T], q='act')
        S.dma(M2T[:, 16 + c0:16 + c0 + 512].rearrange("(j p) n -> p j n", p=128), sm2[:, :, :], reads=[sm2], writes=[M2T], q='act')
    S.barrier()
    S.flush()
    st.close()


def phase_mix1_out(C, Xin, Xout):
    nc, S = C.nc, C.S
    st = ExitStack()
    pconv = [ps(nc, st, 'pc1%d' % i, [128, 512], F32) for i in range(2)]
    pstat = [ps(nc, st, 'pst1%d' % i, [128, 512], F32) for i in range(2)]
    yps = [[ps(nc, st, 'yp1%d_%d' % (i, h), [128, 512], F32) for h in range(2)] for i in range(2)]
    dg3 = build_diag(C, st, pconv[0], C.din['sconv_w'][:, :], 3, 4, 'cw3')
    dg31 = build_diag(C, st, pconv[1], C.din['conf_conv_w'][:, :], 31, 4, 'cw31')
    lng = to_col(C, st, pconv[0], C.din['conf_ln_g'][:, :], 1, 4, 'clng')
    lnb = to_col(C, st, pconv[1], C.din['conf_ln_b'][:, :], 1, 4, 'clnb')
    wout = load_w_bf16(C, st, 'wout1', C.din['odd_w_out'][:, :], 8, D)
    epi = Epi(C, st, 2, C.modrow[:, 2 * D:3 * D], 4)
    m1 = [sb(nc, st, 'm1_%d' % i, [128, 4, 514], BF16) for i in range(2)]
    m2 = [sb(nc, st, 'm2_%d' % i, [128, 4, 542], BF16) for i in range(2)]
    gb = [sb(nc, st, 'gb_%d' % i, [128, 4, 512], BF16) for i in range(2)]
    xt = [sb(nc, st, 'xt1o%d' % i, [128, 4, D], F32) for i in range(2)]
    mixT = sb(nc, st, 'mixT1', [128, 8, 512], BF16)
    z = sb(nc, st, 'z1', [128, 4, 512], F32)
    zsq = sb(nc, st, 'zsq1', [128, 4, 512], F32)
    mean = sb(nc, st, 'mean1', [128, 512], F32)
    rstd = sb(nc, st, 'rstd1', [128, 512], F32)
    msq = sb(nc, st, 'msq1', [128, 512], F32)
    epsl = sb(nc, st, 'epsl1', [128, 1], F32)
    S.pool(lambda e: e.memset(epsl[:], LN_EPS), writes=[epsl])

    def load(t):
        i = t % 2
        c0 = t * 512
        S.dma(m1[i][:, :, :], C.M1T[:, 15 + c0:15 + c0 + 514].rearrange("(f p) n -> p f n", p=128), reads=[C.M1T], writes=[m1[i]])
        S.dma(m2[i][:, :, :], C.M2T[:, 1 + c0:1 + c0 + 542].rearrange("(f p) n -> p f n", p=128), reads=[C.M2T], writes=[m2[i]])
        S.dma(gb[i][:, :, :], C.GBT[:, c0:c0 + 512].rearrange("(f p) n -> p f n", p=128), reads=[C.GBT], writes=[gb[i]])
        S.dma(xt[i][:, :, :], Xin[c0:c0 + 512, :].rearrange("(s p) d -> p s d", p=128), reads=[Xin], writes=[xt[i]])

    load(0)
    ci = 0
    yi = 0
    for t in range(8):
        if t + 1 < 8:
            load(t + 1)
        i = t % 2
        a1, a2, g_, x_ = m1[i], m2[i], gb[i], xt[i]
        for j in range(4):
            pc = pconv[ci % 2]; ci += 1
            for tap in range(3):
                S.pe(lambda e, pc=pc, j=j, tap=tap, a1=a1: e.matmul(pc[:], lhsT=dg3[:, j, tap, :], rhs=a1[:, j, tap:tap + 512], start=(tap == 0), stop=(tap == 2)),
                     reads=[dg3, a1], writes=[pc], sig=(tap == 2))
            S.dve(lambda e, pc=pc, j=j, g_=g_: e.tensor_tensor(out=mixT[:, j, :], in0=pc[:], in1=g_[:, j, :], op=ALU.mult), reads=[pc, g_], writes=[mixT])
        for j in range(4):
            pc = pconv[ci % 2]; ci += 1
            for tap in range(31):
                S.pe(lambda e, pc=pc, j=j, tap=tap, a2=a2: e.matmul(pc[:], lhsT=dg31[:, j, tap, :], rhs=a2[:, j, tap:tap + 512], start=(tap == 0), stop=(tap == 30)),
                     reads=[dg31, a2], writes=[pc], sig=(tap == 30))
            S.act(lambda e, pc=pc, j=j: e.activation(out=z[:, j, :], in_=pc[:], func=AF.Copy), reads=[pc], writes=[z])
            S.pool(lambda e, j=j: e.tensor_tensor(out=zsq[:, j, :], in0=z[:, j, :], in1=z[:, j, :], op=ALU.mult), reads=[z], writes=[zsq])
        for j in range(4):
            S.pe(lambda e, j=j: e.matmul(pstat[0][:], lhsT=C.ones_f[:, 0:128], rhs=z[:, j, :], start=(j == 0), stop=(j == 3)),
                 reads=[C.ones_f, z], writes=[pstat[0]], sig=(j == 3))
        for j in range(4):
            S.pe(lambda e, j=j: e.matmul(pstat[1][:], lhsT=C.ones_f[:, 0:128], rhs=zsq[:, j, :], start=(j == 0), stop=(j == 3)),
                 reads=[C.ones_f, zsq], writes=[pstat[1]], sig=(j == 3))
        S.dve(lambda e: e.tensor_scalar(out=mean[:], in0=pstat[0][:], scalar1=1.0 / 512, scalar2=None, op0=ALU.mult), reads=[pstat[0]], writes=[mean])
        S.dve(lambda e: e.tensor_tensor(out=msq[:], in0=mean[:], in1=mean[:], op=ALU.mult), reads=[mean], writes=[msq])
        S.dve(lambda e: e.scalar_tensor_tensor(out=rstd[:], in0=pstat[1][:], scalar=1.0 / 512, in1=msq[:], op0=ALU.mult, op1=ALU.subtract),
              reads=[pstat[1], msq], writes=[rstd])
        S.act(lambda e: e.activation(out=rstd[:], in_=rstd[:], func=AF.Ln, bias=epsl[:, 0:1]), reads=[rstd, epsl], writes=[rstd])
        S.act(lambda e: e.activation(out=rstd[:], in_=rstd[:], func=AF.Exp, scale=-0.5), reads=[rstd], writes=[rstd])
        for j in range(4):
            S.dve(lambda e, j=j: e.tensor_tensor(out=z[:, j, :], in0=z[:, j, :], in1=mean[:], op=ALU.subtract), reads=[z, mean], writes=[z])
            S.pool(lambda e, j=j: e.tensor_tensor(out=z[:, j, :], in0=z[:, j, :], in1=rstd[:], op=ALU.mult), reads=[z, rstd], writes=[z])
            S.act(lambda e, j=j: e.activation(out=mixT[:, 4 + j, :], in_=z[:, j, :], func=AF.Silu, scale=lng[:, j, 0:1], bias=lnb[:, j, 0:1]),
                  reads=[z, lng, lnb], writes=[mixT])
        for s_ in range(4):
            yp = yps[yi % 2]; yi += 1
            out_proj(C, mixT, wout, s_, yp, 8)
            epi.sub(s_, yp, x_)
        epi.finish((Xout, t * 512))
    S.barrier()
    S.flush()
    st.close()


_NC_CACHE = {}


def kernel(**inputs):
    if 'nc' not in _NC_CACHE:
        _NC_CACHE['nc'] = build()
    nc = _NC_CACHE['nc']
    f = lambda a: np.ascontiguousarray(np.asarray(a, dtype=np.float32))
    shared = {
        'c_ctx': f(inputs['c_ctx']).reshape(1, D),
        'ada_w': f(inputs['ada_w']), 'ada_b': f(inputs['ada_b']),
        'ln_g': f(inputs['ln_g']).reshape(4, D), 'ln_b': f(inputs['ln_b']).reshape(4, D),
        'even_w_in': f(inputs['even_w_in']), 'even_w_out': f(inputs['even_w_out']),
        'gdn_conv_w': f(inputs['gdn_conv_w']), 'gdn_a_log': f(inputs['gdn_a_log']).reshape(1, 8),
        'gdn_dt_bias': f(inputs['gdn_dt_bias']).reshape(1, 8), 'gdn_norm_w': f(inputs['gdn_norm_w']).reshape(1, 128),
        'pool_w': f(inputs['pool_w']), 'pool_scale': f(inputs['pool_scale']).reshape(1, 512),
        'odd_w_in': f(inputs['odd_w_in']), 'odd_w_out': f(inputs['odd_w_out']),
        'sconv_w': f(inputs['sconv_w']), 'conf_conv_w': f(inputs['conf_conv_w']),
        'conf_ln_g': f(inputs['conf_ln_g']).reshape(1, 512), 'conf_ln_b': f(inputs['conf_ln_b']).reshape(1, 512),
        'ffn_w_up': f(inputs['ffn_w_up']), 'ffn_conv_w': f(inputs['ffn_conv_w']).reshape(2, 9, DFF),
        'ffn_w_down': f(inputs['ffn_w_down']),
    }
    x = f(inputs['x']); c = f(inputs['c']); ctx = f(inputs['ctx'])
    in_maps = []
    for b in range(NCORES):
        m = dict(shared)
        m['x'] = x[b]; m['c'] = c[b:b + 1]; m['ctx'] = ctx[b]
        in_maps.append(m)
    res = run_bass_kernel_spmd(nc, in_maps, core_ids=list(range(NCORES)))
    return np.stack([r['out'] for r in res.results], axis=0)
```

```python
import numpy as np
from contextlib import ExitStack
import concourse.bass as bass
import concourse.mybir as mybir
from concourse.bass_utils import run_bass_kernel_spmd

F32 = mybir.dt.float32
BF16 = mybir.dt.bfloat16
AF = mybir.ActivationFunctionType
ALU = mybir.AluOpType

D = 1024
T = 4096
TC = 256
NCORES = 8
DFF = 2816
ALPHA = 4 ** 0.25
LN_EPS = 1e-5
RMS_EPS = 1e-6
BIG = 30000.0
ENG = ('pe', 'act', 'dve', 'pool', 'sp')
NDS = 24
import os as _os
SELF_SYNC = ('pool',) if _os.environ.get('RELAX') == '1' else ('act', 'dve', 'pool')

DEBUG_OUT = []


class Res:
    __slots__ = ('w', 'r', 'excl')

    def __init__(self):
        self.w = None
        self.r = []
        self.excl = False


class TileT:
    def __init__(self, t):
        self.t = t
        self.res = Res()

    def __getitem__(self, k):
        return self.t[k]


class Sched:
    def __init__(self, nc, stack):
        self.nc = nc
        self.sem = {e: stack.enter_context(nc.semaphore('s_' + e)) for e in ENG}
        self.cnt = {e: 0 for e in ENG}
        self.known = {e: {} for e in ENG}
        self.q = {e: [] for e in ENG}
        self.dsem = [stack.enter_context(nc.semaphore('dq%d' % i)) for i in range(NDS)]
        self.dcnt = [0] * NDS
        self.dpool = {'sp': list(range(0, 12)), 'act': list(range(12, 20)), 'pool': list(range(20, 24))}
        self.dnext = {'sp': 0, 'act': 0, 'pool': 0}
        self.unsig = {e: False for e in ENG}

    def semof(self, key):
        return self.sem[key] if isinstance(key, str) else self.dsem[key]

    def _collect(self, eng, reads, writes):
        toks = []
        for r in reads:
            if r.w is not None:
                toks.append(r.w)
        for w in writes:
            if w.w is not None and (w.w[0] != eng or eng in SELF_SYNC):
                toks.append(w.w)
            for t in w.r:
                if t[0] != eng or eng in SELF_SYNC:
                    toks.append(t)
        waits = {}
        kn = self.known[eng]
        for key, val in toks:
            if kn.get(key, 0) < val:
                waits[key] = max(waits.get(key, 0), val)
        for key, val in waits.items():
            kn[key] = val
        return list(waits.items())

    def _update(self, tok, reads, writes):
        for r in reads:
            r.r.append(tok)
        for w in writes:
            w.w = tok
            w.r = []

    def emit(self, eng, fn, reads=(), writes=(), sig=True):
        reads = [getattr(x, "res", x) for x in reads]
        writes = [getattr(x, "res", x) for x in writes]
        writes = writes + [r for r in reads if r.excl and eng != 'pe']
        waits = self._collect(eng, reads, writes)
        if sig:
            self.cnt[eng] += 1
            tok = (eng, self.cnt[eng])
            self.unsig[eng] = False
        else:
            tok = (eng, self.cnt[eng] + 1)
            self.unsig[eng] = True
        self.q[eng].append((waits, fn, sig, None))
        self._update(tok, reads, writes)

    def pe(self, fn, reads=(), writes=(), sig=True):
        self.emit('pe', fn, reads, writes, sig)

    def act(self, fn, reads=(), writes=()):
        self.emit('act', fn, reads, writes)

    def dve(self, fn, reads=(), writes=()):
        self.emit('dve', fn, reads, writes)

    def pool(self, fn, reads=(), writes=()):
        self.emit('pool', fn, reads, writes)

    def dma(self, out, in_, reads=(), writes=(), q='sp', **kw):
        reads = [getattr(x, "res", x) for x in reads]
        writes = [getattr(x, "res", x) for x in writes]
        pl = self.dpool[q]
        j = pl[self.dnext[q] % len(pl)]
        self.dnext[q] += 1
        waits = dict(self._collect(q, reads, writes))
        if self.dcnt[j] > 0 and self.known[q].get(j, 0) < self.dcnt[j]:
            waits[j] = self.dcnt[j]
            self.known[q][j] = self.dcnt[j]
        self.dcnt[j] += 16
        tok = (j, self.dcnt[j])
        self.q[q].append((list(waits.items()), lambda e: e.dma_start(out=out, in_=in_, **kw), False, j))
        self._update(tok, reads, writes)

    def barrier(self):
        for e in ENG:
            assert not self.unsig[e], e
        for e in ENG:
            waits = []
            for f in ENG:
                if f != e and self.known[e].get(f, 0) < self.cnt[f]:
                    waits.append((f, self.cnt[f]))
                    self.known[e][f] = self.cnt[f]
            for j in range(NDS):
                if self.dcnt[j] > 0 and self.known[e].get(j, 0) < self.dcnt[j]:
                    waits.append((j, self.dcnt[j]))
                    self.known[e][j] = self.dcnt[j]
            if waits:
                self.q[e].append((waits, None, False, None))

    def flush(self):
        nc = self.nc
        q = self.q
        self.q = {e: [] for e in ENG}

        def replay(eng, e):
            for waits, fn, sig, dj in q[eng]:
                for key, val in waits:
                    e.wait_ge(self.semof(key), val)
                if fn is None:
                    continue
                ins = fn(e)
                if dj is not None:
                    ins.then_inc(self.dsem[dj], 16)
                elif sig:
                    ins.then_inc(self.sem[eng], 1)

        with nc.Block() as block:
            @block.tensor
            def _(e):
                replay('pe', e)

            @block.scalar
            def _(e):
                replay('act', e)

            @block.vector
            def _(e):
                replay('dve', e)

            @block.gpsimd
            def _(e):
                replay('pool', e)

            @block.sync
            def _(e):
                replay('sp', e)


class Ctx:
    pass


_UID = [0]


def _uname(name):
    _UID[0] += 1
    return '%s_u%d' % (name, _UID[0])


def sb(nc, stack, name, shape, dt):
    return TileT(stack.enter_context(nc.sbuf_tensor(_uname(name), list(shape), dt)))


def ps(nc, stack, name, shape, dt):
    t = TileT(stack.enter_context(nc.psum_tensor(_uname(name), list(shape), dt)))
    t.res.excl = True
    return t


def build(debug_out=()):
    nc = bass.Bass("TRN2", target_bir_lowering=False)
    top = ExitStack()
    S = Sched(nc, top)
    C = Ctx()
    C.nc, C.S = nc, S
    din = {}

    def inp(name, shape):
        din[name] = TileT(nc.dram_tensor(name, list(shape), F32, kind="ExternalInput").ap())
        return din[name]

    inp('x', [T, D]); inp('c', [1, D]); inp('ctx', [TC, D]); inp('c_ctx', [1, D])
    inp('ada_w', [2, D, 6 * D]); inp('ada_b', [2, 6 * D])
    inp('ln_g', [4, D]); inp('ln_b', [4, D])
    inp('even_w_in', [D, 2576]); inp('even_w_out', [D, D])
    inp('gdn_conv_w', [5, 1536]); inp('gdn_a_log', [1, 8]); inp('gdn_dt_bias', [1, 8])
    inp('gdn_norm_w', [1, 128]); inp('pool_w', [4, 128, 128]); inp('pool_scale', [1, 512])
    inp('odd_w_in', [D, 2560]); inp('odd_w_out', [D, D])
    inp('sconv_w', [3, 512]); inp('conf_conv_w', [31, 512])
    inp('conf_ln_g', [1, 512]); inp('conf_ln_b', [1, 512])
    inp('ffn_w_up', [2, D, 2 * DFF]); inp('ffn_conv_w', [2, 9, DFF]); inp('ffn_w_down', [2, DFF, D])
    C.din = din
    C.out = TileT(nc.dram_tensor('out', [T, D], F32, kind="ExternalOutput").ap())

    def scratch(name, shape, dt):
        kind = "ExternalOutput" if name in debug_out else "Internal"
        t = TileT(nc.dram_tensor(name, list(shape), dt, kind=kind).ap())
        return t
    C.scratch = scratch

    C.ident_f = sb(nc, top, 'ident_f', [128, 128], F32)
    C.ident_b = sb(nc, top, 'ident_b', [128, 128], BF16)
    C.ones_f = sb(nc, top, 'ones_f', [128, 512], F32)
    C.ones_b = sb(nc, top, 'ones_b', [128, 128], BF16)
    C.zeros_b = sb(nc, top, 'zeros_b', [128, 512], BF16)
    C.modrow = sb(nc, top, 'modrow', [128, 6 * D], F32)
    st01 = ExitStack()
    C.modc = sb(nc, st01, 'modc', [128, 2 * D], F32)

    S.pool(lambda e: e.memset(C.ones_f[:], 1.0), writes=[C.ones_f])
    S.pool(lambda e: e.memset(C.ones_b[:], 1.0), writes=[C.ones_b])
    S.pool(lambda e: e.memset(C.zeros_b[:], 0.0), writes=[C.zeros_b])
    S.pool(lambda e: e.affine_select(out=C.ident_f[:], in_=C.ones_f[:, 0:128], pattern=[[-1, 128]],
                                     compare_op=ALU.is_equal, fill=0.0, base=0, channel_multiplier=1),
           reads=[C.ones_f], writes=[C.ident_f])
    S.pool(lambda e: e.tensor_copy(out=C.ident_b[:], in_=C.ident_f[:]), reads=[C.ident_f], writes=[C.ident_b])

    C.debug_out = debug_out
    phase_mod(C, 0)
    dbg_dump(C, 'dbg_mod0', C.modrow, C.modrow[0:1, :], [1, 6 * D], F32)
    dbg_dump(C, 'dbg_modc', C.modc, C.modc[0:1, :], [1, 2 * D], F32)
    phase_inproj0(C)
    st01.close()
    phase_qkv0(C)
    import os
    if os.environ.get('NOGDN') != '1':
        phase_gdn(C)
    else:
        C.OACC = C.scratch('OACC', [T, 512], F32)
        for r0 in range(0, T, 128):
            S.dma(C.OACC[r0:r0 + 128, :], C.ones_f[:, :], reads=[C.ones_f], writes=[C.OACC])
    PSTOP = int(os.environ.get('PSTOP', '99'))
    X1 = C.scratch('X1', [T, D], F32)
    X2 = C.scratch('X2', [T, D], F32)
    X3 = C.scratch('X3', [T, D], F32)
    if PSTOP >= 1:
        phase_mix0_out(C, X1)
    if PSTOP >= 2:
        phase_ffn_up(C, 0, X1)
    if PSTOP >= 3:
        phase_ffn_down(C, 0, X1, X2)
    if PSTOP >= 4:
        phase_mod(C, 1)
        phase_inproj1(C, X2)
    if PSTOP >= 5:
        phase_mix1_out(C, X2, X3)
    if PSTOP >= 6:
        phase_ffn_up(C, 1, X3)
        phase_ffn_down(C, 1, X3, C.out)

    S.barrier()
    S.flush()
    top.close()
    return nc


def dbg_dump(C, name, tile, ap, shape, dt):
    if name not in C.debug_out:
        return
    d = TileT(C.nc.dram_tensor(name, list(shape), dt, kind="ExternalOutput").ap())
    C.S.dma(d[:], ap, reads=[tile], writes=[d])


def to_col(C, st, psb, dram, R, ncol, name):
    nc, S = C.nc, C.S
    BL = 4
    tmp = sb(nc, st, name + '_row', [R, BL * 128], F32)
    outt = sb(nc, st, name + '_col', [128, ncol, R], F32)
    for c0 in range(0, ncol, BL):
        c1 = min(ncol, c0 + BL)
        S.dma(tmp[:, 0:(c1 - c0) * 128], dram[:, c0 * 128:c1 * 128], writes=[tmp])
        for c in range(c0, c1):
            S.pe(lambda e, c=c, c0=c0: e.transpose(out=psb[:, 0:R], in_=tmp[0:R, (c - c0) * 128:(c - c0 + 1) * 128],
                                                   identity=C.ident_f[0:R, 0:R]),
                 reads=[tmp, C.ident_f], writes=[psb])
            S.dve(lambda e, c=c: e.tensor_copy(out=outt[:, c, :], in_=psb[:, 0:R]), reads=[psb], writes=[outt])
    return outt


def phase_mod(C, layer):
    nc, S = C.nc, C.S
    st = ExitStack()
    pst = ps(nc, st, 'pm_t', [128, 512], F32)
    pacc = [ps(nc, st, 'pm_a%d' % i, [128, 512], F32) for i in range(2)]
    paccc = [ps(nc, st, 'pm_c%d' % i, [128, 512], F32) for i in range(2)]
    ccol = to_col(C, st, pst, C.din['c'][:, :], 1, 8, 'c')
    S.act(lambda e: e.activation(out=ccol[:], in_=ccol[:], func=AF.Silu), reads=[ccol], writes=[ccol])
    rep = sb(nc, st, 'c_rep', [128, 8, 128], F32)
    for k in range(8):
        S.dve(lambda e, k=k: e.tensor_scalar(out=rep[:, k, :], in0=C.ones_f[:, 0:128], scalar1=ccol[:, k, 0:1],
                                             scalar2=None, op0=ALU.mult), reads=[C.ones_f, ccol], writes=[rep])
    brow = sb(nc, st, 'adab_row', [1, 6 * D], F32)
    S.dma(brow[:], C.din['ada_b'][layer:layer + 1, :], writes=[brow])
    if layer == 0:
        cccol = to_col(C, st, pst, C.din['c_ctx'][:, :], 1, 8, 'cc')
        S.act(lambda e: e.activation(out=cccol[:], in_=cccol[:], func=AF.Silu), reads=[cccol], writes=[cccol])
        repc = sb(nc, st, 'cc_rep', [128, 8, 128], F32)
        for k in range(8):
            S.dve(lambda e, k=k: e.tensor_scalar(out=repc[:, k, :], in0=C.ones_f[:, 0:128],
                                                 scalar1=cccol[:, k, 0:1], scalar2=None, op0=ALU.mult),
                  reads=[C.ones_f, cccol], writes=[repc])
    wt = [sb(nc, st, 'adaw%d' % i, [128, 8, 512], F32) for i in range(2)]
    aw = C.din['ada_w']
    for n in range(12):
        w = wt[n % 2]
        S.dma(w[:], aw[layer, :, n * 512:(n + 1) * 512].rearrange("(k p) n -> p k n", p=128), writes=[w])
        pa = pacc[n % 2]
        for k in range(8):
            S.pe(lambda e, k=k, w=w, pa=pa: e.matmul(pa[:], lhsT=rep[:, k, :], rhs=w[:, k, :], start=(k == 0), stop=False),
                 reads=[rep, w], writes=[pa], sig=False)
        S.pe(lambda e, pa=pa, n=n: e.matmul(pa[:], lhsT=C.ones_f[0:1, 0:128], rhs=brow[0:1, n * 512:(n + 1) * 512],
                                            start=False, stop=True), reads=[C.ones_f, brow], writes=[pa])
        S.act(lambda e, pa=pa, n=n: e.activation(out=C.modrow[:, n * 512:(n + 1) * 512], in_=pa[:], func=AF.Copy),
              reads=[pa], writes=[C.modrow])
        if layer == 0 and n < 4:
            pc = paccc[n % 2]
            for k in range(8):
                S.pe(lambda e, k=k, w=w, pc=pc: e.matmul(pc[:], lhsT=repc[:, k, :], rhs=w[:, k, :], start=(k == 0), stop=False),
                     reads=[repc, w], writes=[pc], sig=False)
            S.pe(lambda e, pc=pc, n=n: e.matmul(pc[:], lhsT=C.ones_f[0:1, 0:128], rhs=brow[0:1, n * 512:(n + 1) * 512],
                                                start=False, stop=True), reads=[C.ones_f, brow], writes=[pc])
            S.dve(lambda e, pc=pc, n=n: e.tensor_copy(out=C.modc[:, n * 512:(n + 1) * 512], in_=pc[:]),
                  reads=[pc], writes=[C.modc])
    S.dve(lambda e: e.tensor_scalar_add(out=C.modrow[:, D:2 * D], in0=C.modrow[:, D:2 * D], scalar1=1.0),
          reads=[C.modrow], writes=[C.modrow])
    S.dve(lambda e: e.tensor_scalar_add(out=C.modrow[:, 4 * D:5 * D], in0=C.modrow[:, 4 * D:5 * D], scalar1=1.0),
          reads=[C.modrow], writes=[C.modrow])
    if layer == 0:
        S.dve(lambda e: e.tensor_scalar_add(out=C.modc[:, D:2 * D], in0=C.modc[:, D:2 * D], scalar1=1.0),
              reads=[C.modc], writes=[C.modc])
    S.barrier()
    S.flush()
    st.close()


def modulate_transpose(C, xt, nsub, shift, scale1, ub, uT, pT, evac_i):
    S = C.S
    for s in range(nsub):
        S.pool(lambda e, s=s: e.tensor_tensor(out=xt[:, s, :], in0=xt[:, s, :], in1=scale1, op=ALU.mult),
               reads=[xt, C.modrow, C.modc], writes=[xt])
        S.dve(lambda e, s=s: e.tensor_tensor(out=ub[:, s, :], in0=xt[:, s, :], in1=shift, op=ALU.add),
              reads=[xt, C.modrow, C.modc], writes=[ub])
    for k in range(8):
        p = pT[k % len(pT)]
        for s in range(nsub):
            S.pe(lambda e, s=s, k=k, p=p: e.transpose(out=p[:, s * 128:(s + 1) * 128], in_=ub[:, s, k * 128:(k + 1) * 128],
                                                      identity=C.ident_b[:]),
                 reads=[ub, C.ident_b], writes=[p], sig=(s == nsub - 1))
        if (k + evac_i) % 2 == 0:
            S.act(lambda e, k=k, p=p: e.activation(out=uT[:, k, 0:nsub * 128], in_=p[:, 0:nsub * 128], func=AF.Copy),
                  reads=[p], writes=[uT])
        else:
            S.dve(lambda e, k=k, p=p: e.tensor_copy(out=uT[:, k, 0:nsub * 128], in_=p[:, 0:nsub * 128]),
                  reads=[p], writes=[uT])


def load_w_bf16(C, st, name, dram, kchunks, ncols):
    nc, S = C.nc, C.S
    w = sb(nc, st, name, [128, kchunks, ncols], BF16)
    SW = min(2048, ncols)
    wres = [Res(), Res()]
    NSTG = 3 if ncols > 1024 else 2
    stg = [sb(nc, st, name + '_stg%d' % i, [128, SW], F32) for i in range(NSTG)]
    i = 0
    for k in range(kchunks):
        for c0 in range(0, ncols, SW):
            c1 = min(ncols, c0 + SW)
            b = stg[i % NSTG]
            S.dma(b[:, 0:c1 - c0], dram[k * 128:(k + 1) * 128, c0:c1], writes=[b], q='sp')
            wr = wres[i % 2]
            if i % 2 == 0:
                S.dve(lambda e, b=b, k=k, c0=c0, c1=c1: e.tensor_copy(out=w[:, k, c0:c1], in_=b[:, 0:c1 - c0]), reads=[b], writes=[wr])
            else:
                S.act(lambda e, b=b, k=k, c0=c0, c1=c1: e.activation(out=w[:, k, c0:c1], in_=b[:, 0:c1 - c0], func=AF.Copy), reads=[b], writes=[wr])
            i += 1
    S.dve(lambda e: e.tensor_copy(out=w[:, 0, 0:1], in_=w[:, 0, 0:1]), reads=wres, writes=[w] + wres)
    return w


def phase_inproj0(C):
    nc, S = C.nc, C.S
    st = ExitStack()
    P0T = C.scratch('P0T', [2048, T + 16], BF16); C.P0T = P0T
    G0 = C.scratch('G0', [T, 512], BF16); C.G0 = G0
    SG = C.scratch('SG', [T + TC, 16], F32); C.SG = SG
    PCT = C.scratch('PCT', [1024, TC + 16], BF16); C.PCT = PCT
    w = load_w_bf16(C, st, 'w_in0', C.din['even_w_in'][:, :], 8, 2576)
    for r0 in range(0, 2048, 128):
        S.dma(P0T[r0:r0 + 128, 0:8], C.zeros_b[:, 0:8], reads=[C.zeros_b], writes=[P0T])
        S.dma(P0T[r0:r0 + 128, T + 8:T + 16], C.zeros_b[:, 0:8], reads=[C.zeros_b], writes=[P0T])
    for r0 in range(0, 1024, 128):
        S.dma(PCT[r0:r0 + 128, 0:8], C.zeros_b[:, 0:8], reads=[C.zeros_b], writes=[PCT])
        S.dma(PCT[r0:r0 + 128, TC + 8:TC + 16], C.zeros_b[:, 0:8], reads=[C.zeros_b], writes=[PCT])
    xt = [sb(nc, st, 'xt%d' % i, [128, 4, D], F32) for i in range(2)]
    ub = [sb(nc, st, 'ub%d' % i, [128, 4, D], BF16) for i in range(2)]
    uT = [sb(nc, st, 'uT%d' % i, [128, 8, 512], BF16) for i in range(2)]
    pstg = [sb(nc, st, 'pstg%d' % i, [128, 4, 512], BF16) for i in range(2)]
    gstg = [sb(nc, st, 'gstg%d' % i, [128, 4, 512], BF16) for i in range(2)]
    sstg = [sb(nc, st, 'sstg%d' % i, [128, 4, 16], F32) for i in range(2)]
    pT = [ps(nc, st, 'pT%d' % i, [128, 512], BF16) for i in range(2)]
    pm = [ps(nc, st, 'pm%d' % i, [128, 512], F32) for i in range(4)]
    pss = ps(nc, st, 'pss', [128, 4, 16], F32)
    x = C.din['x']
    tiles = [('ctx', 0)] + [('lat', i) for i in range(8)]

    def load(i):
        kind, t = tiles[i]
        b = xt[i % 2]
        if kind == 'ctx':
            S.dma(b[:, 0:2, :], C.din['ctx'][:, :].rearrange("(s p) d -> p s d", p=128), writes=[b])
        else:
            S.dma(b[:, :, :], x[t * 512:(t + 1) * 512, :].rearrange("(s p) d -> p s d", p=128), writes=[b])

    load(0)
    pmi = 0
    for i, (kind, t) in enumerate(tiles):
        if i + 1 < len(tiles):
            load(i + 1)
        b, u, ut = xt[i % 2], ub[i % 2], uT[i % 2]
        isctx = kind == 'ctx'
        nsub = 2 if isctx else 4
        ntok = nsub * 128
        if isctx:
            modulate_transpose(C, b, nsub, C.modc[:, 0:D], C.modc[:, D:2 * D], u, ut, pT, i)
            fchunks = list(range(4, 12))
        else:
            modulate_transpose(C, b, nsub, C.modrow[:, 0:D], C.modrow[:, D:2 * D], u, ut, pT, i)
            fchunks = list(range(0, 12)) + list(range(16, 20))
        for gi in range(0, len(fchunks), 4):
            grp = fchunks[gi:gi + 4]
            stg = pstg[(gi // 4) % 2]
            for j, fc in enumerate(grp):
                p = pm[pmi % 4]; pmi += 1
                for k in range(8):
                    S.pe(lambda e, p=p, k=k, fc=fc, ut=ut, ntok=ntok: e.matmul(
                        p[:, 0:ntok], lhsT=w[:, k, fc * 128:(fc + 1) * 128], rhs=ut[:, k, 0:ntok],
                        start=(k == 0), stop=(k == 7)), reads=[w, ut], writes=[p], sig=(k == 7))
                if j % 2 == 0:
                    S.act(lambda e, p=p, j=j, stg=stg, ntok=ntok: e.activation(out=stg[:, j, 0:ntok], in_=p[:, 0:ntok], func=AF.Copy),
                          reads=[p], writes=[stg])
                else:
                    S.dve(lambda e, p=p, j=j, stg=stg, ntok=ntok: e.tensor_copy(out=stg[:, j, 0:ntok], in_=p[:, 0:ntok]),
                          reads=[p], writes=[stg])
            if isctx:
                r0 = (grp[0] - 4) * 128
                dst = PCT[r0:r0 + 512, 8:8 + ntok].rearrange("(j p) n -> p j n", p=128)
                S.dma(dst, stg[:, :, 0:ntok], reads=[stg], writes=[PCT], q='act')
            else:
                fc0 = grp[0]
                r0 = fc0 * 128 if fc0 < 12 else (fc0 - 4) * 128
                dst = P0T[r0:r0 + 512, 8 + t * 512:8 + (t + 1) * 512].rearrange("(j p) n -> p j n", p=128)
                S.dma(dst, stg[:, :, :], reads=[stg], writes=[P0T], q='act')
        gs = gstg[i % 2]
        ss = sstg[i % 2]
        for s in range(nsub):
            if not isctx:
                p = pm[pmi % 4]; pmi += 1
                for k in range(8):
                    S.pe(lambda e, p=p, k=k, s=s, ut=ut: e.matmul(p[:], lhsT=ut[:, k, s * 128:(s + 1) * 128], rhs=w[:, k, 1536:2048],
                                                            start=(k == 0), stop=(k == 7)), reads=[w, ut], writes=[p], sig=(k == 7))
                S.act(lambda e, p=p, s=s, gs=gs: e.activation(out=gs[:, s, :], in_=p[:], func=AF.Silu), reads=[p], writes=[gs])
            for k in range(8):
                S.pe(lambda e, k=k, s=s, ut=ut: e.matmul(pss[:, s, :], lhsT=ut[:, k, s * 128:(s + 1) * 128], rhs=w[:, k, 2560:2576],
                                                       start=(k == 0), stop=(k == 7)), reads=[w, ut], writes=[pss], sig=(k == 7))
        S.dve(lambda e, ss=ss, nsub=nsub: e.tensor_copy(out=ss[:, 0:nsub, :], in_=pss[:, 0:nsub, :]), reads=[pss], writes=[ss])
        if isctx:
            S.dma(SG[T:T + TC, :].rearrange("(s p) c -> p s c", p=128), ss[:, 0:2, :], reads=[ss], writes=[SG], q='act')
        else:
            S.dma(SG[t * 512:(t + 1) * 512, :].rearrange("(s p) c -> p s c", p=128), ss[:, :, :], reads=[ss], writes=[SG], q='act')
            S.dma(G0[t * 512:(t + 1) * 512, :].rearrange("(s p) c -> p s c", p=128), gs[:, :, :], reads=[gs], writes=[G0], q='act')
    S.barrier()
    S.flush()
    st.close()


def build_diag(C, st, psb, dram, R, ncol, name):
    nc, S = C.nc, C.S
    cw = to_col(C, st, psb, dram, R, ncol, name)
    dg = sb(nc, st, name + '_dg', [128, ncol, R, 128], BF16)
    dgres = [Res() for _ in range(ncol)]
    dg.chunk_res = dgres
    i = 0
    for c in range(ncol):
        for r in range(R):
            dres = dgres[c]
            if i % 2 == 0:
                S.dve(lambda e, c=c, r=r: e.tensor_scalar(out=dg[:, c, r, :], in0=C.ident_b[:], scalar1=cw[:, c, r:r + 1],
                                                          scalar2=None, op0=ALU.mult), reads=[C.ident_b, cw], writes=[dres])
            else:
                S.act(lambda e, c=c, r=r: e.activation(out=dg[:, c, r, :], in_=C.ident_b[:], func=AF.Copy, scale=cw[:, c, r:r + 1]),
                      reads=[C.ident_b, cw], writes=[dres])
            i += 1
    S.dve(lambda e: e.tensor_copy(out=dg[:, 0, 0, 0:1], in_=dg[:, 0, 0, 0:1]), reads=dgres, writes=[dg] + dgres)
    return dg


def phase_qkv0(C):
    nc, S = C.nc, C.S
    st = ExitStack()
    QT = C.scratch('QT', [512, T], BF16); C.QT = QT
    KT = C.scratch('KT', [512, T + TC], BF16); C.KT = KT
    QTOK = C.scratch('QTOK', [T, 512], BF16); C.QTOK = QTOK
    KTOK = C.scratch('KTOK', [T + TC, 512], BF16); C.KTOK = KTOK
    VTOK = C.scratch('VTOK', [T + TC, 512], BF16); C.VTOK = VTOK
    YPT = C.scratch('YPT', [512, T], BF16); C.YPT = YPT
    pconv = [ps(nc, st, 'pconv%d' % i, [128, 512], F32) for i in range(2)]
    pssq = [ps(nc, st, 'pssq%d' % i, [128, 512], F32) for i in range(2)]
    pT = [ps(nc, st, 'pTq%d' % i, [128, 512], BF16) for i in range(2)]
    ppool = ps(nc, st, 'ppool', [128, 512], F32)
    dg = build_diag(C, st, pconv[0], C.din['gdn_conv_w'][:, :], 5, 12, 'cw5')
    pscale = to_col(C, st, pconv[1], C.din['pool_scale'][:, :], 1, 4, 'pscale')
    poolw = sb(nc, st, 'poolw', [128, 4, 128], BF16)
    S.dma(poolw[:], C.din['pool_w'][:, :, :].rearrange("g c d -> c g d"), writes=[poolw], q='pool')
    corrF = sb(nc, st, 'corrF', [128, 4, 8], F32)
    corrL = sb(nc, st, 'corrL', [128, 4, 8], F32)
    S.pool(lambda e: e.memset(corrF[:], 1.0), writes=[corrF])
    S.pool(lambda e: e.memset(corrL[:], 1.0), writes=[corrL])
    for g in range(4):
        hw = 1 << g
        for j in range(hw):
            S.pool(lambda e, g=g, j=j, hw=hw: e.memset(corrF[:, g, j:j + 1], 2.0 * hw / (j + hw)), writes=[corrF])
        for m in range(hw - 1):
            S.pool(lambda e, g=g, m=m, hw=hw: e.memset(corrL[:, g, 7 - m:8 - m], 2.0 * hw / (1 + m + hw)), writes=[corrL])
    pin = [sb(nc, st, 'pin%d' % i, [128, 12, 516], BF16) for i in range(2)]
    pp = [sb(nc, st, 'pp%d' % i, [128, 4, 528], BF16) for i in range(2)]
    xs8 = sb(nc, st, 'xs8', [128, 8, 512], F32)
    ss8 = sb(nc, st, 'ss8', [128, 8, 512], F32)
    epsb = sb(nc, st, 'epsb', [128, 1], F32)
    S.pool(lambda e: e.memset(epsb[:], RMS_EPS), writes=[epsb])
    sqb = [sb(nc, st, 'sqb%d' % i, [128, 512], BF16) for i in range(2)]
    qkn = [sb(nc, st, 'qkn0', [128, 12, 512], BF16)] * 2
    tokst = [[sb(nc, st, 'tok%d_%d' % (g, i), [128, 4, 512], BF16) for i in range(2)] for g in range(3)]
    wa = [sb(nc, st, 'wa%d' % i, [128, 528], F32) for i in range(2)]
    wb = [sb(nc, st, 'wb%d' % i, [128, 528], F32) for i in range(2)]
    pld = [sb(nc, st, 'pld%d' % i, [128, 4, 512], BF16) for i in range(2)]
    ypst = [sb(nc, st, 'ypst%d' % i, [128, 4, 512], BF16) for i in range(2)]
    tiles = [('ctx', 0)] + [('lat', i) for i in range(8)]

    def load(i):
        kind, t = tiles[i]
        b = pin[i % 2]
        if kind == 'ctx':
            S.dma(b[:, 4:12, 0:260], C.PCT[:, 6:266].rearrange("(f p) n -> p f n", p=128), reads=[C.PCT], writes=[b])
        else:
            S.dma(b[:, :, :], C.P0T[0:1536, 6 + t * 512:6 + t * 512 + 516].rearrange("(f p) n -> p f n", p=128),
                  reads=[C.P0T], writes=[b])
            S.dma(pp[i % 2][:, :, :], C.P0T[1536:2048, t * 512:t * 512 + 528].rearrange("(f p) n -> p f n", p=128),
                  reads=[C.P0T], writes=[pp[i % 2]])

    load(0)
    ci = 0
    for i, (kind, t) in enumerate(tiles):
        if i + 1 < len(tiles):
            load(i + 1)
        isctx = kind == 'ctx'
        ntok = 256 if isctx else 512
        nsub = ntok // 128
        b = pin[i % 2]
        qk = qkn[i % 2]
        for fc in (range(4, 12) if isctx else range(12)):
            pc = pconv[ci % 2]
            sq_ = sqb[ci % 2]; pq = pssq[ci % 2]
            ci += 1
            for tap in range(5):
                S.pe(lambda e, pc=pc, fc=fc, tap=tap, b=b, ntok=ntok: e.matmul(
                    pc[:, 0:ntok], lhsT=dg[:, fc, tap, :], rhs=b[:, fc, tap:tap + ntok], start=(tap == 0), stop=(tap == 4)),
                    reads=[dg, b], writes=[pc], sig=(tap == 4))
            if fc >= 8:
                S.act(lambda e, pc=pc, fc=fc, qk=qk, ntok=ntok: e.activation(out=qk[:, fc, 0:ntok], in_=pc[:, 0:ntok], func=AF.Silu),
                      reads=[pc], writes=[qk])
                continue
            S.act(lambda e, pc=pc, fc=fc, ntok=ntok: e.activation(out=xs8[:, fc, 0:ntok], in_=pc[:, 0:ntok], func=AF.Silu),
                  reads=[pc], writes=[xs8])
            S.pool(lambda e, fc=fc, sq_=sq_, ntok=ntok: e.tensor_tensor(out=sq_[:, 0:ntok], in0=xs8[:, fc, 0:ntok], in1=xs8[:, fc, 0:ntok], op=ALU.mult),
                   reads=[xs8], writes=[sq_])
            S.pe(lambda e, pq=pq, sq_=sq_, ntok=ntok: e.matmul(pq[:, 0:ntok], lhsT=C.ones_b[:], rhs=sq_[:, 0:ntok], start=True, stop=True),
                 reads=[C.ones_b, sq_], writes=[pq])
            S.dve(lambda e, pq=pq, fc=fc, ntok=ntok: e.tensor_copy(out=ss8[:, fc, 0:ntok], in_=pq[:, 0:ntok]), reads=[pq], writes=[ss8])
        f0 = 4 if isctx else 0
        S.act(lambda e, f0=f0, ntok=ntok: e.activation(out=ss8[:, f0:8, 0:ntok], in_=ss8[:, f0:8, 0:ntok], func=AF.Ln, bias=epsb[:, 0:1]),
              reads=[ss8, epsb], writes=[ss8])
        S.act(lambda e, f0=f0, ntok=ntok: e.activation(out=ss8[:, f0:8, 0:ntok], in_=ss8[:, f0:8, 0:ntok], func=AF.Exp, scale=-0.5),
              reads=[ss8], writes=[ss8])
        for fc in range(f0, 8):
            sc = (128.0 ** -0.5) if fc < 4 else 1.0
            fn = lambda e, qk=qk, fc=fc, sc=sc, ntok=ntok: e.scalar_tensor_tensor(
                out=qk[:, fc, 0:ntok], in0=xs8[:, fc, 0:ntok], scalar=sc, in1=ss8[:, fc, 0:ntok], op0=ALU.mult, op1=ALU.mult)
            if fc < 4:
                S.dve(fn, reads=[xs8, ss8], writes=[qk])
            else:
                S.pool(lambda e, qk=qk, fc=fc, ntok=ntok: e.tensor_tensor(out=qk[:, fc, 0:ntok], in0=xs8[:, fc, 0:ntok],
                                                                       in1=ss8[:, fc, 0:ntok], op=ALU.mult),
                       reads=[xs8, ss8], writes=[qk])
        ti = 0
        for g in ((1, 2) if isctx else (0, 1, 2)):
            tk = tokst[g][i % 2]
            for s_ in range(nsub):
                p = pT[ti % 2]; ti += 1
                for h in range(4):
                    S.pe(lambda e, p=p, h=h, g=g, s_=s_, qk=qk: e.transpose(out=p[:, h * 128:(h + 1) * 128],
                                                                       in_=qk[:, g * 4 + h, s_ * 128:(s_ + 1) * 128], identity=C.ident_b[:]),
                         reads=[qk, C.ident_b], writes=[p], sig=(h == 3))
                if ti % 2 == 0:
                    S.act(lambda e, p=p, tk=tk, s_=s_: e.activation(out=tk[:, s_, :], in_=p[:], func=AF.Copy), reads=[p], writes=[tk])
                else:
                    S.dve(lambda e, p=p, tk=tk, s_=s_: e.tensor_copy(out=tk[:, s_, :], in_=p[:]), reads=[p], writes=[tk])
        c0 = T if isctx else t * 512
        if not isctx:
            S.dma(QT[:, c0:c0 + 512].rearrange("(f p) n -> p f n", p=128), qk[:, 0:4, :], reads=[qk], writes=[QT], q='act')
            S.dma(QTOK[c0:c0 + 512, :].rearrange("(s p) c -> p s c", p=128), tokst[0][i % 2][:, :, :], reads=[tokst[0][i % 2]], writes=[QTOK], q='act')
        S.dma(KT[:, c0:c0 + ntok].rearrange("(f p) n -> p f n", p=128), qk[:, 4:8, 0:ntok], reads=[qk], writes=[KT], q='act')
        S.dma(KTOK[c0:c0 + ntok, :].rearrange("(s p) c -> p s c", p=128), tokst[1][i % 2][:, 0:nsub, :], reads=[tokst[1][i % 2]], writes=[KTOK], q='act')
        S.dma(VTOK[c0:c0 + ntok, :].rearrange("(s p) c -> p s c", p=128), tokst[2][i % 2][:, 0:nsub, :], reads=[tokst[2][i % 2]], writes=[VTOK], q='act')
        if isctx:
            continue
        ppb = pp[i % 2]
        pl = pld[i % 2]
        yp = ypst[i % 2]
        for g in range(4):
            a_, b_ = wa[g % 2], wb[g % 2]
            S.pool(lambda e, a_=a_, g=g, ppb=ppb: e.tensor_tensor(out=a_[:, 1:527], in0=ppb[:, g, 0:526], in1=ppb[:, g, 1:527], op=ALU.add),
                   reads=[ppb], writes=[a_])
            cur, oth = a_, b_
            lo, hi = 1, 527
            for lvl in range(g):
                sh = 1 << lvl
                lo, hi = lo + sh, hi - sh
                S.pool(lambda e, cur=cur, oth=oth, lo=lo, hi=hi, sh=sh: e.tensor_tensor(
                    out=oth[:, lo:hi], in0=cur[:, lo - sh:hi - sh], in1=cur[:, lo + sh:hi + sh], op=ALU.add),
                    reads=[cur], writes=[oth])
                cur, oth = oth, cur
            S.dve(lambda e, cur=cur, g=g: e.tensor_scalar(out=cur[:, 8:520], in0=cur[:, 8:520], scalar1=1.0 / (2 << g), scalar2=None, op0=ALU.mult),
                  reads=[cur], writes=[cur])
            if t == 0:
                S.dve(lambda e, cur=cur, g=g: e.tensor_tensor(out=cur[:, 8:16], in0=cur[:, 8:16], in1=corrF[:, g, :], op=ALU.mult),
                      reads=[cur, corrF], writes=[cur])
            if t == 7:
                S.dve(lambda e, cur=cur, g=g: e.tensor_tensor(out=cur[:, 512:520], in0=cur[:, 512:520], in1=corrL[:, g, :], op=ALU.mult),
                      reads=[cur, corrL], writes=[cur])
            S.dve(lambda e, cur=cur, g=g, pl=pl, ppb=ppb: e.tensor_tensor(out=pl[:, g, :], in0=cur[:, 8:520], in1=ppb[:, g, 8:520], op=ALU.subtract),
                  reads=[cur, ppb], writes=[pl])
            S.pe(lambda e, g=g, pl=pl: e.matmul(ppool[:], lhsT=poolw[:, g, :], rhs=pl[:, g, :], start=True, stop=True),
                 reads=[poolw, pl], writes=[ppool])
            S.act(lambda e, g=g, yp=yp: e.activation(out=yp[:, g, :], in_=ppool[:], func=AF.Identity, scale=pscale[:, g, 0:1]),
                  reads=[ppool, pscale], writes=[yp])
        S.dma(YPT[:, c0:c0 + 512].rearrange("(f p) n -> p f n", p=128), yp[:, :, :], reads=[yp], writes=[YPT], q='act')
    S.barrier()
    S.flush()
    st.close()


class Slot:
    def __init__(self, bank, k):
        self.f = bank.t[:, k * 128:(k + 1) * 128]
        self.b = bank.t[:, :].bitcast(BF16)[:, k * 256:k * 256 + 128]
        self.res = bank.res


def run_interleaved(gens):
    gens = list(gens)
    while gens:
        nxt = []
        for g in gens:
            try:
                next(g)
                nxt.append(g)
            except StopIteration:
                pass
        gens = nxt


def phase_gdn(C):
    nc, S = C.nc, C.S
    st = ExitStack()
    NT = 34
    import os
    OACC = C.scratch('OACC', [T, 512], F32); C.OACC = OACC
    banks = [ps(nc, st, 'gbank%d' % i, [128, 512], F32) for i in range(8)]
    for b_ in banks:
        b_.res.excl = True
    slots = [[Slot(banks[c], k) for k in range(4)] for c in range(8)]
    sall = sb(nc, st, 'sall', [128, NT, 16], F32)
    for n0 in ([] if os.environ.get('NOSALL') == '1' else range(0, NT, 6)):
        n1 = min(NT, n0 + 6)
        S.dma(sall[:, n0:n1, :], C.SG[n0 * 128:n1 * 128, :].rearrange("(n p) c -> p n c", p=128), reads=[C.SG], writes=[sall])
    adb = sb(nc, st, 'adb', [128, 16], F32)
    S.dma(adb[:, 0:8], C.din['gdn_a_log'][0:1, :].to_broadcast([128, 8]), writes=[adb])
    S.dma(adb[:, 8:16], C.din['gdn_dt_bias'][0:1, :].to_broadcast([128, 8]), writes=[adb])
    S.act(lambda e: e.activation(out=adb[:, 0:8], in_=adb[:, 0:8], func=AF.Exp), reads=[adb], writes=[adb])
    S.dve(lambda e: e.tensor_scalar(out=adb[:, 0:8], in0=adb[:, 0:8], scalar1=-1.0, scalar2=None, op0=ALU.mult),
          reads=[adb], writes=[adb])

    GCUT = int(os.environ.get('GCUT', '0'))

    def fin():
        S.barrier(); S.flush(); st.close()
    if GCUT == 1:
        return fin()

    def gt(name):
        return sb(nc, st, name, [128, NT, 8], F32)
    beta, g_, gc, eg, be, kds, gl = gt('g_beta'), gt('g_g'), gt('g_gc'), gt('g_eg'), gt('g_be'), gt('g_kds'), gt('g_gl')
    S.act(lambda e: e.activation(out=beta[:], in_=sall[:, :, 0:8], func=AF.Sigmoid), reads=[sall], writes=[beta])
    S.dve(lambda e: e.tensor_tensor(out=g_[:], in0=sall[:, :, 8:16], in1=adb[:, 8:16].unsqueeze(1).to_broadcast([128, NT, 8]), op=ALU.add),
          reads=[sall, adb], writes=[g_])
    S.act(lambda e: e.activation(out=g_[:], in_=g_[:], func=AF.Exp), reads=[g_], writes=[g_])
    S.act(lambda e: e.activation(out=g_[:], in_=g_[:], func=AF.Ln, bias=1.0), reads=[g_], writes=[g_])
    S.dve(lambda e: e.tensor_tensor(out=g_[:], in0=g_[:], in1=adb[:, 0:8].unsqueeze(1).to_broadcast([128, NT, 8]), op=ALU.mult),
          reads=[g_, adb], writes=[g_])
    if GCUT == 2:
        return fin()
    Lt = sb(nc, st, 'Lt', [128, 128], F32)
    Ut = sb(nc, st, 'Ut', [128, 128], F32)
    bigm = [sb(nc, st, 'bigm%d' % i, [128, 128], F32) for i in range(2)]
    strict = [sb(nc, st, 'strict%d' % i, [128, 128], F32) for i in range(2)]
    bigfull = sb(nc, st, 'bigfull', [128, 128], F32)
    S.pool(lambda e: e.memset(bigfull[:], BIG), writes=[bigfull])
    one = C.ones_f[:, 0:128]
    S.pool(lambda e: e.affine_select(out=Lt[:], in_=one, pattern=[[1, 128]], compare_op=ALU.is_ge, fill=0.0, base=0, channel_multiplier=-1),
           reads=[C.ones_f], writes=[Lt])
    S.pool(lambda e: e.affine_select(out=Ut[:], in_=one, pattern=[[-1, 128]], compare_op=ALU.is_ge, fill=0.0, base=0, channel_multiplier=1),
           reads=[C.ones_f], writes=[Ut])
    S.pool(lambda e: e.affine_select(out=bigm[0][:], in_=bigfull[:], pattern=[[1, 128]], compare_op=ALU.is_gt, fill=0.0, base=0, channel_multiplier=-1),
           reads=[bigfull], writes=[bigm[0]])
    S.pool(lambda e: e.affine_select(out=bigm[1][:], in_=bigfull[:], pattern=[[-1, 128]], compare_op=ALU.is_gt, fill=0.0, base=0, channel_multiplier=1),
           reads=[bigfull], writes=[bigm[1]])
    S.pool(lambda e: e.affine_select(out=strict[0][:], in_=one, pattern=[[-1, 128]], compare_op=ALU.is_gt, fill=0.0, base=0, channel_multiplier=1),
           reads=[C.ones_f], writes=[strict[0]])
    S.pool(lambda e: e.affine_select(out=strict[1][:], in_=one, pattern=[[1, 128]], compare_op=ALU.is_gt, fill=0.0, base=0, channel_multiplier=-1),
           reads=[C.ones_f], writes=[strict[1]])
    if GCUT == 3:
        return fin()
    Bm = {}
    for s_ in (16, 32, 64):
        G = 128 // s_
        E = sb(nc, st, 'E%d' % s_, [G, 128], F32)
        S.pool(lambda e, E=E, G=G, s_=s_: e.affine_select(out=E[:], in_=C.ones_f[0:G, 0:128], pattern=[[1, 128]], compare_op=ALU.is_ge,
                                                         fill=0.0, base=0, channel_multiplier=-s_), reads=[C.ones_f], writes=[E])
        S.pool(lambda e, E=E, G=G, s_=s_: e.affine_select(out=E[:], in_=E[:], pattern=[[-1, 128]], compare_op=ALU.is_gt,
                                                         fill=0.0, base=s_, channel_multiplier=s_), reads=[E], writes=[E])
        pb_ = banks[4]
        S.pe(lambda e, E=E, pb_=pb_: e.matmul(pb_[:, 0:128], lhsT=E[:], rhs=E[:], start=True, stop=True), reads=[E], writes=[pb_])
        Bm[s_] = sb(nc, st, 'Bm%d' % s_, [128, 128], F32)
        S.dve(lambda e, s_=s_, pb_=pb_: e.tensor_copy(out=Bm[s_][:], in_=pb_[:, 0:128]), reads=[pb_], writes=[Bm[s_]])
    Md = [sb(nc, st, 'Md%d' % d, [128, 128], F32) for d in range(2)]
    Mo = [[sb(nc, st, 'Mo%d_%d' % (d, l), [128, 128], F32) for l in range(3)] for d in range(2)]
    for d in range(2):
        S.dve(lambda e, d=d: e.tensor_tensor(out=Md[d][:], in0=strict[d][:], in1=Bm[16][:], op=ALU.mult), reads=[strict[d], Bm[16]], writes=[Md[d]])
        for l, (big_, small_) in enumerate(((32, 16), (64, 32), (None, 64))):
            t_ = Mo[d][l]
            if big_ is None:
                S.dve(lambda e, t_=t_, small_=small_: e.tensor_scalar(out=t_[:], in0=Bm[small_][:], scalar1=-1.0, scalar2=1.0, op0=ALU.mult, op1=ALU.add),
                      reads=[Bm[small_]], writes=[t_])
            else:
                S.dve(lambda e, t_=t_, big_=big_, small_=small_: e.tensor_tensor(out=t_[:], in0=Bm[big_][:], in1=Bm[small_][:], op=ALU.subtract),
                      reads=[Bm[big_], Bm[small_]], writes=[t_])
            S.dve(lambda e, t_=t_, d=d: e.tensor_tensor(out=t_[:], in0=t_[:], in1=strict[d][:], op=ALU.mult), reads=[t_, strict[d]], writes=[t_])
    pgc = banks[1]
    S.pe(lambda e: e.matmul(pgc[:, 0:NT * 8], lhsT=Lt[:], rhs=g_[:, :, :], start=True, stop=True), reads=[Lt, g_], writes=[pgc])
    S.dve(lambda e: e.tensor_copy(out=gc[:, :, 0:4], in_=pgc[:, 0:NT * 8].rearrange("p (n c) -> p n c", c=8)[:, :, 0:4]), reads=[pgc], writes=[gc])
    pgc2 = banks[2]
    S.pe(lambda e: e.matmul(pgc2[:, 0:NT * 8], lhsT=Ut[:], rhs=g_[:, :, :], start=True, stop=True), reads=[Ut, g_], writes=[pgc2])
    S.dve(lambda e: e.tensor_copy(out=gc[:, :, 4:8], in_=pgc2[:, 0:NT * 8].rearrange("p (n c) -> p n c", c=8)[:, :, 4:8]), reads=[pgc2], writes=[gc])
    if GCUT == 5:
        return fin()
    pgt = banks[3]
    S.pe(lambda e: e.matmul(pgt[:, 0:NT * 8], lhsT=C.ones_f[:, 0:128], rhs=g_[:, :, :], start=True, stop=True), reads=[C.ones_f, g_], writes=[pgt])
    S.act(lambda e: e.activation(out=gl[:], in_=pgt[:, 0:NT * 8].rearrange("p (n c) -> p n c", c=8), func=AF.Exp), reads=[pgt], writes=[gl])
    S.dve(lambda e: e.tensor_tensor(out=kds[:], in0=pgt[:, 0:NT * 8].rearrange("p (n c) -> p n c", c=8), in1=gc[:], op=ALU.subtract),
          reads=[pgt, gc], writes=[kds])
    if GCUT == 6:
        return fin()
    S.act(lambda e: e.activation(out=kds[:], in_=kds[:], func=AF.Exp), reads=[kds], writes=[kds])
    S.act(lambda e: e.activation(out=eg[:], in_=gc[:], func=AF.Exp), reads=[gc], writes=[eg])
    S.dve(lambda e: e.tensor_tensor(out=be[:], in0=beta[:], in1=eg[:], op=ALU.mult), reads=[beta, eg], writes=[be])
    if GCUT == 4:
        return fin()
    dbg_dump(C, 'dbg_gc', gc, gc[:, :, :], [128, NT, 8], F32)
    dbg_dump(C, 'dbg_beta', beta, beta[:, :, :], [128, NT, 8], F32)
    dbg_dump(C, 'dbg_g', g_, g_[:, :, :], [128, NT, 8], F32)

    S.barrier()
    import os
    GSTOP = int(os.environ.get('GSTOP', '99'))
    def tile_of(d, n):
        if n < 2:
            return 32 + n if d == 0 else 33 - n
        return n - 2 if d == 0 else 33 - n
    opnd = [[{k: sb(nc, st, 'op_%s_%d_%d' % (k, d, i), [128, 4, 128], BF16) for k in ('kT', 'qT', 'ktok', 'qtok', 'vtok')}
             for i in range(2)] for d in range(2)]

    def load_tile(d, n):
        nt = tile_of(d, n)
        o = opnd[d][n % 2]
        c0 = T + (nt - 32) * 128 if nt >= 32 else nt * 128
        S.dma(o['kT'][:, :, :], C.KT[:, c0:c0 + 128].rearrange("(h p) n -> p h n", p=128), reads=[C.KT], writes=[o['kT']])
        S.dma(o['ktok'][:, :, :], C.KTOK[c0:c0 + 128, :].rearrange("p (h d) -> p h d", d=128), reads=[C.KTOK], writes=[o['ktok']])
        S.dma(o['vtok'][:, :, :], C.VTOK[c0:c0 + 128, :].rearrange("p (h d) -> p h d", d=128), reads=[C.VTOK], writes=[o['vtok']])
        if nt < 32:
            S.dma(o['qT'][:, :, :], C.QT[:, c0:c0 + 128].rearrange("(h p) n -> p h n", p=128), reads=[C.QT], writes=[o['qT']])
            S.dma(o['qtok'][:, :, :], C.QTOK[c0:c0 + 128, :].rearrange("p (h d) -> p h d", d=128), reads=[C.QTOK], writes=[o['qtok']])

    def cb(name, dt, n=1, shape=(128, 128)):
        return [[sb(nc, st, '%s_%d_%d' % (name, c, i), list(shape), dt) for i in range(n)] for c in range(8)]
    dgc = cb('dgc', F32); Dm = dgc; Ai = cb('Ai', F32)
    Pb = cb('Pb', BF16, 2); PTb = cb('PTb', BF16, 2); Yb = cb('Yb', BF16, 2)
    bv = cb('bv', BF16); kbe = cb('kbe', BF16); qe = cb('qe', BF16); AOb = cb('AOb', BF16, 3)
    attnT = cb('attnT', BF16, 2); u_ = cb('u_', F32, 2); wT = cb('wT', BF16, 2); kd = cb('kd', BF16, 2); qdT = cb('qdT', BF16, 2)
    S32 = cb('S32', F32); Sbf = cb('Sbf', BF16, 2); vn = cb('vn', BF16)
    for c in range(8):
        S.pool(lambda e, c=c: e.memset(S32[c][0][:], 0.0), writes=[S32[c][0]])
        S.pool(lambda e, c=c: e.memset(Sbf[c][0][:], 0.0), writes=[Sbf[c][0]])
    oacc = sb(nc, st, 'oacc', [128, 32, 512], F32)
    ores = [[Res() for h in range(4)] for nt in range(32)]
    ofirst = [[True] * 4 for nt in range(32)]

    def precompute(c, n):
        d, h = c // 4, c % 4
        nt = tile_of(d, n)
        lat = nt < 32
        o = opnd[d][n % 2]
        r = n % 2
        sl = slots[c]
        gcol = gc[:, nt, c:c + 1]
        S.act(lambda e: e.activation(out=dgc[c][0][:], in_=C.ident_f[:], func=AF.Copy, scale=gcol),
              reads=[C.ident_f, gc], writes=[dgc[c][0]])
        S.pe(lambda e: e.matmul(sl[1].f, lhsT=o['kT'][:, h, :], rhs=o['kT'][:, h, :], start=True, stop=True),
             reads=[o['kT']], writes=[sl[1]])
        if lat:
            S.pe(lambda e: e.matmul(sl[2].f, lhsT=o['qT'][:, h, :], rhs=o['kT'][:, h, :], start=True, stop=True),
                 reads=[o['kT'], o['qT']], writes=[sl[2]])
        yield
        S.pe(lambda e: e.matmul(sl[0].f, lhsT=C.ones_f[:, 0:128], rhs=dgc[c][0][:], start=True, stop=False),
             reads=[C.ones_f, dgc[c][0]], writes=[sl[0]], sig=False)
        S.pe(lambda e: e.matmul(sl[0].f, lhsT=C.ident_f[:], rhs=bigm[d][:], start=False, stop=True),
             reads=[C.ident_f, bigm[d]], writes=[sl[0]])
        yield
        S.act(lambda e: e.activation(out=Dm[c][0][:], in_=sl[0].f, func=AF.Exp, bias=gcol, scale=-1.0),
              reads=[sl[0], gc], writes=[Dm[c][0]])
        yield
        S.dve(lambda e: e.scalar_tensor_tensor(out=Ai[c][0][:], in0=sl[1].f, scalar=beta[:, nt, c:c + 1], in1=Dm[c][0][:],
                                               op0=ALU.mult, op1=ALU.mult), reads=[sl[1], beta, Dm[c][0]], writes=[Ai[c][0]])
        if lat:
            S.dve(lambda e: e.tensor_tensor(out=qe[c][0][:], in0=sl[2].f, in1=Dm[c][0][:], op=ALU.mult),
                  reads=[sl[2], Dm[c][0]], writes=[qe[c][0]])
        yield
        A = Pb[c][0]
        S.dve(lambda e: e.tensor_tensor(out=A[:], in0=Ai[c][0][:], in1=Md[d][:], op=ALU.mult),
              reads=[Ai[c][0], Md[d]], writes=[A])
        for li in range(3):
            fn = lambda e, li=li: e.tensor_tensor(out=AOb[c][li][:], in0=Ai[c][0][:], in1=Mo[d][li][:], op=ALU.mult)
            if li < 2:
                S.dve(fn, reads=[Ai[c][0], Mo[d][li]], writes=[AOb[c][li]])
            else:
                S.pool(fn, reads=[Ai[c][0], Mo[d][li]], writes=[AOb[c][li]])
        yield
        S.pe(lambda e: e.transpose(out=sl[0].b, in_=A[:], identity=C.ident_b[:]), reads=[A, C.ident_b], writes=[sl[0]])
        if lat:
            S.pe(lambda e: e.transpose(out=sl[1].b, in_=qe[c][0][:], identity=C.ident_b[:]), reads=[qe[c][0], C.ident_b], writes=[sl[1]])
        yield
        AT = PTb[c][0]
        Y = Yb[c][0]
        S.act(lambda e: e.activation(out=AT[:], in_=sl[0].b, func=AF.Copy), reads=[sl[0]], writes=[AT])
        S.dve(lambda e: e.scalar_tensor_tensor(out=Y[:], in0=sl[0].b, scalar=-1.0, in1=C.ident_b[:], op0=ALU.mult, op1=ALU.add),
              reads=[sl[0], C.ident_b], writes=[Y])
        if lat:
            S.act(lambda e: e.activation(out=attnT[c][r][:], in_=sl[1].b, func=AF.Copy), reads=[sl[1]], writes=[attnT[c][r]])
        yield
        S.act(lambda e: e.activation(out=bv[c][0][:], in_=o['vtok'][:, h, :], func=AF.Copy, scale=beta[:, nt, c:c + 1]),
              reads=[o['vtok'], beta], writes=[bv[c][0]])
        S.act(lambda e: e.activation(out=kbe[c][0][:], in_=o['ktok'][:, h, :], func=AF.Copy, scale=be[:, nt, c:c + 1]),
              reads=[o['ktok'], be], writes=[kbe[c][0]])
        S.pool(lambda e: e.tensor_scalar(out=kd[c][r][:], in0=o['ktok'][:, h, :], scalar1=kds[:, nt, c:c + 1], scalar2=None, op0=ALU.mult),
               reads=[o['ktok'], kds], writes=[kd[c][r]])
        if lat:
            S.act(lambda e: e.activation(out=qe[c][0][:], in_=o['qtok'][:, h, :], func=AF.Copy, scale=eg[:, nt, c:c + 1]),
                  reads=[o['qtok'], eg], writes=[qe[c][0]])
        cur = 0
        for lvl in range(1, 4):
            P, PT, Yc = Pb[c][cur], PTb[c][cur], Yb[c][cur]
            Pn, PTn, Yn = Pb[c][1 - cur], PTb[c][1 - cur], Yb[c][1 - cur]
            S.pe(lambda e, P=P, PT=PT: e.matmul(sl[0].f, lhsT=PT[:], rhs=P[:], start=True, stop=True), reads=[P, PT], writes=[sl[0]])
            if lvl < 3:
                S.pe(lambda e, P=P, PT=PT: e.matmul(sl[1].f, lhsT=P[:], rhs=PT[:], start=True, stop=True), reads=[P, PT], writes=[sl[1]])
            yield
            S.act(lambda e, Pn=Pn: e.activation(out=Pn[:], in_=sl[0].f, func=AF.Copy), reads=[sl[0]], writes=[Pn])
            if lvl < 3:
                S.dve(lambda e, PTn=PTn: e.tensor_copy(out=PTn[:], in_=sl[1].f), reads=[sl[1]], writes=[PTn])
            yield
            S.pe(lambda e, Pn=Pn, Yc=Yc: e.matmul(sl[2].f, lhsT=Pn[:], rhs=Yc[:], start=True, stop=True), reads=[Pn, Yc], writes=[sl[2]])
            yield
            S.dve(lambda e, Yc=Yc, Yn=Yn: e.tensor_tensor(out=Yn[:], in0=sl[2].f, in1=Yc[:], op=ALU.add), reads=[sl[2], Yc], writes=[Yn])
            yield
            cur = 1 - cur
        for li in range(3):
            Yc, Yn = Yb[c][cur], Yb[c][1 - cur]
            Tt, N1 = Pb[c][0], PTb[c][0]
            S.pe(lambda e, Yc=Yc: e.transpose(out=sl[0].b, in_=Yc[:], identity=C.ident_b[:]), reads=[Yc, C.ident_b], writes=[sl[0]])
            S.pe(lambda e, Yc=Yc, li=li: e.matmul(sl[1].f, lhsT=AOb[c][li][:], rhs=Yc[:], start=True, stop=True),
                 reads=[AOb[c][li], Yc], writes=[sl[1]])
            yield
            S.act(lambda e, Tt=Tt: e.activation(out=Tt[:], in_=sl[0].b, func=AF.Copy), reads=[sl[0]], writes=[Tt])
            S.dve(lambda e, N1=N1: e.tensor_copy(out=N1[:], in_=sl[1].f), reads=[sl[1]], writes=[N1])
            yield
            S.pe(lambda e, Tt=Tt, N1=N1: e.matmul(sl[2].f, lhsT=Tt[:], rhs=N1[:], start=True, stop=True), reads=[Tt, N1], writes=[sl[2]])
            yield
            S.dve(lambda e, Yc=Yc, Yn=Yn: e.scalar_tensor_tensor(out=Yn[:], in0=sl[2].f, scalar=-1.0, in1=Yc[:], op0=ALU.mult, op1=ALU.add),
                  reads=[sl[2], Yc], writes=[Yn])
            yield
            cur = 1 - cur
        Y = Yb[c][cur]
        S.pe(lambda e: e.matmul(sl[0].f, lhsT=Y[:], rhs=bv[c][0][:], start=True, stop=True), reads=[Y, bv[c][0]], writes=[sl[0]])
        S.pe(lambda e: e.matmul(sl[1].f, lhsT=kbe[c][0][:], rhs=Y[:], start=True, stop=True), reads=[Y, kbe[c][0]], writes=[sl[1]])
        if lat:
            S.pe(lambda e: e.transpose(out=sl[2].b, in_=qe[c][0][:], identity=C.ident_b[:]), reads=[qe[c][0], C.ident_b], writes=[sl[2]])
        yield
        S.act(lambda e: e.activation(out=u_[c][r][:], in_=sl[0].f, func=AF.Copy), reads=[sl[0]], writes=[u_[c][r]])
        S.dve(lambda e: e.tensor_copy(out=wT[c][r][:], in_=sl[1].f), reads=[sl[1]], writes=[wT[c][r]])
        if lat:
            S.act(lambda e: e.activation(out=qdT[c][r][:], in_=sl[2].b, func=AF.Copy), reads=[sl[2]], writes=[qdT[c][r]])
        yield

    def scan(c, n):
        d, h = c // 4, c % 4
        nt = tile_of(d, n)
        lat = nt < 32
        r = n % 2
        sl = slots[c][3]
        Sold, Snew = Sbf[c][n % 2], Sbf[c][1 - n % 2]
        S.pe(lambda e: e.matmul(sl.f, lhsT=wT[c][r][:], rhs=Sold[:], start=True, stop=True), reads=[wT[c][r], Sold], writes=[sl])
        yield
        S.dve(lambda e: e.scalar_tensor_tensor(out=vn[c][0][:], in0=sl.f, scalar=-1.0, in1=u_[c][r][:], op0=ALU.mult, op1=ALU.add),
              reads=[sl, u_[c][r]], writes=[vn[c][0]])
        yield
        S.pe(lambda e: e.matmul(sl.f, lhsT=kd[c][r][:], rhs=vn[c][0][:], start=True, stop=True), reads=[kd[c][r], vn[c][0]], writes=[sl])
        yield
        glc = gl[:, nt, c:c + 1]
        S.dve(lambda e: e.scalar_tensor_tensor(out=Snew[:], in0=S32[c][0][:], scalar=glc, in1=sl.f, op0=ALU.mult, op1=ALU.add),
              reads=[S32[c][0], gl, sl], writes=[Snew])
        S.dve(lambda e: e.scalar_tensor_tensor(out=S32[c][0][:], in0=S32[c][0][:], scalar=glc, in1=sl.f, op0=ALU.mult, op1=ALU.add),
              reads=[S32[c][0], gl, sl], writes=[S32[c][0]])
        yield
        if lat:
            S.pe(lambda e: e.matmul(sl.f, lhsT=qdT[c][r][:], rhs=Sold[:], start=True, stop=False), reads=[qdT[c][r], Sold], writes=[sl], sig=False)
            S.pe(lambda e: e.matmul(sl.f, lhsT=attnT[c][r][:], rhs=vn[c][0][:], start=False, stop=True), reads=[attnT[c][r], vn[c][0]], writes=[sl])
            yield
            orr = ores[nt][h]
            if ofirst[nt][h]:
                ofirst[nt][h] = False
                S.act(lambda e: e.activation(out=oacc[:, nt, h * 128:(h + 1) * 128], in_=sl.f, func=AF.Copy), reads=[sl], writes=[orr])
            else:
                S.dve(lambda e: e.tensor_tensor(out=oacc[:, nt, h * 128:(h + 1) * 128], in0=sl.f, in1=oacc[:, nt, h * 128:(h + 1) * 128], op=ALU.add),
                      reads=[sl, orr], writes=[orr])
            yield

    NR = min(34, GSTOP)
    if GSTOP >= 0:
        for d in range(2):
            load_tile(d, 0)
        run_interleaved([precompute(c, 0) for c in range(8)])
    for n in range(NR):
        gens = [scan(c, n) for c in range(8)]
        if n + 1 < NR:
            for d in range(2):
                load_tile(d, n + 1)
            gens += [precompute(c, n + 1) for c in range(8)]
        run_interleaved(gens)
    for nt in (range(32) if NR == 34 else []):
        S.dma(OACC[nt * 128:(nt + 1) * 128, :], oacc[:, nt, :], reads=ores[nt], writes=[OACC], q='sp')
    for c in range(8):
        dbg_dump(C, 'dbg_S%d' % c, S32[c][0], S32[c][0][:], [128, 128], F32)
    S.barrier()
    S.flush()
    st.close()


def load_rows_bcast(C, st, name, dram_row, n):
    t = sb(C.nc, st, name, [128, n], F32)
    C.S.dma(t[:], dram_row.to_broadcast([128, n]), writes=[t])
    return t


class Epi:
    def __init__(self, C, st, ln_idx, gate_ap, nsub, nbuf=1):
        nc = C.nc
        self.C, self.nsub, self.gate = C, nsub, gate_ap
        self.g = load_rows_bcast(C, st, 'ln_g%d' % ln_idx, C.din['ln_g'][ln_idx:ln_idx + 1, :], D)
        self.b = load_rows_bcast(C, st, 'ln_b%d' % ln_idx, C.din['ln_b'][ln_idx:ln_idx + 1, :], D)
        self.t2s = [sb(nc, st, 'ep_t2_%d' % i, [128, nsub, D], F32) for i in range(nbuf)]
        self.t2 = self.t2s[0]
        self.junk = sb(nc, st, 'ep_junk', [128, D], BF16)
        self.st = sb(nc, st, 'ep_st', [128, 6, nsub], F32)
        self.eps = sb(nc, st, 'ep_eps', [128, 1], F32)
        C.S.pool(lambda e: e.memset(self.eps[:], LN_EPS), writes=[self.eps])
        self.i = 0

    def sub(self, s_, ypair, xt):
        S, t2, stt = self.C.S, self.t2, self.st
        for hf in range(2):
            S.dve(lambda e, hf=hf: e.tensor_tensor(out=t2[:, s_, hf * 512:(hf + 1) * 512], in0=ypair[hf][:],
                                                   in1=self.gate[:, hf * 512:(hf + 1) * 512], op=ALU.mult),
                  reads=[ypair[hf], self.C.modrow], writes=[t2])
        S.dve(lambda e: e.scalar_tensor_tensor(out=t2[:, s_, :], in0=xt[:, s_, :], scalar=ALPHA, in1=t2[:, s_, :], op0=ALU.mult, op1=ALU.add),
              reads=[xt, t2], writes=[t2])
        S.act(lambda e: e.activation(out=self.junk[:], in_=t2[:, s_, :], func=AF.Copy, accum_out=stt[:, 0, s_:s_ + 1]),
              reads=[t2], writes=[self.junk, stt])
        S.act(lambda e: e.activation(out=self.junk[:], in_=t2[:, s_, :], func=AF.Square, accum_out=stt[:, 1, s_:s_ + 1]),
              reads=[t2], writes=[self.junk, stt])

    def finish(self, dst_rows, xt_unused=None):
        S, t2, stt, n = self.C.S, self.t2, self.st, self.nsub
        dst, r0 = dst_rows
        xo = t2
        self.i += 1
        self.t2 = self.t2s[self.i % len(self.t2s)]
        S.dve(lambda e: e.tensor_scalar(out=stt[:, 2, :], in0=stt[:, 0, :], scalar1=1.0 / D, scalar2=None, op0=ALU.mult), reads=[stt], writes=[stt])
        S.dve(lambda e: e.tensor_tensor(out=stt[:, 4, :], in0=stt[:, 2, :], in1=stt[:, 2, :], op=ALU.mult), reads=[stt], writes=[stt])
        S.dve(lambda e: e.scalar_tensor_tensor(out=stt[:, 3, :], in0=stt[:, 1, :], scalar=1.0 / D, in1=stt[:, 4, :], op0=ALU.mult, op1=ALU.subtract),
              reads=[stt], writes=[stt])
        S.act(lambda e: e.activation(out=stt[:, 3, :], in_=stt[:, 3, :], func=AF.Ln, bias=self.eps[:, 0:1]), reads=[stt, self.eps], writes=[stt])
        S.act(lambda e: e.activation(out=stt[:, 3, :], in_=stt[:, 3, :], func=AF.Exp, scale=-0.5), reads=[stt], writes=[stt])
        S.dve(lambda e: e.scalar_tensor_tensor(out=stt[:, 5, :], in0=stt[:, 2, :], scalar=-1.0, in1=stt[:, 3, :], op0=ALU.mult, op1=ALU.mult),
              reads=[stt], writes=[stt])
        for s_ in range(n):
            S.act(lambda e, s_=s_: e.activation(out=t2[:, s_, :], in_=t2[:, s_, :], func=AF.Identity, scale=stt[:, 3, s_:s_ + 1], bias=stt[:, 5, s_:s_ + 1]),
                  reads=[t2, stt], writes=[t2])
            S.pool(lambda e, s_=s_: e.tensor_tensor(out=xo[:, s_, :], in0=t2[:, s_, :], in1=self.g[:], op=ALU.mult), reads=[t2, self.g], writes=[xo])
            S.dve(lambda e, s_=s_: e.tensor_tensor(out=xo[:, s_, :], in0=xo[:, s_, :], in1=self.b[:], op=ALU.add), reads=[xo, self.b], writes=[xo])
        S.dma(dst[r0:r0 + n * 128, :].rearrange("(s p) d -> p s d", p=128), xo[:, :, :], reads=[xo], writes=[dst], q='sp')


def out_proj(C, mixT, wout, s_, ypair, nk):
    S = C.S
    for hf in range(2):
        for k in range(nk):
            S.pe(lambda e, hf=hf, k=k: e.matmul(ypair[hf][:], lhsT=mixT[:, k, s_ * 128:(s_ + 1) * 128], rhs=wout[:, k, hf * 512:(hf + 1) * 512],
                                                start=(k == 0), stop=(k == nk - 1)), reads=[mixT, wout], writes=[ypair[hf]], sig=(k == nk - 1))


def phase_mix0_out(C, X1):
    nc, S = C.nc, C.S
    st = ExitStack()
    wout = load_w_bf16(C, st, 'wout0', C.din['even_w_out'][:, :], 8, D)
    normw = load_rows_bcast(C, st, 'normw', C.din['gdn_norm_w'][0:1, :], 128)
    epi = Epi(C, st, 0, C.modrow[:, 2 * D:3 * D], 4)
    yps = [[ps(nc, st, 'yps%d_%d' % (i, h), [128, 512], F32) for h in range(2)] for i in range(2)]
    pT = [ps(nc, st, 'pTm%d' % i, [128, 512], BF16) for i in range(2)]
    ot = [sb(nc, st, 'ot%d' % i, [128, 4, 512], F32) for i in range(2)]
    gt_ = [sb(nc, st, 'gt%d' % i, [128, 4, 512], BF16) for i in range(2)]
    xt = [sb(nc, st, 'xtm%d' % i, [128, 4, D], F32) for i in range(2)]
    mixT = [sb(nc, st, 'mixT%d' % i, [128, 8, 512], BF16) for i in range(2)]
    osq = sb(nc, st, 'osq', [128, 4, 512], F32)
    ss = sb(nc, st, 'oss', [128, 16], F32)
    og = sb(nc, st, 'og', [128, 4, 512], BF16)
    epsr = sb(nc, st, 'epsr', [128, 1], F32)
    S.pool(lambda e: e.memset(epsr[:], RMS_EPS), writes=[epsr])

    def load(t):
        i = t % 2
        S.dma(ot[i][:, :, :], C.OACC[t * 512:(t + 1) * 512, :].rearrange("(s p) d -> p s d", p=128), reads=[C.OACC], writes=[ot[i]])
        S.dma(gt_[i][:, :, :], C.G0[t * 512:(t + 1) * 512, :].rearrange("(s p) d -> p s d", p=128), reads=[C.G0], writes=[gt_[i]])
        S.dma(xt[i][:, :, :], C.din['x'][t * 512:(t + 1) * 512, :].rearrange("(s p) d -> p s d", p=128), writes=[xt[i]])
        S.dma(mixT[i][:, 4:8, :], C.YPT[:, t * 512:(t + 1) * 512].rearrange("(f p) n -> p f n", p=128), reads=[C.YPT], writes=[mixT[i]])

    load(0)
    yi = 0
    for t in range(8):
        if t + 1 < 8:
            load(t + 1)
        i = t % 2
        o_, g_, x_, m_ = ot[i], gt_[i], xt[i], mixT[i]
        S.pool(lambda e, o_=o_: e.tensor_tensor(out=osq[:], in0=o_[:], in1=o_[:], op=ALU.mult), reads=[o_], writes=[osq])
        S.dve(lambda e: e.tensor_reduce(out=ss[:], in_=osq[:].rearrange("p s (h d) -> p (s h) d", d=128), axis=mybir.AxisListType.X, op=ALU.add),
              reads=[osq], writes=[ss])
        S.act(lambda e: e.activation(out=ss[:], in_=ss[:], func=AF.Ln, scale=1.0 / 128, bias=epsr[:, 0:1]), reads=[ss, epsr], writes=[ss])
        S.act(lambda e: e.activation(out=ss[:], in_=ss[:], func=AF.Exp, scale=-0.5), reads=[ss], writes=[ss])
        S.dve(lambda e, o_=o_: e.tensor_tensor(out=osq[:].rearrange("p s (h d) -> p (s h) d", d=128), in0=o_[:].rearrange("p s (h d) -> p (s h) d", d=128),
                                              in1=ss[:].unsqueeze(2).to_broadcast([128, 16, 128]), op=ALU.mult), reads=[o_, ss], writes=[osq])
        S.pool(lambda e: e.tensor_tensor(out=osq[:].rearrange("p s (h d) -> p (s h) d", d=128), in0=osq[:].rearrange("p s (h d) -> p (s h) d", d=128),
                                         in1=normw[:].unsqueeze(1).to_broadcast([128, 16, 128]), op=ALU.mult), reads=[osq, normw], writes=[osq])
        S.dve(lambda e, g_=g_: e.tensor_tensor(out=og[:], in0=osq[:], in1=g_[:], op=ALU.mult), reads=[osq, g_], writes=[og])
        for h in range(4):
            p = pT[h % 2]
            for s_ in range(4):
                S.pe(lambda e, p=p, h=h, s_=s_: e.transpose(out=p[:, s_ * 128:(s_ + 1) * 128], in_=og[:, s_, h * 128:(h + 1) * 128], identity=C.ident_b[:]),
                     reads=[og, C.ident_b], writes=[p], sig=(s_ == 3))
            if h % 2 == 0:
                S.act(lambda e, p=p, h=h, m_=m_: e.activation(out=m_[:, h, :], in_=p[:], func=AF.Copy), reads=[p], writes=[m_])
            else:
                S.dve(lambda e, p=p, h=h, m_=m_: e.tensor_copy(out=m_[:, h, :], in_=p[:]), reads=[p], writes=[m_])
        for s_ in range(4):
            yp = yps[yi % 2]; yi += 1
            out_proj(C, m_, wout, s_, yp, 8)
            epi.sub(s_, yp, x_)
        epi.finish((X1, t * 512))
    dbg_dump(C, 'dbg_x1', X1, X1[0:128, :], [128, D], F32)
    S.barrier()
    S.flush()
    st.close()


def phase_ffn_up(C, layer, Xin):
    nc, S = C.nc, C.S
    st = ExitStack()
    if layer == 0:
        C.AT = C.scratch('AT', [DFF, T + 128], BF16)
        C.GTt = C.scratch('GTt', [DFF, T], BF16)
        for r0 in range(0, DFF, 128):
            S.dma(C.AT[r0:r0 + 128, 0:64], C.zeros_b[:, 0:64], reads=[C.zeros_b], writes=[C.AT])
            S.dma(C.AT[r0:r0 + 128, T + 64:T + 128], C.zeros_b[:, 0:64], reads=[C.zeros_b], writes=[C.AT])
    AT, GTt = C.AT, C.GTt
    w = load_w_bf16(C, st, 'wup', C.din['ffn_w_up'][layer, :, :], 8, 2 * DFF)
    xt = sb(nc, st, 'xtu', [128, 4, D], F32)
    ub = sb(nc, st, 'ubu', [128, 4, D], BF16)
    uT = [sb(nc, st, 'uTu%d' % i, [128, 8, 512], BF16) for i in range(2)]
    stg = [sb(nc, st, 'stgu%d' % i, [128, 4, 512], BF16) for i in range(2)]
    pT = [ps(nc, st, 'pTu%d' % i, [128, 512], BF16) for i in range(2)]
    pm = [ps(nc, st, 'pmu%d' % i, [128, 512], F32) for i in range(4)]
    pmi = 0
    for t in range(8):
        S.dma(xt[:, :, :], Xin[t * 512:(t + 1) * 512, :].rearrange("(s p) d -> p s d", p=128), reads=[Xin], writes=[xt])
        ut = uT[t % 2]
        modulate_transpose(C, xt, 4, C.modrow[:, 3 * D:4 * D], C.modrow[:, 4 * D:5 * D], ub, ut, pT, t)
        gi = 0
        for f0 in range(0, 44, 4):
            sg = stg[gi % 2]; gi += 1
            nf = min(4, 44 - f0)
            for j in range(nf):
                fc = f0 + j
                p = pm[pmi % 4]; pmi += 1
                for k in range(8):
                    S.pe(lambda e, p=p, k=k, fc=fc, ut=ut: e.matmul(p[:], lhsT=w[:, k, fc * 128:(fc + 1) * 128], rhs=ut[:, k, :],
                                                                 start=(k == 0), stop=(k == 7)), reads=[w, ut], writes=[p], sig=(k == 7))
                if pmi % 2 == 0:
                    S.act(lambda e, p=p, j=j, sg=sg: e.activation(out=sg[:, j, :], in_=p[:], func=AF.Copy), reads=[p], writes=[sg])
                else:
                    S.dve(lambda e, p=p, j=j, sg=sg: e.tensor_copy(out=sg[:, j, :], in_=p[:]), reads=[p], writes=[sg])
            if f0 < 22:
                na = min(nf, 22 - f0)
                S.dma(AT[f0 * 128:(f0 + na) * 128, 64 + t * 512:64 + (t + 1) * 512].rearrange("(j p) n -> p j n", p=128), sg[:, 0:na, :],
                      reads=[sg], writes=[AT], q='act')
                if na < nf:
                    S.dma(GTt[0:(nf - na) * 128, t * 512:(t + 1) * 512].rearrange("(j p) n -> p j n", p=128), sg[:, na:nf, :],
                          reads=[sg], writes=[GTt], q='act')
            else:
                g0 = f0 - 22
                S.dma(GTt[g0 * 128:(g0 + nf) * 128, t * 512:(t + 1) * 512].rearrange("(j p) n -> p j n", p=128), sg[:, 0:nf, :],
                      reads=[sg], writes=[GTt], q='act')
    S.barrier()
    S.flush()
    st.close()


def phase_ffn_down(C, layer, Xin, Xout):
    phase_ffn_conv(C, layer)
    phase_ffn_proj(C, layer, Xin, Xout)


def phase_ffn_conv(C, layer):
    nc, S = C.nc, C.S
    st = ExitStack()
    AT, GTt = C.AT, C.GTt
    if layer == 0:
        C.HGT = C.scratch('HGT', [DFF, T], BF16)
    HGT = C.HGT
    NTK = 256
    pconv = [ps(nc, st, 'pcv%d' % i, [128, 512], F32) for i in range(4)]
    dg = build_diag(C, st, pconv[0], C.din['ffn_conv_w'][layer, :, :], 9, 22, 'cw9')
    ad = [sb(nc, st, 'ad%d' % i, [128, 22, 384], BF16) for i in range(2)]
    gt_ = [sb(nc, st, 'gd%d' % i, [128, 22, 256], BF16) for i in range(2)]
    apad = [sb(nc, st, 'apad%d' % i, [128, 22, 6, 66], BF16) for i in range(2)]
    hgs = [sb(nc, st, 'hgs%d' % i, [128, 22, 256], BF16) for i in range(2)]
    hs = [sb(nc, st, 'hs%d' % i, [128, 256], BF16) for i in range(4)]
    for i in range(2):
        S.pool(lambda e, i=i: e.memset(apad[i][:], 0.0), writes=[apad[i]])
    ntile = T // NTK

    def load_a(t):
        i = t % 2
        S.dma(ad[i][:, :, :], AT[:, t * NTK:t * NTK + 384].rearrange("(f p) n -> p f n", p=128), reads=[AT], writes=[ad[i]])

    def load_g(t):
        i = t % 2
        S.dma(gt_[i][:, :, :], GTt[:, t * NTK:(t + 1) * NTK].rearrange("(f p) n -> p f n", p=128), reads=[GTt], writes=[gt_[i]])

    def pad(t):
        a_, ap_ = ad[t % 2], apad[t % 2]
        for fc in range(22):
            fn = lambda e, fc=fc, a_=a_, ap_=ap_: e.tensor_copy(out=ap_[:, fc, :, 1:65], in_=a_[:, fc, :].rearrange("p (r c) -> p r c", c=64))
            if fc % 2 == 0:
                S.pool(fn, reads=[a_], writes=[ap_])
            else:
                S.dve(fn, reads=[a_], writes=[ap_])

    load_a(0)
    load_g(0)
    pad(0)
    load_a(1)
    ci = 0
    for t in range(ntile):
        if t + 1 < ntile:
            pad(t + 1)
            load_g(t + 1)
        if t + 2 < ntile:
            load_a(t + 2)
        i = t % 2
        a_, g_, ap_, hg = ad[i], gt_[i], apad[i], hgs[i]
        for fc in range(22):
            pc = pconv[ci % 4]; h_ = hs[ci % 4]; ci += 1
            for tap in range(9):
                dy, dx = tap // 3 - 1, tap % 3 - 1
                S.pe(lambda e, pc=pc, fc=fc, tap=tap, dy=dy, dx=dx, ap_=ap_: e.matmul(
                    pc[:, 0:256], lhsT=dg[:, fc, tap, :], rhs=ap_[:, fc, 1 + dy:5 + dy, 1 + dx:65 + dx], start=(tap == 0), stop=(tap == 8)),
                    reads=[dg, ap_], writes=[pc], sig=(tap == 8))
            S.act(lambda e, pc=pc, h_=h_: e.activation(out=h_[:], in_=pc[:, 0:256], func=AF.Silu), reads=[pc], writes=[h_])
            S.dve(lambda e, fc=fc, h_=h_, g_=g_, hg=hg: e.tensor_tensor(out=hg[:, fc, :], in0=h_[:], in1=g_[:, fc, :], op=ALU.mult),
                  reads=[h_, g_], writes=[hg])
        S.dma(HGT[:, t * NTK:(t + 1) * NTK].rearrange("(f p) n -> p f n", p=128), hg[:, :, :], reads=[hg], writes=[HGT], q='act')
    S.barrier()
    S.flush()
    st.close()


def phase_ffn_proj(C, layer, Xin, Xout):
    nc, S = C.nc, C.S
    st = ExitStack()
    HGT = C.HGT
    yps = [[ps(nc, st, 'ypd%d_%d' % (i, h), [128, 512], F32) for h in range(2)] for i in range(3)]
    wd = load_w_bf16(C, st, 'wdn', C.din['ffn_w_down'][layer, :, :], 22, D)
    epi = Epi(C, st, layer * 2 + 1, C.modrow[:, 5 * D:6 * D], 4, nbuf=2)
    hg = [sb(nc, st, 'hgp%d' % i, [128, 22, 512], BF16) for i in range(2)]
    xt = [sb(nc, st, 'xtd%d' % i, [128, 4, D], F32) for i in range(2)]

    def load(t):
        i = t % 2
        S.dma(hg[i][:, :, :], HGT[:, t * 512:(t + 1) * 512].rearrange("(f p) n -> p f n", p=128), reads=[HGT], writes=[hg[i]])
        S.dma(xt[i][:, :, :], Xin[t * 512:(t + 1) * 512, :].rearrange("(s p) d -> p s d", p=128), reads=[Xin], writes=[xt[i]])

    load(0)
    yi = 0
    for t in range(8):
        if t + 1 < 8:
            load(t + 1)
        h_, x_ = hg[t % 2], xt[t % 2]
        for s_ in range(4):
            yp = yps[yi % 3]; yi += 1
            out_proj(C, h_, wd, s_, yp, 22)
            epi.sub(s_, yp, x_)
        epi.finish((Xout, t * 512))
    S.barrier()
    S.flush()
    st.close()


def phase_inproj1(C, Xin):
    nc, S = C.nc, C.S
    st = ExitStack()
    GBT = C.scratch('GBT', [512, T], BF16); C.GBT = GBT
    M1T = C.scratch('M1T', [512, T + 32], BF16); C.M1T = M1T
    M2T = C.scratch('M2T', [512, T + 32], BF16); C.M2T = M2T
    for M in (M1T, M2T):
        for r0 in range(0, 512, 128):
            S.dma(M[r0:r0 + 128, 0:16], C.zeros_b[:, 0:16], reads=[C.zeros_b], writes=[M])
            S.dma(M[r0:r0 + 128, T + 16:T + 32], C.zeros_b[:, 0:16], reads=[C.zeros_b], writes=[M])
    w = load_w_bf16(C, st, 'w_in1', C.din['odd_w_in'][:, :], 8, 2560)
    xt = sb(nc, st, 'xt1', [128, 4, D], F32)
    ub = sb(nc, st, 'ub1', [128, 4, D], BF16)
    uT = [sb(nc, st, 'uT1%d' % i, [128, 8, 512], BF16) for i in range(2)]
    stg = [[sb(nc, st, 'stg1_%d_%d' % (g, i), [128, 4, 512], BF16) for i in range(2)] for g in range(3)]
    tmp = [sb(nc, st, 'tmp1_%d' % i, [128, 512], F32) for i in range(2)]
    pT = [ps(nc, st, 'pT1%d' % i, [128, 512], BF16) for i in range(2)]
    pm = [ps(nc, st, 'pm1%d' % i, [128, 512], F32) for i in range(4)]
    pmi = 0
    ti = 0

    def mm(fc, ut):
        nonlocal pmi
        p = pm[pmi % 4]; pmi += 1
        for k in range(8):
            S.pe(lambda e, p=p, k=k: e.matmul(p[:], lhsT=w[:, k, fc * 128:(fc + 1) * 128], rhs=ut[:, k, :], start=(k == 0), stop=(k == 7)),
                 reads=[w, ut], writes=[p], sig=(k == 7))
        return p

    for t in range(8):
        S.dma(xt[:, :, :], Xin[t * 512:(t + 1) * 512, :].rearrange("(s p) d -> p s d", p=128), reads=[Xin], writes=[xt])
        ut = uT[t % 2]
        modulate_transpose(C, xt, 4, C.modrow[:, 0:D], C.modrow[:, D:2 * D], ub, ut, pT, t)
        sgb, sm1, sm2 = stg[0][t % 2], stg[1][t % 2], stg[2][t % 2]
        for j in range(4):
            p = mm(j, ut)
            S.act(lambda e, p=p, j=j, sgb=sgb: e.activation(out=sgb[:, j, :], in_=p[:], func=AF.Copy), reads=[p], writes=[sgb])
            tm = tmp[ti % 2]; ti += 1
            p = mm(4 + j, ut)
            S.act(lambda e, p=p, tm=tm: e.activation(out=tm[:], in_=p[:], func=AF.Copy), reads=[p], writes=[tm])
            p = mm(8 + j, ut)
            S.dve(lambda e, p=p, tm=tm, j=j, sm1=sm1: e.tensor_tensor(out=sm1[:, j, :], in0=p[:], in1=tm[:], op=ALU.mult), reads=[p, tm], writes=[sm1])
            tm = tmp[ti % 2]; ti += 1
            p = mm(16 + j, ut)
            S.act(lambda e, p=p, tm=tm: e.activation(out=tm[:], in_=p[:], func=AF.Sigmoid), reads=[p], writes=[tm])
            p = mm(12 + j, ut)
            S.dve(lambda e, p=p, tm=tm, j=j, sm2=sm2: e.tensor_tensor(out=sm2[:, j, :], in0=p[:], in1=tm[:], op=ALU.mult), reads=[p, tm], writes=[sm2])
        c0 = t * 512
        S.dma(GBT[:, c0:c0 + 512].rearrange("(j p) n -> p j n", p=128), sgb[:, :, :], reads=[sgb], writes=[GBT], q='act')
        S.dma(M1T[:, 16 + c0:16 + c0 + 512].rearrange("(j p) n -> p j n", p=128), sm1[:, :, :], reads=[sm1], writes=[M1T], q='act')
        S.dma(M2T[:, 16 + c0:16 + c0 + 512].rearrange("(j p) n -> p j n", p=128), sm2[:, :, :], reads=[sm2], writes=[M2T], q='act')
    S.barrier()
    S.flush()
    st.close()


def phase_mix1_out(C, Xin, Xout):
    nc, S = C.nc, C.S
    st = ExitStack()
    pconv = [ps(nc, st, 'pc1%d' % i, [128, 512], F32) for i in range(2)]
    pstat = [ps(nc, st, 'pst1%d' % i, [128, 512], F32) for i in range(2)]
    yps = [[ps(nc, st, 'yp1%d_%d' % (i, h), [128, 512], F32) for h in range(2)] for i in range(2)]
    dg3 = build_diag(C, st, pconv[0], C.din['sconv_w'][:, :], 3, 4, 'cw3')
    dg31 = build_diag(C, st, pconv[1], C.din['conf_conv_w'][:, :], 31, 4, 'cw31')
    lng = to_col(C, st, pconv[0], C.din['conf_ln_g'][:, :], 1, 4, 'clng')
    lnb = to_col(C, st, pconv[1], C.din['conf_ln_b'][:, :], 1, 4, 'clnb')
    wout = load_w_bf16(C, st, 'wout1', C.din['odd_w_out'][:, :], 8, D)
    epi = Epi(C, st, 2, C.modrow[:, 2 * D:3 * D], 4)
    m1 = [sb(nc, st, 'm1_%d' % i, [128, 4, 514], BF16) for i in range(2)]
    m2 = [sb(nc, st, 'm2_%d' % i, [128, 4, 542], BF16) for i in range(2)]
    gb = [sb(nc, st, 'gb_%d' % i, [128, 4, 512], BF16) for i in range(2)]
    xt = [sb(nc, st, 'xt1o%d' % i, [128, 4, D], F32) for i in range(2)]
    mixT = sb(nc, st, 'mixT1', [128, 8, 512], BF16)
    z = sb(nc, st, 'z1', [128, 4, 512], F32)
    zsq = sb(nc, st, 'zsq1', [128, 4, 512], F32)
    mean = sb(nc, st, 'mean1', [128, 512], F32)
    rstd = sb(nc, st, 'rstd1', [128, 512], F32)
    msq = sb(nc, st, 'msq1', [128, 512], F32)
    epsl = sb(nc, st, 'epsl1', [128, 1], F32)
    S.pool(lambda e: e.memset(epsl[:], LN_EPS), writes=[epsl])

    def load(t):
        i = t % 2
        c0 = t * 512
        S.dma(m1[i][:, :, :], C.M1T[:, 15 + c0:15 + c0 + 514].rearrange("(f p) n -> p f n", p=128), reads=[C.M1T], writes=[m1[i]])
        S.dma(m2[i][:, :, :], C.M2T[:, 1 + c0:1 + c0 + 542].rearrange("(f p) n -> p f n", p=128), reads=[C.M2T], writes=[m2[i]])
        S.dma(gb[i][:, :, :], C.GBT[:, c0:c0 + 512].rearrange("(f p) n -> p f n", p=128), reads=[C.GBT], writes=[gb[i]])
        S.dma(xt[i][:, :, :], Xin[c0:c0 + 512, :].rearrange("(s p) d -> p s d", p=128), reads=[Xin], writes=[xt[i]])

    load(0)
    ci = 0
    yi = 0
    for t in range(8):
        if t + 1 < 8:
            load(t + 1)
        i = t % 2
        a1, a2, g_, x_ = m1[i], m2[i], gb[i], xt[i]
        for j in range(4):
            pc = pconv[ci % 2]; ci += 1
            for tap in range(3):
                S.pe(lambda e, pc=pc, j=j, tap=tap, a1=a1: e.matmul(pc[:], lhsT=dg3[:, j, tap, :], rhs=a1[:, j, tap:tap + 512], start=(tap == 0), stop=(tap == 2)),
                     reads=[dg3, a1], writes=[pc], sig=(tap == 2))
            S.dve(lambda e, pc=pc, j=j, g_=g_: e.tensor_tensor(out=mixT[:, j, :], in0=pc[:], in1=g_[:, j, :], op=ALU.mult), reads=[pc, g_], writes=[mixT])
        for j in range(4):
            pc = pconv[ci % 2]; ci += 1
            for tap in range(31):
                S.pe(lambda e, pc=pc, j=j, tap=tap, a2=a2: e.matmul(pc[:], lhsT=dg31[:, j, tap, :], rhs=a2[:, j, tap:tap + 512], start=(tap == 0), stop=(tap == 30)),
                     reads=[dg31, a2], writes=[pc], sig=(tap == 30))
            S.act(lambda e, pc=pc, j=j: e.activation(out=z[:, j, :], in_=pc[:], func=AF.Copy), reads=[pc], writes=[z])
            S.pool(lambda e, j=j: e.tensor_tensor(out=zsq[:, j, :], in0=z[:, j, :], in1=z[:, j, :], op=ALU.mult), reads=[z], writes=[zsq])
        for j in range(4):
            S.pe(lambda e, j=j: e.matmul(pstat[0][:], lhsT=C.ones_f[:, 0:128], rhs=z[:, j, :], start=(j == 0), stop=(j == 3)),
                 reads=[C.ones_f, z], writes=[pstat[0]], sig=(j == 3))
        for j in range(4):
            S.pe(lambda e, j=j: e.matmul(pstat[1][:], lhsT=C.ones_f[:, 0:128], rhs=zsq[:, j, :], start=(j == 0), stop=(j == 3)),
                 reads=[C.ones_f, zsq], writes=[pstat[1]], sig=(j == 3))
        S.dve(lambda e: e.tensor_scalar(out=mean[:], in0=pstat[0][:], scalar1=1.0 / 512, scalar2=None, op0=ALU.mult), reads=[pstat[0]], writes=[mean])
        S.dve(lambda e: e.tensor_tensor(out=msq[:], in0=mean[:], in1=mean[:], op=ALU.mult), reads=[mean], writes=[msq])
        S.dve(lambda e: e.scalar_tensor_tensor(out=rstd[:], in0=pstat[1][:], scalar=1.0 / 512, in1=msq[:], op0=ALU.mult, op1=ALU.subtract),
              reads=[pstat[1], msq], writes=[rstd])
        S.act(lambda e: e.activation(out=rstd[:], in_=rstd[:], func=AF.Ln, bias=epsl[:, 0:1]), reads=[rstd, epsl], writes=[rstd])
        S.act(lambda e: e.activation(out=rstd[:], in_=rstd[:], func=AF.Exp, scale=-0.5), reads=[rstd], writes=[rstd])
        for j in range(4):
            S.dve(lambda e, j=j: e.tensor_tensor(out=z[:, j, :], in0=z[:, j, :], in1=mean[:], op=ALU.subtract), reads=[z, mean], writes=[z])
            S.pool(lambda e, j=j: e.tensor_tensor(out=z[:, j, :], in0=z[:, j, :], in1=rstd[:], op=ALU.mult), reads=[z, rstd], writes=[z])
            S.act(lambda e, j=j: e.activation(out=mixT[:, 4 + j, :], in_=z[:, j, :], func=AF.Silu, scale=lng[:, j, 0:1], bias=lnb[:, j, 0:1]),
                  reads=[z, lng, lnb], writes=[mixT])
        for s_ in range(4):
            yp = yps[yi % 2]; yi += 1
            out_proj(C, mixT, wout, s_, yp, 8)
            epi.sub(s_, yp, x_)
        epi.finish((Xout, t * 512))
    S.barrier()
    S.flush()
    st.close()


_NC_CACHE = {}


def kernel(**inputs):
    if 'nc' not in _NC_CACHE:
        _NC_CACHE['nc'] = build()
    nc = _NC_CACHE['nc']
    f = lambda a: np.ascontiguousarray(np.asarray(a, dtype=np.float32))
    shared = {
        'c_ctx': f(inputs['c_ctx']).reshape(1, D),
        'ada_w': f(inputs['ada_w']), 'ada_b': f(inputs['ada_b']),
        'ln_g': f(inputs['ln_g']).reshape(4, D), 'ln_b': f(inputs['ln_b']).reshape(4, D),
        'even_w_in': f(inputs['even_w_in']), 'even_w_out': f(inputs['even_w_out']),
        'gdn_conv_w': f(inputs['gdn_conv_w']), 'gdn_a_log': f(inputs['gdn_a_log']).reshape(1, 8),
        'gdn_dt_bias': f(inputs['gdn_dt_bias']).reshape(1, 8), 'gdn_norm_w': f(inputs['gdn_norm_w']).reshape(1, 128),
        'pool_w': f(inputs['pool_w']), 'pool_scale': f(inputs['pool_scale']).reshape(1, 512),
        'odd_w_in': f(inputs['odd_w_in']), 'odd_w_out': f(inputs['odd_w_out']),
        'sconv_w': f(inputs['sconv_w']), 'conf_conv_w': f(inputs['conf_conv_w']),
        'conf_ln_g': f(inputs['conf_ln_g']).reshape(1, 512), 'conf_ln_b': f(inputs['conf_ln_b']).reshape(1, 512),
        'ffn_w_up': f(inputs['ffn_w_up']), 'ffn_conv_w': f(inputs['ffn_conv_w']).reshape(2, 9, DFF),
        'ffn_w_down': f(inputs['ffn_w_down']),
    }
    x = f(inputs['x']); c = f(inputs['c']); ctx = f(inputs['ctx'])
    in_maps = []
    for b in range(NCORES):
        m = dict(shared)
        m['x'] = x[b]; m['c'] = c[b:b + 1]; m['ctx'] = ctx[b]
        in_maps.append(m)
    res = run_bass_kernel_spmd(nc, in_maps, core_ids=list(range(NCORES)))
    return np.stack([r['out'] for r in res.results], axis=0)
```

```python
import numpy as np
from contextlib import ExitStack
import concourse.bass as bass
import concourse.mybir as mybir
from concourse.bass_utils import run_bass_kernel_spmd

F32 = mybir.dt.float32
BF16 = mybir.dt.bfloat16
AF = mybir.ActivationFunctionType
ALU = mybir.AluOpType

D = 1024
T = 4096
TC = 256
NCORES = 8
DFF = 2816
ALPHA = 4 ** 0.25
LN_EPS = 1e-5
RMS_EPS = 1e-6
BIG = 30000.0
ENG = ('pe', 'act', 'dve', 'pool', 'sp')
NDS = 24
import os as _os
SELF_SYNC = ('pool',) if _os.environ.get('RELAX') == '1' else ('act', 'dve', 'pool')

DEBUG_OUT = []


class Res:
    __slots__ = ('w', 'r', 'excl')

    def __init__(self):
        self.w = None
        self.r = []
        self.excl = False


class TileT:
    def __init__(self, t):
        self.t = t
        self.res = Res()

    def __getitem__(self, k):
        return self.t[k]


class Sched:
    def __init__(self, nc, stack):
        self.nc = nc
        self.sem = {e: stack.enter_context(nc.semaphore('s_' + e)) for e in ENG}
        self.cnt = {e: 0 for e in ENG}
        self.known = {e: {} for e in ENG}
        self.q = {e: [] for e in ENG}
        self.dsem = [stack.enter_context(nc.semaphore('dq%d' % i)) for i in range(NDS)]
        self.dcnt = [0] * NDS
        self.dpool = {'sp': list(range(0, 12)), 'act': list(range(12, 20)), 'pool': list(range(20, 24))}
        self.dnext = {'sp': 0, 'act': 0, 'pool': 0}
        self.unsig = {e: False for e in ENG}

    def semof(self, key):
        return self.sem[key] if isinstance(key, str) else self.dsem[key]

    def _collect(self, eng, reads, writes):
        toks = []
        for r in reads:
            if r.w is not None:
                toks.append(r.w)
        for w in writes:
            if w.w is not None and (w.w[0] != eng or eng in SELF_SYNC):
                toks.append(w.w)
            for t in w.r:
                if t[0] != eng or eng in SELF_SYNC:
                    toks.append(t)
        waits = {}
        kn = self.known[eng]
        for key, val in toks:
            if kn.get(key, 0) < val:
                waits[key] = max(waits.get(key, 0), val)
        for key, val in waits.items():
            kn[key] = val
        return list(waits.items())

    def _update(self, tok, reads, writes):
        for r in reads:
            r.r.append(tok)
        for w in writes:
            w.w = tok
            w.r = []

    def emit(self, eng, fn, reads=(), writes=(), sig=True):
        reads = [getattr(x, "res", x) for x in reads]
        writes = [getattr(x, "res", x) for x in writes]
        writes = writes + [r for r in reads if r.excl and eng != 'pe']
        waits = self._collect(eng, reads, writes)
        if sig:
            self.cnt[eng] += 1
            tok = (eng, self.cnt[eng])
            self.unsig[eng] = False
        else:
            tok = (eng, self.cnt[eng] + 1)
            self.unsig[eng] = True
        self.q[eng].append((waits, fn, sig, None))
        self._update(tok, reads, writes)

    def pe(self, fn, reads=(), writes=(), sig=True):
        self.emit('pe', fn, reads, writes, sig)

    def act(self, fn, reads=(), writes=()):
        self.emit('act', fn, reads, writes)

    def dve(self, fn, reads=(), writes=()):
        self.emit('dve', fn, reads, writes)

    def pool(self, fn, reads=(), writes=()):
        self.emit('pool', fn, reads, writes)

    def dma(self, out, in_, reads=(), writes=(), q='sp', **kw):
        reads = [getattr(x, "res", x) for x in reads]
        writes = [getattr(x, "res", x) for x in writes]
        pl = self.dpool[q]
        j = pl[self.dnext[q] % len(pl)]
        self.dnext[q] += 1
        waits = dict(self._collect(q, reads, writes))
        if self.dcnt[j] > 0 and self.known[q].get(j, 0) < self.dcnt[j]:
            waits[j] = self.dcnt[j]
            self.known[q][j] = self.dcnt[j]
        self.dcnt[j] += 16
        tok = (j, self.dcnt[j])
        self.q[q].append((list(waits.items()), lambda e: e.dma_start(out=out, in_=in_, **kw), False, j))
        self._update(tok, reads, writes)

    def barrier(self):
        for e in ENG:
            assert not self.unsig[e], e
        for e in ENG:
            waits = []
            for f in ENG:
                if f != e and self.known[e].get(f, 0) < self.cnt[f]:
                    waits.append((f, self.cnt[f]))
                    self.known[e][f] = self.cnt[f]
            for j in range(NDS):
                if self.dcnt[j] > 0 and self.known[e].get(j, 0) < self.dcnt[j]:
                    waits.append((j, self.dcnt[j]))
                    self.known[e][j] = self.dcnt[j]
            if waits:
                self.q[e].append((waits, None, False, None))

    def flush(self):
        nc = self.nc
        q = self.q
        self.q = {e: [] for e in ENG}

        def replay(eng, e):
            for waits, fn, sig, dj in q[eng]:
                for key, val in waits:
                    e.wait_ge(self.semof(key), val)
                if fn is None:
                    continue
                ins = fn(e)
                if dj is not None:
                    ins.then_inc(self.dsem[dj], 16)
                elif sig:
                    ins.then_inc(self.sem[eng], 1)

        with nc.Block() as block:
            @block.tensor
            def _(e):
                replay('pe', e)

            @block.scalar
            def _(e):
                replay('act', e)

            @block.vector
            def _(e):
                replay('dve', e)

            @block.gpsimd
            def _(e):
                replay('pool', e)

            @block.sync
            def _(e):
                replay('sp', e)


class Ctx:
    pass


_UID = [0]


def _uname(name):
    _UID[0] += 1
    return '%s_u%d' % (name, _UID[0])


def sb(nc, stack, name, shape, dt):
    return TileT(stack.enter_context(nc.sbuf_tensor(_uname(name), list(shape), dt)))


def ps(nc, stack, name, shape, dt):
    t = TileT(stack.enter_context(nc.psum_tensor(_uname(name), list(shape), dt)))
    t.res.excl = True
    return t


def build(debug_out=()):
    nc = bass.Bass("TRN2", target_bir_lowering=False)
    top = ExitStack()
    S = Sched(nc, top)
    C = Ctx()
    C.nc, C.S = nc, S
    din = {}

    def inp(name, shape):
        din[name] = TileT(nc.dram_tensor(name, list(shape), F32, kind="ExternalInput").ap())
        return din[name]

    inp('x', [T, D]); inp('c', [1, D]); inp('ctx', [TC, D]); inp('c_ctx', [1, D])
    inp('ada_w', [2, D, 6 * D]); inp('ada_b', [2, 6 * D])
    inp('ln_g', [4, D]); inp('ln_b', [4, D])
    inp('even_w_in', [D, 2576]); inp('even_w_out', [D, D])
    inp('gdn_conv_w', [5, 1536]); inp('gdn_a_log', [1, 8]); inp('gdn_dt_bias', [1, 8])
    inp('gdn_norm_w', [1, 128]); inp('pool_w', [4, 128, 128]); inp('pool_scale', [1, 512])
    inp('odd_w_in', [D, 2560]); inp('odd_w_out', [D, D])
    inp('sconv_w', [3, 512]); inp('conf_conv_w', [31, 512])
    inp('conf_ln_g', [1, 512]); inp('conf_ln_b', [1, 512])
    inp('ffn_w_up', [2, D, 2 * DFF]); inp('ffn_conv_w', [2, 9, DFF]); inp('ffn_w_down', [2, DFF, D])
    C.din = din
    C.out = TileT(nc.dram_tensor('out', [T, D], F32, kind="ExternalOutput").ap())

    def scratch(name, shape, dt):
        kind = "ExternalOutput" if name in debug_out else "Internal"
        t = TileT(nc.dram_tensor(name, list(shape), dt, kind=kind).ap())
        return t
    C.scratch = scratch

    C.ident_f = sb(nc, top, 'ident_f', [128, 128], F32)
    C.ident_b = sb(nc, top, 'ident_b', [128, 128], BF16)
    C.ones_f = sb(nc, top, 'ones_f', [128, 512], F32)
    C.ones_b = sb(nc, top, 'ones_b', [128, 128], BF16)
    C.zeros_b = sb(nc, top, 'zeros_b', [128, 512], BF16)
    C.modrow = sb(nc, top, 'modrow', [128, 6 * D], F32)
    st01 = ExitStack()
    C.modc = sb(nc, st01, 'modc', [128, 2 * D], F32)

    S.pool(lambda e: e.memset(C.ones_f[:], 1.0), writes=[C.ones_f])
    S.pool(lambda e: e.memset(C.ones_b[:], 1.0), writes=[C.ones_b])
    S.pool(lambda e: e.memset(C.zeros_b[:], 0.0), writes=[C.zeros_b])
    S.pool(lambda e: e.affine_select(out=C.ident_f[:], in_=C.ones_f[:, 0:128], pattern=[[-1, 128]],
                                     compare_op=ALU.is_equal, fill=0.0, base=0, channel_multiplier=1),
           reads=[C.ones_f], writes=[C.ident_f])
    S.pool(lambda e: e.tensor_copy(out=C.ident_b[:], in_=C.ident_f[:]), reads=[C.ident_f], writes=[C.ident_b])

    C.debug_out = debug_out
    phase_mod(C, 0)
    dbg_dump(C, 'dbg_mod0', C.modrow, C.modrow[0:1, :], [1, 6 * D], F32)
    dbg_dump(C, 'dbg_modc', C.modc, C.modc[0:1, :], [1, 2 * D], F32)
    phase_inproj0(C)
    st01.close()
    phase_qkv0(C)
    import os
    if os.environ.get('NOGDN') != '1':
        phase_gdn(C)
    else:
        C.OACC = C.scratch('OACC', [T, 512], F32)
        for r0 in range(0, T, 128):
            S.dma(C.OACC[r0:r0 + 128, :], C.ones_f[:, :], reads=[C.ones_f], writes=[C.OACC])
    PSTOP = int(os.environ.get('PSTOP', '99'))
    X1 = C.scratch('X1', [T, D], F32)
    X2 = C.scratch('X2', [T, D], F32)
    X3 = C.scratch('X3', [T, D], F32)
    if PSTOP >= 1:
        phase_mix0_out(C, X1)
    if PSTOP >= 2:
        phase_ffn_up(C, 0, X1)
    if PSTOP >= 3:
        phase_ffn_down(C, 0, X1, X2)
    if PSTOP >= 4:
        phase_mod(C, 1)
        phase_inproj1(C, X2)
    if PSTOP >= 5:
        phase_mix1_out(C, X2, X3)
    if PSTOP >= 6:
        phase_ffn_up(C, 1, X3)
        phase_ffn_down(C, 1, X3, C.out)

    S.barrier()
    S.flush()
    top.close()
    return nc


def dbg_dump(C, name, tile, ap, shape, dt):
    if name not in C.debug_out:
        return
    d = TileT(C.nc.dram_tensor(name, list(shape), dt, kind="ExternalOutput").ap())
    C.S.dma(d[:], ap, reads=[tile], writes=[d])


def to_col(C, st, psb, dram, R, ncol, name):
    nc, S = C.nc, C.S
    BL = 4
    tmp = sb(nc, st, name + '_row', [R, BL * 128], F32)
    outt = sb(nc, st, name + '_col', [128, ncol, R], F32)
    for c0 in range(0, ncol, BL):
        c1 = min(ncol, c0 + BL)
        S.dma(tmp[:, 0:(c1 - c0) * 128], dram[:, c0 * 128:c1 * 128], writes=[tmp])
        for c in range(c0, c1):
            S.pe(lambda e, c=c, c0=c0: e.transpose(out=psb[:, 0:R], in_=tmp[0:R, (c - c0) * 128:(c - c0 + 1) * 128],
                                                   identity=C.ident_f[0:R, 0:R]),
                 reads=[tmp, C.ident_f], writes=[psb])
            S.dve(lambda e, c=c: e.tensor_copy(out=outt[:, c, :], in_=psb[:, 0:R]), reads=[psb], writes=[outt])
    return outt


def phase_mod(C, layer):
    nc, S = C.nc, C.S
    st = ExitStack()
    pst = ps(nc, st, 'pm_t', [128, 512], F32)
    pacc = [ps(nc, st, 'pm_a%d' % i, [128, 512], F32) for i in range(2)]
    paccc = [ps(nc, st, 'pm_c%d' % i, [128, 512], F32) for i in range(2)]
    ccol = to_col(C, st, pst, C.din['c'][:, :], 1, 8, 'c')
    S.act(lambda e: e.activation(out=ccol[:], in_=ccol[:], func=AF.Silu), reads=[ccol], writes=[ccol])
    rep = sb(nc, st, 'c_rep', [128, 8, 128], F32)
    for k in range(8):
        S.dve(lambda e, k=k: e.tensor_scalar(out=rep[:, k, :], in0=C.ones_f[:, 0:128], scalar1=ccol[:, k, 0:1],
                                             scalar2=None, op0=ALU.mult), reads=[C.ones_f, ccol], writes=[rep])
    brow = sb(nc, st, 'adab_row', [1, 6 * D], F32)
    S.dma(brow[:], C.din['ada_b'][layer:layer + 1, :], writes=[brow])
    if layer == 0:
        cccol = to_col(C, st, pst, C.din['c_ctx'][:, :], 1, 8, 'cc')
        S.act(lambda e: e.activation(out=cccol[:], in_=cccol[:], func=AF.Silu), reads=[cccol], writes=[cccol])
        repc = sb(nc, st, 'cc_rep', [128, 8, 128], F32)
        for k in range(8):
            S.dve(lambda e, k=k: e.tensor_scalar(out=repc[:, k, :], in0=C.ones_f[:, 0:128],
                                                 scalar1=cccol[:, k, 0:1], scalar2=None, op0=ALU.mult),
                  reads=[C.ones_f, cccol], writes=[repc])
    wt = [sb(nc, st, 'adaw%d' % i, [128, 8, 512], F32) for i in range(2)]
    aw = C.din['ada_w']
    for n in range(12):
        w = wt[n % 2]
        S.dma(w[:], aw[layer, :, n * 512:(n + 1) * 512].rearrange("(k p) n -> p k n", p=128), writes=[w])
        pa = pacc[n % 2]
        for k in range(8):
            S.pe(lambda e, k=k, w=w, pa=pa: e.matmul(pa[:], lhsT=rep[:, k, :], rhs=w[:, k, :], start=(k == 0), stop=False),
                 reads=[rep, w], writes=[pa], sig=False)
        S.pe(lambda e, pa=pa, n=n: e.matmul(pa[:], lhsT=C.ones_f[0:1, 0:128], rhs=brow[0:1, n * 512:(n + 1) * 512],
                                            start=False, stop=True), reads=[C.ones_f, brow], writes=[pa])
        S.act(lambda e, pa=pa, n=n: e.activation(out=C.modrow[:, n * 512:(n + 1) * 512], in_=pa[:], func=AF.Copy),
              reads=[pa], writes=[C.modrow])
        if layer == 0 and n < 4:
            pc = paccc[n % 2]
            for k in range(8):
                S.pe(lambda e, k=k, w=w, pc=pc: e.matmul(pc[:], lhsT=repc[:, k, :], rhs=w[:, k, :], start=(k == 0), stop=False),
                     reads=[repc, w], writes=[pc], sig=False)
            S.pe(lambda e, pc=pc, n=n: e.matmul(pc[:], lhsT=C.ones_f[0:1, 0:128], rhs=brow[0:1, n * 512:(n + 1) * 512],
                                                start=False, stop=True), reads=[C.ones_f, brow], writes=[pc])
            S.dve(lambda e, pc=pc, n=n: e.tensor_copy(out=C.modc[:, n * 512:(n + 1) * 512], in_=pc[:]),
                  reads=[pc], writes=[C.modc])
    S.dve(lambda e: e.tensor_scalar_add(out=C.modrow[:, D:2 * D], in0=C.modrow[:, D:2 * D], scalar1=1.0),
          reads=[C.modrow], writes=[C.modrow])
    S.dve(lambda e: e.tensor_scalar_add(out=C.modrow[:, 4 * D:5 * D], in0=C.modrow[:, 4 * D:5 * D], scalar1=1.0),
          reads=[C.modrow], writes=[C.modrow])
    if layer == 0:
        S.dve(lambda e: e.tensor_scalar_add(out=C.modc[:, D:2 * D], in0=C.modc[:, D:2 * D], scalar1=1.0),
              reads=[C.modc], writes=[C.modc])
    S.barrier()
    S.flush()
    st.close()


def modulate_transpose(C, xt, nsub, shift, scale1, ub, uT, pT, evac_i):
    S = C.S
    for s in range(nsub):
        S.pool(lambda e, s=s: e.tensor_tensor(out=xt[:, s, :], in0=xt[:, s, :], in1=scale1, op=ALU.mult),
               reads=[xt, C.modrow, C.modc], writes=[xt])
        S.dve(lambda e, s=s: e.tensor_tensor(out=ub[:, s, :], in0=xt[:, s, :], in1=shift, op=ALU.add),
              reads=[xt, C.modrow, C.modc], writes=[ub])
    for k in range(8):
        p = pT[k % len(pT)]
        for s in range(nsub):
            S.pe(lambda e, s=s, k=k, p=p: e.transpose(out=p[:, s * 128:(s + 1) * 128], in_=ub[:, s, k * 128:(k + 1) * 128],
                                                      identity=C.ident_b[:]),
                 reads=[ub, C.ident_b], writes=[p], sig=(s == nsub - 1))
        if (k + evac_i) % 2 == 0:
            S.act(lambda e, k=k, p=p: e.activation(out=uT[:, k, 0:nsub * 128], in_=p[:, 0:nsub * 128], func=AF.Copy),
                  reads=[p], writes=[uT])
        else:
            S.dve(lambda e, k=k, p=p: e.tensor_copy(out=uT[:, k, 0:nsub * 128], in_=p[:, 0:nsub * 128]),
                  reads=[p], writes=[uT])


def load_w_bf16(C, st, name, dram, kchunks, ncols):
    nc, S = C.nc, C.S
    w = sb(nc, st, name, [128, kchunks, ncols], BF16)
    SW = min(2048, ncols)
    wres = [Res(), Res()]
    NSTG = 3 if ncols > 1024 else 2
    stg = [sb(nc, st, name + '_stg%d' % i, [128, SW], F32) for i in range(NSTG)]
    i = 0
    for k in range(kchunks):
        for c0 in range(0, ncols, SW):
            c1 = min(ncols, c0 + SW)
            b = stg[i % NSTG]
            S.dma(b[:, 0:c1 - c0], dram[k * 128:(k + 1) * 128, c0:c1], writes=[b], q='sp')
            wr = wres[i % 2]
            if i % 2 == 0:
                S.dve(lambda e, b=b, k=k, c0=c0, c1=c1: e.tensor_copy(out=w[:, k, c0:c1], in_=b[:, 0:c1 - c0]), reads=[b], writes=[wr])
            else:
                S.act(lambda e, b=b, k=k, c0=c0, c1=c1: e.activation(out=w[:, k, c0:c1], in_=b[:, 0:c1 - c0], func=AF.Copy), reads=[b], writes=[wr])
            i += 1
    S.dve(lambda e: e.tensor_copy(out=w[:, 0, 0:1], in_=w[:, 0, 0:1]), reads=wres, writes=[w] + wres)
    return w


def phase_inproj0(C):
    nc, S = C.nc, C.S
    st = ExitStack()
    P0T = C.scratch('P0T', [2048, T + 16], BF16); C.P0T = P0T
    G0 = C.scratch('G0', [T, 512], BF16); C.G0 = G0
    SG = C.scratch('SG', [T + TC, 16], F32); C.SG = SG
    PCT = C.scratch('PCT', [1024, TC + 16], BF16); C.PCT = PCT
    w = load_w_bf16(C, st, 'w_in0', C.din['even_w_in'][:, :], 8, 2576)
    for r0 in range(0, 2048, 128):
        S.dma(P0T[r0:r0 + 128, 0:8], C.zeros_b[:, 0:8], reads=[C.zeros_b], writes=[P0T])
        S.dma(P0T[r0:r0 + 128, T + 8:T + 16], C.zeros_b[:, 0:8], reads=[C.zeros_b], writes=[P0T])
    for r0 in range(0, 1024, 128):
        S.dma(PCT[r0:r0 + 128, 0:8], C.zeros_b[:, 0:8], reads=[C.zeros_b], writes=[PCT])
        S.dma(PCT[r0:r0 + 128, TC + 8:TC + 16], C.zeros_b[:, 0:8], reads=[C.zeros_b], writes=[PCT])
    xt = [sb(nc, st, 'xt%d' % i, [128, 4, D], F32) for i in range(2)]
    ub = [sb(nc, st, 'ub%d' % i, [128, 4, D], BF16) for i in range(2)]
    uT = [sb(nc, st, 'uT%d' % i, [128, 8, 512], BF16) for i in range(2)]
    pstg = [sb(nc, st, 'pstg%d' % i, [128, 4, 512], BF16) for i in range(2)]
    gstg = [sb(nc, st, 'gstg%d' % i, [128, 4, 512], BF16) for i in range(2)]
    sstg = [sb(nc, st, 'sstg%d' % i, [128, 4, 16], F32) for i in range(2)]
    pT = [ps(nc, st, 'pT%d' % i, [128, 512], BF16) for i in range(2)]
    pm = [ps(nc, st, 'pm%d' % i, [128, 512], F32) for i in range(4)]
    pss = ps(nc, st, 'pss', [128, 4, 16], F32)
    x = C.din['x']
    tiles = [('ctx', 0)] + [('lat', i) for i in range(8)]

    def load(i):
        kind, t = tiles[i]
        b = xt[i % 2]
        if kind == 'ctx':
            S.dma(b[:, 0:2, :], C.din['ctx'][:, :].rearrange("(s p) d -> p s d", p=128), writes=[b])
        else:
            S.dma(b[:, :, :], x[t * 512:(t + 1) * 512, :].rearrange("(s p) d -> p s d", p=128), writes=[b])

    load(0)
    pmi = 0
    for i, (kind, t) in enumerate(tiles):
        if i + 1 < len(tiles):
            load(i + 1)
        b, u, ut = xt[i % 2], ub[i % 2], uT[i % 2]
        isctx = kind == 'ctx'
        nsub = 2 if isctx else 4
        ntok = nsub * 128
        if isctx:
            modulate_transpose(C, b, nsub, C.modc[:, 0:D], C.modc[:, D:2 * D], u, ut, pT, i)
            fchunks = list(range(4, 12))
        else:
            modulate_transpose(C, b, nsub, C.modrow[:, 0:D], C.modrow[:, D:2 * D], u, ut, pT, i)
            fchunks = list(range(0, 12)) + list(range(16, 20))
        for gi in range(0, len(fchunks), 4):
            grp = fchunks[gi:gi + 4]
            stg = pstg[(gi // 4) % 2]
            for j, fc in enumerate(grp):
                p = pm[pmi % 4]; pmi += 1
                for k in range(8):
                    S.pe(lambda e, p=p, k=k, fc=fc, ut=ut, ntok=ntok: e.matmul(
                        p[:, 0:ntok], lhsT=w[:, k, fc * 128:(fc + 1) * 128], rhs=ut[:, k, 0:ntok],
                        start=(k == 0), stop=(k == 7)), reads=[w, ut], writes=[p], sig=(k == 7))
                if j % 2 == 0:
                    S.act(lambda e, p=p, j=j, stg=stg, ntok=ntok: e.activation(out=stg[:, j, 0:ntok], in_=p[:, 0:ntok], func=AF.Copy),
                          reads=[p], writes=[stg])
                else:
                    S.dve(lambda e, p=p, j=j, stg=stg, ntok=ntok: e.tensor_copy(out=stg[:, j, 0:ntok], in_=p[:, 0:ntok]),
                          reads=[p], writes=[stg])
            if isctx:
                r0 = (grp[0] - 4) * 128
                dst = PCT[r0:r0 + 512, 8:8 + ntok].rearrange("(j p) n -> p j n", p=128)
                S.dma(dst, stg[:, :, 0:ntok], reads=[stg], writes=[PCT], q='act')
            else:
                fc0 = grp[0]
                r0 = fc0 * 128 if fc0 < 12 else (fc0 - 4) * 128
                dst = P0T[r0:r0 + 512, 8 + t * 512:8 + (t + 1) * 512].rearrange("(j p) n -> p j n", p=128)
                S.dma(dst, stg[:, :, :], reads=[stg], writes=[P0T], q='act')
        gs = gstg[i % 2]
        ss = sstg[i % 2]
        for s in range(nsub):
            if not isctx:
                p = pm[pmi % 4]; pmi += 1
                for k in range(8):
                    S.pe(lambda e, p=p, k=k, s=s, ut=ut: e.matmul(p[:], lhsT=ut[:, k, s * 128:(s + 1) * 128], rhs=w[:, k, 1536:2048],
                                                            start=(k == 0), stop=(k == 7)), reads=[w, ut], writes=[p], sig=(k == 7))
                S.act(lambda e, p=p, s=s, gs=gs: e.activation(out=gs[:, s, :], in_=p[:], func=AF.Silu), reads=[p], writes=[gs])
            for k in range(8):
                S.pe(lambda e, k=k, s=s, ut=ut: e.matmul(pss[:, s, :], lhsT=ut[:, k, s * 128:(s + 1) * 128], rhs=w[:, k, 2560:2576],
                                                       start=(k == 0), stop=(k == 7)), reads=[w, ut], writes=[pss], sig=(k == 7))
        S.dve(lambda e, ss=ss, nsub=nsub: e.tensor_copy(out=ss[:, 0:nsub, :], in_=pss[:, 0:nsub, :]), reads=[pss], writes=[ss])
        if isctx:
            S.dma(SG[T:T + TC, :].rearrange("(s p) c -> p s c", p=128), ss[:, 0:2, :], reads=[ss], writes=[SG], q='act')
        else:
            S.dma(SG[t * 512:(t + 1) * 512, :].rearrange("(s p) c -> p s c", p=128), ss[:, :, :], reads=[ss], writes=[SG], q='act')
            S.dma(G0[t * 512:(t + 1) * 512, :].rearrange("(s p) c -> p s c", p=128), gs[:, :, :], reads=[gs], writes=[G0], q='act')
    S.barrier()
    S.flush()
    st.close()


def build_diag(C, st, psb, dram, R, ncol, name):
    nc, S = C.nc, C.S
    cw = to_col(C, st, psb, dram, R, ncol, name)
    dg = sb(nc, st, name + '_dg', [128, ncol, R, 128], BF16)
    dgres = [Res() for _ in range(ncol)]
    dg.chunk_res = dgres
    i = 0
    for c in range(ncol):
        for r in range(R):
            dres = dgres[c]
            if i % 2 == 0:
                S.dve(lambda e, c=c, r=r: e.tensor_scalar(out=dg[:, c, r, :], in0=C.ident_b[:], scalar1=cw[:, c, r:r + 1],
                                                          scalar2=None, op0=ALU.mult), reads=[C.ident_b, cw], writes=[dres])
            else:
                S.act(lambda e, c=c, r=r: e.activation(out=dg[:, c, r, :], in_=C.ident_b[:], func=AF.Copy, scale=cw[:, c, r:r + 1]),
                      reads=[C.ident_b, cw], writes=[dres])
            i += 1
    S.dve(lambda e: e.tensor_copy(out=dg[:, 0, 0, 0:1], in_=dg[:, 0, 0, 0:1]), reads=dgres, writes=[dg] + dgres)
    return dg


def phase_qkv0(C):
    nc, S = C.nc, C.S
    st = ExitStack()
    QT = C.scratch('QT', [512, T], BF16); C.QT = QT
    KT = C.scratch('KT', [512, T + TC], BF16); C.KT = KT
    QTOK = C.scratch('QTOK', [T, 512], BF16); C.QTOK = QTOK
    KTOK = C.scratch('KTOK', [T + TC, 512], BF16); C.KTOK = KTOK
    VTOK = C.scratch('VTOK', [T + TC, 512], BF16); C.VTOK = VTOK
    YPT = C.scratch('YPT', [512, T], BF16); C.YPT = YPT
    pconv = [ps(nc, st, 'pconv%d' % i, [128, 512], F32) for i in range(2)]
    pssq = [ps(nc, st, 'pssq%d' % i, [128, 512], F32) for i in range(2)]
    pT = [ps(nc, st, 'pTq%d' % i, [128, 512], BF16) for i in range(2)]
    ppool = ps(nc, st, 'ppool', [128, 512], F32)
    dg = build_diag(C, st, pconv[0], C.din['gdn_conv_w'][:, :], 5, 12, 'cw5')
    pscale = to_col(C, st, pconv[1], C.din['pool_scale'][:, :], 1, 4, 'pscale')
    poolw = sb(nc, st, 'poolw', [128, 4, 128], BF16)
    S.dma(poolw[:], C.din['pool_w'][:, :, :].rearrange("g c d -> c g d"), writes=[poolw], q='pool')
    corrF = sb(nc, st, 'corrF', [128, 4, 8], F32)
    corrL = sb(nc, st, 'corrL', [128, 4, 8], F32)
    S.pool(lambda e: e.memset(corrF[:], 1.0), writes=[corrF])
    S.pool(lambda e: e.memset(corrL[:], 1.0), writes=[corrL])
    for g in range(4):
        hw = 1 << g
        for j in range(hw):
            S.pool(lambda e, g=g, j=j, hw=hw: e.memset(corrF[:, g, j:j + 1], 2.0 * hw / (j + hw)), writes=[corrF])
        for m in range(hw - 1):
            S.pool(lambda e, g=g, m=m, hw=hw: e.memset(corrL[:, g, 7 - m:8 - m], 2.0 * hw / (1 + m + hw)), writes=[corrL])
    pin = [sb(nc, st, 'pin%d' % i, [128, 12, 516], BF16) for i in range(2)]
    pp = [sb(nc, st, 'pp%d' % i, [128, 4, 528], BF16) for i in range(2)]
    xs8 = sb(nc, st, 'xs8', [128, 8, 512], F32)
    ss8 = sb(nc, st, 'ss8', [128, 8, 512], F32)
    epsb = sb(nc, st, 'epsb', [128, 1], F32)
    S.pool(lambda e: e.memset(epsb[:], RMS_EPS), writes=[epsb])
    sqb = [sb(nc, st, 'sqb%d' % i, [128, 512], BF16) for i in range(2)]
    qkn = [sb(nc, st, 'qkn0', [128, 12, 512], BF16)] * 2
    tokst = [[sb(nc, st, 'tok%d_%d' % (g, i), [128, 4, 512], BF16) for i in range(2)] for g in range(3)]
    wa = [sb(nc, st, 'wa%d' % i, [128, 528], F32) for i in range(2)]
    wb = [sb(nc, st, 'wb%d' % i, [128, 528], F32) for i in range(2)]
    pld = [sb(nc, st, 'pld%d' % i, [128, 4, 512], BF16) for i in range(2)]
    ypst = [sb(nc, st, 'ypst%d' % i, [128, 4, 512], BF16) for i in range(2)]
    tiles = [('ctx', 0)] + [('lat', i) for i in range(8)]

    def load(i):
        kind, t = tiles[i]
        b = pin[i % 2]
        if kind == 'ctx':
            S.dma(b[:, 4:12, 0:260], C.PCT[:, 6:266].rearrange("(f p) n -> p f n", p=128), reads=[C.PCT], writes=[b])
        else:
            S.dma(b[:, :, :], C.P0T[0:1536, 6 + t * 512:6 + t * 512 + 516].rearrange("(f p) n -> p f n", p=128),
                  reads=[C.P0T], writes=[b])
            S.dma(pp[i % 2][:, :, :], C.P0T[1536:2048, t * 512:t * 512 + 528].rearrange("(f p) n -> p f n", p=128),
                  reads=[C.P0T], writes=[pp[i % 2]])

    load(0)
    ci = 0
    for i, (kind, t) in enumerate(tiles):
        if i + 1 < len(tiles):
            load(i + 1)
        isctx = kind == 'ctx'
        ntok = 256 if isctx else 512
        nsub = ntok // 128
        b = pin[i % 2]
        qk = qkn[i % 2]
        for fc in (range(4, 12) if isctx else range(12)):
            pc = pconv[ci % 2]
            sq_ = sqb[ci % 2]; pq = pssq[ci % 2]
            ci += 1
            for tap in range(5):
                S.pe(lambda e, pc=pc, fc=fc, tap=tap, b=b, ntok=ntok: e.matmul(
                    pc[:, 0:ntok], lhsT=dg[:, fc, tap, :], rhs=b[:, fc, tap:tap + ntok], start=(tap == 0), stop=(tap == 4)),
                    reads=[dg, b], writes=[pc], sig=(tap == 4))
            if fc >= 8:
                S.act(lambda e, pc=pc, fc=fc, qk=qk, ntok=ntok: e.activation(out=qk[:, fc, 0:ntok], in_=pc[:, 0:ntok], func=AF.Silu),
                      reads=[pc], writes=[qk])
                continue
            S.act(lambda e, pc=pc, fc=fc, ntok=ntok: e.activation(out=xs8[:, fc, 0:ntok], in_=pc[:, 0:ntok], func=AF.Silu),
                  reads=[pc], writes=[xs8])
            S.act(lambda e, fc=fc, sq_=sq_, ntok=ntok: e.activation(out=sq_[:, 0:ntok], in_=xs8[:, fc, 0:ntok], func=AF.Square),
                  reads=[xs8], writes=[sq_])
            S.pe(lambda e, pq=pq, sq_=sq_, ntok=ntok: e.matmul(pq[:, 0:ntok], lhsT=C.ones_b[:], rhs=sq_[:, 0:ntok], start=True, stop=True),
                 reads=[C.ones_b, sq_], writes=[pq])
            S.dve(lambda e, pq=pq, fc=fc, ntok=ntok: e.tensor_copy(out=ss8[:, fc, 0:ntok], in_=pq[:, 0:ntok]), reads=[pq], writes=[ss8])
        f0 = 4 if isctx else 0
        S.act(lambda e, f0=f0, ntok=ntok: e.activation(out=ss8[:, f0:8, 0:ntok], in_=ss8[:, f0:8, 0:ntok], func=AF.Ln, bias=epsb[:, 0:1]),
              reads=[ss8, epsb], writes=[ss8])
        S.act(lambda e, f0=f0, ntok=ntok: e.activation(out=ss8[:, f0:8, 0:ntok], in_=ss8[:, f0:8, 0:ntok], func=AF.Exp, scale=-0.5),
              reads=[ss8], writes=[ss8])
        for fc in range(f0, 8):
            sc = (128.0 ** -0.5) if fc < 4 else 1.0
            fn = lambda e, qk=qk, fc=fc, sc=sc, ntok=ntok: e.scalar_tensor_tensor(
                out=qk[:, fc, 0:ntok], in0=xs8[:, fc, 0:ntok], scalar=sc, in1=ss8[:, fc, 0:ntok], op0=ALU.mult, op1=ALU.mult)
            if fc < 4:
                S.dve(fn, reads=[xs8, ss8], writes=[qk])
            elif fc % 2 == 0:
                S.dve(lambda e, qk=qk, fc=fc, ntok=ntok: e.tensor_tensor(out=qk[:, fc, 0:ntok], in0=xs8[:, fc, 0:ntok],
                                                                      in1=ss8[:, fc, 0:ntok], op=ALU.mult),
                      reads=[xs8, ss8], writes=[qk])
            else:
                S.pool(lambda e, qk=qk, fc=fc, ntok=ntok: e.tensor_tensor(out=qk[:, fc, 0:ntok], in0=xs8[:, fc, 0:ntok],
                                                                       in1=ss8[:, fc, 0:ntok], op=ALU.mult),
                       reads=[xs8, ss8], writes=[qk])
        ti = 0
        for g in ((1, 2) if isctx else (0, 1, 2)):
            tk = tokst[g][i % 2]
            for s_ in range(nsub):
                p = pT[ti % 2]; ti += 1
                for h in range(4):
                    S.pe(lambda e, p=p, h=h, g=g, s_=s_, qk=qk: e.transpose(out=p[:, h * 128:(h + 1) * 128],
                                                                       in_=qk[:, g * 4 + h, s_ * 128:(s_ + 1) * 128], identity=C.ident_b[:]),
                         reads=[qk, C.ident_b], writes=[p], sig=(h == 3))
                if ti % 2 == 0:
                    S.act(lambda e, p=p, tk=tk, s_=s_: e.activation(out=tk[:, s_, :], in_=p[:], func=AF.Copy), reads=[p], writes=[tk])
                else:
                    S.dve(lambda e, p=p, tk=tk, s_=s_: e.tensor_copy(out=tk[:, s_, :], in_=p[:]), reads=[p], writes=[tk])
        c0 = T if isctx else t * 512
        if not isctx:
            S.dma(QT[:, c0:c0 + 512].rearrange("(f p) n -> p f n", p=128), qk[:, 0:4, :], reads=[qk], writes=[QT], q='act')
            S.dma(QTOK[c0:c0 + 512, :].rearrange("(s p) c -> p s c", p=128), tokst[0][i % 2][:, :, :], reads=[tokst[0][i % 2]], writes=[QTOK], q='act')
        S.dma(KT[:, c0:c0 + ntok].rearrange("(f p) n -> p f n", p=128), qk[:, 4:8, 0:ntok], reads=[qk], writes=[KT], q='act')
        S.dma(KTOK[c0:c0 + ntok, :].rearrange("(s p) c -> p s c", p=128), tokst[1][i % 2][:, 0:nsub, :], reads=[tokst[1][i % 2]], writes=[KTOK], q='act')
        S.dma(VTOK[c0:c0 + ntok, :].rearrange("(s p) c -> p s c", p=128), tokst[2][i % 2][:, 0:nsub, :], reads=[tokst[2][i % 2]], writes=[VTOK], q='act')
        if isctx:
            continue
        ppb = pp[i % 2]
        pl = pld[i % 2]
        yp = ypst[i % 2]
        for g in range(4):
            a_, b_ = wa[g % 2], wb[g % 2]
            eng_add = S.dve if g >= 2 else S.pool
            eng_add(lambda e, a_=a_, g=g, ppb=ppb: e.tensor_tensor(out=a_[:, 1:527], in0=ppb[:, g, 0:526], in1=ppb[:, g, 1:527], op=ALU.add),
                    reads=[ppb], writes=[a_])
            cur, oth = a_, b_
            lo, hi = 1, 527
            for lvl in range(g):
                sh = 1 << lvl
                lo, hi = lo + sh, hi - sh
                eng_add(lambda e, cur=cur, oth=oth, lo=lo, hi=hi, sh=sh: e.tensor_tensor(
                    out=oth[:, lo:hi], in0=cur[:, lo - sh:hi - sh], in1=cur[:, lo + sh:hi + sh], op=ALU.add),
                    reads=[cur], writes=[oth])
                cur, oth = oth, cur
            S.dve(lambda e, cur=cur, g=g: e.tensor_scalar(out=cur[:, 8:520], in0=cur[:, 8:520], scalar1=1.0 / (2 << g), scalar2=None, op0=ALU.mult),
                  reads=[cur], writes=[cur])
            if t == 0:
                S.dve(lambda e, cur=cur, g=g: e.tensor_tensor(out=cur[:, 8:16], in0=cur[:, 8:16], in1=corrF[:, g, :], op=ALU.mult),
                      reads=[cur, corrF], writes=[cur])
            if t == 7:
                S.dve(lambda e, cur=cur, g=g: e.tensor_tensor(out=cur[:, 512:520], in0=cur[:, 512:520], in1=corrL[:, g, :], op=ALU.mult),
                      reads=[cur, corrL], writes=[cur])
            S.dve(lambda e, cur=cur, g=g, pl=pl, ppb=ppb: e.tensor_tensor(out=pl[:, g, :], in0=cur[:, 8:520], in1=ppb[:, g, 8:520], op=ALU.subtract),
                  reads=[cur, ppb], writes=[pl])
            S.pe(lambda e, g=g, pl=pl: e.matmul(ppool[:], lhsT=poolw[:, g, :], rhs=pl[:, g, :], start=True, stop=True),
                 reads=[poolw, pl], writes=[ppool])
            S.act(lambda e, g=g, yp=yp: e.activation(out=yp[:, g, :], in_=ppool[:], func=AF.Identity, scale=pscale[:, g, 0:1]),
                  reads=[ppool, pscale], writes=[yp])
        S.dma(YPT[:, c0:c0 + 512].rearrange("(f p) n -> p f n", p=128), yp[:, :, :], reads=[yp], writes=[YPT], q='act')
    S.barrier()
    S.flush()
    st.close()


class Slot:
    def __init__(self, bank, k):
        self.f = bank.t[:, k * 128:(k + 1) * 128]
        self.b = bank.t[:, :].bitcast(BF16)[:, k * 256:k * 256 + 128]
        self.res = bank.res


def run_interleaved(gens):
    gens = list(gens)
    while gens:
        nxt = []
        for g in gens:
            try:
                next(g)
                nxt.append(g)
            except StopIteration:
                pass
        gens = nxt


def phase_gdn(C):
    nc, S = C.nc, C.S
    st = ExitStack()
    NT = 34
    import os
    OACC = C.scratch('OACC', [T, 512], F32); C.OACC = OACC
    banks = [ps(nc, st, 'gbank%d' % i, [128, 512], F32) for i in range(8)]
    for b_ in banks:
        b_.res.excl = True
    slots = [[Slot(banks[c], k) for k in range(4)] for c in range(8)]
    sall = sb(nc, st, 'sall', [128, NT, 16], F32)
    for n0 in ([] if os.environ.get('NOSALL') == '1' else range(0, NT, 6)):
        n1 = min(NT, n0 + 6)
        S.dma(sall[:, n0:n1, :], C.SG[n0 * 128:n1 * 128, :].rearrange("(n p) c -> p n c", p=128), reads=[C.SG], writes=[sall])
    adb = sb(nc, st, 'adb', [128, 16], F32)
    S.dma(adb[:, 0:8], C.din['gdn_a_log'][0:1, :].to_broadcast([128, 8]), writes=[adb])
    S.dma(adb[:, 8:16], C.din['gdn_dt_bias'][0:1, :].to_broadcast([128, 8]), writes=[adb])
    S.act(lambda e: e.activation(out=adb[:, 0:8], in_=adb[:, 0:8], func=AF.Exp), reads=[adb], writes=[adb])
    S.dve(lambda e: e.tensor_scalar(out=adb[:, 0:8], in0=adb[:, 0:8], scalar1=-1.0, scalar2=None, op0=ALU.mult),
          reads=[adb], writes=[adb])

    GCUT = int(os.environ.get('GCUT', '0'))

    def fin():
        S.barrier(); S.flush(); st.close()
    if GCUT == 1:
        return fin()

    def gt(name):
        return sb(nc, st, name, [128, NT, 8], F32)
    beta, g_, gc, eg, be, kds, gl = gt('g_beta'), gt('g_g'), gt('g_gc'), gt('g_eg'), gt('g_be'), gt('g_kds'), gt('g_gl')
    S.act(lambda e: e.activation(out=beta[:], in_=sall[:, :, 0:8], func=AF.Sigmoid), reads=[sall], writes=[beta])
    S.dve(lambda e: e.tensor_tensor(out=g_[:], in0=sall[:, :, 8:16], in1=adb[:, 8:16].unsqueeze(1).to_broadcast([128, NT, 8]), op=ALU.add),
          reads=[sall, adb], writes=[g_])
    S.act(lambda e: e.activation(out=g_[:], in_=g_[:], func=AF.Exp), reads=[g_], writes=[g_])
    S.act(lambda e: e.activation(out=g_[:], in_=g_[:], func=AF.Ln, bias=1.0), reads=[g_], writes=[g_])
    S.dve(lambda e: e.tensor_tensor(out=g_[:], in0=g_[:], in1=adb[:, 0:8].unsqueeze(1).to_broadcast([128, NT, 8]), op=ALU.mult),
          reads=[g_, adb], writes=[g_])
    if GCUT == 2:
        return fin()
    Lt = sb(nc, st, 'Lt', [128, 128], F32)
    Ut = sb(nc, st, 'Ut', [128, 128], F32)
    bigm = [sb(nc, st, 'bigm%d' % i, [128, 128], F32) for i in range(2)]
    strict = [sb(nc, st, 'strict%d' % i, [128, 128], F32) for i in range(2)]
    bigfull = sb(nc, st, 'bigfull', [128, 128], F32)
    S.pool(lambda e: e.memset(bigfull[:], BIG), writes=[bigfull])
    one = C.ones_f[:, 0:128]
    S.pool(lambda e: e.affine_select(out=Lt[:], in_=one, pattern=[[1, 128]], compare_op=ALU.is_ge, fill=0.0, base=0, channel_multiplier=-1),
           reads=[C.ones_f], writes=[Lt])
    S.pool(lambda e: e.affine_select(out=Ut[:], in_=one, pattern=[[-1, 128]], compare_op=ALU.is_ge, fill=0.0, base=0, channel_multiplier=1),
           reads=[C.ones_f], writes=[Ut])
    S.pool(lambda e: e.affine_select(out=bigm[0][:], in_=bigfull[:], pattern=[[1, 128]], compare_op=ALU.is_gt, fill=0.0, base=0, channel_multiplier=-1),
           reads=[bigfull], writes=[bigm[0]])
    S.pool(lambda e: e.affine_select(out=bigm[1][:], in_=bigfull[:], pattern=[[-1, 128]], compare_op=ALU.is_gt, fill=0.0, base=0, channel_multiplier=1),
           reads=[bigfull], writes=[bigm[1]])
    S.pool(lambda e: e.affine_select(out=strict[0][:], in_=one, pattern=[[-1, 128]], compare_op=ALU.is_gt, fill=0.0, base=0, channel_multiplier=1),
           reads=[C.ones_f], writes=[strict[0]])
    S.pool(lambda e: e.affine_select(out=strict[1][:], in_=one, pattern=[[1, 128]], compare_op=ALU.is_gt, fill=0.0, base=0, channel_multiplier=-1),
           reads=[C.ones_f], writes=[strict[1]])
    if GCUT == 3:
        return fin()
    Bm = {}
    for s_ in (16, 32, 64):
        G = 128 // s_
        E = sb(nc, st, 'E%d' % s_, [G, 128], F32)
        S.pool(lambda e, E=E, G=G, s_=s_: e.affine_select(out=E[:], in_=C.ones_f[0:G, 0:128], pattern=[[1, 128]], compare_op=ALU.is_ge,
                                                         fill=0.0, base=0, channel_multiplier=-s_), reads=[C.ones_f], writes=[E])
        S.pool(lambda e, E=E, G=G, s_=s_: e.affine_select(out=E[:], in_=E[:], pattern=[[-1, 128]], compare_op=ALU.is_gt,
                                                         fill=0.0, base=s_, channel_multiplier=s_), reads=[E], writes=[E])
        pb_ = banks[4]
        S.pe(lambda e, E=E, pb_=pb_: e.matmul(pb_[:, 0:128], lhsT=E[:], rhs=E[:], start=True, stop=True), reads=[E], writes=[pb_])
        Bm[s_] = sb(nc, st, 'Bm%d' % s_, [128, 128], F32)
        S.dve(lambda e, s_=s_, pb_=pb_: e.tensor_copy(out=Bm[s_][:], in_=pb_[:, 0:128]), reads=[pb_], writes=[Bm[s_]])
    Md = [sb(nc, st, 'Md%d' % d, [128, 128], F32) for d in range(2)]
    Mo = [[sb(nc, st, 'Mo%d_%d' % (d, l), [128, 128], F32) for l in range(3)] for d in range(2)]
    for d in range(2):
        S.dve(lambda e, d=d: e.tensor_tensor(out=Md[d][:], in0=strict[d][:], in1=Bm[16][:], op=ALU.mult), reads=[strict[d], Bm[16]], writes=[Md[d]])
        for l, (big_, small_) in enumerate(((32, 16), (64, 32), (None, 64))):
            t_ = Mo[d][l]
            if big_ is None:
                S.dve(lambda e, t_=t_, small_=small_: e.tensor_scalar(out=t_[:], in0=Bm[small_][:], scalar1=-1.0, scalar2=1.0, op0=ALU.mult, op1=ALU.add),
                      reads=[Bm[small_]], writes=[t_])
            else:
                S.dve(lambda e, t_=t_, big_=big_, small_=small_: e.tensor_tensor(out=t_[:], in0=Bm[big_][:], in1=Bm[small_][:], op=ALU.subtract),
                      reads=[Bm[big_], Bm[small_]], writes=[t_])
            S.dve(lambda e, t_=t_, d=d: e.tensor_tensor(out=t_[:], in0=t_[:], in1=strict[d][:], op=ALU.mult), reads=[t_, strict[d]], writes=[t_])
    pgc = banks[1]
    S.pe(lambda e: e.matmul(pgc[:, 0:NT * 8], lhsT=Lt[:], rhs=g_[:, :, :], start=True, stop=True), reads=[Lt, g_], writes=[pgc])
    S.dve(lambda e: e.tensor_copy(out=gc[:, :, 0:4], in_=pgc[:, 0:NT * 8].rearrange("p (n c) -> p n c", c=8)[:, :, 0:4]), reads=[pgc], writes=[gc])
    pgc2 = banks[2]
    S.pe(lambda e: e.matmul(pgc2[:, 0:NT * 8], lhsT=Ut[:], rhs=g_[:, :, :], start=True, stop=True), reads=[Ut, g_], writes=[pgc2])
    S.dve(lambda e: e.tensor_copy(out=gc[:, :, 4:8], in_=pgc2[:, 0:NT * 8].rearrange("p (n c) -> p n c", c=8)[:, :, 4:8]), reads=[pgc2], writes=[gc])
    if GCUT == 5:
        return fin()
    pgt = banks[3]
    S.pe(lambda e: e.matmul(pgt[:, 0:NT * 8], lhsT=C.ones_f[:, 0:128], rhs=g_[:, :, :], start=True, stop=True), reads=[C.ones_f, g_], writes=[pgt])
    S.act(lambda e: e.activation(out=gl[:], in_=pgt[:, 0:NT * 8].rearrange("p (n c) -> p n c", c=8), func=AF.Exp), reads=[pgt], writes=[gl])
    S.dve(lambda e: e.tensor_tensor(out=kds[:], in0=pgt[:, 0:NT * 8].rearrange("p (n c) -> p n c", c=8), in1=gc[:], op=ALU.subtract),
          reads=[pgt, gc], writes=[kds])
    if GCUT == 6:
        return fin()
    S.act(lambda e: e.activation(out=kds[:], in_=kds[:], func=AF.Exp), reads=[kds], writes=[kds])
    S.act(lambda e: e.activation(out=eg[:], in_=gc[:], func=AF.Exp), reads=[gc], writes=[eg])
    S.dve(lambda e: e.tensor_tensor(out=be[:], in0=beta[:], in1=eg[:], op=ALU.mult), reads=[beta, eg], writes=[be])
    if GCUT == 4:
        return fin()
    dbg_dump(C, 'dbg_gc', gc, gc[:, :, :], [128, NT, 8], F32)
    dbg_dump(C, 'dbg_beta', beta, beta[:, :, :], [128, NT, 8], F32)
    dbg_dump(C, 'dbg_g', g_, g_[:, :, :], [128, NT, 8], F32)

    S.barrier()
    import os
    GSTOP = int(os.environ.get('GSTOP', '99'))
    def tile_of(d, n):
        if n < 2:
            return 32 + n if d == 0 else 33 - n
        return n - 2 if d == 0 else 33 - n
    opnd = [[{k: sb(nc, st, 'op_%s_%d_%d' % (k, d, i), [128, 4, 128], BF16) for k in ('kT', 'qT', 'ktok', 'qtok', 'vtok')}
             for i in range(2)] for d in range(2)]

    def load_tile(d, n):
        nt = tile_of(d, n)
        o = opnd[d][n % 2]
        c0 = T + (nt - 32) * 128 if nt >= 32 else nt * 128
        S.dma(o['kT'][:, :, :], C.KT[:, c0:c0 + 128].rearrange("(h p) n -> p h n", p=128), reads=[C.KT], writes=[o['kT']])
        S.dma(o['ktok'][:, :, :], C.KTOK[c0:c0 + 128, :].rearrange("p (h d) -> p h d", d=128), reads=[C.KTOK], writes=[o['ktok']])
        S.dma(o['vtok'][:, :, :], C.VTOK[c0:c0 + 128, :].rearrange("p (h d) -> p h d", d=128), reads=[C.VTOK], writes=[o['vtok']])
        if nt < 32:
            S.dma(o['qT'][:, :, :], C.QT[:, c0:c0 + 128].rearrange("(h p) n -> p h n", p=128), reads=[C.QT], writes=[o['qT']])
            S.dma(o['qtok'][:, :, :], C.QTOK[c0:c0 + 128, :].rearrange("p (h d) -> p h d", d=128), reads=[C.QTOK], writes=[o['qtok']])

    def cb(name, dt, n=1, shape=(128, 128)):
        return [[sb(nc, st, '%s_%d_%d' % (name, c, i), list(shape), dt) for i in range(n)] for c in range(8)]
    dgc = cb('dgc', F32); Dm = dgc; Ai = cb('Ai', F32)
    Pb = cb('Pb', BF16, 2); PTb = cb('PTb', BF16, 2); Yb = cb('Yb', BF16, 2)
    bv = cb('bv', BF16); kbe = cb('kbe', BF16); qe = cb('qe', BF16); AOb = cb('AOb', BF16, 3)
    attnT = cb('attnT', BF16, 2); u_ = cb('u_', F32, 2); wT = cb('wT', BF16, 2); kd = cb('kd', BF16, 2); qdT = cb('qdT', BF16, 2)
    S32 = cb('S32', F32); Sbf = cb('Sbf', BF16, 2); vn = cb('vn', BF16)
    for c in range(8):
        S.pool(lambda e, c=c: e.memset(S32[c][0][:], 0.0), writes=[S32[c][0]])
        S.pool(lambda e, c=c: e.memset(Sbf[c][0][:], 0.0), writes=[Sbf[c][0]])
    oacc = sb(nc, st, 'oacc', [128, 32, 512], F32)
    ores = [[Res() for h in range(4)] for nt in range(32)]
    ofirst = [[True] * 4 for nt in range(32)]

    def precompute(c, n):
        d, h = c // 4, c % 4
        nt = tile_of(d, n)
        lat = nt < 32
        o = opnd[d][n % 2]
        r = n % 2
        sl = slots[c]
        gcol = gc[:, nt, c:c + 1]
        S.act(lambda e: e.activation(out=dgc[c][0][:], in_=C.ident_f[:], func=AF.Copy, scale=gcol),
              reads=[C.ident_f, gc], writes=[dgc[c][0]])
        S.pe(lambda e: e.matmul(sl[1].f, lhsT=o['kT'][:, h, :], rhs=o['kT'][:, h, :], start=True, stop=True),
             reads=[o['kT']], writes=[sl[1]])
        if lat:
            S.pe(lambda e: e.matmul(sl[2].f, lhsT=o['qT'][:, h, :], rhs=o['kT'][:, h, :], start=True, stop=True),
                 reads=[o['kT'], o['qT']], writes=[sl[2]])
        yield
        S.pe(lambda e: e.matmul(sl[0].f, lhsT=C.ones_f[:, 0:128], rhs=dgc[c][0][:], start=True, stop=False),
             reads=[C.ones_f, dgc[c][0]], writes=[sl[0]], sig=False)
        S.pe(lambda e: e.matmul(sl[0].f, lhsT=C.ident_f[:], rhs=bigm[d][:], start=False, stop=True),
             reads=[C.ident_f, bigm[d]], writes=[sl[0]])
        yield
        S.act(lambda e: e.activation(out=Dm[c][0][:], in_=sl[0].f, func=AF.Exp, bias=gcol, scale=-1.0),
              reads=[sl[0], gc], writes=[Dm[c][0]])
        yield
        S.dve(lambda e: e.scalar_tensor_tensor(out=Ai[c][0][:], in0=sl[1].f, scalar=beta[:, nt, c:c + 1], in1=Dm[c][0][:],
                                               op0=ALU.mult, op1=ALU.mult), reads=[sl[1], beta, Dm[c][0]], writes=[Ai[c][0]])
        if lat:
            S.dve(lambda e: e.tensor_tensor(out=qe[c][0][:], in0=sl[2].f, in1=Dm[c][0][:], op=ALU.mult),
                  reads=[sl[2], Dm[c][0]], writes=[qe[c][0]])
        yield
        A = Pb[c][0]
        S.dve(lambda e: e.tensor_tensor(out=A[:], in0=Ai[c][0][:], in1=Md[d][:], op=ALU.mult),
              reads=[Ai[c][0], Md[d]], writes=[A])
        for li in range(3):
            fn = lambda e, li=li: e.tensor_tensor(out=AOb[c][li][:], in0=Ai[c][0][:], in1=Mo[d][li][:], op=ALU.mult)
            if li < 2:
                S.dve(fn, reads=[Ai[c][0], Mo[d][li]], writes=[AOb[c][li]])
            else:
                S.pool(fn, reads=[Ai[c][0], Mo[d][li]], writes=[AOb[c][li]])
        yield
        S.pe(lambda e: e.transpose(out=sl[0].b, in_=A[:], identity=C.ident_b[:]), reads=[A, C.ident_b], writes=[sl[0]])
        if lat:
            S.pe(lambda e: e.transpose(out=sl[1].b, in_=qe[c][0][:], identity=C.ident_b[:]), reads=[qe[c][0], C.ident_b], writes=[sl[1]])
        yield
        AT = PTb[c][0]
        Y = Yb[c][0]
        S.act(lambda e: e.activation(out=AT[:], in_=sl[0].b, func=AF.Copy), reads=[sl[0]], writes=[AT])
        S.dve(lambda e: e.scalar_tensor_tensor(out=Y[:], in0=sl[0].b, scalar=-1.0, in1=C.ident_b[:], op0=ALU.mult, op1=ALU.add),
              reads=[sl[0], C.ident_b], writes=[Y])
        if lat:
            S.act(lambda e: e.activation(out=attnT[c][r][:], in_=sl[1].b, func=AF.Copy), reads=[sl[1]], writes=[attnT[c][r]])
        yield
        S.act(lambda e: e.activation(out=bv[c][0][:], in_=o['vtok'][:, h, :], func=AF.Copy, scale=beta[:, nt, c:c + 1]),
              reads=[o['vtok'], beta], writes=[bv[c][0]])
        S.act(lambda e: e.activation(out=kbe[c][0][:], in_=o['ktok'][:, h, :], func=AF.Copy, scale=be[:, nt, c:c + 1]),
              reads=[o['ktok'], be], writes=[kbe[c][0]])
        S.pool(lambda e: e.tensor_scalar(out=kd[c][r][:], in0=o['ktok'][:, h, :], scalar1=kds[:, nt, c:c + 1], scalar2=None, op0=ALU.mult),
               reads=[o['ktok'], kds], writes=[kd[c][r]])
        if lat:
            S.act(lambda e: e.activation(out=qe[c][0][:], in_=o['qtok'][:, h, :], func=AF.Copy, scale=eg[:, nt, c:c + 1]),
                  reads=[o['qtok'], eg], writes=[qe[c][0]])
        cur = 0
        for lvl in range(1, 4):
            P, PT, Yc = Pb[c][cur], PTb[c][cur], Yb[c][cur]
            Pn, PTn, Yn = Pb[c][1 - cur], PTb[c][1 - cur], Yb[c][1 - cur]
            S.pe(lambda e, P=P, PT=PT: e.matmul(sl[0].f, lhsT=PT[:], rhs=P[:], start=True, stop=True), reads=[P, PT], writes=[sl[0]])
            if lvl < 3:
                S.pe(lambda e, P=P, PT=PT: e.matmul(sl[1].f, lhsT=P[:], rhs=PT[:], start=True, stop=True), reads=[P, PT], writes=[sl[1]])
            yield
            S.act(lambda e, Pn=Pn: e.activation(out=Pn[:], in_=sl[0].f, func=AF.Copy), reads=[sl[0]], writes=[Pn])
            if lvl < 3:
                S.dve(lambda e, PTn=PTn: e.tensor_copy(out=PTn[:], in_=sl[1].f), reads=[sl[1]], writes=[PTn])
            yield
            S.pe(lambda e, Pn=Pn, Yc=Yc: e.matmul(sl[2].f, lhsT=Pn[:], rhs=Yc[:], start=True, stop=True), reads=[Pn, Yc], writes=[sl[2]])
            yield
            S.dve(lambda e, Yc=Yc, Yn=Yn: e.tensor_tensor(out=Yn[:], in0=sl[2].f, in1=Yc[:], op=ALU.add), reads=[sl[2], Yc], writes=[Yn])
            yield
            cur = 1 - cur
        for li in range(3):
            Yc, Yn = Yb[c][cur], Yb[c][1 - cur]
            Tt, N1 = Pb[c][0], PTb[c][0]
            S.pe(lambda e, Yc=Yc: e.transpose(out=sl[0].b, in_=Yc[:], identity=C.ident_b[:]), reads=[Yc, C.ident_b], writes=[sl[0]])
            S.pe(lambda e, Yc=Yc, li=li: e.matmul(sl[1].f, lhsT=AOb[c][li][:], rhs=Yc[:], start=True, stop=True),
                 reads=[AOb[c][li], Yc], writes=[sl[1]])
            yield
            S.act(lambda e, Tt=Tt: e.activation(out=Tt[:], in_=sl[0].b, func=AF.Copy), reads=[sl[0]], writes=[Tt])
            S.dve(lambda e, N1=N1: e.tensor_copy(out=N1[:], in_=sl[1].f), reads=[sl[1]], writes=[N1])
            yield
            S.pe(lambda e, Tt=Tt, N1=N1: e.matmul(sl[2].f, lhsT=Tt[:], rhs=N1[:], start=True, stop=True), reads=[Tt, N1], writes=[sl[2]])
            yield
            S.dve(lambda e, Yc=Yc, Yn=Yn: e.scalar_tensor_tensor(out=Yn[:], in0=sl[2].f, scalar=-1.0, in1=Yc[:], op0=ALU.mult, op1=ALU.add),
                  reads=[sl[2], Yc], writes=[Yn])
            yield
            cur = 1 - cur
        Y = Yb[c][cur]
        S.pe(lambda e: e.matmul(sl[0].f, lhsT=Y[:], rhs=bv[c][0][:], start=True, stop=True), reads=[Y, bv[c][0]], writes=[sl[0]])
        S.pe(lambda e: e.matmul(sl[1].f, lhsT=kbe[c][0][:], rhs=Y[:], start=True, stop=True), reads=[Y, kbe[c][0]], writes=[sl[1]])
        if lat:
            S.pe(lambda e: e.transpose(out=sl[2].b, in_=qe[c][0][:], identity=C.ident_b[:]), reads=[qe[c][0], C.ident_b], writes=[sl[2]])
        yield
        S.act(lambda e: e.activation(out=u_[c][r][:], in_=sl[0].f, func=AF.Copy), reads=[sl[0]], writes=[u_[c][r]])
        S.dve(lambda e: e.tensor_copy(out=wT[c][r][:], in_=sl[1].f), reads=[sl[1]], writes=[wT[c][r]])
        if lat:
            S.act(lambda e: e.activation(out=qdT[c][r][:], in_=sl[2].b, func=AF.Copy), reads=[sl[2]], writes=[qdT[c][r]])
        yield

    def scan(c, n):
        d, h = c // 4, c % 4
        nt = tile_of(d, n)
        lat = nt < 32
        r = n % 2
        sl = slots[c][3]
        Sold, Snew = Sbf[c][n % 2], Sbf[c][1 - n % 2]
        S.pe(lambda e: e.matmul(sl.f, lhsT=wT[c][r][:], rhs=Sold[:], start=True, stop=True), reads=[wT[c][r], Sold], writes=[sl])
        yield
        S.dve(lambda e: e.scalar_tensor_tensor(out=vn[c][0][:], in0=sl.f, scalar=-1.0, in1=u_[c][r][:], op0=ALU.mult, op1=ALU.add),
              reads=[sl, u_[c][r]], writes=[vn[c][0]])
        yield
        S.pe(lambda e: e.matmul(sl.f, lhsT=kd[c][r][:], rhs=vn[c][0][:], start=True, stop=True), reads=[kd[c][r], vn[c][0]], writes=[sl])
        yield
        glc = gl[:, nt, c:c + 1]
        S.dve(lambda e: e.scalar_tensor_tensor(out=Snew[:], in0=S32[c][0][:], scalar=glc, in1=sl.f, op0=ALU.mult, op1=ALU.add),
              reads=[S32[c][0], gl, sl], writes=[Snew])
        S.dve(lambda e: e.scalar_tensor_tensor(out=S32[c][0][:], in0=S32[c][0][:], scalar=glc, in1=sl.f, op0=ALU.mult, op1=ALU.add),
              reads=[S32[c][0], gl, sl], writes=[S32[c][0]])
        yield
        if lat:
            S.pe(lambda e: e.matmul(sl.f, lhsT=qdT[c][r][:], rhs=Sold[:], start=True, stop=False), reads=[qdT[c][r], Sold], writes=[sl], sig=False)
            S.pe(lambda e: e.matmul(sl.f, lhsT=attnT[c][r][:], rhs=vn[c][0][:], start=False, stop=True), reads=[attnT[c][r], vn[c][0]], writes=[sl])
            yield
            orr = ores[nt][h]
            if ofirst[nt][h]:
                ofirst[nt][h] = False
                S.act(lambda e: e.activation(out=oacc[:, nt, h * 128:(h + 1) * 128], in_=sl.f, func=AF.Copy), reads=[sl], writes=[orr])
            else:
                S.dve(lambda e: e.tensor_tensor(out=oacc[:, nt, h * 128:(h + 1) * 128], in0=sl.f, in1=oacc[:, nt, h * 128:(h + 1) * 128], op=ALU.add),
                      reads=[sl, orr], writes=[orr])
            yield

    NR = min(34, GSTOP)
    if GSTOP >= 0:
        for d in range(2):
            load_tile(d, 0)
        run_interleaved([precompute(c, 0) for c in range(8)])
    for n in range(NR):
        gens = [scan(c, n) for c in range(8)]
        if n + 1 < NR:
            for d in range(2):
                load_tile(d, n + 1)
            gens += [precompute(c, n + 1) for c in range(8)]
        run_interleaved(gens)
    for nt in (range(32) if NR == 34 else []):
        S.dma(OACC[nt * 128:(nt + 1) * 128, :], oacc[:, nt, :], reads=ores[nt], writes=[OACC], q='sp')
    for c in range(8):
        dbg_dump(C, 'dbg_S%d' % c, S32[c][0], S32[c][0][:], [128, 128], F32)
    S.barrier()
    S.flush()
    st.close()


def load_rows_bcast(C, st, name, dram_row, n):
    t = sb(C.nc, st, name, [128, n], F32)
    C.S.dma(t[:], dram_row.to_broadcast([128, n]), writes=[t])
    return t


class Epi:
    def __init__(self, C, st, ln_idx, gate_ap, nsub, nbuf=1):
        nc = C.nc
        self.C, self.nsub, self.gate = C, nsub, gate_ap
        self.g = load_rows_bcast(C, st, 'ln_g%d' % ln_idx, C.din['ln_g'][ln_idx:ln_idx + 1, :], D)
        self.b = load_rows_bcast(C, st, 'ln_b%d' % ln_idx, C.din['ln_b'][ln_idx:ln_idx + 1, :], D)
        self.t2s = [sb(nc, st, 'ep_t2_%d' % i, [128, nsub, D], F32) for i in range(nbuf)]
        self.t2 = self.t2s[0]
        self.junk = sb(nc, st, 'ep_junk', [128, D], BF16)
        self.st = sb(nc, st, 'ep_st', [128, 6, nsub], F32)
        self.eps = sb(nc, st, 'ep_eps', [128, 1], F32)
        C.S.pool(lambda e: e.memset(self.eps[:], LN_EPS), writes=[self.eps])
        self.i = 0

    def sub(self, s_, ypair, xt):
        S, t2, stt = self.C.S, self.t2, self.st
        for hf in range(2):
            S.dve(lambda e, hf=hf: e.tensor_tensor(out=t2[:, s_, hf * 512:(hf + 1) * 512], in0=ypair[hf][:],
                                                   in1=self.gate[:, hf * 512:(hf + 1) * 512], op=ALU.mult),
                  reads=[ypair[hf], self.C.modrow], writes=[t2])
        S.dve(lambda e: e.scalar_tensor_tensor(out=t2[:, s_, :], in0=xt[:, s_, :], scalar=ALPHA, in1=t2[:, s_, :], op0=ALU.mult, op1=ALU.add),
              reads=[xt, t2], writes=[t2])
        S.act(lambda e: e.activation(out=self.junk[:], in_=t2[:, s_, :], func=AF.Copy, accum_out=stt[:, 0, s_:s_ + 1]),
              reads=[t2], writes=[self.junk, stt])
        S.act(lambda e: e.activation(out=self.junk[:], in_=t2[:, s_, :], func=AF.Square, accum_out=stt[:, 1, s_:s_ + 1]),
              reads=[t2], writes=[self.junk, stt])

    def finish(self, dst_rows, xt_unused=None):
        S, t2, stt, n = self.C.S, self.t2, self.st, self.nsub
        dst, r0 = dst_rows
        xo = t2
        self.i += 1
        self.t2 = self.t2s[self.i % len(self.t2s)]
        S.dve(lambda e: e.tensor_scalar(out=stt[:, 2, :], in0=stt[:, 0, :], scalar1=1.0 / D, scalar2=None, op0=ALU.mult), reads=[stt], writes=[stt])
        S.dve(lambda e: e.tensor_tensor(out=stt[:, 4, :], in0=stt[:, 2, :], in1=stt[:, 2, :], op=ALU.mult), reads=[stt], writes=[stt])
        S.dve(lambda e: e.scalar_tensor_tensor(out=stt[:, 3, :], in0=stt[:, 1, :], scalar=1.0 / D, in1=stt[:, 4, :], op0=ALU.mult, op1=ALU.subtract),
              reads=[stt], writes=[stt])
        S.act(lambda e: e.activation(out=stt[:, 3, :], in_=stt[:, 3, :], func=AF.Ln, bias=self.eps[:, 0:1]), reads=[stt, self.eps], writes=[stt])
        S.act(lambda e: e.activation(out=stt[:, 3, :], in_=stt[:, 3, :], func=AF.Exp, scale=-0.5), reads=[stt], writes=[stt])
        S.dve(lambda e: e.scalar_tensor_tensor(out=stt[:, 5, :], in0=stt[:, 2, :], scalar=-1.0, in1=stt[:, 3, :], op0=ALU.mult, op1=ALU.mult),
              reads=[stt], writes=[stt])
        for s_ in range(n):
            S.act(lambda e, s_=s_: e.activation(out=t2[:, s_, :], in_=t2[:, s_, :], func=AF.Identity, scale=stt[:, 3, s_:s_ + 1], bias=stt[:, 5, s_:s_ + 1]),
                  reads=[t2, stt], writes=[t2])
            S.pool(lambda e, s_=s_: e.tensor_tensor(out=xo[:, s_, :], in0=t2[:, s_, :], in1=self.g[:], op=ALU.mult), reads=[t2, self.g], writes=[xo])
            S.dve(lambda e, s_=s_: e.tensor_tensor(out=xo[:, s_, :], in0=xo[:, s_, :], in1=self.b[:], op=ALU.add), reads=[xo, self.b], writes=[xo])
        S.dma(dst[r0:r0 + n * 128, :].rearrange("(s p) d -> p s d", p=128), xo[:, :, :], reads=[xo], writes=[dst], q='sp')


def out_proj(C, mixT, wout, s_, ypair, nk):
    S = C.S
    for hf in range(2):
        for k in range(nk):
            S.pe(lambda e, hf=hf, k=k: e.matmul(ypair[hf][:], lhsT=mixT[:, k, s_ * 128:(s_ + 1) * 128], rhs=wout[:, k, hf * 512:(hf + 1) * 512],
                                                start=(k == 0), stop=(k == nk - 1)), reads=[mixT, wout], writes=[ypair[hf]], sig=(k == nk - 1))


def phase_mix0_out(C, X1):
    nc, S = C.nc, C.S
    st = ExitStack()
    wout = load_w_bf16(C, st, 'wout0', C.din['even_w_out'][:, :], 8, D)
    normw = load_rows_bcast(C, st, 'normw', C.din['gdn_norm_w'][0:1, :], 128)
    epi = Epi(C, st, 0, C.modrow[:, 2 * D:3 * D], 4)
    yps = [[ps(nc, st, 'yps%d_%d' % (i, h), [128, 512], F32) for h in range(2)] for i in range(2)]
    pT = [ps(nc, st, 'pTm%d' % i, [128, 512], BF16) for i in range(2)]
    ot = [sb(nc, st, 'ot%d' % i, [128, 4, 512], F32) for i in range(2)]
    gt_ = [sb(nc, st, 'gt%d' % i, [128, 4, 512], BF16) for i in range(2)]
    xt = [sb(nc, st, 'xtm%d' % i, [128, 4, D], F32) for i in range(2)]
    mixT = [sb(nc, st, 'mixT%d' % i, [128, 8, 512], BF16) for i in range(2)]
    osq = sb(nc, st, 'osq', [128, 4, 512], F32)
    ss = sb(nc, st, 'oss', [128, 16], F32)
    ogs = [sb(nc, st, 'og%d' % i, [128, 4, 512], BF16) for i in range(2)]
    epsr = sb(nc, st, 'epsr', [128, 1], F32)
    S.pool(lambda e: e.memset(epsr[:], RMS_EPS), writes=[epsr])

    def load_front(t):
        i = t % 2
        S.dma(ot[i][:, :, :], C.OACC[t * 512:(t + 1) * 512, :].rearrange("(s p) d -> p s d", p=128), reads=[C.OACC], writes=[ot[i]])
        S.dma(gt_[i][:, :, :], C.G0[t * 512:(t + 1) * 512, :].rearrange("(s p) d -> p s d", p=128), reads=[C.G0], writes=[gt_[i]])

    def load_back(t):
        i = t % 2
        S.dma(xt[i][:, :, :], C.din['x'][t * 512:(t + 1) * 512, :].rearrange("(s p) d -> p s d", p=128), writes=[xt[i]])
        S.dma(mixT[i][:, 4:8, :], C.YPT[:, t * 512:(t + 1) * 512].rearrange("(f p) n -> p f n", p=128), reads=[C.YPT], writes=[mixT[i]])

    def front(t):
        i = t % 2
        o_, g_, x_, m_, og = ot[i], gt_[i], xt[i], mixT[i], ogs[i]
        S.pool(lambda e, o_=o_: e.tensor_tensor(out=osq[:], in0=o_[:], in1=o_[:], op=ALU.mult), reads=[o_], writes=[osq])
        S.dve(lambda e: e.tensor_reduce(out=ss[:], in_=osq[:].rearrange("p s (h d) -> p (s h) d", d=128), axis=mybir.AxisListType.X, op=ALU.add),
              reads=[osq], writes=[ss])
        S.act(lambda e: e.activation(out=ss[:], in_=ss[:], func=AF.Ln, scale=1.0 / 128, bias=epsr[:, 0:1]), reads=[ss, epsr], writes=[ss])
        S.act(lambda e: e.activation(out=ss[:], in_=ss[:], func=AF.Exp, scale=-0.5), reads=[ss], writes=[ss])
        S.dve(lambda e, o_=o_: e.tensor_tensor(out=osq[:].rearrange("p s (h d) -> p (s h) d", d=128), in0=o_[:].rearrange("p s (h d) -> p (s h) d", d=128),
                                              in1=ss[:].unsqueeze(2).to_broadcast([128, 16, 128]), op=ALU.mult), reads=[o_, ss], writes=[osq])
        S.pool(lambda e: e.tensor_tensor(out=osq[:].rearrange("p s (h d) -> p (s h) d", d=128), in0=osq[:].rearrange("p s (h d) -> p (s h) d", d=128),
                                         in1=normw[:].unsqueeze(1).to_broadcast([128, 16, 128]), op=ALU.mult), reads=[osq, normw], writes=[osq])
        S.dve(lambda e, g_=g_: e.tensor_tensor(out=og[:], in0=osq[:], in1=g_[:], op=ALU.mult), reads=[osq, g_], writes=[og])
        for h in range(4):
            p = pT[h % 2]
            for s_ in range(4):
                S.pe(lambda e, p=p, h=h, s_=s_: e.transpose(out=p[:, s_ * 128:(s_ + 1) * 128], in_=og[:, s_, h * 128:(h + 1) * 128], identity=C.ident_b[:]),
                     reads=[og, C.ident_b], writes=[p], sig=(s_ == 3))
            if h % 2 == 0:
                S.act(lambda e, p=p, h=h, m_=m_: e.activation(out=m_[:, h, :], in_=p[:], func=AF.Copy), reads=[p], writes=[m_])
            else:
                S.dve(lambda e, p=p, h=h, m_=m_: e.tensor_copy(out=m_[:, h, :], in_=p[:]), reads=[p], writes=[m_])

    load_front(0)
    load_back(0)
    front(0)
    load_front(1)
    load_back(1)
    yi = 0
    for t in range(8):
        if t + 1 < 8:
            front(t + 1)
        if t + 2 < 8:
            load_front(t + 2)
        m_, x_ = mixT[t % 2], xt[t % 2]
        for s_ in range(4):
            yp = yps[yi % 2]; yi += 1
            out_proj(C, m_, wout, s_, yp, 8)
            epi.sub(s_, yp, x_)
        epi.finish((X1, t * 512))
        if t + 2 < 8:
            load_back(t + 2)
    dbg_dump(C, 'dbg_x1', X1, X1[0:128, :], [128, D], F32)
    S.barrier()
    S.flush()
    st.close()


def phase_ffn_up(C, layer, Xin):
    nc, S = C.nc, C.S
    st = ExitStack()
    if layer == 0:
        C.AT = C.scratch('AT', [DFF, T + 128], BF16)
        C.GTt = C.scratch('GTt', [DFF, T], BF16)
        for r0 in range(0, DFF, 128):
            S.dma(C.AT[r0:r0 + 128, 0:64], C.zeros_b[:, 0:64], reads=[C.zeros_b], writes=[C.AT])
            S.dma(C.AT[r0:r0 + 128, T + 64:T + 128], C.zeros_b[:, 0:64], reads=[C.zeros_b], writes=[C.AT])
    AT, GTt = C.AT, C.GTt
    w = load_w_bf16(C, st, 'wup', C.din['ffn_w_up'][layer, :, :], 8, 2 * DFF)
    xt = sb(nc, st, 'xtu', [128, 4, D], F32)
    ub = sb(nc, st, 'ubu', [128, 4, D], BF16)
    uT = [sb(nc, st, 'uTu%d' % i, [128, 8, 512], BF16) for i in range(2)]
    stg = [sb(nc, st, 'stgu%d' % i, [128, 4, 512], BF16) for i in range(2)]
    pT = [ps(nc, st, 'pTu%d' % i, [128, 512], BF16) for i in range(2)]
    pm = [ps(nc, st, 'pmu%d' % i, [128, 512], F32) for i in range(4)]
    pmi = 0
    for t in range(8):
        S.dma(xt[:, :, :], Xin[t * 512:(t + 1) * 512, :].rearrange("(s p) d -> p s d", p=128), reads=[Xin], writes=[xt])
        ut = uT[t % 2]
        modulate_transpose(C, xt, 4, C.modrow[:, 3 * D:4 * D], C.modrow[:, 4 * D:5 * D], ub, ut, pT, t)
        gi = 0
        for f0 in range(0, 44, 4):
            sg = stg[gi % 2]; gi += 1
            nf = min(4, 44 - f0)
            for j in range(nf):
                fc = f0 + j
                p = pm[pmi % 4]; pmi += 1
                for k in range(8):
                    S.pe(lambda e, p=p, k=k, fc=fc, ut=ut: e.matmul(p[:], lhsT=w[:, k, fc * 128:(fc + 1) * 128], rhs=ut[:, k, :],
                                                                 start=(k == 0), stop=(k == 7)), reads=[w, ut], writes=[p], sig=(k == 7))
                if pmi % 2 == 0:
                    S.act(lambda e, p=p, j=j, sg=sg: e.activation(out=sg[:, j, :], in_=p[:], func=AF.Copy), reads=[p], writes=[sg])
                else:
                    S.dve(lambda e, p=p, j=j, sg=sg: e.tensor_copy(out=sg[:, j, :], in_=p[:]), reads=[p], writes=[sg])
            if f0 < 22:
                na = min(nf, 22 - f0)
                S.dma(AT[f0 * 128:(f0 + na) * 128, 64 + t * 512:64 + (t + 1) * 512].rearrange("(j p) n -> p j n", p=128), sg[:, 0:na, :],
                      reads=[sg], writes=[AT], q='act')
                if na < nf:
                    S.dma(GTt[0:(nf - na) * 128, t * 512:(t + 1) * 512].rearrange("(j p) n -> p j n", p=128), sg[:, na:nf, :],
                          reads=[sg], writes=[GTt], q='act')
            else:
                g0 = f0 - 22
                S.dma(GTt[g0 * 128:(g0 + nf) * 128, t * 512:(t + 1) * 512].rearrange("(j p) n -> p j n", p=128), sg[:, 0:nf, :],
                      reads=[sg], writes=[GTt], q='act')
    S.barrier()
    S.flush()
    st.close()


def phase_ffn_down(C, layer, Xin, Xout):
    phase_ffn_conv(C, layer)
    phase_ffn_proj(C, layer, Xin, Xout)


def phase_ffn_conv(C, layer):
    nc, S = C.nc, C.S
    st = ExitStack()
    AT, GTt = C.AT, C.GTt
    if layer == 0:
        C.HGT = C.scratch('HGT', [DFF, T], BF16)
    HGT = C.HGT
    NTK, NH = 512, 11
    pconv = [ps(nc, st, 'pcv%d' % i, [128, 512], F32) for i in range(4)]
    dg = build_diag(C, st, pconv[0], C.din['ffn_conv_w'][layer, :, :], 9, 22, 'cw9')
    ad = [sb(nc, st, 'ad%d' % i, [128, NH, 640], BF16) for i in range(2)]
    gt_ = [sb(nc, st, 'gd%d' % i, [128, NH, 512], BF16) for i in range(2)]
    apad = [sb(nc, st, 'apad%d' % i, [128, NH, 10, 66], BF16) for i in range(2)]
    hgs = [sb(nc, st, 'hgs%d' % i, [128, NH, 512], BF16) for i in range(2)]
    hs = [sb(nc, st, 'hs%d' % i, [128, 512], BF16) for i in range(4)]
    for i in range(2):
        S.pool(lambda e, i=i: e.memset(apad[i][:], 0.0), writes=[apad[i]])
    items = [(t, hf) for t in range(T // NTK) for hf in range(2)]
    n = len(items)

    def load_a(w):
        t, hf = items[w]
        S.dma(ad[w % 2][:, :, :], AT[hf * NH * 128:(hf + 1) * NH * 128, t * NTK:t * NTK + 640].rearrange("(f p) n -> p f n", p=128),
              reads=[AT], writes=[ad[w % 2]])

    def load_g(w):
        t, hf = items[w]
        S.dma(gt_[w % 2][:, :, :], GTt[hf * NH * 128:(hf + 1) * NH * 128, t * NTK:(t + 1) * NTK].rearrange("(f p) n -> p f n", p=128),
              reads=[GTt], writes=[gt_[w % 2]])

    def pad(w):
        a_, ap_ = ad[w % 2], apad[w % 2]
        for j in range(NH):
            fn = lambda e, j=j, a_=a_, ap_=ap_: e.tensor_copy(out=ap_[:, j, :, 1:65], in_=a_[:, j, :].rearrange("p (r c) -> p r c", c=64))
            if j % 3 == 0:
                S.pool(fn, reads=[a_], writes=[ap_])
            else:
                S.dve(fn, reads=[a_], writes=[ap_])

    load_a(0)
    load_g(0)
    pad(0)
    load_a(1)
    ci = 0
    for w in range(n):
        t, hf = items[w]
        if w + 1 < n:
            pad(w + 1)
            load_g(w + 1)
        if w + 2 < n:
            load_a(w + 2)
        g_, ap_, hg = gt_[w % 2], apad[w % 2], hgs[w % 2]
        for j in range(NH):
            fc = hf * NH + j
            pc = pconv[ci % 4]; h_ = hs[ci % 4]; ci += 1
            for tap in range(9):
                dy, dx = tap // 3 - 1, tap % 3 - 1
                S.pe(lambda e, pc=pc, fc=fc, j=j, tap=tap, dy=dy, dx=dx, ap_=ap_: e.matmul(
                    pc[:], lhsT=dg[:, fc, tap, :], rhs=ap_[:, j, 1 + dy:9 + dy, 1 + dx:65 + dx], start=(tap == 0), stop=(tap == 8)),
                    reads=[dg, ap_], writes=[pc], sig=(tap == 8))
            S.act(lambda e, pc=pc, h_=h_: e.activation(out=h_[:], in_=pc[:], func=AF.Silu), reads=[pc], writes=[h_])
            S.dve(lambda e, j=j, h_=h_, g_=g_, hg=hg: e.tensor_tensor(out=hg[:, j, :], in0=h_[:], in1=g_[:, j, :], op=ALU.mult),
                  reads=[h_, g_], writes=[hg])
        S.dma(HGT[hf * NH * 128:(hf + 1) * NH * 128, t * NTK:(t + 1) * NTK].rearrange("(f p) n -> p f n", p=128), hg[:, :, :],
              reads=[hg], writes=[HGT], q='act')
    S.barrier()
    S.flush()
    st.close()


def phase_ffn_proj(C, layer, Xin, Xout):
    nc, S = C.nc, C.S
    st = ExitStack()
    HGT = C.HGT
    yps = [[ps(nc, st, 'ypd%d_%d' % (i, h), [128, 512], F32) for h in range(2)] for i in range(3)]
    wd = load_w_bf16(C, st, 'wdn', C.din['ffn_w_down'][layer, :, :], 22, D)
    epi = Epi(C, st, layer * 2 + 1, C.modrow[:, 5 * D:6 * D], 4, nbuf=2)
    hg = [sb(nc, st, 'hgp%d' % i, [128, 22, 512], BF16) for i in range(2)]
    xt = [sb(nc, st, 'xtd%d' % i, [128, 4, D], F32) for i in range(2)]

    def load(t):
        i = t % 2
        S.dma(hg[i][:, :, :], HGT[:, t * 512:(t + 1) * 512].rearrange("(f p) n -> p f n", p=128), reads=[HGT], writes=[hg[i]])
        S.dma(xt[i][:, :, :], Xin[t * 512:(t + 1) * 512, :].rearrange("(s p) d -> p s d", p=128), reads=[Xin], writes=[xt[i]])

    load(0)
    yi = 0
    for t in range(8):
        if t + 1 < 8:
            load(t + 1)
        h_, x_ = hg[t % 2], xt[t % 2]
        for s_ in range(4):
            yp = yps[yi % 3]; yi += 1
            out_proj(C, h_, wd, s_, yp, 22)
            epi.sub(s_, yp, x_)
        epi.finish((Xout, t * 512))
    S.barrier()
    S.flush()
    st.close()


def phase_inproj1(C, Xin):
    nc, S = C.nc, C.S
    st = ExitStack()
    GBT = C.scratch('GBT', [512, T], BF16); C.GBT = GBT
    M1T = C.scratch('M1T', [512, T + 32], BF16); C.M1T = M1T
    M2T = C.scratch('M2T', [512, T + 32], BF16); C.M2T = M2T
    for M in (M1T, M2T):
        for r0 in range(0, 512, 128):
            S.dma(M[r0:r0 + 128, 0:16], C.zeros_b[:, 0:16], reads=[C.zeros_b], writes=[M])
            S.dma(M[r0:r0 + 128, T + 16:T + 32], C.zeros_b[:, 0:16], reads=[C.zeros_b], writes=[M])
    w = load_w_bf16(C, st, 'w_in1', C.din['odd_w_in'][:, :], 8, 2560)
    xt = sb(nc, st, 'xt1', [128, 4, D], F32)
    ub = sb(nc, st, 'ub1', [128, 4, D], BF16)
    uT = [sb(nc, st, 'uT1%d' % i, [128, 8, 512], BF16) for i in range(2)]
    stg = [[sb(nc, st, 'stg1_%d_%d' % (g, i), [128, 4, 512], BF16) for i in range(2)] for g in range(3)]
    tmp = [sb(nc, st, 'tmp1_%d' % i, [128, 512], F32) for i in range(2)]
    pT = [ps(nc, st, 'pT1%d' % i, [128, 512], BF16) for i in range(2)]
    pm = [ps(nc, st, 'pm1%d' % i, [128, 512], F32) for i in range(4)]
    pmi = 0
    ti = 0

    def mm(fc, ut):
        nonlocal pmi
        p = pm[pmi % 4]; pmi += 1
        for k in range(8):
            S.pe(lambda e, p=p, k=k: e.matmul(p[:], lhsT=w[:, k, fc * 128:(fc + 1) * 128], rhs=ut[:, k, :], start=(k == 0), stop=(k == 7)),
                 reads=[w, ut], writes=[p], sig=(k == 7))
        return p

    for t in range(8):
        S.dma(xt[:, :, :], Xin[t * 512:(t + 1) * 512, :].rearrange("(s p) d -> p s d", p=128), reads=[Xin], writes=[xt])
        ut = uT[t % 2]
        modulate_transpose(C, xt, 4, C.modrow[:, 0:D], C.modrow[:, D:2 * D], ub, ut, pT, t)
        sgb, sm1, sm2 = stg[0][t % 2], stg[1][t % 2], stg[2][t % 2]
        for j in range(4):
            p = mm(j, ut)
            S.act(lambda e, p=p, j=j, sgb=sgb: e.activation(out=sgb[:, j, :], in_=p[:], func=AF.Copy), reads=[p], writes=[sgb])
            tm = tmp[ti % 2]; ti += 1
            p = mm(4 + j, ut)
            S.act(lambda e, p=p, tm=tm: e.activation(out=tm[:], in_=p[:], func=AF.Copy), reads=[p], writes=[tm])
            p = mm(8 + j, ut)
            S.dve(lambda e, p=p, tm=tm, j=j, sm1=sm1: e.tensor_tensor(out=sm1[:, j, :], in0=p[:], in1=tm[:], op=ALU.mult), reads=[p, tm], writes=[sm1])
            tm = tmp[ti % 2]; ti += 1
            p = mm(16 + j, ut)
            S.act(lambda e, p=p, tm=tm: e.activation(out=tm[:], in_=p[:], func=AF.Sigmoid), reads=[p], writes=[tm])
            p = mm(12 + j, ut)
            S.dve(lambda e, p=p, tm=tm, j=j, sm2=sm2: e.tensor_tensor(out=sm2[:, j, :], in0=p[:], in1=tm[:], op=ALU.mult), reads=[p, tm], writes=[sm2])
        c0 = t * 512
        S.dma(GBT[:, c0:c0 + 512].rearrange("(j p) n -> p j n", p=128), sgb[:, :, :], reads=[sgb], writes=[GBT], q='act')
        S.dma(M1T[:, 16 + c0:16 + c0 + 512].rearrange("(j p) n -> p j n", p=128), sm1[:, :, :], reads=[sm1], writes=[M1T], q='act')
        S.dma(M2T[:, 16 + c0:16 + c0 + 512].rearrange("(j p) n -> p j n", p=128), sm2[:, :, :], reads=[sm2], writes=[M2T], q='act')
    S.barrier()
    S.flush()
    st.close()


def phase_mix1_out(C, Xin, Xout):
    nc, S = C.nc, C.S
    st = ExitStack()
    pconv = [ps(nc, st, 'pc1%d' % i, [128, 512], F32) for i in range(2)]
    pstat = [ps(nc, st, 'pst1%d' % i, [128, 512], F32) for i in range(2)]
    yps = [[ps(nc, st, 'yp1%d_%d' % (i, h), [128, 512], F32) for h in range(2)] for i in range(2)]
    dg3 = build_diag(C, st, pconv[0], C.din['sconv_w'][:, :], 3, 4, 'cw3')
    dg31 = build_diag(C, st, pconv[1], C.din['conf_conv_w'][:, :], 31, 4, 'cw31')
    lng = to_col(C, st, pconv[0], C.din['conf_ln_g'][:, :], 1, 4, 'clng')
    lnb = to_col(C, st, pconv[1], C.din['conf_ln_b'][:, :], 1, 4, 'clnb')
    wout = load_w_bf16(C, st, 'wout1', C.din['odd_w_out'][:, :], 8, D)
    epi = Epi(C, st, 2, C.modrow[:, 2 * D:3 * D], 4)
    m1 = [sb(nc, st, 'm1_%d' % i, [128, 4, 514], BF16) for i in range(2)]
    m2 = [sb(nc, st, 'm2_%d' % i, [128, 4, 542], BF16) for i in range(2)]
    gb = [sb(nc, st, 'gb_%d' % i, [128, 4, 512], BF16) for i in range(2)]
    xt = [sb(nc, st, 'xt1o%d' % i, [128, 4, D], F32) for i in range(2)]
    mixT = sb(nc, st, 'mixT1', [128, 8, 512], BF16)
    z = sb(nc, st, 'z1', [128, 4, 512], F32)
    zsq = sb(nc, st, 'zsq1', [128, 4, 512], F32)
    mean = sb(nc, st, 'mean1', [128, 512], F32)
    rstd = sb(nc, st, 'rstd1', [128, 512], F32)
    msq = sb(nc, st, 'msq1', [128, 512], F32)
    epsl = sb(nc, st, 'epsl1', [128, 1], F32)
    S.pool(lambda e: e.memset(epsl[:], LN_EPS), writes=[epsl])

    def load(t):
        i = t % 2
        c0 = t * 512
        S.dma(m1[i][:, :, :], C.M1T[:, 15 + c0:15 + c0 + 514].rearrange("(f p) n -> p f n", p=128), reads=[C.M1T], writes=[m1[i]])
        S.dma(m2[i][:, :, :], C.M2T[:, 1 + c0:1 + c0 + 542].rearrange("(f p) n -> p f n", p=128), reads=[C.M2T], writes=[m2[i]])
        S.dma(gb[i][:, :, :], C.GBT[:, c0:c0 + 512].rearrange("(f p) n -> p f n", p=128), reads=[C.GBT], writes=[gb[i]])
        S.dma(xt[i][:, :, :], Xin[c0:c0 + 512, :].rearrange("(s p) d -> p s d", p=128), reads=[Xin], writes=[xt[i]])

    load(0)
    ci = 0
    yi = 0
    for t in range(8):
        if t + 1 < 8:
            load(t + 1)
        i = t % 2
        a1, a2, g_, x_ = m1[i], m2[i], gb[i], xt[i]
        for j in range(4):
            pc = pconv[ci % 2]; ci += 1
            for tap in range(3):
                S.pe(lambda e, pc=pc, j=j, tap=tap, a1=a1: e.matmul(pc[:], lhsT=dg3[:, j, tap, :], rhs=a1[:, j, tap:tap + 512], start=(tap == 0), stop=(tap == 2)),
                     reads=[dg3, a1], writes=[pc], sig=(tap == 2))
            S.dve(lambda e, pc=pc, j=j, g_=g_: e.tensor_tensor(out=mixT[:, j, :], in0=pc[:], in1=g_[:, j, :], op=ALU.mult), reads=[pc, g_], writes=[mixT])
        for j in range(4):
            pc = pconv[ci % 2]; ci += 1
            for tap in range(31):
                S.pe(lambda e, pc=pc, j=j, tap=tap, a2=a2: e.matmul(pc[:], lhsT=dg31[:, j, tap, :], rhs=a2[:, j, tap:tap + 512], start=(tap == 0), stop=(tap == 30)),
                     reads=[dg31, a2], writes=[pc], sig=(tap == 30))
            S.act(lambda e, pc=pc, j=j: e.activation(out=z[:, j, :], in_=pc[:], func=AF.Copy), reads=[pc], writes=[z])
            S.pool(lambda e, j=j: e.tensor_tensor(out=zsq[:, j, :], in0=z[:, j, :], in1=z[:, j, :], op=ALU.mult), reads=[z], writes=[zsq])
        for j in range(4):
            S.pe(lambda e, j=j: e.matmul(pstat[0][:], lhsT=C.ones_f[:, 0:128], rhs=z[:, j, :], start=(j == 0), stop=(j == 3)),
                 reads=[C.ones_f, z], writes=[pstat[0]], sig=(j == 3))
        for j in range(4):
            S.pe(lambda e, j=j: e.matmul(pstat[1][:], lhsT=C.ones_f[:, 0:128], rhs=zsq[:, j, :], start=(j == 0), stop=(j == 3)),
                 reads=[C.ones_f, zsq], writes=[pstat[1]], sig=(j == 3))
        S.dve(lambda e: e.tensor_scalar(out=mean[:], in0=pstat[0][:], scalar1=1.0 / 512, scalar2=None, op0=ALU.mult), reads=[pstat[0]], writes=[mean])
        S.dve(lambda e: e.tensor_tensor(out=msq[:], in0=mean[:], in1=mean[:], op=ALU.mult), reads=[mean], writes=[msq])
        S.dve(lambda e: e.scalar_tensor_tensor(out=rstd[:], in0=pstat[1][:], scalar=1.0 / 512, in1=msq[:], op0=ALU.mult, op1=ALU.subtract),
              reads=[pstat[1], msq], writes=[rstd])
        S.act(lambda e: e.activation(out=rstd[:], in_=rstd[:], func=AF.Ln, bias=epsl[:, 0:1]), reads=[rstd, epsl], writes=[rstd])
        S.act(lambda e: e.activation(out=rstd[:], in_=rstd[:], func=AF.Exp, scale=-0.5), reads=[rstd], writes=[rstd])
        for j in range(4):
            S.dve(lambda e, j=j: e.tensor_tensor(out=z[:, j, :], in0=z[:, j, :], in1=mean[:], op=ALU.subtract), reads=[z, mean], writes=[z])
            S.pool(lambda e, j=j: e.tensor_tensor(out=z[:, j, :], in0=z[:, j, :], in1=rstd[:], op=ALU.mult), reads=[z, rstd], writes=[z])
            S.act(lambda e, j=j: e.activation(out=mixT[:, 4 + j, :], in_=z[:, j, :], func=AF.Silu, scale=lng[:, j, 0:1], bias=lnb[:, j, 0:1]),
                  reads=[z, lng, lnb], writes=[mixT])
        for s_ in range(4):
            yp = yps[yi % 2]; yi += 1
            out_proj(C, mixT, wout, s_, yp, 8)
            epi.sub(s_, yp, x_)
        epi.finish((Xout, t * 512))
    S.barrier()
    S.flush()
    st.close()


_NC_CACHE = {}


def kernel(**inputs):
    if 'nc' not in _NC_CACHE:
        _NC_CACHE['nc'] = build()
    nc = _NC_CACHE['nc']
    f = lambda a: np.ascontiguousarray(np.asarray(a, dtype=np.float32))
    shared = {
        'c_ctx': f(inputs['c_ctx']).reshape(1, D),
        'ada_w': f(inputs['ada_w']), 'ada_b': f(inputs['ada_b']),
        'ln_g': f(inputs['ln_g']).reshape(4, D), 'ln_b': f(inputs['ln_b']).reshape(4, D),
        'even_w_in': f(inputs['even_w_in']), 'even_w_out': f(inputs['even_w_out']),
        'gdn_conv_w': f(inputs['gdn_conv_w']), 'gdn_a_log': f(inputs['gdn_a_log']).reshape(1, 8),
        'gdn_dt_bias': f(inputs['gdn_dt_bias']).reshape(1, 8), 'gdn_norm_w': f(inputs['gdn_norm_w']).reshape(1, 128),
        'pool_w': f(inputs['pool_w']), 'pool_scale': f(inputs['pool_scale']).reshape(1, 512),
        'odd_w_in': f(inputs['odd_w_in']), 'odd_w_out': f(inputs['odd_w_out']),
        'sconv_w': f(inputs['sconv_w']), 'conf_conv_w': f(inputs['conf_conv_w']),
        'conf_ln_g': f(inputs['conf_ln_g']).reshape(1, 512), 'conf_ln_b': f(inputs['conf_ln_b']).reshape(1, 512),
        'ffn_w_up': f(inputs['ffn_w_up']), 'ffn_conv_w': f(inputs['ffn_conv_w']).reshape(2, 9, DFF),
        'ffn_w_down': f(inputs['ffn_w_down']),
    }
    x = f(inputs['x']); c = f(inputs['c']); ctx = f(inputs['ctx'])
    in_maps = []
    for b in range(NCORES):
        m = dict(shared)
        m['x'] = x[b]; m['c'] = c[b:b + 1]; m['ctx'] = ctx[b]
        in_maps.append(m)
    res = run_bass_kernel_spmd(nc, in_maps, core_ids=list(range(NCORES)))
    return np.stack([r['out'] for r in res.results], axis=0)
```

```python
import numpy as np
from contextlib import ExitStack
import concourse.bass as bass
import concourse.mybir as mybir
from concourse.bass_utils import run_bass_kernel_spmd

F32 = mybir.dt.float32
BF16 = mybir.dt.bfloat16
AF = mybir.ActivationFunctionType
ALU = mybir.AluOpType

D = 1024
T = 4096
TC = 256
NCORES = 8
DFF = 2816
ALPHA = 4 ** 0.25
LN_EPS = 1e-5
RMS_EPS = 1e-6
BIG = 30000.0
ENG = ('pe', 'act', 'dve', 'pool', 'sp')
NDS = 24
import os as _os
SELF_SYNC = ('pool',) if _os.environ.get('RELAX') == '1' else ('act', 'dve', 'pool')

DEBUG_OUT = []


class Res:
    __slots__ = ('w', 'r', 'excl')

    def __init__(self):
        self.w = None
        self.r = []
        self.excl = False


class TileT:
    def __init__(self, t):
        self.t = t
        self.res = Res()

    def __getitem__(self, k):
        return self.t[k]


class Sched:
    def __init__(self, nc, stack):
        self.nc = nc
        self.sem = {e: stack.enter_context(nc.semaphore('s_' + e)) for e in ENG}
        self.cnt = {e: 0 for e in ENG}
        self.known = {e: {} for e in ENG}
        self.q = {e: [] for e in ENG}
        self.dsem = [stack.enter_context(nc.semaphore('dq%d' % i)) for i in range(NDS)]
        self.dcnt = [0] * NDS
        self.dpool = {'sp': list(range(0, 12)), 'act': list(range(12, 20)), 'pool': list(range(20, 24))}
        self.dnext = {'sp': 0, 'act': 0, 'pool': 0}
        self.unsig = {e: False for e in ENG}

    def semof(self, key):
        return self.sem[key] if isinstance(key, str) else self.dsem[key]

    def _collect(self, eng, reads, writes):
        toks = []
        for r in reads:
            if r.w is not None:
                toks.append(r.w)
        for w in writes:
            if w.w is not None and (w.w[0] != eng or eng in SELF_SYNC):
                toks.append(w.w)
            for t in w.r:
                if t[0] != eng or eng in SELF_SYNC:
                    toks.append(t)
        waits = {}
        kn = self.known[eng]
        for key, val in toks:
            if kn.get(key, 0) < val:
                waits[key] = max(waits.get(key, 0), val)
        for key, val in waits.items():
            kn[key] = val
        return list(waits.items())

    def _update(self, tok, reads, writes):
        for r in reads:
            r.r.append(tok)
        for w in writes:
            w.w = tok
            w.r = []

    def emit(self, eng, fn, reads=(), writes=(), sig=True):
        reads = [getattr(x, "res", x) for x in reads]
        writes = [getattr(x, "res", x) for x in writes]
        writes = writes + [r for r in reads if r.excl and eng != 'pe']
        waits = self._collect(eng, reads, writes)
        if sig:
            self.cnt[eng] += 1
            tok = (eng, self.cnt[eng])
            self.unsig[eng] = False
        else:
            tok = (eng, self.cnt[eng] + 1)
            self.unsig[eng] = True
        self.q[eng].append((waits, fn, sig, None))
        self._update(tok, reads, writes)

    def pe(self, fn, reads=(), writes=(), sig=True):
        self.emit('pe', fn, reads, writes, sig)

    def act(self, fn, reads=(), writes=()):
        self.emit('act', fn, reads, writes)

    def dve(self, fn, reads=(), writes=()):
        self.emit('dve', fn, reads, writes)

    def pool(self, fn, reads=(), writes=()):
        self.emit('pool', fn, reads, writes)

    def dma(self, out, in_, reads=(), writes=(), q='sp', **kw):
        reads = [getattr(x, "res", x) for x in reads]
        writes = [getattr(x, "res", x) for x in writes]
        pl = self.dpool[q]
        j = pl[self.dnext[q] % len(pl)]
        self.dnext[q] += 1
        waits = dict(self._collect(q, reads, writes))
        if self.dcnt[j] > 0 and self.known[q].get(j, 0) < self.dcnt[j]:
            waits[j] = self.dcnt[j]
            self.known[q][j] = self.dcnt[j]
        self.dcnt[j] += 16
        tok = (j, self.dcnt[j])
        self.q[q].append((list(waits.items()), lambda e: e.dma_start(out=out, in_=in_, **kw), False, j))
        self._update(tok, reads, writes)

    def barrier(self):
        for e in ENG:
            assert not self.unsig[e], e
        for e in ENG:
            waits = []
            for f in ENG:
                if f != e and self.known[e].get(f, 0) < self.cnt[f]:
                    waits.append((f, self.cnt[f]))
                    self.known[e][f] = self.cnt[f]
            for j in range(NDS):
                if self.dcnt[j] > 0 and self.known[e].get(j, 0) < self.dcnt[j]:
                    waits.append((j, self.dcnt[j]))
                    self.known[e][j] = self.dcnt[j]
            if waits:
                self.q[e].append((waits, None, False, None))

    def flush(self):
        nc = self.nc
        q = self.q
        self.q = {e: [] for e in ENG}

        def replay(eng, e):
            for waits, fn, sig, dj in q[eng]:
                for key, val in waits:
                    e.wait_ge(self.semof(key), val)
                if fn is None:
                    continue
                ins = fn(e)
                if dj is not None:
                    ins.then_inc(self.dsem[dj], 16)
                elif sig:
                    ins.then_inc(self.sem[eng], 1)

        with nc.Block() as block:
            @block.tensor
            def _(e):
                replay('pe', e)

            @block.scalar
            def _(e):
                replay('act', e)

            @block.vector
            def _(e):
                replay('dve', e)

            @block.gpsimd
            def _(e):
                replay('pool', e)

            @block.sync
            def _(e):
                replay('sp', e)


class Ctx:
    pass


_UID = [0]


def _uname(name):
    _UID[0] += 1
    return '%s_u%d' % (name, _UID[0])


def sb(nc, stack, name, shape, dt):
    return TileT(stack.enter_context(nc.sbuf_tensor(_uname(name), list(shape), dt)))


def ps(nc, stack, name, shape, dt):
    t = TileT(stack.enter_context(nc.psum_tensor(_uname(name), list(shape), dt)))
    t.res.excl = True
    return t


def build(debug_out=()):
    nc = bass.Bass("TRN2", target_bir_lowering=False)
    top = ExitStack()
    S = Sched(nc, top)
    C = Ctx()
    C.nc, C.S = nc, S
    din = {}

    def inp(name, shape):
        din[name] = TileT(nc.dram_tensor(name, list(shape), F32, kind="ExternalInput").ap())
        return din[name]

    inp('x', [T, D]); inp('c', [1, D]); inp('ctx', [TC, D]); inp('c_ctx', [1, D])
    inp('ada_w', [2, D, 6 * D]); inp('ada_b', [2, 6 * D])
    inp('ln_g', [4, D]); inp('ln_b', [4, D])
    inp('even_w_in', [D, 2576]); inp('even_w_out', [D, D])
    inp('gdn_conv_w', [5, 1536]); inp('gdn_a_log', [1, 8]); inp('gdn_dt_bias', [1, 8])
    inp('gdn_norm_w', [1, 128]); inp('pool_w', [4, 128, 128]); inp('pool_scale', [1, 512])
    inp('odd_w_in', [D, 2560]); inp('odd_w_out', [D, D])
    inp('sconv_w', [3, 512]); inp('conf_conv_w', [31, 512])
    inp('conf_ln_g', [1, 512]); inp('conf_ln_b', [1, 512])
    inp('ffn_w_up', [2, D, 2 * DFF]); inp('ffn_conv_w', [2, 9, DFF]); inp('ffn_w_down', [2, DFF, D])
    C.din = din
    C.out = TileT(nc.dram_tensor('out', [T, D], F32, kind="ExternalOutput").ap())

    def scratch(name, shape, dt):
        kind = "ExternalOutput" if name in debug_out else "Internal"
        t = TileT(nc.dram_tensor(name, list(shape), dt, kind=kind).ap())
        return t
    C.scratch = scratch

    C.ident_f = sb(nc, top, 'ident_f', [128, 128], F32)
    C.ident_b = sb(nc, top, 'ident_b', [128, 128], BF16)
    C.ones_f = sb(nc, top, 'ones_f', [128, 512], F32)
    C.ones_b = sb(nc, top, 'ones_b', [128, 128], BF16)
    C.zeros_b = sb(nc, top, 'zeros_b', [128, 512], BF16)
    C.modrow = sb(nc, top, 'modrow', [128, 6 * D], F32)
    st01 = ExitStack()
    C.modc = sb(nc, st01, 'modc', [128, 2 * D], F32)

    S.pool(lambda e: e.memset(C.ones_f[:], 1.0), writes=[C.ones_f])
    S.pool(lambda e: e.memset(C.ones_b[:], 1.0), writes=[C.ones_b])
    S.pool(lambda e: e.memset(C.zeros_b[:], 0.0), writes=[C.zeros_b])
    S.pool(lambda e: e.affine_select(out=C.ident_f[:], in_=C.ones_f[:, 0:128], pattern=[[-1, 128]],
                                     compare_op=ALU.is_equal, fill=0.0, base=0, channel_multiplier=1),
           reads=[C.ones_f], writes=[C.ident_f])
    S.pool(lambda e: e.tensor_copy(out=C.ident_b[:], in_=C.ident_f[:]), reads=[C.ident_f], writes=[C.ident_b])

    C.debug_out = debug_out
    phase_mod(C, 0)
    dbg_dump(C, 'dbg_mod0', C.modrow, C.modrow[0:1, :], [1, 6 * D], F32)
    dbg_dump(C, 'dbg_modc', C.modc, C.modc[0:1, :], [1, 2 * D], F32)
    phase_inproj0(C)
    st01.close()
    phase_qkv0(C)
    import os
    if os.environ.get('NOGDN') != '1':
        phase_gdn(C)
    else:
        C.OACC = C.scratch('OACC', [T, 512], F32)
        for r0 in range(0, T, 128):
            S.dma(C.OACC[r0:r0 + 128, :], C.ones_f[:, :], reads=[C.ones_f], writes=[C.OACC])
    PSTOP = int(os.environ.get('PSTOP', '99'))
    X1 = C.scratch('X1', [T, D], F32)
    X2 = C.scratch('X2', [T, D], F32)
    X3 = C.scratch('X3', [T, D], F32)
    if PSTOP >= 1:
        phase_mix0_out(C, X1)
    if PSTOP >= 2:
        phase_ffn_up(C, 0, X1)
    if PSTOP >= 3:
        phase_ffn_down(C, 0, X1, X2)
    if PSTOP >= 4:
        phase_mod(C, 1)
        phase_inproj1(C, X2)
    if PSTOP >= 5:
        phase_mix1_out(C, X2, X3)
    if PSTOP >= 6:
        phase_ffn_up(C, 1, X3)
        phase_ffn_down(C, 1, X3, C.out)

    S.barrier()
    S.flush()
    top.close()
    return nc


def dbg_dump(C, name, tile, ap, shape, dt):
    if name not in C.debug_out:
        return
    d = TileT(C.nc.dram_tensor(name, list(shape), dt, kind="ExternalOutput").ap())
    C.S.dma(d[:], ap, reads=[tile], writes=[d])


def to_col(C, st, psb, dram, R, ncol, name):
    nc, S = C.nc, C.S
    BL = 4
    tmp = sb(nc, st, name + '_row', [R, BL * 128], F32)
    outt = sb(nc, st, name + '_col', [128, ncol, R], F32)
    for c0 in range(0, ncol, BL):
        c1 = min(ncol, c0 + BL)
        S.dma(tmp[:, 0:(c1 - c0) * 128], dram[:, c0 * 128:c1 * 128], writes=[tmp])
        for c in range(c0, c1):
            S.pe(lambda e, c=c, c0=c0: e.transpose(out=psb[:, 0:R], in_=tmp[0:R, (c - c0) * 128:(c - c0 + 1) * 128],
                                                   identity=C.ident_f[0:R, 0:R]),
                 reads=[tmp, C.ident_f], writes=[psb])
            S.dve(lambda e, c=c: e.tensor_copy(out=outt[:, c, :], in_=psb[:, 0:R]), reads=[psb], writes=[outt])
    return outt


def phase_mod(C, layer):
    nc, S = C.nc, C.S
    st = ExitStack()
    pst = ps(nc, st, 'pm_t', [128, 512], F32)
    pacc = [ps(nc, st, 'pm_a%d' % i, [128, 512], F32) for i in range(2)]
    paccc = [ps(nc, st, 'pm_c%d' % i, [128, 512], F32) for i in range(2)]
    ccol = to_col(C, st, pst, C.din['c'][:, :], 1, 8, 'c')
    S.act(lambda e: e.activation(out=ccol[:], in_=ccol[:], func=AF.Silu), reads=[ccol], writes=[ccol])
    rep = sb(nc, st, 'c_rep', [128, 8, 128], F32)
    for k in range(8):
        S.dve(lambda e, k=k: e.tensor_scalar(out=rep[:, k, :], in0=C.ones_f[:, 0:128], scalar1=ccol[:, k, 0:1],
                                             scalar2=None, op0=ALU.mult), reads=[C.ones_f, ccol], writes=[rep])
    brow = sb(nc, st, 'adab_row', [1, 6 * D], F32)
    S.dma(brow[:], C.din['ada_b'][layer:layer + 1, :], writes=[brow])
    if layer == 0:
        cccol = to_col(C, st, pst, C.din['c_ctx'][:, :], 1, 8, 'cc')
        S.act(lambda e: e.activation(out=cccol[:], in_=cccol[:], func=AF.Silu), reads=[cccol], writes=[cccol])
        repc = sb(nc, st, 'cc_rep', [128, 8, 128], F32)
        for k in range(8):
            S.dve(lambda e, k=k: e.tensor_scalar(out=repc[:, k, :], in0=C.ones_f[:, 0:128],
                                                 scalar1=cccol[:, k, 0:1], scalar2=None, op0=ALU.mult),
                  reads=[C.ones_f, cccol], writes=[repc])
    wt = [sb(nc, st, 'adaw%d' % i, [128, 8, 512], F32) for i in range(2)]
    aw = C.din['ada_w']
    for n in range(12):
        w = wt[n % 2]
        S.dma(w[:], aw[layer, :, n * 512:(n + 1) * 512].rearrange("(k p) n -> p k n", p=128), writes=[w])
        pa = pacc[n % 2]
        for k in range(8):
            S.pe(lambda e, k=k, w=w, pa=pa: e.matmul(pa[:], lhsT=rep[:, k, :], rhs=w[:, k, :], start=(k == 0), stop=False),
                 reads=[rep, w], writes=[pa], sig=False)
        S.pe(lambda e, pa=pa, n=n: e.matmul(pa[:], lhsT=C.ones_f[0:1, 0:128], rhs=brow[0:1, n * 512:(n + 1) * 512],
                                            start=False, stop=True), reads=[C.ones_f, brow], writes=[pa])
        S.act(lambda e, pa=pa, n=n: e.activation(out=C.modrow[:, n * 512:(n + 1) * 512], in_=pa[:], func=AF.Copy),
              reads=[pa], writes=[C.modrow])
        if layer == 0 and n < 4:
            pc = paccc[n % 2]
            for k in range(8):
                S.pe(lambda e, k=k, w=w, pc=pc: e.matmul(pc[:], lhsT=repc[:, k, :], rhs=w[:, k, :], start=(k == 0), stop=False),
                     reads=[repc, w], writes=[pc], sig=False)
            S.pe(lambda e, pc=pc, n=n: e.matmul(pc[:], lhsT=C.ones_f[0:1, 0:128], rhs=brow[0:1, n * 512:(n + 1) * 512],
                                                start=False, stop=True), reads=[C.ones_f, brow], writes=[pc])
            S.dve(lambda e, pc=pc, n=n: e.tensor_copy(out=C.modc[:, n * 512:(n + 1) * 512], in_=pc[:]),
                  reads=[pc], writes=[C.modc])
    S.dve(lambda e: e.tensor_scalar_add(out=C.modrow[:, D:2 * D], in0=C.modrow[:, D:2 * D], scalar1=1.0),
          reads=[C.modrow], writes=[C.modrow])
    S.dve(lambda e: e.tensor_scalar_add(out=C.modrow[:, 4 * D:5 * D], in0=C.modrow[:, 4 * D:5 * D], scalar1=1.0),
          reads=[C.modrow], writes=[C.modrow])
    if layer == 0:
        S.dve(lambda e: e.tensor_scalar_add(out=C.modc[:, D:2 * D], in0=C.modc[:, D:2 * D], scalar1=1.0),
              reads=[C.modc], writes=[C.modc])
    S.barrier()
    S.flush()
    st.close()


def modulate_transpose(C, xt, nsub, shift, scale1, ub, uT, pT, evac_i):
    S = C.S
    for s in range(nsub):
        S.pool(lambda e, s=s: e.tensor_tensor(out=xt[:, s, :], in0=xt[:, s, :], in1=scale1, op=ALU.mult),
               reads=[xt, C.modrow, C.modc], writes=[xt])
        S.dve(lambda e, s=s: e.tensor_tensor(out=ub[:, s, :], in0=xt[:, s, :], in1=shift, op=ALU.add),
              reads=[xt, C.modrow, C.modc], writes=[ub])
    for k in range(8):
        p = pT[k % len(pT)]
        for s in range(nsub):
            S.pe(lambda e, s=s, k=k, p=p: e.transpose(out=p[:, s * 128:(s + 1) * 128], in_=ub[:, s, k * 128:(k + 1) * 128],
                                                      identity=C.ident_b[:]),
                 reads=[ub, C.ident_b], writes=[p], sig=(s == nsub - 1))
        if (k + evac_i) % 2 == 0:
            S.act(lambda e, k=k, p=p: e.activation(out=uT[:, k, 0:nsub * 128], in_=p[:, 0:nsub * 128], func=AF.Copy),
                  reads=[p], writes=[uT])
        else:
            S.dve(lambda e, k=k, p=p: e.tensor_copy(out=uT[:, k, 0:nsub * 128], in_=p[:, 0:nsub * 128]),
                  reads=[p], writes=[uT])


def load_w_bf16(C, st, name, dram, kchunks, ncols):
    nc, S = C.nc, C.S
    w = sb(nc, st, name, [128, kchunks, ncols], BF16)
    SW = min(2048, ncols)
    wres = [Res(), Res()]
    NSTG = 3 if ncols > 1024 else 2
    stg = [sb(nc, st, name + '_stg%d' % i, [128, SW], F32) for i in range(NSTG)]
    i = 0
    for k in range(kchunks):
        for c0 in range(0, ncols, SW):
            c1 = min(ncols, c0 + SW)
            b = stg[i % NSTG]
            S.dma(b[:, 0:c1 - c0], dram[k * 128:(k + 1) * 128, c0:c1], writes=[b], q=('sp' if i % 2 == 0 else 'act'))
            wr = wres[i % 2]
            if i % 2 == 0:
                S.dve(lambda e, b=b, k=k, c0=c0, c1=c1: e.tensor_copy(out=w[:, k, c0:c1], in_=b[:, 0:c1 - c0]), reads=[b], writes=[wr])
            else:
                S.act(lambda e, b=b, k=k, c0=c0, c1=c1: e.activation(out=w[:, k, c0:c1], in_=b[:, 0:c1 - c0], func=AF.Copy), reads=[b], writes=[wr])
            i += 1
    S.dve(lambda e: e.tensor_copy(out=w[:, 0, 0:1], in_=w[:, 0, 0:1]), reads=wres, writes=[w] + wres)
    return w


def phase_inproj0(C):
    nc, S = C.nc, C.S
    st = ExitStack()
    P0T = C.scratch('P0T', [2048, T + 16], BF16); C.P0T = P0T
    G0 = C.scratch('G0', [T, 512], BF16); C.G0 = G0
    SG = C.scratch('SG', [T + TC, 16], F32); C.SG = SG
    PCT = C.scratch('PCT', [1024, TC + 16], BF16); C.PCT = PCT
    w = load_w_bf16(C, st, 'w_in0', C.din['even_w_in'][:, :], 8, 2576)
    for r0 in range(0, 2048, 128):
        S.dma(P0T[r0:r0 + 128, 0:8], C.zeros_b[:, 0:8], reads=[C.zeros_b], writes=[P0T])
        S.dma(P0T[r0:r0 + 128, T + 8:T + 16], C.zeros_b[:, 0:8], reads=[C.zeros_b], writes=[P0T])
    for r0 in range(0, 1024, 128):
        S.dma(PCT[r0:r0 + 128, 0:8], C.zeros_b[:, 0:8], reads=[C.zeros_b], writes=[PCT])
        S.dma(PCT[r0:r0 + 128, TC + 8:TC + 16], C.zeros_b[:, 0:8], reads=[C.zeros_b], writes=[PCT])
    xt = [sb(nc, st, 'xt%d' % i, [128, 4, D], F32) for i in range(2)]
    ub = [sb(nc, st, 'ub%d' % i, [128, 4, D], BF16) for i in range(2)]
    uT = [sb(nc, st, 'uT%d' % i, [128, 8, 512], BF16) for i in range(2)]
    pstg = [sb(nc, st, 'pstg%d' % i, [128, 4, 512], BF16) for i in range(2)]
    gstg = [sb(nc, st, 'gstg%d' % i, [128, 4, 512], BF16) for i in range(2)]
    sstg = [sb(nc, st, 'sstg%d' % i, [128, 4, 16], F32) for i in range(2)]
    pT = [ps(nc, st, 'pT%d' % i, [128, 512], BF16) for i in range(2)]
    pm = [ps(nc, st, 'pm%d' % i, [128, 512], F32) for i in range(4)]
    pss = ps(nc, st, 'pss', [128, 4, 16], F32)
    x = C.din['x']
    tiles = [('ctx', 0)] + [('lat', i) for i in range(8)]

    def load(i):
        kind, t = tiles[i]
        b = xt[i % 2]
        if kind == 'ctx':
            S.dma(b[:, 0:2, :], C.din['ctx'][:, :].rearrange("(s p) d -> p s d", p=128), writes=[b])
        else:
            S.dma(b[:, :, :], x[t * 512:(t + 1) * 512, :].rearrange("(s p) d -> p s d", p=128), writes=[b])

    load(0)
    pmi = 0
    for i, (kind, t) in enumerate(tiles):
        if i + 1 < len(tiles):
            load(i + 1)
        b, u, ut = xt[i % 2], ub[i % 2], uT[i % 2]
        isctx = kind == 'ctx'
        nsub = 2 if isctx else 4
        ntok = nsub * 128
        if isctx:
            modulate_transpose(C, b, nsub, C.modc[:, 0:D], C.modc[:, D:2 * D], u, ut, pT, i)
            fchunks = list(range(4, 12))
        else:
            modulate_transpose(C, b, nsub, C.modrow[:, 0:D], C.modrow[:, D:2 * D], u, ut, pT, i)
            fchunks = list(range(0, 12)) + list(range(16, 20))
        for gi in range(0, len(fchunks), 4):
            grp = fchunks[gi:gi + 4]
            stg = pstg[(gi // 4) % 2]
            for j, fc in enumerate(grp):
                p = pm[pmi % 4]; pmi += 1
                for k in range(8):
                    S.pe(lambda e, p=p, k=k, fc=fc, ut=ut, ntok=ntok: e.matmul(
                        p[:, 0:ntok], lhsT=w[:, k, fc * 128:(fc + 1) * 128], rhs=ut[:, k, 0:ntok],
                        start=(k == 0), stop=(k == 7)), reads=[w, ut], writes=[p], sig=(k == 7))
                if j % 2 == 0:
                    S.act(lambda e, p=p, j=j, stg=stg, ntok=ntok: e.activation(out=stg[:, j, 0:ntok], in_=p[:, 0:ntok], func=AF.Copy),
                          reads=[p], writes=[stg])
                else:
                    S.dve(lambda e, p=p, j=j, stg=stg, ntok=ntok: e.tensor_copy(out=stg[:, j, 0:ntok], in_=p[:, 0:ntok]),
                          reads=[p], writes=[stg])
            if isctx:
                r0 = (grp[0] - 4) * 128
                dst = PCT[r0:r0 + 512, 8:8 + ntok].rearrange("(j p) n -> p j n", p=128)
                S.dma(dst, stg[:, :, 0:ntok], reads=[stg], writes=[PCT], q='act')
            else:
                fc0 = grp[0]
                r0 = fc0 * 128 if fc0 < 12 else (fc0 - 4) * 128
                dst = P0T[r0:r0 + 512, 8 + t * 512:8 + (t + 1) * 512].rearrange("(j p) n -> p j n", p=128)
                S.dma(dst, stg[:, :, :], reads=[stg], writes=[P0T], q='act')
        gs = gstg[i % 2]
        ss = sstg[i % 2]
        for s in range(nsub):
            if not isctx:
                p = pm[pmi % 4]; pmi += 1
                for k in range(8):
                    S.pe(lambda e, p=p, k=k, s=s, ut=ut: e.matmul(p[:], lhsT=ut[:, k, s * 128:(s + 1) * 128], rhs=w[:, k, 1536:2048],
                                                            start=(k == 0), stop=(k == 7)), reads=[w, ut], writes=[p], sig=(k == 7))
                S.act(lambda e, p=p, s=s, gs=gs: e.activation(out=gs[:, s, :], in_=p[:], func=AF.Silu), reads=[p], writes=[gs])
            for k in range(8):
                S.pe(lambda e, k=k, s=s, ut=ut: e.matmul(pss[:, s, :], lhsT=ut[:, k, s * 128:(s + 1) * 128], rhs=w[:, k, 2560:2576],
                                                       start=(k == 0), stop=(k == 7)), reads=[w, ut], writes=[pss], sig=(k == 7))
        S.dve(lambda e, ss=ss, nsub=nsub: e.tensor_copy(out=ss[:, 0:nsub, :], in_=pss[:, 0:nsub, :]), reads=[pss], writes=[ss])
        if isctx:
            S.dma(SG[T:T + TC, :].rearrange("(s p) c -> p s c", p=128), ss[:, 0:2, :], reads=[ss], writes=[SG], q='act')
        else:
            S.dma(SG[t * 512:(t + 1) * 512, :].rearrange("(s p) c -> p s c", p=128), ss[:, :, :], reads=[ss], writes=[SG], q='act')
            S.dma(G0[t * 512:(t + 1) * 512, :].rearrange("(s p) c -> p s c", p=128), gs[:, :, :], reads=[gs], writes=[G0], q='act')
    S.barrier()
    S.flush()
    st.close()


def build_diag(C, st, psb, dram, R, ncol, name):
    nc, S = C.nc, C.S
    cw = to_col(C, st, psb, dram, R, ncol, name)
    dg = sb(nc, st, name + '_dg', [128, ncol, R, 128], BF16)
    dgres = [Res() for _ in range(ncol)]
    dg.chunk_res = dgres
    i = 0
    for c in range(ncol):
        for r in range(R):
            dres = dgres[c]
            if i % 2 == 0:
                S.dve(lambda e, c=c, r=r: e.tensor_scalar(out=dg[:, c, r, :], in0=C.ident_b[:], scalar1=cw[:, c, r:r + 1],
                                                          scalar2=None, op0=ALU.mult), reads=[C.ident_b, cw], writes=[dres])
            else:
                S.act(lambda e, c=c, r=r: e.activation(out=dg[:, c, r, :], in_=C.ident_b[:], func=AF.Copy, scale=cw[:, c, r:r + 1]),
                      reads=[C.ident_b, cw], writes=[dres])
            i += 1
    S.dve(lambda e: e.tensor_copy(out=dg[:, 0, 0, 0:1], in_=dg[:, 0, 0, 0:1]), reads=dgres, writes=[dg] + dgres)
    return dg


def phase_qkv0(C):
    nc, S = C.nc, C.S
    st = ExitStack()
    QT = C.scratch('QT', [512, T], BF16); C.QT = QT
    KT = C.scratch('KT', [512, T + TC], BF16); C.KT = KT
    QTOK = C.scratch('QTOK', [T, 512], BF16); C.QTOK = QTOK
    KTOK = C.scratch('KTOK', [T + TC, 512], BF16); C.KTOK = KTOK
    VTOK = C.scratch('VTOK', [T + TC, 512], BF16); C.VTOK = VTOK
    YPT = C.scratch('YPT', [512, T], BF16); C.YPT = YPT
    pconv = [ps(nc, st, 'pconv%d' % i, [128, 512], F32) for i in range(2)]
    pssq = [ps(nc, st, 'pssq%d' % i, [128, 512], F32) for i in range(2)]
    pT = [ps(nc, st, 'pTq%d' % i, [128, 512], BF16) for i in range(2)]
    ppool = ps(nc, st, 'ppool', [128, 512], F32)
    dg = build_diag(C, st, pconv[0], C.din['gdn_conv_w'][:, :], 5, 12, 'cw5')
    pscale = to_col(C, st, pconv[1], C.din['pool_scale'][:, :], 1, 4, 'pscale')
    poolw = sb(nc, st, 'poolw', [128, 4, 128], BF16)
    S.dma(poolw[:], C.din['pool_w'][:, :, :].rearrange("g c d -> c g d"), writes=[poolw], q='pool')
    corrF = sb(nc, st, 'corrF', [128, 4, 8], F32)
    corrL = sb(nc, st, 'corrL', [128, 4, 8], F32)
    S.pool(lambda e: e.memset(corrF[:], 1.0), writes=[corrF])
    S.pool(lambda e: e.memset(corrL[:], 1.0), writes=[corrL])
    for g in range(4):
        hw = 1 << g
        for j in range(hw):
            S.pool(lambda e, g=g, j=j, hw=hw: e.memset(corrF[:, g, j:j + 1], 2.0 * hw / (j + hw)), writes=[corrF])
        for m in range(hw - 1):
            S.pool(lambda e, g=g, m=m, hw=hw: e.memset(corrL[:, g, 7 - m:8 - m], 2.0 * hw / (1 + m + hw)), writes=[corrL])
    pin = [sb(nc, st, 'pin%d' % i, [128, 12, 516], BF16) for i in range(2)]
    pp = [sb(nc, st, 'pp%d' % i, [128, 4, 528], BF16) for i in range(2)]
    xs8 = sb(nc, st, 'xs8', [128, 8, 512], F32)
    ss8 = sb(nc, st, 'ss8', [128, 8, 512], F32)
    epsb = sb(nc, st, 'epsb', [128, 1], F32)
    S.pool(lambda e: e.memset(epsb[:], RMS_EPS), writes=[epsb])
    sqb = [sb(nc, st, 'sqb%d' % i, [128, 512], BF16) for i in range(2)]
    qkn = [sb(nc, st, 'qkn0', [128, 12, 512], BF16)] * 2
    tokst = [[sb(nc, st, 'tok%d_%d' % (g, i), [128, 4, 512], BF16) for i in range(2)] for g in range(3)]
    wa = [sb(nc, st, 'wa%d' % i, [128, 528], F32) for i in range(2)]
    wb = [sb(nc, st, 'wb%d' % i, [128, 528], F32) for i in range(2)]
    pld = [sb(nc, st, 'pld%d' % i, [128, 4, 512], BF16) for i in range(2)]
    ypst = [sb(nc, st, 'ypst%d' % i, [128, 4, 512], BF16) for i in range(2)]
    tiles = [('ctx', 0)] + [('lat', i) for i in range(8)]

    def load(i):
        kind, t = tiles[i]
        b = pin[i % 2]
        if kind == 'ctx':
            S.dma(b[:, 4:12, 0:260], C.PCT[:, 6:266].rearrange("(f p) n -> p f n", p=128), reads=[C.PCT], writes=[b])
        else:
            S.dma(b[:, :, :], C.P0T[0:1536, 6 + t * 512:6 + t * 512 + 516].rearrange("(f p) n -> p f n", p=128),
                  reads=[C.P0T], writes=[b])
            S.dma(pp[i % 2][:, :, :], C.P0T[1536:2048, t * 512:t * 512 + 528].rearrange("(f p) n -> p f n", p=128),
                  reads=[C.P0T], writes=[pp[i % 2]])

    load(0)
    ci = 0
    for i, (kind, t) in enumerate(tiles):
        if i + 1 < len(tiles):
            load(i + 1)
        isctx = kind == 'ctx'
        ntok = 256 if isctx else 512
        nsub = ntok // 128
        b = pin[i % 2]
        qk = qkn[i % 2]
        for fc in (range(4, 12) if isctx else range(12)):
            pc = pconv[ci % 2]
            sq_ = sqb[ci % 2]; pq = pssq[ci % 2]
            ci += 1
            for tap in range(5):
                S.pe(lambda e, pc=pc, fc=fc, tap=tap, b=b, ntok=ntok: e.matmul(
                    pc[:, 0:ntok], lhsT=dg[:, fc, tap, :], rhs=b[:, fc, tap:tap + ntok], start=(tap == 0), stop=(tap == 4)),
                    reads=[dg, b], writes=[pc], sig=(tap == 4))
            if fc >= 8:
                S.act(lambda e, pc=pc, fc=fc, qk=qk, ntok=ntok: e.activation(out=qk[:, fc, 0:ntok], in_=pc[:, 0:ntok], func=AF.Silu),
                      reads=[pc], writes=[qk])
                continue
            S.act(lambda e, pc=pc, fc=fc, ntok=ntok: e.activation(out=xs8[:, fc, 0:ntok], in_=pc[:, 0:ntok], func=AF.Silu),
                  reads=[pc], writes=[xs8])
            S.act(lambda e, fc=fc, sq_=sq_, ntok=ntok: e.activation(out=sq_[:, 0:ntok], in_=xs8[:, fc, 0:ntok], func=AF.Square),
                  reads=[xs8], writes=[sq_])
            S.pe(lambda e, pq=pq, sq_=sq_, ntok=ntok: e.matmul(pq[:, 0:ntok], lhsT=C.ones_b[:], rhs=sq_[:, 0:ntok], start=True, stop=True),
                 reads=[C.ones_b, sq_], writes=[pq])
            S.dve(lambda e, pq=pq, fc=fc, ntok=ntok: e.tensor_copy(out=ss8[:, fc, 0:ntok], in_=pq[:, 0:ntok]), reads=[pq], writes=[ss8])
        f0 = 4 if isctx else 0
        S.act(lambda e, f0=f0, ntok=ntok: e.activation(out=ss8[:, f0:8, 0:ntok], in_=ss8[:, f0:8, 0:ntok], func=AF.Ln, bias=epsb[:, 0:1]),
              reads=[ss8, epsb], writes=[ss8])
        S.act(lambda e, f0=f0, ntok=ntok: e.activation(out=ss8[:, f0:8, 0:ntok], in_=ss8[:, f0:8, 0:ntok], func=AF.Exp, scale=-0.5),
              reads=[ss8], writes=[ss8])
        for fc in range(f0, 8):
            sc = (128.0 ** -0.5) if fc < 4 else 1.0
            fn = lambda e, qk=qk, fc=fc, sc=sc, ntok=ntok: e.scalar_tensor_tensor(
                out=qk[:, fc, 0:ntok], in0=xs8[:, fc, 0:ntok], scalar=sc, in1=ss8[:, fc, 0:ntok], op0=ALU.mult, op1=ALU.mult)
            if fc < 4:
                S.dve(fn, reads=[xs8, ss8], writes=[qk])
            elif fc % 2 == 0:
                S.dve(lambda e, qk=qk, fc=fc, ntok=ntok: e.tensor_tensor(out=qk[:, fc, 0:ntok], in0=xs8[:, fc, 0:ntok],
                                                                      in1=ss8[:, fc, 0:ntok], op=ALU.mult),
                      reads=[xs8, ss8], writes=[qk])
            else:
                S.pool(lambda e, qk=qk, fc=fc, ntok=ntok: e.tensor_tensor(out=qk[:, fc, 0:ntok], in0=xs8[:, fc, 0:ntok],
                                                                       in1=ss8[:, fc, 0:ntok], op=ALU.mult),
                       reads=[xs8, ss8], writes=[qk])
        ti = 0
        for g in ((1, 2) if isctx else (0, 1, 2)):
            tk = tokst[g][i % 2]
            for s_ in range(nsub):
                p = pT[ti % 2]; ti += 1
                for h in range(4):
                    S.pe(lambda e, p=p, h=h, g=g, s_=s_, qk=qk: e.transpose(out=p[:, h * 128:(h + 1) * 128],
                                                                       in_=qk[:, g * 4 + h, s_ * 128:(s_ + 1) * 128], identity=C.ident_b[:]),
                         reads=[qk, C.ident_b], writes=[p], sig=(h == 3))
                if ti % 2 == 0:
                    S.act(lambda e, p=p, tk=tk, s_=s_: e.activation(out=tk[:, s_, :], in_=p[:], func=AF.Copy), reads=[p], writes=[tk])
                else:
                    S.dve(lambda e, p=p, tk=tk, s_=s_: e.tensor_copy(out=tk[:, s_, :], in_=p[:]), reads=[p], writes=[tk])
        c0 = T if isctx else t * 512
        if not isctx:
            S.dma(QT[:, c0:c0 + 512].rearrange("(f p) n -> p f n", p=128), qk[:, 0:4, :], reads=[qk], writes=[QT], q='act')
            S.dma(QTOK[c0:c0 + 512, :].rearrange("(s p) c -> p s c", p=128), tokst[0][i % 2][:, :, :], reads=[tokst[0][i % 2]], writes=[QTOK], q='act')
        S.dma(KT[:, c0:c0 + ntok].rearrange("(f p) n -> p f n", p=128), qk[:, 4:8, 0:ntok], reads=[qk], writes=[KT], q='act')
        S.dma(KTOK[c0:c0 + ntok, :].rearrange("(s p) c -> p s c", p=128), tokst[1][i % 2][:, 0:nsub, :], reads=[tokst[1][i % 2]], writes=[KTOK], q='act')
        S.dma(VTOK[c0:c0 + ntok, :].rearrange("(s p) c -> p s c", p=128), tokst[2][i % 2][:, 0:nsub, :], reads=[tokst[2][i % 2]], writes=[VTOK], q='act')
        if isctx:
            continue
        ppb = pp[i % 2]
        pl = pld[i % 2]
        yp = ypst[i % 2]
        for g in range(4):
            a_, b_ = wa[g % 2], wb[g % 2]
            eng_add = S.dve if g >= 2 else S.pool
            eng_add(lambda e, a_=a_, g=g, ppb=ppb: e.tensor_tensor(out=a_[:, 1:527], in0=ppb[:, g, 0:526], in1=ppb[:, g, 1:527], op=ALU.add),
                    reads=[ppb], writes=[a_])
            cur, oth = a_, b_
            lo, hi = 1, 527
            for lvl in range(g):
                sh = 1 << lvl
                lo, hi = lo + sh, hi - sh
                eng_add(lambda e, cur=cur, oth=oth, lo=lo, hi=hi, sh=sh: e.tensor_tensor(
                    out=oth[:, lo:hi], in0=cur[:, lo - sh:hi - sh], in1=cur[:, lo + sh:hi + sh], op=ALU.add),
                    reads=[cur], writes=[oth])
                cur, oth = oth, cur
            S.dve(lambda e, cur=cur, g=g: e.tensor_scalar(out=cur[:, 8:520], in0=cur[:, 8:520], scalar1=1.0 / (2 << g), scalar2=None, op0=ALU.mult),
                  reads=[cur], writes=[cur])
            if t == 0:
                S.dve(lambda e, cur=cur, g=g: e.tensor_tensor(out=cur[:, 8:16], in0=cur[:, 8:16], in1=corrF[:, g, :], op=ALU.mult),
                      reads=[cur, corrF], writes=[cur])
            if t == 7:
                S.dve(lambda e, cur=cur, g=g: e.tensor_tensor(out=cur[:, 512:520], in0=cur[:, 512:520], in1=corrL[:, g, :], op=ALU.mult),
                      reads=[cur, corrL], writes=[cur])
            S.dve(lambda e, cur=cur, g=g, pl=pl, ppb=ppb: e.tensor_tensor(out=pl[:, g, :], in0=cur[:, 8:520], in1=ppb[:, g, 8:520], op=ALU.subtract),
                  reads=[cur, ppb], writes=[pl])
            S.pe(lambda e, g=g, pl=pl: e.matmul(ppool[:], lhsT=poolw[:, g, :], rhs=pl[:, g, :], start=True, stop=True),
                 reads=[poolw, pl], writes=[ppool])
            S.act(lambda e, g=g, yp=yp: e.activation(out=yp[:, g, :], in_=ppool[:], func=AF.Identity, scale=pscale[:, g, 0:1]),
                  reads=[ppool, pscale], writes=[yp])
        S.dma(YPT[:, c0:c0 + 512].rearrange("(f p) n -> p f n", p=128), yp[:, :, :], reads=[yp], writes=[YPT], q='act')
    S.barrier()
    S.flush()
    st.close()


class Slot:
    def __init__(self, bank, k):
        self.f = bank.t[:, k * 128:(k + 1) * 128]
        self.b = bank.t[:, :].bitcast(BF16)[:, k * 256:k * 256 + 128]
        self.res = bank.res


def run_interleaved(gens):
    gens = list(gens)
    while gens:
        nxt = []
        for g in gens:
            try:
                next(g)
                nxt.append(g)
            except StopIteration:
                pass
        gens = nxt


def phase_gdn(C):
    nc, S = C.nc, C.S
    st = ExitStack()
    NT = 34
    import os
    OACC = C.scratch('OACC', [T, 512], F32); C.OACC = OACC
    banks = [ps(nc, st, 'gbank%d' % i, [128, 512], F32) for i in range(8)]
    for b_ in banks:
        b_.res.excl = True
    slots = [[Slot(banks[c], k) for k in range(4)] for c in range(8)]
    sall = sb(nc, st, 'sall', [128, NT, 16], F32)
    for n0 in ([] if os.environ.get('NOSALL') == '1' else range(0, NT, 6)):
        n1 = min(NT, n0 + 6)
        S.dma(sall[:, n0:n1, :], C.SG[n0 * 128:n1 * 128, :].rearrange("(n p) c -> p n c", p=128), reads=[C.SG], writes=[sall])
    adb = sb(nc, st, 'adb', [128, 16], F32)
    S.dma(adb[:, 0:8], C.din['gdn_a_log'][0:1, :].to_broadcast([128, 8]), writes=[adb])
    S.dma(adb[:, 8:16], C.din['gdn_dt_bias'][0:1, :].to_broadcast([128, 8]), writes=[adb])
    S.act(lambda e: e.activation(out=adb[:, 0:8], in_=adb[:, 0:8], func=AF.Exp), reads=[adb], writes=[adb])
    S.dve(lambda e: e.tensor_scalar(out=adb[:, 0:8], in0=adb[:, 0:8], scalar1=-1.0, scalar2=None, op0=ALU.mult),
          reads=[adb], writes=[adb])

    GCUT = int(os.environ.get('GCUT', '0'))

    def fin():
        S.barrier(); S.flush(); st.close()
    if GCUT == 1:
        return fin()

    def gt(name):
        return sb(nc, st, name, [128, NT, 8], F32)
    beta, g_, gc, eg, be, kds, gl = gt('g_beta'), gt('g_g'), gt('g_gc'), gt('g_eg'), gt('g_be'), gt('g_kds'), gt('g_gl')
    S.act(lambda e: e.activation(out=beta[:], in_=sall[:, :, 0:8], func=AF.Sigmoid), reads=[sall], writes=[beta])
    S.dve(lambda e: e.tensor_tensor(out=g_[:], in0=sall[:, :, 8:16], in1=adb[:, 8:16].unsqueeze(1).to_broadcast([128, NT, 8]), op=ALU.add),
          reads=[sall, adb], writes=[g_])
    S.act(lambda e: e.activation(out=g_[:], in_=g_[:], func=AF.Exp), reads=[g_], writes=[g_])
    S.act(lambda e: e.activation(out=g_[:], in_=g_[:], func=AF.Ln, bias=1.0), reads=[g_], writes=[g_])
    S.dve(lambda e: e.tensor_tensor(out=g_[:], in0=g_[:], in1=adb[:, 0:8].unsqueeze(1).to_broadcast([128, NT, 8]), op=ALU.mult),
          reads=[g_, adb], writes=[g_])
    if GCUT == 2:
        return fin()
    Lt = sb(nc, st, 'Lt', [128, 128], F32)
    Ut = sb(nc, st, 'Ut', [128, 128], F32)
    bigm = [sb(nc, st, 'bigm%d' % i, [128, 128], F32) for i in range(2)]
    strict = [sb(nc, st, 'strict%d' % i, [128, 128], F32) for i in range(2)]
    bigfull = sb(nc, st, 'bigfull', [128, 128], F32)
    S.pool(lambda e: e.memset(bigfull[:], BIG), writes=[bigfull])
    one = C.ones_f[:, 0:128]
    S.pool(lambda e: e.affine_select(out=Lt[:], in_=one, pattern=[[1, 128]], compare_op=ALU.is_ge, fill=0.0, base=0, channel_multiplier=-1),
           reads=[C.ones_f], writes=[Lt])
    S.pool(lambda e: e.affine_select(out=Ut[:], in_=one, pattern=[[-1, 128]], compare_op=ALU.is_ge, fill=0.0, base=0, channel_multiplier=1),
           reads=[C.ones_f], writes=[Ut])
    S.pool(lambda e: e.affine_select(out=bigm[0][:], in_=bigfull[:], pattern=[[1, 128]], compare_op=ALU.is_gt, fill=0.0, base=0, channel_multiplier=-1),
           reads=[bigfull], writes=[bigm[0]])
    S.pool(lambda e: e.affine_select(out=bigm[1][:], in_=bigfull[:], pattern=[[-1, 128]], compare_op=ALU.is_gt, fill=0.0, base=0, channel_multiplier=1),
           reads=[bigfull], writes=[bigm[1]])
    S.pool(lambda e: e.affine_select(out=strict[0][:], in_=one, pattern=[[-1, 128]], compare_op=ALU.is_gt, fill=0.0, base=0, channel_multiplier=1),
           reads=[C.ones_f], writes=[strict[0]])
    S.pool(lambda e: e.affine_select(out=strict[1][:], in_=one, pattern=[[1, 128]], compare_op=ALU.is_gt, fill=0.0, base=0, channel_multiplier=-1),
           reads=[C.ones_f], writes=[strict[1]])
    if GCUT == 3:
        return fin()
    Bm = {}
    for s_ in (16, 32, 64):
        G = 128 // s_
        E = sb(nc, st, 'E%d' % s_, [G, 128], F32)
        S.pool(lambda e, E=E, G=G, s_=s_: e.affine_select(out=E[:], in_=C.ones_f[0:G, 0:128], pattern=[[1, 128]], compare_op=ALU.is_ge,
                                                         fill=0.0, base=0, channel_multiplier=-s_), reads=[C.ones_f], writes=[E])
        S.pool(lambda e, E=E, G=G, s_=s_: e.affine_select(out=E[:], in_=E[:], pattern=[[-1, 128]], compare_op=ALU.is_gt,
                                                         fill=0.0, base=s_, channel_multiplier=s_), reads=[E], writes=[E])
        pb_ = banks[4]
        S.pe(lambda e, E=E, pb_=pb_: e.matmul(pb_[:, 0:128], lhsT=E[:], rhs=E[:], start=True, stop=True), reads=[E], writes=[pb_])
        Bm[s_] = sb(nc, st, 'Bm%d' % s_, [128, 128], F32)
        S.dve(lambda e, s_=s_, pb_=pb_: e.tensor_copy(out=Bm[s_][:], in_=pb_[:, 0:128]), reads=[pb_], writes=[Bm[s_]])
    Md = [sb(nc, st, 'Md%d' % d, [128, 128], F32) for d in range(2)]
    Mo = [[sb(nc, st, 'Mo%d_%d' % (d, l), [128, 128], F32) for l in range(3)] for d in range(2)]
    for d in range(2):
        S.dve(lambda e, d=d: e.tensor_tensor(out=Md[d][:], in0=strict[d][:], in1=Bm[16][:], op=ALU.mult), reads=[strict[d], Bm[16]], writes=[Md[d]])
        for l, (big_, small_) in enumerate(((32, 16), (64, 32), (None, 64))):
            t_ = Mo[d][l]
            if big_ is None:
                S.dve(lambda e, t_=t_, small_=small_: e.tensor_scalar(out=t_[:], in0=Bm[small_][:], scalar1=-1.0, scalar2=1.0, op0=ALU.mult, op1=ALU.add),
                      reads=[Bm[small_]], writes=[t_])
            else:
                S.dve(lambda e, t_=t_, big_=big_, small_=small_: e.tensor_tensor(out=t_[:], in0=Bm[big_][:], in1=Bm[small_][:], op=ALU.subtract),
                      reads=[Bm[big_], Bm[small_]], writes=[t_])
            S.dve(lambda e, t_=t_, d=d: e.tensor_tensor(out=t_[:], in0=t_[:], in1=strict[d][:], op=ALU.mult), reads=[t_, strict[d]], writes=[t_])
    pgc = banks[1]
    S.pe(lambda e: e.matmul(pgc[:, 0:NT * 8], lhsT=Lt[:], rhs=g_[:, :, :], start=True, stop=True), reads=[Lt, g_], writes=[pgc])
    S.dve(lambda e: e.tensor_copy(out=gc[:, :, 0:4], in_=pgc[:, 0:NT * 8].rearrange("p (n c) -> p n c", c=8)[:, :, 0:4]), reads=[pgc], writes=[gc])
    pgc2 = banks[2]
    S.pe(lambda e: e.matmul(pgc2[:, 0:NT * 8], lhsT=Ut[:], rhs=g_[:, :, :], start=True, stop=True), reads=[Ut, g_], writes=[pgc2])
    S.dve(lambda e: e.tensor_copy(out=gc[:, :, 4:8], in_=pgc2[:, 0:NT * 8].rearrange("p (n c) -> p n c", c=8)[:, :, 4:8]), reads=[pgc2], writes=[gc])
    if GCUT == 5:
        return fin()
    pgt = banks[3]
    S.pe(lambda e: e.matmul(pgt[:, 0:NT * 8], lhsT=C.ones_f[:, 0:128], rhs=g_[:, :, :], start=True, stop=True), reads=[C.ones_f, g_], writes=[pgt])
    S.act(lambda e: e.activation(out=gl[:], in_=pgt[:, 0:NT * 8].rearrange("p (n c) -> p n c", c=8), func=AF.Exp), reads=[pgt], writes=[gl])
    S.dve(lambda e: e.tensor_tensor(out=kds[:], in0=pgt[:, 0:NT * 8].rearrange("p (n c) -> p n c", c=8), in1=gc[:], op=ALU.subtract),
          reads=[pgt, gc], writes=[kds])
    if GCUT == 6:
        return fin()
    S.act(lambda e: e.activation(out=kds[:], in_=kds[:], func=AF.Exp), reads=[kds], writes=[kds])
    S.act(lambda e: e.activation(out=eg[:], in_=gc[:], func=AF.Exp), reads=[gc], writes=[eg])
    S.dve(lambda e: e.tensor_tensor(out=be[:], in0=beta[:], in1=eg[:], op=ALU.mult), reads=[beta, eg], writes=[be])
    GCT = C.scratch('GCT', [NT * 8, 128], F32)
    gcTs = sb(nc, st, 'gcTs', [128, 3, 128], F32)
    gcf = gc[:].rearrange("p n c -> p (n c)")
    for bi, (q0, q1) in enumerate(((0, 128), (128, 256), (256, NT * 8))):
        pb_ = banks[5 + bi]
        S.pe(lambda e, pb_=pb_, q0=q0, q1=q1: e.transpose(out=pb_[0:q1 - q0, 0:128], in_=gcf[:, q0:q1], identity=C.ident_f[:]),
             reads=[gc, C.ident_f], writes=[pb_])
        S.dve(lambda e, pb_=pb_, bi=bi, q0=q0, q1=q1: e.tensor_copy(out=gcTs[0:q1 - q0, bi, :], in_=pb_[0:q1 - q0, 0:128]), reads=[pb_], writes=[gcTs])
        S.dma(GCT[q0:q1, :], gcTs[0:q1 - q0, bi, :], reads=[gcTs], writes=[GCT])
    if GCUT == 4:
        return fin()
    dbg_dump(C, 'dbg_gc', gc, gc[:, :, :], [128, NT, 8], F32)
    dbg_dump(C, 'dbg_beta', beta, beta[:, :, :], [128, NT, 8], F32)
    dbg_dump(C, 'dbg_g', g_, g_[:, :, :], [128, NT, 8], F32)

    S.barrier()
    import os
    GSTOP = int(os.environ.get('GSTOP', '99'))
    def tile_of(d, n):
        if n < 2:
            return 32 + n if d == 0 else 33 - n
        return n - 2 if d == 0 else 33 - n
    opnd = [[{k: sb(nc, st, 'op_%s_%d_%d' % (k, d, i), [128, 4, 128], BF16) for k in ('kT', 'qT', 'ktok', 'qtok', 'vtok')}
             for i in range(2)] for d in range(2)]

    Rb = [[sb(nc, st, 'Rb_%d_%d' % (c, i), [128, 128], F32) for i in range(2)] for c in range(8)]

    def load_tile(d, n):
        nt = tile_of(d, n)
        o = opnd[d][n % 2]
        for h_ in range(4):
            c_ = d * 4 + h_
            S.dma(Rb[c_][n % 2][:], GCT[nt * 8 + c_:nt * 8 + c_ + 1, :].to_broadcast([128, 128]), reads=[GCT], writes=[Rb[c_][n % 2]])
        c0 = T + (nt - 32) * 128 if nt >= 32 else nt * 128
        S.dma(o['kT'][:, :, :], C.KT[:, c0:c0 + 128].rearrange("(h p) n -> p h n", p=128), reads=[C.KT], writes=[o['kT']])
        S.dma(o['ktok'][:, :, :], C.KTOK[c0:c0 + 128, :].rearrange("p (h d) -> p h d", d=128), reads=[C.KTOK], writes=[o['ktok']])
        S.dma(o['vtok'][:, :, :], C.VTOK[c0:c0 + 128, :].rearrange("p (h d) -> p h d", d=128), reads=[C.VTOK], writes=[o['vtok']])
        if nt < 32:
            S.dma(o['qT'][:, :, :], C.QT[:, c0:c0 + 128].rearrange("(h p) n -> p h n", p=128), reads=[C.QT], writes=[o['qT']])
            S.dma(o['qtok'][:, :, :], C.QTOK[c0:c0 + 128, :].rearrange("p (h d) -> p h d", d=128), reads=[C.QTOK], writes=[o['qtok']])

    def cb(name, dt, n=1, shape=(128, 128)):
        return [[sb(nc, st, '%s_%d_%d' % (name, c, i), list(shape), dt) for i in range(n)] for c in range(8)]
    dgc = cb('dgc', F32); Dm = dgc; Ai = dgc
    Pb = cb('Pb', BF16, 2); PTb = cb('PTb', BF16, 2); Yb = cb('Yb', BF16, 2)
    bv = cb('bv', BF16); kbe = cb('kbe', BF16); qe = cb('qe', BF16); AOb = cb('AOb', BF16, 3)
    attnT = cb('attnT', BF16, 2); u_ = cb('u_', F32, 2); wT = cb('wT', BF16, 2); kd = cb('kd', BF16, 2); qdT = cb('qdT', BF16, 2)
    S32 = cb('S32', F32); Sbf = cb('Sbf', BF16, 2); vn = cb('vn', BF16)
    for c in range(8):
        S.pool(lambda e, c=c: e.memset(S32[c][0][:], 0.0), writes=[S32[c][0]])
        S.pool(lambda e, c=c: e.memset(Sbf[c][0][:], 0.0), writes=[Sbf[c][0]])
    oacc = sb(nc, st, 'oacc', [128, 32, 512], F32)
    ores = [[Res() for h in range(4)] for nt in range(32)]
    ofirst = [[True] * 4 for nt in range(32)]

    def precompute(c, n):
        d, h = c // 4, c % 4
        nt = tile_of(d, n)
        lat = nt < 32
        o = opnd[d][n % 2]
        r = n % 2
        sl = slots[c]
        gcol = gc[:, nt, c:c + 1]
        S.dve(lambda e: e.tensor_tensor(out=Dm[c][0][:], in0=Rb[c][r][:], in1=bigm[d][:], op=ALU.add),
              reads=[Rb[c][r], bigm[d]], writes=[Dm[c][0]])
        S.pe(lambda e: e.matmul(sl[1].f, lhsT=o['kT'][:, h, :], rhs=o['kT'][:, h, :], start=True, stop=True),
             reads=[o['kT']], writes=[sl[1]])
        if lat:
            S.pe(lambda e: e.matmul(sl[2].f, lhsT=o['qT'][:, h, :], rhs=o['kT'][:, h, :], start=True, stop=True),
                 reads=[o['kT'], o['qT']], writes=[sl[2]])
        yield
        S.act(lambda e: e.activation(out=Dm[c][0][:], in_=Dm[c][0][:], func=AF.Exp, bias=gcol, scale=-1.0),
              reads=[Dm[c][0], gc], writes=[Dm[c][0]])
        yield
        if lat:
            S.dve(lambda e: e.tensor_tensor(out=qe[c][0][:], in0=sl[2].f, in1=Dm[c][0][:], op=ALU.mult),
                  reads=[sl[2], Dm[c][0]], writes=[qe[c][0]])
        S.dve(lambda e: e.scalar_tensor_tensor(out=Ai[c][0][:], in0=sl[1].f, scalar=beta[:, nt, c:c + 1], in1=Dm[c][0][:],
                                               op0=ALU.mult, op1=ALU.mult), reads=[sl[1], beta, Dm[c][0]], writes=[Ai[c][0]])
        yield
        A = Pb[c][0]
        S.dve(lambda e: e.tensor_tensor(out=A[:], in0=Ai[c][0][:], in1=Md[d][:], op=ALU.mult),
              reads=[Ai[c][0], Md[d]], writes=[A])
        for li in range(3):
            fn = lambda e, li=li: e.tensor_tensor(out=AOb[c][li][:], in0=Ai[c][0][:], in1=Mo[d][li][:], op=ALU.mult)
            if li < 2:
                S.dve(fn, reads=[Ai[c][0], Mo[d][li]], writes=[AOb[c][li]])
            else:
                S.pool(fn, reads=[Ai[c][0], Mo[d][li]], writes=[AOb[c][li]])
        yield
        S.pe(lambda e: e.transpose(out=sl[0].b, in_=A[:], identity=C.ident_b[:]), reads=[A, C.ident_b], writes=[sl[0]])
        if lat:
            S.pe(lambda e: e.transpose(out=sl[1].b, in_=qe[c][0][:], identity=C.ident_b[:]), reads=[qe[c][0], C.ident_b], writes=[sl[1]])
        yield
        AT = PTb[c][0]
        Y = Yb[c][0]
        S.act(lambda e: e.activation(out=AT[:], in_=sl[0].b, func=AF.Copy), reads=[sl[0]], writes=[AT])
        S.dve(lambda e: e.scalar_tensor_tensor(out=Y[:], in0=sl[0].b, scalar=-1.0, in1=C.ident_b[:], op0=ALU.mult, op1=ALU.add),
              reads=[sl[0], C.ident_b], writes=[Y])
        if lat:
            S.act(lambda e: e.activation(out=attnT[c][r][:], in_=sl[1].b, func=AF.Copy), reads=[sl[1]], writes=[attnT[c][r]])
        yield
        S.act(lambda e: e.activation(out=bv[c][0][:], in_=o['vtok'][:, h, :], func=AF.Copy, scale=beta[:, nt, c:c + 1]),
              reads=[o['vtok'], beta], writes=[bv[c][0]])
        S.act(lambda e: e.activation(out=kbe[c][0][:], in_=o['ktok'][:, h, :], func=AF.Copy, scale=be[:, nt, c:c + 1]),
              reads=[o['ktok'], be], writes=[kbe[c][0]])
        S.pool(lambda e: e.tensor_scalar(out=kd[c][r][:], in0=o['ktok'][:, h, :], scalar1=kds[:, nt, c:c + 1], scalar2=None, op0=ALU.mult),
               reads=[o['ktok'], kds], writes=[kd[c][r]])
        if lat:
            S.act(lambda e: e.activation(out=qe[c][0][:], in_=o['qtok'][:, h, :], func=AF.Copy, scale=eg[:, nt, c:c + 1]),
                  reads=[o['qtok'], eg], writes=[qe[c][0]])
        cur = 0
        for lvl in range(1, 4):
            P, PT, Yc = Pb[c][cur], PTb[c][cur], Yb[c][cur]
            Pn, PTn, Yn = Pb[c][1 - cur], PTb[c][1 - cur], Yb[c][1 - cur]
            S.pe(lambda e, P=P, PT=PT: e.matmul(sl[0].f, lhsT=PT[:], rhs=P[:], start=True, stop=True), reads=[P, PT], writes=[sl[0]])
            if lvl < 3:
                S.pe(lambda e, P=P, PT=PT: e.matmul(sl[1].f, lhsT=P[:], rhs=PT[:], start=True, stop=True), reads=[P, PT], writes=[sl[1]])
            yield
            S.act(lambda e, Pn=Pn: e.activation(out=Pn[:], in_=sl[0].f, func=AF.Copy), reads=[sl[0]], writes=[Pn])
            if lvl < 3:
                S.dve(lambda e, PTn=PTn: e.tensor_copy(out=PTn[:], in_=sl[1].f), reads=[sl[1]], writes=[PTn])
            yield
            S.pe(lambda e, Pn=Pn, Yc=Yc: e.matmul(sl[2].f, lhsT=Pn[:], rhs=Yc[:], start=True, stop=True), reads=[Pn, Yc], writes=[sl[2]])
            yield
            S.dve(lambda e, Yc=Yc, Yn=Yn: e.tensor_tensor(out=Yn[:], in0=sl[2].f, in1=Yc[:], op=ALU.add), reads=[sl[2], Yc], writes=[Yn])
            yield
            cur = 1 - cur
        for li in range(3):
            Yc, Yn = Yb[c][cur], Yb[c][1 - cur]
            Tt, N1 = Pb[c][0], PTb[c][0]
            S.pe(lambda e, Yc=Yc: e.transpose(out=sl[0].b, in_=Yc[:], identity=C.ident_b[:]), reads=[Yc, C.ident_b], writes=[sl[0]])
            S.pe(lambda e, Yc=Yc, li=li: e.matmul(sl[1].f, lhsT=AOb[c][li][:], rhs=Yc[:], start=True, stop=True),
                 reads=[AOb[c][li], Yc], writes=[sl[1]])
            yield
            S.act(lambda e, Tt=Tt: e.activation(out=Tt[:], in_=sl[0].b, func=AF.Copy), reads=[sl[0]], writes=[Tt])
            S.dve(lambda e, N1=N1: e.tensor_copy(out=N1[:], in_=sl[1].f), reads=[sl[1]], writes=[N1])
            yield
            S.pe(lambda e, Tt=Tt, N1=N1: e.matmul(sl[2].f, lhsT=Tt[:], rhs=N1[:], start=True, stop=True), reads=[Tt, N1], writes=[sl[2]])
            yield
            S.dve(lambda e, Yc=Yc, Yn=Yn: e.scalar_tensor_tensor(out=Yn[:], in0=sl[2].f, scalar=-1.0, in1=Yc[:], op0=ALU.mult, op1=ALU.add),
                  reads=[sl[2], Yc], writes=[Yn])
            yield
            cur = 1 - cur
        Y = Yb[c][cur]
        S.pe(lambda e: e.matmul(sl[0].f, lhsT=Y[:], rhs=bv[c][0][:], start=True, stop=True), reads=[Y, bv[c][0]], writes=[sl[0]])
        S.pe(lambda e: e.matmul(sl[1].f, lhsT=kbe[c][0][:], rhs=Y[:], start=True, stop=True), reads=[Y, kbe[c][0]], writes=[sl[1]])
        if lat:
            S.pe(lambda e: e.transpose(out=sl[2].b, in_=qe[c][0][:], identity=C.ident_b[:]), reads=[qe[c][0], C.ident_b], writes=[sl[2]])
        yield
        S.act(lambda e: e.activation(out=u_[c][r][:], in_=sl[0].f, func=AF.Copy), reads=[sl[0]], writes=[u_[c][r]])
        S.dve(lambda e: e.tensor_copy(out=wT[c][r][:], in_=sl[1].f), reads=[sl[1]], writes=[wT[c][r]])
        if lat:
            S.act(lambda e: e.activation(out=qdT[c][r][:], in_=sl[2].b, func=AF.Copy), reads=[sl[2]], writes=[qdT[c][r]])
        yield

    def scan(c, n):
        d, h = c // 4, c % 4
        nt = tile_of(d, n)
        lat = nt < 32
        r = n % 2
        sl = slots[c][3]
        Sold, Snew = Sbf[c][n % 2], Sbf[c][1 - n % 2]
        S.pe(lambda e: e.matmul(sl.f, lhsT=wT[c][r][:], rhs=Sold[:], start=True, stop=True), reads=[wT[c][r], Sold], writes=[sl])
        yield
        S.dve(lambda e: e.scalar_tensor_tensor(out=vn[c][0][:], in0=sl.f, scalar=-1.0, in1=u_[c][r][:], op0=ALU.mult, op1=ALU.add),
              reads=[sl, u_[c][r]], writes=[vn[c][0]])
        yield
        S.pe(lambda e: e.matmul(sl.f, lhsT=kd[c][r][:], rhs=vn[c][0][:], start=True, stop=True), reads=[kd[c][r], vn[c][0]], writes=[sl])
        yield
        glc = gl[:, nt, c:c + 1]
        S.dve(lambda e: e.scalar_tensor_tensor(out=Snew[:], in0=S32[c][0][:], scalar=glc, in1=sl.f, op0=ALU.mult, op1=ALU.add),
              reads=[S32[c][0], gl, sl], writes=[Snew])
        S.dve(lambda e: e.scalar_tensor_tensor(out=S32[c][0][:], in0=S32[c][0][:], scalar=glc, in1=sl.f, op0=ALU.mult, op1=ALU.add),
              reads=[S32[c][0], gl, sl], writes=[S32[c][0]])
        yield
        if lat:
            S.pe(lambda e: e.matmul(sl.f, lhsT=qdT[c][r][:], rhs=Sold[:], start=True, stop=False), reads=[qdT[c][r], Sold], writes=[sl], sig=False)
            S.pe(lambda e: e.matmul(sl.f, lhsT=attnT[c][r][:], rhs=vn[c][0][:], start=False, stop=True), reads=[attnT[c][r], vn[c][0]], writes=[sl])
            yield
            orr = ores[nt][h]
            if ofirst[nt][h]:
                ofirst[nt][h] = False
                S.act(lambda e: e.activation(out=oacc[:, nt, h * 128:(h + 1) * 128], in_=sl.f, func=AF.Copy), reads=[sl], writes=[orr])
            else:
                S.dve(lambda e: e.tensor_tensor(out=oacc[:, nt, h * 128:(h + 1) * 128], in0=sl.f, in1=oacc[:, nt, h * 128:(h + 1) * 128], op=ALU.add),
                      reads=[sl, orr], writes=[orr])
            yield

    NR = min(34, GSTOP)
    if GSTOP >= 0:
        for d in range(2):
            load_tile(d, 0)
        run_interleaved([precompute(c, 0) for c in range(8)])
    for n in range(NR):
        gens = [scan(c, n) for c in range(8)]
        if n + 1 < NR:
            for d in range(2):
                load_tile(d, n + 1)
            gens += [precompute(c, n + 1) for c in range(8)]
        run_interleaved(gens)
    for nt in (range(32) if NR == 34 else []):
        S.dma(OACC[nt * 128:(nt + 1) * 128, :], oacc[:, nt, :], reads=ores[nt], writes=[OACC], q='sp')
    for c in range(8):
        dbg_dump(C, 'dbg_S%d' % c, S32[c][0], S32[c][0][:], [128, 128], F32)
    S.barrier()
    S.flush()
    st.close()


def load_rows_bcast(C, st, name, dram_row, n):
    t = sb(C.nc, st, name, [128, n], F32)
    C.S.dma(t[:], dram_row.to_broadcast([128, n]), writes=[t])
    return t


class Epi:
    def __init__(self, C, st, ln_idx, gate_ap, nsub, nbuf=1):
        nc = C.nc
        self.C, self.nsub, self.gate = C, nsub, gate_ap
        self.g = load_rows_bcast(C, st, 'ln_g%d' % ln_idx, C.din['ln_g'][ln_idx:ln_idx + 1, :], D)
        self.b = load_rows_bcast(C, st, 'ln_b%d' % ln_idx, C.din['ln_b'][ln_idx:ln_idx + 1, :], D)
        self.t2s = [sb(nc, st, 'ep_t2_%d' % i, [128, nsub, D], F32) for i in range(nbuf)]
        self.t2 = self.t2s[0]
        self.junk = sb(nc, st, 'ep_junk', [128, D], BF16)
        self.st = sb(nc, st, 'ep_st', [128, 6, nsub], F32)
        self.eps = sb(nc, st, 'ep_eps', [128, 1], F32)
        C.S.pool(lambda e: e.memset(self.eps[:], LN_EPS), writes=[self.eps])
        self.i = 0

    def sub(self, s_, ypair, xt):
        S, t2, stt = self.C.S, self.t2, self.st
        for hf in range(2):
            S.dve(lambda e, hf=hf: e.tensor_tensor(out=t2[:, s_, hf * 512:(hf + 1) * 512], in0=ypair[hf][:],
                                                   in1=self.gate[:, hf * 512:(hf + 1) * 512], op=ALU.mult),
                  reads=[ypair[hf], self.C.modrow], writes=[t2])
        S.dve(lambda e: e.scalar_tensor_tensor(out=t2[:, s_, :], in0=xt[:, s_, :], scalar=ALPHA, in1=t2[:, s_, :], op0=ALU.mult, op1=ALU.add),
              reads=[xt, t2], writes=[t2])
        S.act(lambda e: e.activation(out=self.junk[:], in_=t2[:, s_, :], func=AF.Copy, accum_out=stt[:, 0, s_:s_ + 1]),
              reads=[t2], writes=[self.junk, stt])
        S.act(lambda e: e.activation(out=self.junk[:], in_=t2[:, s_, :], func=AF.Square, accum_out=stt[:, 1, s_:s_ + 1]),
              reads=[t2], writes=[self.junk, stt])

    def finish(self, dst_rows, xt_unused=None):
        S, t2, stt, n = self.C.S, self.t2, self.st, self.nsub
        dst, r0 = dst_rows
        xo = t2
        self.i += 1
        self.t2 = self.t2s[self.i % len(self.t2s)]
        S.dve(lambda e: e.tensor_scalar(out=stt[:, 2, :], in0=stt[:, 0, :], scalar1=1.0 / D, scalar2=None, op0=ALU.mult), reads=[stt], writes=[stt])
        S.dve(lambda e: e.tensor_tensor(out=stt[:, 4, :], in0=stt[:, 2, :], in1=stt[:, 2, :], op=ALU.mult), reads=[stt], writes=[stt])
        S.dve(lambda e: e.scalar_tensor_tensor(out=stt[:, 3, :], in0=stt[:, 1, :], scalar=1.0 / D, in1=stt[:, 4, :], op0=ALU.mult, op1=ALU.subtract),
              reads=[stt], writes=[stt])
        S.act(lambda e: e.activation(out=stt[:, 3, :], in_=stt[:, 3, :], func=AF.Ln, bias=self.eps[:, 0:1]), reads=[stt, self.eps], writes=[stt])
        S.act(lambda e: e.activation(out=stt[:, 3, :], in_=stt[:, 3, :], func=AF.Exp, scale=-0.5), reads=[stt], writes=[stt])
        S.dve(lambda e: e.scalar_tensor_tensor(out=stt[:, 5, :], in0=stt[:, 2, :], scalar=-1.0, in1=stt[:, 3, :], op0=ALU.mult, op1=ALU.mult),
              reads=[stt], writes=[stt])
        for s_ in range(n):
            S.act(lambda e, s_=s_: e.activation(out=t2[:, s_, :], in_=t2[:, s_, :], func=AF.Identity, scale=stt[:, 3, s_:s_ + 1], bias=stt[:, 5, s_:s_ + 1]),
                  reads=[t2, stt], writes=[t2])
            S.pool(lambda e, s_=s_: e.tensor_tensor(out=xo[:, s_, :], in0=t2[:, s_, :], in1=self.g[:], op=ALU.mult), reads=[t2, self.g], writes=[xo])
            S.dve(lambda e, s_=s_: e.tensor_tensor(out=xo[:, s_, :], in0=xo[:, s_, :], in1=self.b[:], op=ALU.add), reads=[xo, self.b], writes=[xo])
        S.dma(dst[r0:r0 + n * 128, :].rearrange("(s p) d -> p s d", p=128), xo[:, :, :], reads=[xo], writes=[dst], q='sp')


def out_proj(C, mixT, wout, s_, ypair, nk):
    S = C.S
    for hf in range(2):
        for k in range(nk):
            S.pe(lambda e, hf=hf, k=k: e.matmul(ypair[hf][:], lhsT=mixT[:, k, s_ * 128:(s_ + 1) * 128], rhs=wout[:, k, hf * 512:(hf + 1) * 512],
                                                start=(k == 0), stop=(k == nk - 1)), reads=[mixT, wout], writes=[ypair[hf]], sig=(k == nk - 1))


def phase_mix0_out(C, X1):
    nc, S = C.nc, C.S
    st = ExitStack()
    wout = load_w_bf16(C, st, 'wout0', C.din['even_w_out'][:, :], 8, D)
    normw = load_rows_bcast(C, st, 'normw', C.din['gdn_norm_w'][0:1, :], 128)
    epi = Epi(C, st, 0, C.modrow[:, 2 * D:3 * D], 4)
    yps = [[ps(nc, st, 'yps%d_%d' % (i, h), [128, 512], F32) for h in range(2)] for i in range(2)]
    pT = [ps(nc, st, 'pTm%d' % i, [128, 512], BF16) for i in range(2)]
    ot = [sb(nc, st, 'ot%d' % i, [128, 4, 512], F32) for i in range(2)]
    gt_ = [sb(nc, st, 'gt%d' % i, [128, 4, 512], BF16) for i in range(2)]
    xt = [sb(nc, st, 'xtm%d' % i, [128, 4, D], F32) for i in range(2)]
    mixT = [sb(nc, st, 'mixT%d' % i, [128, 8, 512], BF16) for i in range(2)]
    osq = sb(nc, st, 'osq', [128, 4, 512], F32)
    ss = sb(nc, st, 'oss', [128, 16], F32)
    ogs = [sb(nc, st, 'og%d' % i, [128, 4, 512], BF16) for i in range(2)]
    epsr = sb(nc, st, 'epsr', [128, 1], F32)
    S.pool(lambda e: e.memset(epsr[:], RMS_EPS), writes=[epsr])

    def load_front(t):
        i = t % 2
        S.dma(ot[i][:, :, :], C.OACC[t * 512:(t + 1) * 512, :].rearrange("(s p) d -> p s d", p=128), reads=[C.OACC], writes=[ot[i]])
        S.dma(gt_[i][:, :, :], C.G0[t * 512:(t + 1) * 512, :].rearrange("(s p) d -> p s d", p=128), reads=[C.G0], writes=[gt_[i]])

    def load_back(t):
        i = t % 2
        S.dma(xt[i][:, :, :], C.din['x'][t * 512:(t + 1) * 512, :].rearrange("(s p) d -> p s d", p=128), writes=[xt[i]])
        S.dma(mixT[i][:, 4:8, :], C.YPT[:, t * 512:(t + 1) * 512].rearrange("(f p) n -> p f n", p=128), reads=[C.YPT], writes=[mixT[i]])

    def front(t):
        i = t % 2
        o_, g_, x_, m_, og = ot[i], gt_[i], xt[i], mixT[i], ogs[i]
        S.pool(lambda e, o_=o_: e.tensor_tensor(out=osq[:], in0=o_[:], in1=o_[:], op=ALU.mult), reads=[o_], writes=[osq])
        S.dve(lambda e: e.tensor_reduce(out=ss[:], in_=osq[:].rearrange("p s (h d) -> p (s h) d", d=128), axis=mybir.AxisListType.X, op=ALU.add),
              reads=[osq], writes=[ss])
        S.act(lambda e: e.activation(out=ss[:], in_=ss[:], func=AF.Ln, scale=1.0 / 128, bias=epsr[:, 0:1]), reads=[ss, epsr], writes=[ss])
        S.act(lambda e: e.activation(out=ss[:], in_=ss[:], func=AF.Exp, scale=-0.5), reads=[ss], writes=[ss])
        S.dve(lambda e, o_=o_: e.tensor_tensor(out=osq[:].rearrange("p s (h d) -> p (s h) d", d=128), in0=o_[:].rearrange("p s (h d) -> p (s h) d", d=128),
                                              in1=ss[:].unsqueeze(2).to_broadcast([128, 16, 128]), op=ALU.mult), reads=[o_, ss], writes=[osq])
        S.pool(lambda e: e.tensor_tensor(out=osq[:].rearrange("p s (h d) -> p (s h) d", d=128), in0=osq[:].rearrange("p s (h d) -> p (s h) d", d=128),
                                         in1=normw[:].unsqueeze(1).to_broadcast([128, 16, 128]), op=ALU.mult), reads=[osq, normw], writes=[osq])
        S.dve(lambda e, g_=g_: e.tensor_tensor(out=og[:], in0=osq[:], in1=g_[:], op=ALU.mult), reads=[osq, g_], writes=[og])
        for h in range(4):
            p = pT[h % 2]
            for s_ in range(4):
                S.pe(lambda e, p=p, h=h, s_=s_: e.transpose(out=p[:, s_ * 128:(s_ + 1) * 128], in_=og[:, s_, h * 128:(h + 1) * 128], identity=C.ident_b[:]),
                     reads=[og, C.ident_b], writes=[p], sig=(s_ == 3))
            if h % 2 == 0:
                S.act(lambda e, p=p, h=h, m_=m_: e.activation(out=m_[:, h, :], in_=p[:], func=AF.Copy), reads=[p], writes=[m_])
            else:
                S.dve(lambda e, p=p, h=h, m_=m_: e.tensor_copy(out=m_[:, h, :], in_=p[:]), reads=[p], writes=[m_])

    load_front(0)
    load_back(0)
    front(0)
    load_front(1)
    load_back(1)
    yi = 0
    for t in range(8):
        if t + 1 < 8:
            front(t + 1)
        if t + 2 < 8:
            load_front(t + 2)
        m_, x_ = mixT[t % 2], xt[t % 2]
        for s_ in range(4):
            yp = yps[yi % 2]; yi += 1
            out_proj(C, m_, wout, s_, yp, 8)
            epi.sub(s_, yp, x_)
        epi.finish((X1, t * 512))
        if t + 2 < 8:
            load_back(t + 2)
    dbg_dump(C, 'dbg_x1', X1, X1[0:128, :], [128, D], F32)
    S.barrier()
    S.flush()
    st.close()


def phase_ffn_up(C, layer, Xin):
    nc, S = C.nc, C.S
    st = ExitStack()
    if layer == 0:
        C.AT = C.scratch('AT', [DFF, T + 128], BF16)
        C.GTt = C.scratch('GTt', [DFF, T], BF16)
        for r0 in range(0, DFF, 128):
            S.dma(C.AT[r0:r0 + 128, 0:64], C.zeros_b[:, 0:64], reads=[C.zeros_b], writes=[C.AT])
            S.dma(C.AT[r0:r0 + 128, T + 64:T + 128], C.zeros_b[:, 0:64], reads=[C.zeros_b], writes=[C.AT])
    AT, GTt = C.AT, C.GTt
    w = load_w_bf16(C, st, 'wup', C.din['ffn_w_up'][layer, :, :], 8, 2 * DFF)
    xt = sb(nc, st, 'xtu', [128, 4, D], F32)
    ub = sb(nc, st, 'ubu', [128, 4, D], BF16)
    uT = [sb(nc, st, 'uTu%d' % i, [128, 8, 512], BF16) for i in range(2)]
    stg = [sb(nc, st, 'stgu%d' % i, [128, 4, 512], BF16) for i in range(2)]
    pT = [ps(nc, st, 'pTu%d' % i, [128, 512], BF16) for i in range(2)]
    pm = [ps(nc, st, 'pmu%d' % i, [128, 512], F32) for i in range(4)]
    pmi = 0
    for t in range(8):
        S.dma(xt[:, :, :], Xin[t * 512:(t + 1) * 512, :].rearrange("(s p) d -> p s d", p=128), reads=[Xin], writes=[xt])
        ut = uT[t % 2]
        modulate_transpose(C, xt, 4, C.modrow[:, 3 * D:4 * D], C.modrow[:, 4 * D:5 * D], ub, ut, pT, t)
        gi = 0
        for f0 in range(0, 44, 4):
            sg = stg[gi % 2]; gi += 1
            nf = min(4, 44 - f0)
            for j in range(nf):
                fc = f0 + j
                p = pm[pmi % 4]; pmi += 1
                for k in range(8):
                    S.pe(lambda e, p=p, k=k, fc=fc, ut=ut: e.matmul(p[:], lhsT=w[:, k, fc * 128:(fc + 1) * 128], rhs=ut[:, k, :],
                                                                 start=(k == 0), stop=(k == 7)), reads=[w, ut], writes=[p], sig=(k == 7))
                if pmi % 2 == 0:
                    S.act(lambda e, p=p, j=j, sg=sg: e.activation(out=sg[:, j, :], in_=p[:], func=AF.Copy), reads=[p], writes=[sg])
                else:
                    S.dve(lambda e, p=p, j=j, sg=sg: e.tensor_copy(out=sg[:, j, :], in_=p[:]), reads=[p], writes=[sg])
            if f0 < 22:
                na = min(nf, 22 - f0)
                S.dma(AT[f0 * 128:(f0 + na) * 128, 64 + t * 512:64 + (t + 1) * 512].rearrange("(j p) n -> p j n", p=128), sg[:, 0:na, :],
                      reads=[sg], writes=[AT], q='act')
                if na < nf:
                    S.dma(GTt[0:(nf - na) * 128, t * 512:(t + 1) * 512].rearrange("(j p) n -> p j n", p=128), sg[:, na:nf, :],
                          reads=[sg], writes=[GTt], q='act')
            else:
                g0 = f0 - 22
                S.dma(GTt[g0 * 128:(g0 + nf) * 128, t * 512:(t + 1) * 512].rearrange("(j p) n -> p j n", p=128), sg[:, 0:nf, :],
                      reads=[sg], writes=[GTt], q='act')
    S.barrier()
    S.flush()
    st.close()


def phase_ffn_down(C, layer, Xin, Xout):
    phase_ffn_conv(C, layer)
    phase_ffn_proj(C, layer, Xin, Xout)


def phase_ffn_conv(C, layer):
    nc, S = C.nc, C.S
    st = ExitStack()
    AT, GTt = C.AT, C.GTt
    if layer == 0:
        C.HGT = C.scratch('HGT', [DFF, T], BF16)
    HGT = C.HGT
    NTK, NH = 512, 11
    pconv = [ps(nc, st, 'pcv%d' % i, [128, 512], F32) for i in range(4)]
    dg = build_diag(C, st, pconv[0], C.din['ffn_conv_w'][layer, :, :], 9, 22, 'cw9')
    ad = [sb(nc, st, 'ad%d' % i, [128, NH, 640], BF16) for i in range(2)]
    gt_ = [sb(nc, st, 'gd%d' % i, [128, NH, 512], BF16) for i in range(2)]
    apad = [sb(nc, st, 'apad%d' % i, [128, NH, 10, 66], BF16) for i in range(2)]
    hgs = [sb(nc, st, 'hgs%d' % i, [128, NH, 512], BF16) for i in range(2)]
    hs = [sb(nc, st, 'hs%d' % i, [128, 512], BF16) for i in range(4)]
    for i in range(2):
        S.pool(lambda e, i=i: e.memset(apad[i][:], 0.0), writes=[apad[i]])
    items = [(t, hf) for t in range(T // NTK) for hf in range(2)]
    n = len(items)

    def load_a(w):
        t, hf = items[w]
        S.dma(ad[w % 2][:, :, :], AT[hf * NH * 128:(hf + 1) * NH * 128, t * NTK:t * NTK + 640].rearrange("(f p) n -> p f n", p=128),
              reads=[AT], writes=[ad[w % 2]])

    def load_g(w):
        t, hf = items[w]
        S.dma(gt_[w % 2][:, :, :], GTt[hf * NH * 128:(hf + 1) * NH * 128, t * NTK:(t + 1) * NTK].rearrange("(f p) n -> p f n", p=128),
              reads=[GTt], writes=[gt_[w % 2]])

    def pad(w):
        a_, ap_ = ad[w % 2], apad[w % 2]
        for j in range(NH):
            fn = lambda e, j=j, a_=a_, ap_=ap_: e.tensor_copy(out=ap_[:, j, :, 1:65], in_=a_[:, j, :].rearrange("p (r c) -> p r c", c=64))
            if j % 3 == 0:
                S.pool(fn, reads=[a_], writes=[ap_])
            else:
                S.dve(fn, reads=[a_], writes=[ap_])

    load_a(0)
    load_g(0)
    pad(0)
    load_a(1)
    ci = 0
    for w in range(n):
        t, hf = items[w]
        if w + 1 < n:
            pad(w + 1)
            load_g(w + 1)
        if w + 2 < n:
            load_a(w + 2)
        g_, ap_, hg = gt_[w % 2], apad[w % 2], hgs[w % 2]
        for j in range(NH):
            fc = hf * NH + j
            pc = pconv[ci % 4]; h_ = hs[ci % 4]; ci += 1
            for tap in range(9):
                dy, dx = tap // 3 - 1, tap % 3 - 1
                S.pe(lambda e, pc=pc, fc=fc, j=j, tap=tap, dy=dy, dx=dx, ap_=ap_: e.matmul(
                    pc[:], lhsT=dg[:, fc, tap, :], rhs=ap_[:, j, 1 + dy:9 + dy, 1 + dx:65 + dx], start=(tap == 0), stop=(tap == 8)),
                    reads=[dg, ap_], writes=[pc], sig=(tap == 8))
            S.act(lambda e, pc=pc, h_=h_: e.activation(out=h_[:], in_=pc[:], func=AF.Silu), reads=[pc], writes=[h_])
            S.dve(lambda e, j=j, h_=h_, g_=g_, hg=hg: e.tensor_tensor(out=hg[:, j, :], in0=h_[:], in1=g_[:, j, :], op=ALU.mult),
                  reads=[h_, g_], writes=[hg])
        S.dma(HGT[hf * NH * 128:(hf + 1) * NH * 128, t * NTK:(t + 1) * NTK].rearrange("(f p) n -> p f n", p=128), hg[:, :, :],
              reads=[hg], writes=[HGT], q='act')
    S.barrier()
    S.flush()
    st.close()


def phase_ffn_proj(C, layer, Xin, Xout):
    nc, S = C.nc, C.S
    st = ExitStack()
    HGT = C.HGT
    yps = [[ps(nc, st, 'ypd%d_%d' % (i, h), [128, 512], F32) for h in range(2)] for i in range(3)]
    wd = load_w_bf16(C, st, 'wdn', C.din['ffn_w_down'][layer, :, :], 22, D)
    epi = Epi(C, st, layer * 2 + 1, C.modrow[:, 5 * D:6 * D], 4, nbuf=2)
    hg = [sb(nc, st, 'hgp%d' % i, [128, 22, 512], BF16) for i in range(2)]
    xt = [sb(nc, st, 'xtd%d' % i, [128, 4, D], F32) for i in range(2)]

    def load(t):
        i = t % 2
        S.dma(hg[i][:, :, :], HGT[:, t * 512:(t + 1) * 512].rearrange("(f p) n -> p f n", p=128), reads=[HGT], writes=[hg[i]])
        S.dma(xt[i][:, :, :], Xin[t * 512:(t + 1) * 512, :].rearrange("(s p) d -> p s d", p=128), reads=[Xin], writes=[xt[i]])

    load(0)
    yi = 0
    for t in range(8):
        if t + 1 < 8:
            load(t + 1)
        h_, x_ = hg[t % 2], xt[t % 2]
        for s_ in range(4):
            yp = yps[yi % 3]; yi += 1
            out_proj(C, h_, wd, s_, yp, 22)
            epi.sub(s_, yp, x_)
        epi.finish((Xout, t * 512))
    S.barrier()
    S.flush()
    st.close()


def phase_inproj1(C, Xin):
    nc, S = C.nc, C.S
    st = ExitStack()
    GBT = C.scratch('GBT', [512, T], BF16); C.GBT = GBT
    M1T = C.scratch('M1T', [512, T + 32], BF16); C.M1T = M1T
    M2T = C.scratch('M2T', [512, T + 32], BF16); C.M2T = M2T
    for M in (M1T, M2T):
        for r0 in range(0, 512, 128):
            S.dma(M[r0:r0 + 128, 0:16], C.zeros_b[:, 0:16], reads=[C.zeros_b], writes=[M])
            S.dma(M[r0:r0 + 128, T + 16:T + 32], C.zeros_b[:, 0:16], reads=[C.zeros_b], writes=[M])
    w = load_w_bf16(C, st, 'w_in1', C.din['odd_w_in'][:, :], 8, 2560)
    xt = sb(nc, st, 'xt1', [128, 4, D], F32)
    ub = sb(nc, st, 'ub1', [128, 4, D], BF16)
    uT = [sb(nc, st, 'uT1%d' % i, [128, 8, 512], BF16) for i in range(2)]
    stg = [[sb(nc, st, 'stg1_%d_%d' % (g, i), [128, 4, 512], BF16) for i in range(2)] for g in range(3)]
    tmp = [sb(nc, st, 'tmp1_%d' % i, [128, 512], F32) for i in range(2)]
    pT = [ps(nc, st, 'pT1%d' % i, [128, 512], BF16) for i in range(2)]
    pm = [ps(nc, st, 'pm1%d' % i, [128, 512], F32) for i in range(4)]
    pmi = 0
    ti = 0

    def mm(fc, ut):
        nonlocal pmi
        p = pm[pmi % 4]; pmi += 1
        for k in range(8):
            S.pe(lambda e, p=p, k=k: e.matmul(p[:], lhsT=w[:, k, fc * 128:(fc + 1) * 128], rhs=ut[:, k, :], start=(k == 0), stop=(k == 7)),
                 reads=[w, ut], writes=[p], sig=(k == 7))
        return p

    for t in range(8):
        S.dma(xt[:, :, :], Xin[t * 512:(t + 1) * 512, :].rearrange("(s p) d -> p s d", p=128), reads=[Xin], writes=[xt])
        ut = uT[t % 2]
        modulate_transpose(C, xt, 4, C.modrow[:, 0:D], C.modrow[:, D:2 * D], ub, ut, pT, t)
        sgb, sm1, sm2 = stg[0][t % 2], stg[1][t % 2], stg[2][t % 2]
        for j in range(4):
            p = mm(j, ut)
            S.act(lambda e, p=p, j=j, sgb=sgb: e.activation(out=sgb[:, j, :], in_=p[:], func=AF.Copy), reads=[p], writes=[sgb])
            tm = tmp[ti % 2]; ti += 1
            p = mm(4 + j, ut)
            S.act(lambda e, p=p, tm=tm: e.activation(out=tm[:], in_=p[:], func=AF.Copy), reads=[p], writes=[tm])
            p = mm(8 + j, ut)
            S.dve(lambda e, p=p, tm=tm, j=j, sm1=sm1: e.tensor_tensor(out=sm1[:, j, :], in0=p[:], in1=tm[:], op=ALU.mult), reads=[p, tm], writes=[sm1])
            tm = tmp[ti % 2]; ti += 1
            p = mm(16 + j, ut)
            S.act(lambda e, p=p, tm=tm: e.activation(out=tm[:], in_=p[:], func=AF.Sigmoid), reads=[p], writes=[tm])
            p = mm(12 + j, ut)
            S.dve(lambda e, p=p, tm=tm, j=j, sm2=sm2: e.tensor_tensor(out=sm2[:, j, :], in0=p[:], in1=tm[:], op=ALU.mult), reads=[p, tm], writes=[sm2])
        c0 = t * 512
        S.dma(GBT[:, c0:c0 + 512].rearrange("(j p) n -> p j n", p=128), sgb[:, :, :], reads=[sgb], writes=[GBT], q='act')
        S.dma(M1T[:, 16 + c0:16 + c0 + 512].rearrange("(j p) n -> p j n", p=128), sm1[:, :, :], reads=[sm1], writes=[M1T], q='act')
        S.dma(M2T[:, 16 + c0:16 + c0 + 512].rearrange("(j p) n -> p j n", p=128), sm2[:, :, :], reads=[sm2], writes=[M2T], q='act')
    S.barrier()
    S.flush()
    st.close()


def phase_mix1_out(C, Xin, Xout):
    nc, S = C.nc, C.S
    st = ExitStack()
    pconv = [ps(nc, st, 'pc1%d' % i, [128, 512], F32) for i in range(2)]
    pstat = [ps(nc, st, 'pst1%d' % i, [128, 512], F32) for i in range(2)]
    yps = [[ps(nc, st, 'yp1%d_%d' % (i, h), [128, 512], F32) for h in range(2)] for i in range(2)]
    dg3 = build_diag(C, st, pconv[0], C.din['sconv_w'][:, :], 3, 4, 'cw3')
    dg31 = build_diag(C, st, pconv[1], C.din['conf_conv_w'][:, :], 31, 4, 'cw31')
    lng = to_col(C, st, pconv[0], C.din['conf_ln_g'][:, :], 1, 4, 'clng')
    lnb = to_col(C, st, pconv[1], C.din['conf_ln_b'][:, :], 1, 4, 'clnb')
    wout = load_w_bf16(C, st, 'wout1', C.din['odd_w_out'][:, :], 8, D)
    epi = Epi(C, st, 2, C.modrow[:, 2 * D:3 * D], 4)
    m1 = [sb(nc, st, 'm1_%d' % i, [128, 4, 514], BF16) for i in range(2)]
    m2 = [sb(nc, st, 'm2_%d' % i, [128, 4, 542], BF16) for i in range(2)]
    gb = [sb(nc, st, 'gb_%d' % i, [128, 4, 512], BF16) for i in range(2)]
    xt = [sb(nc, st, 'xt1o%d' % i, [128, 4, D], F32) for i in range(2)]
    mixT = sb(nc, st, 'mixT1', [128, 8, 512], BF16)
    z = sb(nc, st, 'z1', [128, 4, 512], F32)
    zsq = sb(nc, st, 'zsq1', [128, 4, 512], F32)
    mean = sb(nc, st, 'mean1', [128, 512], F32)
    rstd = sb(nc, st, 'rstd1', [128, 512], F32)
    msq = sb(nc, st, 'msq1', [128, 512], F32)
    epsl = sb(nc, st, 'epsl1', [128, 1], F32)
    S.pool(lambda e: e.memset(epsl[:], LN_EPS), writes=[epsl])

    def load(t):
        i = t % 2
        c0 = t * 512
        S.dma(m1[i][:, :, :], C.M1T[:, 15 + c0:15 + c0 + 514].rearrange("(f p) n -> p f n", p=128), reads=[C.M1T], writes=[m1[i]])
        S.dma(m2[i][:, :, :], C.M2T[:, 1 + c0:1 + c0 + 542].rearrange("(f p) n -> p f n", p=128), reads=[C.M2T], writes=[m2[i]])
        S.dma(gb[i][:, :, :], C.GBT[:, c0:c0 + 512].rearrange("(f p) n -> p f n", p=128), reads=[C.GBT], writes=[gb[i]])
        S.dma(xt[i][:, :, :], Xin[c0:c0 + 512, :].rearrange("(s p) d -> p s d", p=128), reads=[Xin], writes=[xt[i]])

    load(0)
    ci = 0
    yi = 0
    for t in range(8):
        if t + 1 < 8:
            load(t + 1)
        i = t % 2
        a1, a2, g_, x_ = m1[i], m2[i], gb[i], xt[i]
        for j in range(4):
            pc = pconv[ci % 2]; ci += 1
            for tap in range(3):
                S.pe(lambda e, pc=pc, j=j, tap=tap, a1=a1: e.matmul(pc[:], lhsT=dg3[:, j, tap, :], rhs=a1[:, j, tap:tap + 512], start=(tap == 0), stop=(tap == 2)),
                     reads=[dg3, a1], writes=[pc], sig=(tap == 2))
            S.dve(lambda e, pc=pc, j=j, g_=g_: e.tensor_tensor(out=mixT[:, j, :], in0=pc[:], in1=g_[:, j, :], op=ALU.mult), reads=[pc, g_], writes=[mixT])
        for j in range(4):
            pc = pconv[ci % 2]; ci += 1
            for tap in range(31):
                S.pe(lambda e, pc=pc, j=j, tap=tap, a2=a2: e.matmul(pc[:], lhsT=dg31[:, j, tap, :], rhs=a2[:, j, tap:tap + 512], start=(tap == 0), stop=(tap == 30)),
                     reads=[dg31, a2], writes=[pc], sig=(tap == 30))
            S.act(lambda e, pc=pc, j=j: e.activation(out=z[:, j, :], in_=pc[:], func=AF.Copy), reads=[pc], writes=[z])
            S.pool(lambda e, j=j: e.tensor_tensor(out=zsq[:, j, :], in0=z[:, j, :], in1=z[:, j, :], op=ALU.mult), reads=[z], writes=[zsq])
        for j in range(4):
            S.pe(lambda e, j=j: e.matmul(pstat[0][:], lhsT=C.ones_f[:, 0:128], rhs=z[:, j, :], start=(j == 0), stop=(j == 3)),
                 reads=[C.ones_f, z], writes=[pstat[0]], sig=(j == 3))
        for j in range(4):
            S.pe(lambda e, j=j: e.matmul(pstat[1][:], lhsT=C.ones_f[:, 0:128], rhs=zsq[:, j, :], start=(j == 0), stop=(j == 3)),
                 reads=[C.ones_f, zsq], writes=[pstat[1]], sig=(j == 3))
        S.dve(lambda e: e.tensor_scalar(out=mean[:], in0=pstat[0][:], scalar1=1.0 / 512, scalar2=None, op0=ALU.mult), reads=[pstat[0]], writes=[mean])
        S.dve(lambda e: e.tensor_tensor(out=msq[:], in0=mean[:], in1=mean[:], op=ALU.mult), reads=[mean], writes=[msq])
        S.dve(lambda e: e.scalar_tensor_tensor(out=rstd[:], in0=pstat[1][:], scalar=1.0 / 512, in1=msq[:], op0=ALU.mult, op1=ALU.subtract),
              reads=[pstat[1], msq], writes=[rstd])
        S.act(lambda e: e.activation(out=rstd[:], in_=rstd[:], func=AF.Ln, bias=epsl[:, 0:1]), reads=[rstd, epsl], writes=[rstd])
        S.act(lambda e: e.activation(out=rstd[:], in_=rstd[:], func=AF.Exp, scale=-0.5), reads=[rstd], writes=[rstd])
        for j in range(4):
            S.dve(lambda e, j=j: e.tensor_tensor(out=z[:, j, :], in0=z[:, j, :], in1=mean[:], op=ALU.subtract), reads=[z, mean], writes=[z])
            S.pool(lambda e, j=j: e.tensor_tensor(out=z[:, j, :], in0=z[:, j, :], in1=rstd[:], op=ALU.mult), reads=[z, rstd], writes=[z])
            S.act(lambda e, j=j: e.activation(out=mixT[:, 4 + j, :], in_=z[:, j, :], func=AF.Silu, scale=lng[:, j, 0:1], bias=lnb[:, j, 0:1]),
                  reads=[z, lng, lnb], writes=[mixT])
        for s_ in range(4):
            yp = yps[yi % 2]; yi += 1
            out_proj(C, mixT, wout, s_, yp, 8)
            epi.sub(s_, yp, x_)
        epi.finish((Xout, t * 512))
    S.barrier()
    S.flush()
    st.close()


_NC_CACHE = {}


def kernel(**inputs):
    if 'nc' not in _NC_CACHE:
        _NC_CACHE['nc'] = build()
    nc = _NC_CACHE['nc']
    f = lambda a: np.ascontiguousarray(np.asarray(a, dtype=np.float32))
    shared = {
        'c_ctx': f(inputs['c_ctx']).reshape(1, D),
        'ada_w': f(inputs['ada_w']), 'ada_b': f(inputs['ada_b']),
        'ln_g': f(inputs['ln_g']).reshape(4, D), 'ln_b': f(inputs['ln_b']).reshape(4, D),
        'even_w_in': f(inputs['even_w_in']), 'even_w_out': f(inputs['even_w_out']),
        'gdn_conv_w': f(inputs['gdn_conv_w']), 'gdn_a_log': f(inputs['gdn_a_log']).reshape(1, 8),
        'gdn_dt_bias': f(inputs['gdn_dt_bias']).reshape(1, 8), 'gdn_norm_w': f(inputs['gdn_norm_w']).reshape(1, 128),
        'pool_w': f(inputs['pool_w']), 'pool_scale': f(inputs['pool_scale']).reshape(1, 512),
        'odd_w_in': f(inputs['odd_w_in']), 'odd_w_out': f(inputs['odd_w_out']),
        'sconv_w': f(inputs['sconv_w']), 'conf_conv_w': f(inputs['conf_conv_w']),
        'conf_ln_g': f(inputs['conf_ln_g']).reshape(1, 512), 'conf_ln_b': f(inputs['conf_ln_b']).reshape(1, 512),
        'ffn_w_up': f(inputs['ffn_w_up']), 'ffn_conv_w': f(inputs['ffn_conv_w']).reshape(2, 9, DFF),
        'ffn_w_down': f(inputs['ffn_w_down']),
    }
    x = f(inputs['x']); c = f(inputs['c']); ctx = f(inputs['ctx'])
    in_maps = []
    for b in range(NCORES):
        m = dict(shared)
        m['x'] = x[b]; m['c'] = c[b:b + 1]; m['ctx'] = ctx[b]
        in_maps.append(m)
    res = run_bass_kernel_spmd(nc, in_maps, core_ids=list(range(NCORES)))
    return np.stack([r['out'] for r in res.results], axis=0)
```

```python
import numpy as np
from contextlib import ExitStack
import concourse.bass as bass
import concourse.mybir as mybir
from concourse.bass_utils import run_bass_kernel_spmd

F32 = mybir.dt.float32
BF16 = mybir.dt.bfloat16
AF = mybir.ActivationFunctionType
ALU = mybir.AluOpType

D = 1024
T = 4096
TC = 256
NCORES = 8
DFF = 2816
ALPHA = 4 ** 0.25
LN_EPS = 1e-5
RMS_EPS = 1e-6
BIG = 30000.0
ENG = ('pe', 'act', 'dve', 'pool', 'sp')
NDS = 24
import os as _os
SELF_SYNC = ('pool',) if _os.environ.get('RELAX') == '1' else ('act', 'dve', 'pool')

DEBUG_OUT = []


class Res:
    __slots__ = ('w', 'r', 'excl')

    def __init__(self):
        self.w = None
        self.r = []
        self.excl = False


class TileT:
    def __init__(self, t):
        self.t = t
        self.res = Res()

    def __getitem__(self, k):
        return self.t[k]


class Sched:
    def __init__(self, nc, stack):
        self.nc = nc
        self.sem = {e: stack.enter_context(nc.semaphore('s_' + e)) for e in ENG}
        self.cnt = {e: 0 for e in ENG}
        self.known = {e: {} for e in ENG}
        self.q = {e: [] for e in ENG}
        self.dsem = [stack.enter_context(nc.semaphore('dq%d' % i)) for i in range(NDS)]
        self.dcnt = [0] * NDS
        self.dpool = {'sp': list(range(0, 12)), 'act': list(range(12, 20)), 'pool': list(range(20, 24))}
        self.dnext = {'sp': 0, 'act': 0, 'pool': 0}
        self.unsig = {e: False for e in ENG}

    def semof(self, key):
        return self.sem[key] if isinstance(key, str) else self.dsem[key]

    def _collect(self, eng, reads, writes):
        toks = []
        for r in reads:
            if r.w is not None:
                toks.append(r.w)
        for w in writes:
            if w.w is not None and (w.w[0] != eng or eng in SELF_SYNC):
                toks.append(w.w)
            for t in w.r:
                if t[0] != eng or eng in SELF_SYNC:
                    toks.append(t)
        waits = {}
        kn = self.known[eng]
        for key, val in toks:
            if kn.get(key, 0) < val:
                waits[key] = max(waits.get(key, 0), val)
        for key, val in waits.items():
            kn[key] = val
        return list(waits.items())

    def _update(self, tok, reads, writes):
        for r in reads:
            r.r.append(tok)
        for w in writes:
            w.w = tok
            w.r = []

    def emit(self, eng, fn, reads=(), writes=(), sig=True):
        reads = [getattr(x, "res", x) for x in reads]
        writes = [getattr(x, "res", x) for x in writes]
        writes = writes + [r for r in reads if r.excl and eng != 'pe']
        waits = self._collect(eng, reads, writes)
        if sig:
            self.cnt[eng] += 1
            tok = (eng, self.cnt[eng])
            self.unsig[eng] = False
        else:
            tok = (eng, self.cnt[eng] + 1)
            self.unsig[eng] = True
        self.q[eng].append((waits, fn, sig, None))
        self._update(tok, reads, writes)

    def pe(self, fn, reads=(), writes=(), sig=True):
        self.emit('pe', fn, reads, writes, sig)

    def act(self, fn, reads=(), writes=()):
        self.emit('act', fn, reads, writes)

    def dve(self, fn, reads=(), writes=()):
        self.emit('dve', fn, reads, writes)

    def pool(self, fn, reads=(), writes=()):
        self.emit('pool', fn, reads, writes)

    def dma(self, out, in_, reads=(), writes=(), q='sp', **kw):
        reads = [getattr(x, "res", x) for x in reads]
        writes = [getattr(x, "res", x) for x in writes]
        pl = self.dpool[q]
        j = pl[self.dnext[q] % len(pl)]
        self.dnext[q] += 1
        waits = dict(self._collect(q, reads, writes))
        if self.dcnt[j] > 0 and self.known[q].get(j, 0) < self.dcnt[j]:
            waits[j] = self.dcnt[j]
            self.known[q][j] = self.dcnt[j]
        self.dcnt[j] += 16
        tok = (j, self.dcnt[j])
        self.q[q].append((list(waits.items()), lambda e: e.dma_start(out=out, in_=in_, **kw), False, j))
        self._update(tok, reads, writes)

    def barrier(self):
        for e in ENG:
            assert not self.unsig[e], e
        for e in ENG:
            waits = []
            for f in ENG:
                if f != e and self.known[e].get(f, 0) < self.cnt[f]:
                    waits.append((f, self.cnt[f]))
                    self.known[e][f] = self.cnt[f]
            for j in range(NDS):
                if self.dcnt[j] > 0 and self.known[e].get(j, 0) < self.dcnt[j]:
                    waits.append((j, self.dcnt[j]))
                    self.known[e][j] = self.dcnt[j]
            if waits:
                self.q[e].append((waits, None, False, None))

    def flush(self):
        nc = self.nc
        q = self.q
        self.q = {e: [] for e in ENG}

        def replay(eng, e):
            for waits, fn, sig, dj in q[eng]:
                for key, val in waits:
                    e.wait_ge(self.semof(key), val)
                if fn is None:
                    continue
                ins = fn(e)
                if dj is not None:
                    ins.then_inc(self.dsem[dj], 16)
                elif sig:
                    ins.then_inc(self.sem[eng], 1)

        with nc.Block() as block:
            @block.tensor
            def _(e):
                replay('pe', e)

            @block.scalar
            def _(e):
                replay('act', e)

            @block.vector
            def _(e):
                replay('dve', e)

            @block.gpsimd
            def _(e):
                replay('pool', e)

            @block.sync
            def _(e):
                replay('sp', e)


class Ctx:
    pass


_UID = [0]


def _uname(name):
    _UID[0] += 1
    return '%s_u%d' % (name, _UID[0])


def sb(nc, stack, name, shape, dt):
    return TileT(stack.enter_context(nc.sbuf_tensor(_uname(name), list(shape), dt)))


def ps(nc, stack, name, shape, dt):
    t = TileT(stack.enter_context(nc.psum_tensor(_uname(name), list(shape), dt)))
    t.res.excl = True
    return t


def build(debug_out=()):
    nc = bass.Bass("TRN2", target_bir_lowering=False)
    top = ExitStack()
    S = Sched(nc, top)
    C = Ctx()
    C.nc, C.S = nc, S
    din = {}

    def inp(name, shape):
        din[name] = TileT(nc.dram_tensor(name, list(shape), F32, kind="ExternalInput").ap())
        return din[name]

    inp('x', [T, D]); inp('c', [1, D]); inp('ctx', [TC, D]); inp('c_ctx', [1, D])
    inp('ada_w', [2, D, 6 * D]); inp('ada_b', [2, 6 * D])
    inp('ln_g', [4, D]); inp('ln_b', [4, D])
    inp('even_w_in', [D, 2576]); inp('even_w_out', [D, D])
    inp('gdn_conv_w', [5, 1536]); inp('gdn_a_log', [1, 8]); inp('gdn_dt_bias', [1, 8])
    inp('gdn_norm_w', [1, 128]); inp('pool_w', [4, 128, 128]); inp('pool_scale', [1, 512])
    inp('odd_w_in', [D, 2560]); inp('odd_w_out', [D, D])
    inp('sconv_w', [3, 512]); inp('conf_conv_w', [31, 512])
    inp('conf_ln_g', [1, 512]); inp('conf_ln_b', [1, 512])
    inp('ffn_w_up', [2, D, 2 * DFF]); inp('ffn_conv_w', [2, 9, DFF]); inp('ffn_w_down', [2, DFF, D])
    C.din = din
    C.out = TileT(nc.dram_tensor('out', [T, D], F32, kind="ExternalOutput").ap())

    def scratch(name, shape, dt):
        kind = "ExternalOutput" if name in debug_out else "Internal"
        t = TileT(nc.dram_tensor(name, list(shape), dt, kind=kind).ap())
        return t
    C.scratch = scratch

    C.ident_f = sb(nc, top, 'ident_f', [128, 128], F32)
    C.ident_b = sb(nc, top, 'ident_b', [128, 128], BF16)
    C.ones_f = sb(nc, top, 'ones_f', [128, 512], F32)
    C.ones_b = sb(nc, top, 'ones_b', [128, 128], BF16)
    C.zeros_b = sb(nc, top, 'zeros_b', [128, 512], BF16)
    C.modrow = sb(nc, top, 'modrow', [128, 6 * D], F32)
    st01 = ExitStack()
    C.modc = sb(nc, st01, 'modc', [128, 2 * D], F32)

    S.pool(lambda e: e.memset(C.ones_f[:], 1.0), writes=[C.ones_f])
    S.pool(lambda e: e.memset(C.ones_b[:], 1.0), writes=[C.ones_b])
    S.pool(lambda e: e.memset(C.zeros_b[:], 0.0), writes=[C.zeros_b])
    S.pool(lambda e: e.affine_select(out=C.ident_f[:], in_=C.ones_f[:, 0:128], pattern=[[-1, 128]],
                                     compare_op=ALU.is_equal, fill=0.0, base=0, channel_multiplier=1),
           reads=[C.ones_f], writes=[C.ident_f])
    S.pool(lambda e: e.tensor_copy(out=C.ident_b[:], in_=C.ident_f[:]), reads=[C.ident_f], writes=[C.ident_b])

    C.debug_out = debug_out
    phase_mod(C, 0)
    dbg_dump(C, 'dbg_mod0', C.modrow, C.modrow[0:1, :], [1, 6 * D], F32)
    dbg_dump(C, 'dbg_modc', C.modc, C.modc[0:1, :], [1, 2 * D], F32)
    phase_inproj0(C)
    st01.close()
    phase_qkv0(C)
    import os
    if os.environ.get('NOGDN') != '1':
        phase_gdn(C)
    else:
        C.OACC = C.scratch('OACC', [T, 512], F32)
        for r0 in range(0, T, 128):
            S.dma(C.OACC[r0:r0 + 128, :], C.ones_f[:, :], reads=[C.ones_f], writes=[C.OACC])
    PSTOP = int(os.environ.get('PSTOP', '99'))
    X1 = C.scratch('X1', [T, D], F32)
    X2 = C.scratch('X2', [T, D], F32)
    X3 = C.scratch('X3', [T, D], F32)
    if PSTOP >= 1:
        phase_mix0_out(C, X1)
    if PSTOP >= 2:
        phase_ffn_up(C, 0, X1)
    if PSTOP >= 3:
        phase_ffn_down(C, 0, X1, X2)
    if PSTOP >= 4:
        phase_mod(C, 1)
        phase_inproj1(C, X2)
    if PSTOP >= 5:
        phase_mix1_out(C, X2, X3)
    if PSTOP >= 6:
        phase_ffn_up(C, 1, X3)
        phase_ffn_down(C, 1, X3, C.out)

    S.barrier()
    S.flush()
    top.close()
    return nc


def dbg_dump(C, name, tile, ap, shape, dt):
    if name not in C.debug_out:
        return
    d = TileT(C.nc.dram_tensor(name, list(shape), dt, kind="ExternalOutput").ap())
    C.S.dma(d[:], ap, reads=[tile], writes=[d])


def to_col(C, st, psb, dram, R, ncol, name):
    nc, S = C.nc, C.S
    BL = 4
    tmp = sb(nc, st, name + '_row', [R, BL * 128], F32)
    outt = sb(nc, st, name + '_col', [128, ncol, R], F32)
    for c0 in range(0, ncol, BL):
        c1 = min(ncol, c0 + BL)
        S.dma(tmp[:, 0:(c1 - c0) * 128], dram[:, c0 * 128:c1 * 128], writes=[tmp])
        for c in range(c0, c1):
            S.pe(lambda e, c=c, c0=c0: e.transpose(out=psb[:, 0:R], in_=tmp[0:R, (c - c0) * 128:(c - c0 + 1) * 128],
                                                   identity=C.ident_f[0:R, 0:R]),
                 reads=[tmp, C.ident_f], writes=[psb])
            S.dve(lambda e, c=c: e.tensor_copy(out=outt[:, c, :], in_=psb[:, 0:R]), reads=[psb], writes=[outt])
    return outt


def phase_mod(C, layer):
    nc, S = C.nc, C.S
    st = ExitStack()
    pst = ps(nc, st, 'pm_t', [128, 512], F32)
    pacc = [ps(nc, st, 'pm_a%d' % i, [128, 512], F32) for i in range(2)]
    paccc = [ps(nc, st, 'pm_c%d' % i, [128, 512], F32) for i in range(2)]
    ccol = to_col(C, st, pst, C.din['c'][:, :], 1, 8, 'c')
    S.act(lambda e: e.activation(out=ccol[:], in_=ccol[:], func=AF.Silu), reads=[ccol], writes=[ccol])
    rep = sb(nc, st, 'c_rep', [128, 8, 128], F32)
    for k in range(8):
        S.dve(lambda e, k=k: e.tensor_scalar(out=rep[:, k, :], in0=C.ones_f[:, 0:128], scalar1=ccol[:, k, 0:1],
                                             scalar2=None, op0=ALU.mult), reads=[C.ones_f, ccol], writes=[rep])
    brow = sb(nc, st, 'adab_row', [1, 6 * D], F32)
    S.dma(brow[:], C.din['ada_b'][layer:layer + 1, :], writes=[brow])
    if layer == 0:
        cccol = to_col(C, st, pst, C.din['c_ctx'][:, :], 1, 8, 'cc')
        S.act(lambda e: e.activation(out=cccol[:], in_=cccol[:], func=AF.Silu), reads=[cccol], writes=[cccol])
        repc = sb(nc, st, 'cc_rep', [128, 8, 128], F32)
        for k in range(8):
            S.dve(lambda e, k=k: e.tensor_scalar(out=repc[:, k, :], in0=C.ones_f[:, 0:128],
                                                 scalar1=cccol[:, k, 0:1], scalar2=None, op0=ALU.mult),
                  reads=[C.ones_f, cccol], writes=[repc])
    wt = [sb(nc, st, 'adaw%d' % i, [128, 8, 512], F32) for i in range(2)]
    aw = C.din['ada_w']
    for n in range(12):
        w = wt[n % 2]
        S.dma(w[:], aw[layer, :, n * 512:(n + 1) * 512].rearrange("(k p) n -> p k n", p=128), writes=[w])
        pa = pacc[n % 2]
        for k in range(8):
            S.pe(lambda e, k=k, w=w, pa=pa: e.matmul(pa[:], lhsT=rep[:, k, :], rhs=w[:, k, :], start=(k == 0), stop=False),
                 reads=[rep, w], writes=[pa], sig=False)
        S.pe(lambda e, pa=pa, n=n: e.matmul(pa[:], lhsT=C.ones_f[0:1, 0:128], rhs=brow[0:1, n * 512:(n + 1) * 512],
                                            start=False, stop=True), reads=[C.ones_f, brow], writes=[pa])
        S.act(lambda e, pa=pa, n=n: e.activation(out=C.modrow[:, n * 512:(n + 1) * 512], in_=pa[:], func=AF.Copy),
              reads=[pa], writes=[C.modrow])
        if layer == 0 and n < 4:
            pc = paccc[n % 2]
            for k in range(8):
                S.pe(lambda e, k=k, w=w, pc=pc: e.matmul(pc[:], lhsT=repc[:, k, :], rhs=w[:, k, :], start=(k == 0), stop=False),
                     reads=[repc, w], writes=[pc], sig=False)
            S.pe(lambda e, pc=pc, n=n: e.matmul(pc[:], lhsT=C.ones_f[0:1, 0:128], rhs=brow[0:1, n * 512:(n + 1) * 512],
                                                start=False, stop=True), reads=[C.ones_f, brow], writes=[pc])
            S.dve(lambda e, pc=pc, n=n: e.tensor_copy(out=C.modc[:, n * 512:(n + 1) * 512], in_=pc[:]),
                  reads=[pc], writes=[C.modc])
    S.dve(lambda e: e.tensor_scalar_add(out=C.modrow[:, D:2 * D], in0=C.modrow[:, D:2 * D], scalar1=1.0),
          reads=[C.modrow], writes=[C.modrow])
    S.dve(lambda e: e.tensor_scalar_add(out=C.modrow[:, 4 * D:5 * D], in0=C.modrow[:, 4 * D:5 * D], scalar1=1.0),
          reads=[C.modrow], writes=[C.modrow])
    if layer == 0:
        S.dve(lambda e: e.tensor_scalar_add(out=C.modc[:, D:2 * D], in0=C.modc[:, D:2 * D], scalar1=1.0),
              reads=[C.modc], writes=[C.modc])
    S.barrier()
    S.flush()
    st.close()


def modulate_transpose(C, xt, nsub, shift, scale1, ub, uT, pT, evac_i):
    S = C.S
    for s in range(nsub):
        S.pool(lambda e, s=s: e.tensor_tensor(out=xt[:, s, :], in0=xt[:, s, :], in1=scale1, op=ALU.mult),
               reads=[xt, C.modrow, C.modc], writes=[xt])
        S.dve(lambda e, s=s: e.tensor_tensor(out=ub[:, s, :], in0=xt[:, s, :], in1=shift, op=ALU.add),
              reads=[xt, C.modrow, C.modc], writes=[ub])
    for k in range(8):
        p = pT[k % len(pT)]
        for s in range(nsub):
            S.pe(lambda e, s=s, k=k, p=p: e.transpose(out=p[:, s * 128:(s + 1) * 128], in_=ub[:, s, k * 128:(k + 1) * 128],
                                                      identity=C.ident_b[:]),
                 reads=[ub, C.ident_b], writes=[p], sig=(s == nsub - 1))
        if (k + evac_i) % 2 == 0:
            S.act(lambda e, k=k, p=p: e.activation(out=uT[:, k, 0:nsub * 128], in_=p[:, 0:nsub * 128], func=AF.Copy),
                  reads=[p], writes=[uT])
        else:
            S.dve(lambda e, k=k, p=p: e.tensor_copy(out=uT[:, k, 0:nsub * 128], in_=p[:, 0:nsub * 128]),
                  reads=[p], writes=[uT])


def load_w_bf16(C, st, name, dram, kchunks, ncols):
    nc, S = C.nc, C.S
    w = sb(nc, st, name, [128, kchunks, ncols], BF16)
    SW = min(2048, ncols)
    wres = [Res(), Res()]
    NSTG = 3 if ncols > 1024 else 2
    stg = [sb(nc, st, name + '_stg%d' % i, [128, SW], F32) for i in range(NSTG)]
    i = 0
    for k in range(kchunks):
        for c0 in range(0, ncols, SW):
            c1 = min(ncols, c0 + SW)
            b = stg[i % NSTG]
            S.dma(b[:, 0:c1 - c0], dram[k * 128:(k + 1) * 128, c0:c1], writes=[b], q=('sp' if i % 2 == 0 else 'act'))
            wr = wres[i % 2]
            if i % 2 == 0:
                S.dve(lambda e, b=b, k=k, c0=c0, c1=c1: e.tensor_copy(out=w[:, k, c0:c1], in_=b[:, 0:c1 - c0]), reads=[b], writes=[wr])
            else:
                S.act(lambda e, b=b, k=k, c0=c0, c1=c1: e.activation(out=w[:, k, c0:c1], in_=b[:, 0:c1 - c0], func=AF.Copy), reads=[b], writes=[wr])
            i += 1
    S.dve(lambda e: e.tensor_copy(out=w[:, 0, 0:1], in_=w[:, 0, 0:1]), reads=wres, writes=[w] + wres)
    return w


def phase_inproj0(C):
    nc, S = C.nc, C.S
    st = ExitStack()
    P0T = C.scratch('P0T', [2048, T + 16], BF16); C.P0T = P0T
    G0 = C.scratch('G0', [T, 512], BF16); C.G0 = G0
    SG = C.scratch('SG', [T + TC, 16], F32); C.SG = SG
    PCT = C.scratch('PCT', [1024, TC + 16], BF16); C.PCT = PCT
    w = load_w_bf16(C, st, 'w_in0', C.din['even_w_in'][:, :], 8, 2576)
    for r0 in range(0, 2048, 128):
        S.dma(P0T[r0:r0 + 128, 0:8], C.zeros_b[:, 0:8], reads=[C.zeros_b], writes=[P0T])
        S.dma(P0T[r0:r0 + 128, T + 8:T + 16], C.zeros_b[:, 0:8], reads=[C.zeros_b], writes=[P0T])
    for r0 in range(0, 1024, 128):
        S.dma(PCT[r0:r0 + 128, 0:8], C.zeros_b[:, 0:8], reads=[C.zeros_b], writes=[PCT])
        S.dma(PCT[r0:r0 + 128, TC + 8:TC + 16], C.zeros_b[:, 0:8], reads=[C.zeros_b], writes=[PCT])
    xt = [sb(nc, st, 'xt%d' % i, [128, 4, D], F32) for i in range(2)]
    ub = [sb(nc, st, 'ub%d' % i, [128, 4, D], BF16) for i in range(2)]
    uT = [sb(nc, st, 'uT%d' % i, [128, 8, 512], BF16) for i in range(2)]
    pstg = [sb(nc, st, 'pstg%d' % i, [128, 4, 512], BF16) for i in range(2)]
    gstg = [sb(nc, st, 'gstg%d' % i, [128, 4, 512], BF16) for i in range(2)]
    sstg = [sb(nc, st, 'sstg%d' % i, [128, 4, 16], F32) for i in range(2)]
    pT = [ps(nc, st, 'pT%d' % i, [128, 512], BF16) for i in range(2)]
    pm = [ps(nc, st, 'pm%d' % i, [128, 512], F32) for i in range(4)]
    pss = ps(nc, st, 'pss', [128, 4, 16], F32)
    x = C.din['x']
    tiles = [('ctx', 0)] + [('lat', i) for i in range(8)]

    def load(i):
        kind, t = tiles[i]
        b = xt[i % 2]
        if kind == 'ctx':
            S.dma(b[:, 0:2, :], C.din['ctx'][:, :].rearrange("(s p) d -> p s d", p=128), writes=[b])
        else:
            S.dma(b[:, :, :], x[t * 512:(t + 1) * 512, :].rearrange("(s p) d -> p s d", p=128), writes=[b])

    load(0)
    pmi = 0
    for i, (kind, t) in enumerate(tiles):
        if i + 1 < len(tiles):
            load(i + 1)
        b, u, ut = xt[i % 2], ub[i % 2], uT[i % 2]
        isctx = kind == 'ctx'
        nsub = 2 if isctx else 4
        ntok = nsub * 128
        if isctx:
            modulate_transpose(C, b, nsub, C.modc[:, 0:D], C.modc[:, D:2 * D], u, ut, pT, i)
            fchunks = list(range(4, 12))
        else:
            modulate_transpose(C, b, nsub, C.modrow[:, 0:D], C.modrow[:, D:2 * D], u, ut, pT, i)
            fchunks = list(range(0, 12)) + list(range(16, 20))
        for gi in range(0, len(fchunks), 4):
            grp = fchunks[gi:gi + 4]
            stg = pstg[(gi // 4) % 2]
            for j, fc in enumerate(grp):
                p = pm[pmi % 4]; pmi += 1
                for k in range(8):
                    S.pe(lambda e, p=p, k=k, fc=fc, ut=ut, ntok=ntok: e.matmul(
                        p[:, 0:ntok], lhsT=w[:, k, fc * 128:(fc + 1) * 128], rhs=ut[:, k, 0:ntok],
                        start=(k == 0), stop=(k == 7)), reads=[w, ut], writes=[p], sig=(k == 7))
                if j % 2 == 0:
                    S.act(lambda e, p=p, j=j, stg=stg, ntok=ntok: e.activation(out=stg[:, j, 0:ntok], in_=p[:, 0:ntok], func=AF.Copy),
                          reads=[p], writes=[stg])
                else:
                    S.dve(lambda e, p=p, j=j, stg=stg, ntok=ntok: e.tensor_copy(out=stg[:, j, 0:ntok], in_=p[:, 0:ntok]),
                          reads=[p], writes=[stg])
            if isctx:
                r0 = (grp[0] - 4) * 128
                dst = PCT[r0:r0 + 512, 8:8 + ntok].rearrange("(j p) n -> p j n", p=128)
                S.dma(dst, stg[:, :, 0:ntok], reads=[stg], writes=[PCT], q='act')
            else:
                fc0 = grp[0]
                r0 = fc0 * 128 if fc0 < 12 else (fc0 - 4) * 128
                dst = P0T[r0:r0 + 512, 8 + t * 512:8 + (t + 1) * 512].rearrange("(j p) n -> p j n", p=128)
                S.dma(dst, stg[:, :, :], reads=[stg], writes=[P0T], q='act')
        gs = gstg[i % 2]
        ss = sstg[i % 2]
        for s in range(nsub):
            if not isctx:
                p = pm[pmi % 4]; pmi += 1
                for k in range(8):
                    S.pe(lambda e, p=p, k=k, s=s, ut=ut: e.matmul(p[:], lhsT=ut[:, k, s * 128:(s + 1) * 128], rhs=w[:, k, 1536:2048],
                                                            start=(k == 0), stop=(k == 7)), reads=[w, ut], writes=[p], sig=(k == 7))
                S.act(lambda e, p=p, s=s, gs=gs: e.activation(out=gs[:, s, :], in_=p[:], func=AF.Silu), reads=[p], writes=[gs])
            for k in range(8):
                S.pe(lambda e, k=k, s=s, ut=ut: e.matmul(pss[:, s, :], lhsT=ut[:, k, s * 128:(s + 1) * 128], rhs=w[:, k, 2560:2576],
                                                       start=(k == 0), stop=(k == 7)), reads=[w, ut], writes=[pss], sig=(k == 7))
        S.dve(lambda e, ss=ss, nsub=nsub: e.tensor_copy(out=ss[:, 0:nsub, :], in_=pss[:, 0:nsub, :]), reads=[pss], writes=[ss])
        if isctx:
            S.dma(SG[T:T + TC, :].rearrange("(s p) c -> p s c", p=128), ss[:, 0:2, :], reads=[ss], writes=[SG], q='act')
        else:
            S.dma(SG[t * 512:(t + 1) * 512, :].rearrange("(s p) c -> p s c", p=128), ss[:, :, :], reads=[ss], writes=[SG], q='act')
            S.dma(G0[t * 512:(t + 1) * 512, :].rearrange("(s p) c -> p s c", p=128), gs[:, :, :], reads=[gs], writes=[G0], q='act')
    S.barrier()
    S.flush()
    st.close()


def build_diag(C, st, psb, dram, R, ncol, name):
    nc, S = C.nc, C.S
    cw = to_col(C, st, psb, dram, R, ncol, name)
    dg = sb(nc, st, name + '_dg', [128, ncol, R, 128], BF16)
    dgres = [Res() for _ in range(ncol)]
    dg.chunk_res = dgres
    i = 0
    for c in range(ncol):
        for r in range(R):
            dres = dgres[c]
            if i % 2 == 0:
                S.dve(lambda e, c=c, r=r: e.tensor_scalar(out=dg[:, c, r, :], in0=C.ident_b[:], scalar1=cw[:, c, r:r + 1],
                                                          scalar2=None, op0=ALU.mult), reads=[C.ident_b, cw], writes=[dres])
            else:
                S.act(lambda e, c=c, r=r: e.activation(out=dg[:, c, r, :], in_=C.ident_b[:], func=AF.Copy, scale=cw[:, c, r:r + 1]),
                      reads=[C.ident_b, cw], writes=[dres])
            i += 1
    S.dve(lambda e: e.tensor_copy(out=dg[:, 0, 0, 0:1], in_=dg[:, 0, 0, 0:1]), reads=dgres, writes=[dg] + dgres)
    return dg


def phase_qkv0(C):
    nc, S = C.nc, C.S
    st = ExitStack()
    QT = C.scratch('QT', [512, T], BF16); C.QT = QT
    KT = C.scratch('KT', [512, T + TC], BF16); C.KT = KT
    QTOK = C.scratch('QTOK', [T, 512], BF16); C.QTOK = QTOK
    KTOK = C.scratch('KTOK', [T + TC, 512], BF16); C.KTOK = KTOK
    VTOK = C.scratch('VTOK', [T + TC, 512], BF16); C.VTOK = VTOK
    YPT = C.scratch('YPT', [512, T], BF16); C.YPT = YPT
    pconv = [ps(nc, st, 'pconv%d' % i, [128, 512], F32) for i in range(2)]
    pssq = [ps(nc, st, 'pssq%d' % i, [128, 512], F32) for i in range(2)]
    pT = [ps(nc, st, 'pTq%d' % i, [128, 512], BF16) for i in range(2)]
    ppool = ps(nc, st, 'ppool', [128, 512], F32)
    dg = build_diag(C, st, pconv[0], C.din['gdn_conv_w'][:, :], 5, 12, 'cw5')
    pscale = to_col(C, st, pconv[1], C.din['pool_scale'][:, :], 1, 4, 'pscale')
    poolw = sb(nc, st, 'poolw', [128, 4, 128], BF16)
    S.dma(poolw[:], C.din['pool_w'][:, :, :].rearrange("g c d -> c g d"), writes=[poolw], q='pool')
    corrF = sb(nc, st, 'corrF', [128, 4, 8], F32)
    corrL = sb(nc, st, 'corrL', [128, 4, 8], F32)
    S.pool(lambda e: e.memset(corrF[:], 1.0), writes=[corrF])
    S.pool(lambda e: e.memset(corrL[:], 1.0), writes=[corrL])
    for g in range(4):
        hw = 1 << g
        for j in range(hw):
            S.pool(lambda e, g=g, j=j, hw=hw: e.memset(corrF[:, g, j:j + 1], 2.0 * hw / (j + hw)), writes=[corrF])
        for m in range(hw - 1):
            S.pool(lambda e, g=g, m=m, hw=hw: e.memset(corrL[:, g, 7 - m:8 - m], 2.0 * hw / (1 + m + hw)), writes=[corrL])
    pin = [sb(nc, st, 'pin%d' % i, [128, 12, 516], BF16) for i in range(2)]
    pp = [sb(nc, st, 'pp%d' % i, [128, 4, 528], BF16) for i in range(2)]
    xs8 = sb(nc, st, 'xs8', [128, 8, 512], F32)
    ss8 = sb(nc, st, 'ss8', [128, 8, 512], F32)
    epsb = sb(nc, st, 'epsb', [128, 1], F32)
    S.pool(lambda e: e.memset(epsb[:], RMS_EPS), writes=[epsb])
    sqb = [sb(nc, st, 'sqb%d' % i, [128, 512], BF16) for i in range(2)]
    qkn = [sb(nc, st, 'qkn0', [128, 12, 512], BF16)] * 2
    tokst = [[sb(nc, st, 'tok%d_%d' % (g, i), [128, 4, 512], BF16) for i in range(2)] for g in range(3)]
    wa = [sb(nc, st, 'wa%d' % i, [128, 528], F32) for i in range(2)]
    wb = [sb(nc, st, 'wb%d' % i, [128, 528], F32) for i in range(2)]
    pld = [sb(nc, st, 'pld%d' % i, [128, 4, 512], BF16) for i in range(2)]
    ypst = [sb(nc, st, 'ypst%d' % i, [128, 4, 512], BF16) for i in range(2)]
    tiles = [('ctx', 0)] + [('lat', i) for i in range(8)]

    def load(i):
        kind, t = tiles[i]
        b = pin[i % 2]
        if kind == 'ctx':
            S.dma(b[:, 4:12, 0:260], C.PCT[:, 6:266].rearrange("(f p) n -> p f n", p=128), reads=[C.PCT], writes=[b])
        else:
            S.dma(b[:, :, :], C.P0T[0:1536, 6 + t * 512:6 + t * 512 + 516].rearrange("(f p) n -> p f n", p=128),
                  reads=[C.P0T], writes=[b])
            S.dma(pp[i % 2][:, :, :], C.P0T[1536:2048, t * 512:t * 512 + 528].rearrange("(f p) n -> p f n", p=128),
                  reads=[C.P0T], writes=[pp[i % 2]])

    load(0)
    ci = 0
    for i, (kind, t) in enumerate(tiles):
        if i + 1 < len(tiles):
            load(i + 1)
        isctx = kind == 'ctx'
        ntok = 256 if isctx else 512
        nsub = ntok // 128
        b = pin[i % 2]
        qk = qkn[i % 2]
        for fc in (range(4, 12) if isctx else range(12)):
            pc = pconv[ci % 2]
            sq_ = sqb[ci % 2]; pq = pssq[ci % 2]
            ci += 1
            for tap in range(5):
                S.pe(lambda e, pc=pc, fc=fc, tap=tap, b=b, ntok=ntok: e.matmul(
                    pc[:, 0:ntok], lhsT=dg[:, fc, tap, :], rhs=b[:, fc, tap:tap + ntok], start=(tap == 0), stop=(tap == 4)),
                    reads=[dg, b], writes=[pc], sig=(tap == 4))
            if fc >= 8:
                S.act(lambda e, pc=pc, fc=fc, qk=qk, ntok=ntok: e.activation(out=qk[:, fc, 0:ntok], in_=pc[:, 0:ntok], func=AF.Silu),
                      reads=[pc], writes=[qk])
                continue
            S.act(lambda e, pc=pc, fc=fc, ntok=ntok: e.activation(out=xs8[:, fc, 0:ntok], in_=pc[:, 0:ntok], func=AF.Silu),
                  reads=[pc], writes=[xs8])
            S.act(lambda e, fc=fc, sq_=sq_, ntok=ntok: e.activation(out=sq_[:, 0:ntok], in_=xs8[:, fc, 0:ntok], func=AF.Square),
                  reads=[xs8], writes=[sq_])
            S.pe(lambda e, pq=pq, sq_=sq_, ntok=ntok: e.matmul(pq[:, 0:ntok], lhsT=C.ones_b[:], rhs=sq_[:, 0:ntok], start=True, stop=True),
                 reads=[C.ones_b, sq_], writes=[pq])
            S.dve(lambda e, pq=pq, fc=fc, ntok=ntok: e.tensor_copy(out=ss8[:, fc, 0:ntok], in_=pq[:, 0:ntok]), reads=[pq], writes=[ss8])
        f0 = 4 if isctx else 0
        S.act(lambda e, f0=f0, ntok=ntok: e.activation(out=ss8[:, f0:8, 0:ntok], in_=ss8[:, f0:8, 0:ntok], func=AF.Ln, bias=epsb[:, 0:1]),
              reads=[ss8, epsb], writes=[ss8])
        S.act(lambda e, f0=f0, ntok=ntok: e.activation(out=ss8[:, f0:8, 0:ntok], in_=ss8[:, f0:8, 0:ntok], func=AF.Exp, scale=-0.5),
              reads=[ss8], writes=[ss8])
        for fc in range(f0, 8):
            sc = (128.0 ** -0.5) if fc < 4 else 1.0
            fn = lambda e, qk=qk, fc=fc, sc=sc, ntok=ntok: e.scalar_tensor_tensor(
                out=qk[:, fc, 0:ntok], in0=xs8[:, fc, 0:ntok], scalar=sc, in1=ss8[:, fc, 0:ntok], op0=ALU.mult, op1=ALU.mult)
            if fc < 4:
                S.dve(fn, reads=[xs8, ss8], writes=[qk])
            elif fc % 2 == 0:
                S.dve(lambda e, qk=qk, fc=fc, ntok=ntok: e.tensor_tensor(out=qk[:, fc, 0:ntok], in0=xs8[:, fc, 0:ntok],
                                                                      in1=ss8[:, fc, 0:ntok], op=ALU.mult),
                      reads=[xs8, ss8], writes=[qk])
            else:
                S.pool(lambda e, qk=qk, fc=fc, ntok=ntok: e.tensor_tensor(out=qk[:, fc, 0:ntok], in0=xs8[:, fc, 0:ntok],
                                                                       in1=ss8[:, fc, 0:ntok], op=ALU.mult),
                       reads=[xs8, ss8], writes=[qk])
        ti = 0
        for g in ((1, 2) if isctx else (0, 1, 2)):
            tk = tokst[g][i % 2]
            for s_ in range(nsub):
                p = pT[ti % 2]; ti += 1
                for h in range(4):
                    S.pe(lambda e, p=p, h=h, g=g, s_=s_, qk=qk: e.transpose(out=p[:, h * 128:(h + 1) * 128],
                                                                       in_=qk[:, g * 4 + h, s_ * 128:(s_ + 1) * 128], identity=C.ident_b[:]),
                         reads=[qk, C.ident_b], writes=[p], sig=(h == 3))
                if ti % 2 == 0:
                    S.act(lambda e, p=p, tk=tk, s_=s_: e.activation(out=tk[:, s_, :], in_=p[:], func=AF.Copy), reads=[p], writes=[tk])
                else:
                    S.dve(lambda e, p=p, tk=tk, s_=s_: e.tensor_copy(out=tk[:, s_, :], in_=p[:]), reads=[p], writes=[tk])
        c0 = T if isctx else t * 512
        if not isctx:
            S.dma(QT[:, c0:c0 + 512].rearrange("(f p) n -> p f n", p=128), qk[:, 0:4, :], reads=[qk], writes=[QT], q='act')
            S.dma(QTOK[c0:c0 + 512, :].rearrange("(s p) c -> p s c", p=128), tokst[0][i % 2][:, :, :], reads=[tokst[0][i % 2]], writes=[QTOK], q='act')
        S.dma(KT[:, c0:c0 + ntok].rearrange("(f p) n -> p f n", p=128), qk[:, 4:8, 0:ntok], reads=[qk], writes=[KT], q='act')
        S.dma(KTOK[c0:c0 + ntok, :].rearrange("(s p) c -> p s c", p=128), tokst[1][i % 2][:, 0:nsub, :], reads=[tokst[1][i % 2]], writes=[KTOK], q='act')
        S.dma(VTOK[c0:c0 + ntok, :].rearrange("(s p) c -> p s c", p=128), tokst[2][i % 2][:, 0:nsub, :], reads=[tokst[2][i % 2]], writes=[VTOK], q='act')
        if isctx:
            continue
        ppb = pp[i % 2]
        pl = pld[i % 2]
        yp = ypst[i % 2]
        for g in range(4):
            a_, b_ = wa[g % 2], wb[g % 2]
            eng_add = S.dve if g >= 2 else S.pool
            eng_add(lambda e, a_=a_, g=g, ppb=ppb: e.tensor_tensor(out=a_[:, 1:527], in0=ppb[:, g, 0:526], in1=ppb[:, g, 1:527], op=ALU.add),
                    reads=[ppb], writes=[a_])
            cur, oth = a_, b_
            lo, hi = 1, 527
            for lvl in range(g):
                sh = 1 << lvl
                lo, hi = lo + sh, hi - sh
                eng_add(lambda e, cur=cur, oth=oth, lo=lo, hi=hi, sh=sh: e.tensor_tensor(
                    out=oth[:, lo:hi], in0=cur[:, lo - sh:hi - sh], in1=cur[:, lo + sh:hi + sh], op=ALU.add),
                    reads=[cur], writes=[oth])
                cur, oth = oth, cur
            S.dve(lambda e, cur=cur, g=g: e.tensor_scalar(out=cur[:, 8:520], in0=cur[:, 8:520], scalar1=1.0 / (2 << g), scalar2=None, op0=ALU.mult),
                  reads=[cur], writes=[cur])
            if t == 0:
                S.dve(lambda e, cur=cur, g=g: e.tensor_tensor(out=cur[:, 8:16], in0=cur[:, 8:16], in1=corrF[:, g, :], op=ALU.mult),
                      reads=[cur, corrF], writes=[cur])
            if t == 7:
                S.dve(lambda e, cur=cur, g=g: e.tensor_tensor(out=cur[:, 512:520], in0=cur[:, 512:520], in1=corrL[:, g, :], op=ALU.mult),
                      reads=[cur, corrL], writes=[cur])
            S.dve(lambda e, cur=cur, g=g, pl=pl, ppb=ppb: e.tensor_tensor(out=pl[:, g, :], in0=cur[:, 8:520], in1=ppb[:, g, 8:520], op=ALU.subtract),
                  reads=[cur, ppb], writes=[pl])
            S.pe(lambda e, g=g, pl=pl: e.matmul(ppool[:], lhsT=poolw[:, g, :], rhs=pl[:, g, :], start=True, stop=True),
                 reads=[poolw, pl], writes=[ppool])
            S.act(lambda e, g=g, yp=yp: e.activation(out=yp[:, g, :], in_=ppool[:], func=AF.Identity, scale=pscale[:, g, 0:1]),
                  reads=[ppool, pscale], writes=[yp])
        S.dma(YPT[:, c0:c0 + 512].rearrange("(f p) n -> p f n", p=128), yp[:, :, :], reads=[yp], writes=[YPT], q='act')
    S.barrier()
    S.flush()
    st.close()


class Slot:
    def __init__(self, bank, k):
        self.f = bank.t[:, k * 128:(k + 1) * 128]
        self.b = bank.t[:, :].bitcast(BF16)[:, k * 256:k * 256 + 128]
        self.res = bank.res


def run_interleaved(gens):
    gens = list(gens)
    while gens:
        nxt = []
        for g in gens:
            try:
                next(g)
                nxt.append(g)
            except StopIteration:
                pass
        gens = nxt


def phase_gdn(C):
    nc, S = C.nc, C.S
    st = ExitStack()
    NT = 34
    import os
    OACC = C.scratch('OACC', [T, 512], F32); C.OACC = OACC
    banks = [ps(nc, st, 'gbank%d' % i, [128, 512], F32) for i in range(8)]
    for b_ in banks:
        b_.res.excl = True
    slots = [[Slot(banks[c], k) for k in range(4)] for c in range(8)]
    sall = sb(nc, st, 'sall', [128, NT, 16], F32)
    for n0 in ([] if os.environ.get('NOSALL') == '1' else range(0, NT, 6)):
        n1 = min(NT, n0 + 6)
        S.dma(sall[:, n0:n1, :], C.SG[n0 * 128:n1 * 128, :].rearrange("(n p) c -> p n c", p=128), reads=[C.SG], writes=[sall])
    adb = sb(nc, st, 'adb', [128, 16], F32)
    S.dma(adb[:, 0:8], C.din['gdn_a_log'][0:1, :].to_broadcast([128, 8]), writes=[adb])
    S.dma(adb[:, 8:16], C.din['gdn_dt_bias'][0:1, :].to_broadcast([128, 8]), writes=[adb])
    S.act(lambda e: e.activation(out=adb[:, 0:8], in_=adb[:, 0:8], func=AF.Exp), reads=[adb], writes=[adb])
    S.dve(lambda e: e.tensor_scalar(out=adb[:, 0:8], in0=adb[:, 0:8], scalar1=-1.0, scalar2=None, op0=ALU.mult),
          reads=[adb], writes=[adb])

    GCUT = int(os.environ.get('GCUT', '0'))

    def fin():
        S.barrier(); S.flush(); st.close()
    if GCUT == 1:
        return fin()

    def gt(name):
        return sb(nc, st, name, [128, NT, 8], F32)
    beta, g_, gc, eg, be, kds, gl = gt('g_beta'), gt('g_g'), gt('g_gc'), gt('g_eg'), gt('g_be'), gt('g_kds'), gt('g_gl')
    S.act(lambda e: e.activation(out=beta[:], in_=sall[:, :, 0:8], func=AF.Sigmoid), reads=[sall], writes=[beta])
    S.dve(lambda e: e.tensor_tensor(out=g_[:], in0=sall[:, :, 8:16], in1=adb[:, 8:16].unsqueeze(1).to_broadcast([128, NT, 8]), op=ALU.add),
          reads=[sall, adb], writes=[g_])
    S.act(lambda e: e.activation(out=g_[:], in_=g_[:], func=AF.Exp), reads=[g_], writes=[g_])
    S.act(lambda e: e.activation(out=g_[:], in_=g_[:], func=AF.Ln, bias=1.0), reads=[g_], writes=[g_])
    S.dve(lambda e: e.tensor_tensor(out=g_[:], in0=g_[:], in1=adb[:, 0:8].unsqueeze(1).to_broadcast([128, NT, 8]), op=ALU.mult),
          reads=[g_, adb], writes=[g_])
    if GCUT == 2:
        return fin()
    Lt = sb(nc, st, 'Lt', [128, 128], F32)
    Ut = sb(nc, st, 'Ut', [128, 128], F32)
    bigm = [sb(nc, st, 'bigm%d' % i, [128, 128], F32) for i in range(2)]
    strict = [sb(nc, st, 'strict%d' % i, [128, 128], F32) for i in range(2)]
    bigfull = sb(nc, st, 'bigfull', [128, 128], F32)
    S.pool(lambda e: e.memset(bigfull[:], BIG), writes=[bigfull])
    one = C.ones_f[:, 0:128]
    S.pool(lambda e: e.affine_select(out=Lt[:], in_=one, pattern=[[1, 128]], compare_op=ALU.is_ge, fill=0.0, base=0, channel_multiplier=-1),
           reads=[C.ones_f], writes=[Lt])
    S.pool(lambda e: e.affine_select(out=Ut[:], in_=one, pattern=[[-1, 128]], compare_op=ALU.is_ge, fill=0.0, base=0, channel_multiplier=1),
           reads=[C.ones_f], writes=[Ut])
    S.pool(lambda e: e.affine_select(out=bigm[0][:], in_=bigfull[:], pattern=[[1, 128]], compare_op=ALU.is_gt, fill=0.0, base=0, channel_multiplier=-1),
           reads=[bigfull], writes=[bigm[0]])
    S.pool(lambda e: e.affine_select(out=bigm[1][:], in_=bigfull[:], pattern=[[-1, 128]], compare_op=ALU.is_gt, fill=0.0, base=0, channel_multiplier=1),
           reads=[bigfull], writes=[bigm[1]])
    S.pool(lambda e: e.affine_select(out=strict[0][:], in_=one, pattern=[[-1, 128]], compare_op=ALU.is_gt, fill=0.0, base=0, channel_multiplier=1),
           reads=[C.ones_f], writes=[strict[0]])
    S.pool(lambda e: e.affine_select(out=strict[1][:], in_=one, pattern=[[1, 128]], compare_op=ALU.is_gt, fill=0.0, base=0, channel_multiplier=-1),
           reads=[C.ones_f], writes=[strict[1]])
    if GCUT == 3:
        return fin()
    Bm = {}
    for s_ in (16, 32, 64):
        G = 128 // s_
        E = sb(nc, st, 'E%d' % s_, [G, 128], F32)
        S.pool(lambda e, E=E, G=G, s_=s_: e.affine_select(out=E[:], in_=C.ones_f[0:G, 0:128], pattern=[[1, 128]], compare_op=ALU.is_ge,
                                                         fill=0.0, base=0, channel_multiplier=-s_), reads=[C.ones_f], writes=[E])
        S.pool(lambda e, E=E, G=G, s_=s_: e.affine_select(out=E[:], in_=E[:], pattern=[[-1, 128]], compare_op=ALU.is_gt,
                                                         fill=0.0, base=s_, channel_multiplier=s_), reads=[E], writes=[E])
        pb_ = banks[4]
        S.pe(lambda e, E=E, pb_=pb_: e.matmul(pb_[:, 0:128], lhsT=E[:], rhs=E[:], start=True, stop=True), reads=[E], writes=[pb_])
        Bm[s_] = sb(nc, st, 'Bm%d' % s_, [128, 128], F32)
        S.dve(lambda e, s_=s_, pb_=pb_: e.tensor_copy(out=Bm[s_][:], in_=pb_[:, 0:128]), reads=[pb_], writes=[Bm[s_]])
    Md = [sb(nc, st, 'Md%d' % d, [128, 128], F32) for d in range(2)]
    Mo = [[sb(nc, st, 'Mo%d_%d' % (d, l), [128, 128], F32) for l in range(3)] for d in range(2)]
    for d in range(2):
        S.dve(lambda e, d=d: e.tensor_tensor(out=Md[d][:], in0=strict[d][:], in1=Bm[16][:], op=ALU.mult), reads=[strict[d], Bm[16]], writes=[Md[d]])
        for l, (big_, small_) in enumerate(((32, 16), (64, 32), (None, 64))):
            t_ = Mo[d][l]
            if big_ is None:
                S.dve(lambda e, t_=t_, small_=small_: e.tensor_scalar(out=t_[:], in0=Bm[small_][:], scalar1=-1.0, scalar2=1.0, op0=ALU.mult, op1=ALU.add),
                      reads=[Bm[small_]], writes=[t_])
            else:
                S.dve(lambda e, t_=t_, big_=big_, small_=small_: e.tensor_tensor(out=t_[:], in0=Bm[big_][:], in1=Bm[small_][:], op=ALU.subtract),
                      reads=[Bm[big_], Bm[small_]], writes=[t_])
            S.dve(lambda e, t_=t_, d=d: e.tensor_tensor(out=t_[:], in0=t_[:], in1=strict[d][:], op=ALU.mult), reads=[t_, strict[d]], writes=[t_])
    pgc = banks[1]
    S.pe(lambda e: e.matmul(pgc[:, 0:NT * 8], lhsT=Lt[:], rhs=g_[:, :, :], start=True, stop=True), reads=[Lt, g_], writes=[pgc])
    S.dve(lambda e: e.tensor_copy(out=gc[:, :, 0:4], in_=pgc[:, 0:NT * 8].rearrange("p (n c) -> p n c", c=8)[:, :, 0:4]), reads=[pgc], writes=[gc])
    pgc2 = banks[2]
    S.pe(lambda e: e.matmul(pgc2[:, 0:NT * 8], lhsT=Ut[:], rhs=g_[:, :, :], start=True, stop=True), reads=[Ut, g_], writes=[pgc2])
    S.dve(lambda e: e.tensor_copy(out=gc[:, :, 4:8], in_=pgc2[:, 0:NT * 8].rearrange("p (n c) -> p n c", c=8)[:, :, 4:8]), reads=[pgc2], writes=[gc])
    if GCUT == 5:
        return fin()
    pgt = banks[3]
    S.pe(lambda e: e.matmul(pgt[:, 0:NT * 8], lhsT=C.ones_f[:, 0:128], rhs=g_[:, :, :], start=True, stop=True), reads=[C.ones_f, g_], writes=[pgt])
    S.act(lambda e: e.activation(out=gl[:], in_=pgt[:, 0:NT * 8].rearrange("p (n c) -> p n c", c=8), func=AF.Exp), reads=[pgt], writes=[gl])
    S.dve(lambda e: e.tensor_tensor(out=kds[:], in0=pgt[:, 0:NT * 8].rearrange("p (n c) -> p n c", c=8), in1=gc[:], op=ALU.subtract),
          reads=[pgt, gc], writes=[kds])
    if GCUT == 6:
        return fin()
    S.act(lambda e: e.activation(out=kds[:], in_=kds[:], func=AF.Exp), reads=[kds], writes=[kds])
    S.act(lambda e: e.activation(out=eg[:], in_=gc[:], func=AF.Exp), reads=[gc], writes=[eg])
    S.dve(lambda e: e.tensor_tensor(out=be[:], in0=beta[:], in1=eg[:], op=ALU.mult), reads=[beta, eg], writes=[be])
    GCT = C.scratch('GCT', [NT * 8, 128], F32)
    gcTs = sb(nc, st, 'gcTs', [128, 3, 128], F32)
    gcf = gc[:].rearrange("p n c -> p (n c)")
    for bi, (q0, q1) in enumerate(((0, 128), (128, 256), (256, NT * 8))):
        pb_ = banks[5 + bi]
        S.pe(lambda e, pb_=pb_, q0=q0, q1=q1: e.transpose(out=pb_[0:q1 - q0, 0:128], in_=gcf[:, q0:q1], identity=C.ident_f[:]),
             reads=[gc, C.ident_f], writes=[pb_])
        S.dve(lambda e, pb_=pb_, bi=bi, q0=q0, q1=q1: e.tensor_copy(out=gcTs[0:q1 - q0, bi, :], in_=pb_[0:q1 - q0, 0:128]), reads=[pb_], writes=[gcTs])
        S.dma(GCT[q0:q1, :], gcTs[0:q1 - q0, bi, :], reads=[gcTs], writes=[GCT])
    if GCUT == 4:
        return fin()
    dbg_dump(C, 'dbg_gc', gc, gc[:, :, :], [128, NT, 8], F32)
    dbg_dump(C, 'dbg_beta', beta, beta[:, :, :], [128, NT, 8], F32)
    dbg_dump(C, 'dbg_g', g_, g_[:, :, :], [128, NT, 8], F32)

    S.barrier()
    import os
    GSTOP = int(os.environ.get('GSTOP', '99'))
    def tile_of(d, n):
        if n < 2:
            return 32 + n if d == 0 else 33 - n
        return n - 2 if d == 0 else 33 - n
    opnd = [[{k: sb(nc, st, 'op_%s_%d_%d' % (k, d, i), [128, 4, 128], BF16) for k in ('kT', 'qT', 'ktok', 'qtok', 'vtok')}
             for i in range(2)] for d in range(2)]

    Rb = [[sb(nc, st, 'Rb_%d_%d' % (c, i), [128, 128], F32) for i in range(2)] for c in range(8)]

    def load_tile(d, n):
        nt = tile_of(d, n)
        o = opnd[d][n % 2]
        for h_ in range(4):
            c_ = d * 4 + h_
            S.dma(Rb[c_][n % 2][:], GCT[nt * 8 + c_:nt * 8 + c_ + 1, :].to_broadcast([128, 128]), reads=[GCT], writes=[Rb[c_][n % 2]])
        c0 = T + (nt - 32) * 128 if nt >= 32 else nt * 128
        S.dma(o['kT'][:, :, :], C.KT[:, c0:c0 + 128].rearrange("(h p) n -> p h n", p=128), reads=[C.KT], writes=[o['kT']])
        S.dma(o['ktok'][:, :, :], C.KTOK[c0:c0 + 128, :].rearrange("p (h d) -> p h d", d=128), reads=[C.KTOK], writes=[o['ktok']])
        S.dma(o['vtok'][:, :, :], C.VTOK[c0:c0 + 128, :].rearrange("p (h d) -> p h d", d=128), reads=[C.VTOK], writes=[o['vtok']])
        if nt < 32:
            S.dma(o['qT'][:, :, :], C.QT[:, c0:c0 + 128].rearrange("(h p) n -> p h n", p=128), reads=[C.QT], writes=[o['qT']])
            S.dma(o['qtok'][:, :, :], C.QTOK[c0:c0 + 128, :].rearrange("p (h d) -> p h d", d=128), reads=[C.QTOK], writes=[o['qtok']])

    def cb(name, dt, n=1, shape=(128, 128)):
        return [[sb(nc, st, '%s_%d_%d' % (name, c, i), list(shape), dt) for i in range(n)] for c in range(8)]
    dgc = cb('dgc', F32); Dm = dgc; Ai = dgc
    Pb = cb('Pb', BF16, 2); PTb = cb('PTb', BF16, 2); Yb = cb('Yb', BF16, 2)
    bv = cb('bv', BF16); kbe = cb('kbe', BF16); qe = cb('qe', BF16); AOb = cb('AOb', BF16, 3)
    attnT = cb('attnT', BF16, 2); u_ = cb('u_', F32, 2); wT = cb('wT', BF16, 2); kd = cb('kd', BF16, 2); qdT = cb('qdT', BF16, 2)
    S32 = cb('S32', F32); Sbf = cb('Sbf', BF16, 2); vn = cb('vn', BF16)
    for c in range(8):
        S.pool(lambda e, c=c: e.memset(S32[c][0][:], 0.0), writes=[S32[c][0]])
        S.pool(lambda e, c=c: e.memset(Sbf[c][0][:], 0.0), writes=[Sbf[c][0]])
    oacc = sb(nc, st, 'oacc', [128, 32, 512], F32)
    ores = [[Res() for h in range(4)] for nt in range(32)]
    ofirst = [[True] * 4 for nt in range(32)]

    def precompute(c, n):
        d, h = c // 4, c % 4
        nt = tile_of(d, n)
        lat = nt < 32
        o = opnd[d][n % 2]
        r = n % 2
        sl = slots[c]
        gcol = gc[:, nt, c:c + 1]
        S.dve(lambda e: e.tensor_tensor(out=Dm[c][0][:], in0=Rb[c][r][:], in1=bigm[d][:], op=ALU.add),
              reads=[Rb[c][r], bigm[d]], writes=[Dm[c][0]])
        S.pe(lambda e: e.matmul(sl[1].f, lhsT=o['kT'][:, h, :], rhs=o['kT'][:, h, :], start=True, stop=True),
             reads=[o['kT']], writes=[sl[1]])
        if lat:
            S.pe(lambda e: e.matmul(sl[2].f, lhsT=o['qT'][:, h, :], rhs=o['kT'][:, h, :], start=True, stop=True),
                 reads=[o['kT'], o['qT']], writes=[sl[2]])
        yield
        S.act(lambda e: e.activation(out=Dm[c][0][:], in_=Dm[c][0][:], func=AF.Exp, bias=gcol, scale=-1.0),
              reads=[Dm[c][0], gc], writes=[Dm[c][0]])
        yield
        if lat:
            S.dve(lambda e: e.tensor_tensor(out=qe[c][0][:], in0=sl[2].f, in1=Dm[c][0][:], op=ALU.mult),
                  reads=[sl[2], Dm[c][0]], writes=[qe[c][0]])
        S.dve(lambda e: e.scalar_tensor_tensor(out=Ai[c][0][:], in0=sl[1].f, scalar=beta[:, nt, c:c + 1], in1=Dm[c][0][:],
                                               op0=ALU.mult, op1=ALU.mult), reads=[sl[1], beta, Dm[c][0]], writes=[Ai[c][0]])
        yield
        A = Pb[c][0]
        S.dve(lambda e: e.tensor_tensor(out=A[:], in0=Ai[c][0][:], in1=Md[d][:], op=ALU.mult),
              reads=[Ai[c][0], Md[d]], writes=[A])
        for li in range(3):
            fn = lambda e, li=li: e.tensor_tensor(out=AOb[c][li][:], in0=Ai[c][0][:], in1=Mo[d][li][:], op=ALU.mult)
            if li < 2:
                S.dve(fn, reads=[Ai[c][0], Mo[d][li]], writes=[AOb[c][li]])
            else:
                S.pool(fn, reads=[Ai[c][0], Mo[d][li]], writes=[AOb[c][li]])
        yield
        S.pe(lambda e: e.transpose(out=sl[0].b, in_=A[:], identity=C.ident_b[:]), reads=[A, C.ident_b], writes=[sl[0]])
        if lat:
            S.pe(lambda e: e.transpose(out=sl[1].b, in_=qe[c][0][:], identity=C.ident_b[:]), reads=[qe[c][0], C.ident_b], writes=[sl[1]])
        yield
        AT = PTb[c][0]
        Y = Yb[c][0]
        S.act(lambda e: e.activation(out=AT[:], in_=sl[0].b, func=AF.Copy), reads=[sl[0]], writes=[AT])
        S.dve(lambda e: e.scalar_tensor_tensor(out=Y[:], in0=sl[0].b, scalar=-1.0, in1=C.ident_b[:], op0=ALU.mult, op1=ALU.add),
              reads=[sl[0], C.ident_b], writes=[Y])
        if lat:
            S.act(lambda e: e.activation(out=attnT[c][r][:], in_=sl[1].b, func=AF.Copy), reads=[sl[1]], writes=[attnT[c][r]])
        yield
        S.act(lambda e: e.activation(out=bv[c][0][:], in_=o['vtok'][:, h, :], func=AF.Copy, scale=beta[:, nt, c:c + 1]),
              reads=[o['vtok'], beta], writes=[bv[c][0]])
        S.act(lambda e: e.activation(out=kbe[c][0][:], in_=o['ktok'][:, h, :], func=AF.Copy, scale=be[:, nt, c:c + 1]),
              reads=[o['ktok'], be], writes=[kbe[c][0]])
        S.pool(lambda e: e.tensor_scalar(out=kd[c][r][:], in0=o['ktok'][:, h, :], scalar1=kds[:, nt, c:c + 1], scalar2=None, op0=ALU.mult),
               reads=[o['ktok'], kds], writes=[kd[c][r]])
        if lat:
            S.act(lambda e: e.activation(out=qe[c][0][:], in_=o['qtok'][:, h, :], func=AF.Copy, scale=eg[:, nt, c:c + 1]),
                  reads=[o['qtok'], eg], writes=[qe[c][0]])
        cur = 0
        for lvl in range(1, 4):
            P, PT, Yc = Pb[c][cur], PTb[c][cur], Yb[c][cur]
            Pn, PTn, Yn = Pb[c][1 - cur], PTb[c][1 - cur], Yb[c][1 - cur]
            S.pe(lambda e, P=P, PT=PT: e.matmul(sl[0].f, lhsT=PT[:], rhs=P[:], start=True, stop=True), reads=[P, PT], writes=[sl[0]])
            if lvl < 3:
                S.pe(lambda e, P=P, PT=PT: e.matmul(sl[1].f, lhsT=P[:], rhs=PT[:], start=True, stop=True), reads=[P, PT], writes=[sl[1]])
            yield
            S.act(lambda e, Pn=Pn: e.activation(out=Pn[:], in_=sl[0].f, func=AF.Copy), reads=[sl[0]], writes=[Pn])
            if lvl < 3:
                S.dve(lambda e, PTn=PTn: e.tensor_copy(out=PTn[:], in_=sl[1].f), reads=[sl[1]], writes=[PTn])
            yield
            S.pe(lambda e, Pn=Pn, Yc=Yc: e.matmul(sl[2].f, lhsT=Pn[:], rhs=Yc[:], start=True, stop=True), reads=[Pn, Yc], writes=[sl[2]])
            yield
            S.dve(lambda e, Yc=Yc, Yn=Yn: e.tensor_tensor(out=Yn[:], in0=sl[2].f, in1=Yc[:], op=ALU.add), reads=[sl[2], Yc], writes=[Yn])
            yield
            cur = 1 - cur
        for li in range(3):
            Yc, Yn = Yb[c][cur], Yb[c][1 - cur]
            Tt, N1 = Pb[c][0], PTb[c][0]
            S.pe(lambda e, Yc=Yc: e.transpose(out=sl[0].b, in_=Yc[:], identity=C.ident_b[:]), reads=[Yc, C.ident_b], writes=[sl[0]])
            S.pe(lambda e, Yc=Yc, li=li: e.matmul(sl[1].f, lhsT=AOb[c][li][:], rhs=Yc[:], start=True, stop=True),
                 reads=[AOb[c][li], Yc], writes=[sl[1]])
            yield
            S.act(lambda e, Tt=Tt: e.activation(out=Tt[:], in_=sl[0].b, func=AF.Copy), reads=[sl[0]], writes=[Tt])
            S.dve(lambda e, N1=N1: e.tensor_copy(out=N1[:], in_=sl[1].f), reads=[sl[1]], writes=[N1])
            yield
            S.pe(lambda e, Tt=Tt, N1=N1: e.matmul(sl[2].f, lhsT=Tt[:], rhs=N1[:], start=True, stop=True), reads=[Tt, N1], writes=[sl[2]])
            yield
            S.dve(lambda e, Yc=Yc, Yn=Yn: e.scalar_tensor_tensor(out=Yn[:], in0=sl[2].f, scalar=-1.0, in1=Yc[:], op0=ALU.mult, op1=ALU.add),
                  reads=[sl[2], Yc], writes=[Yn])
            yield
            cur = 1 - cur
        Y = Yb[c][cur]
        S.pe(lambda e: e.matmul(sl[0].f, lhsT=Y[:], rhs=bv[c][0][:], start=True, stop=True), reads=[Y, bv[c][0]], writes=[sl[0]])
        S.pe(lambda e: e.matmul(sl[1].f, lhsT=kbe[c][0][:], rhs=Y[:], start=True, stop=True), reads=[Y, kbe[c][0]], writes=[sl[1]])
        if lat:
            S.pe(lambda e: e.transpose(out=sl[2].b, in_=qe[c][0][:], identity=C.ident_b[:]), reads=[qe[c][0], C.ident_b], writes=[sl[2]])
        yield
        S.act(lambda e: e.activation(out=u_[c][r][:], in_=sl[0].f, func=AF.Copy), reads=[sl[0]], writes=[u_[c][r]])
        S.dve(lambda e: e.tensor_copy(out=wT[c][r][:], in_=sl[1].f), reads=[sl[1]], writes=[wT[c][r]])
        if lat:
            S.act(lambda e: e.activation(out=qdT[c][r][:], in_=sl[2].b, func=AF.Copy), reads=[sl[2]], writes=[qdT[c][r]])
        yield

    def scan(c, n):
        d, h = c // 4, c % 4
        nt = tile_of(d, n)
        lat = nt < 32
        r = n % 2
        sl = slots[c][3]
        Sold, Snew = Sbf[c][n % 2], Sbf[c][1 - n % 2]
        S.pe(lambda e: e.matmul(sl.f, lhsT=wT[c][r][:], rhs=Sold[:], start=True, stop=True), reads=[wT[c][r], Sold], writes=[sl])
        yield
        S.dve(lambda e: e.scalar_tensor_tensor(out=vn[c][0][:], in0=sl.f, scalar=-1.0, in1=u_[c][r][:], op0=ALU.mult, op1=ALU.add),
              reads=[sl, u_[c][r]], writes=[vn[c][0]])
        yield
        S.pe(lambda e: e.matmul(sl.f, lhsT=kd[c][r][:], rhs=vn[c][0][:], start=True, stop=True), reads=[kd[c][r], vn[c][0]], writes=[sl])
        yield
        glc = gl[:, nt, c:c + 1]
        S.dve(lambda e: e.scalar_tensor_tensor(out=Snew[:], in0=S32[c][0][:], scalar=glc, in1=sl.f, op0=ALU.mult, op1=ALU.add),
              reads=[S32[c][0], gl, sl], writes=[Snew])
        S.dve(lambda e: e.scalar_tensor_tensor(out=S32[c][0][:], in0=S32[c][0][:], scalar=glc, in1=sl.f, op0=ALU.mult, op1=ALU.add),
              reads=[S32[c][0], gl, sl], writes=[S32[c][0]])
        yield
        if lat:
            S.pe(lambda e: e.matmul(sl.f, lhsT=qdT[c][r][:], rhs=Sold[:], start=True, stop=False), reads=[qdT[c][r], Sold], writes=[sl], sig=False)
            S.pe(lambda e: e.matmul(sl.f, lhsT=attnT[c][r][:], rhs=vn[c][0][:], start=False, stop=True), reads=[attnT[c][r], vn[c][0]], writes=[sl])
            yield
            orr = ores[nt][h]
            if ofirst[nt][h]:
                ofirst[nt][h] = False
                S.act(lambda e: e.activation(out=oacc[:, nt, h * 128:(h + 1) * 128], in_=sl.f, func=AF.Copy), reads=[sl], writes=[orr])
            else:
                S.dve(lambda e: e.tensor_tensor(out=oacc[:, nt, h * 128:(h + 1) * 128], in0=sl.f, in1=oacc[:, nt, h * 128:(h + 1) * 128], op=ALU.add),
                      reads=[sl, orr], writes=[orr])
            yield

    NR = min(34, GSTOP)
    if GSTOP >= 0:
        for d in range(2):
            load_tile(d, 0)
        run_interleaved([precompute(c, 0) for c in range(8)])
    for n in range(NR):
        gens = [scan(c, n) for c in range(8)]
        if n + 1 < NR:
            for d in range(2):
                load_tile(d, n + 1)
            gens += [precompute(c, n + 1) for c in range(8)]
        run_interleaved(gens)
    for nt in (range(32) if NR == 34 else []):
        S.dma(OACC[nt * 128:(nt + 1) * 128, :], oacc[:, nt, :], reads=ores[nt], writes=[OACC], q='sp')
    for c in range(8):
        dbg_dump(C, 'dbg_S%d' % c, S32[c][0], S32[c][0][:], [128, 128], F32)
    S.barrier()
    S.flush()
    st.close()


def load_rows_bcast(C, st, name, dram_row, n):
    t = sb(C.nc, st, name, [128, n], F32)
    C.S.dma(t[:], dram_row.to_broadcast([128, n]), writes=[t])
    return t


class Epi:
    def __init__(self, C, st, ln_idx, gate_ap, nsub, nbuf=1):
        nc = C.nc
        self.C, self.nsub, self.gate = C, nsub, gate_ap
        self.g = load_rows_bcast(C, st, 'ln_g%d' % ln_idx, C.din['ln_g'][ln_idx:ln_idx + 1, :], D)
        self.b = load_rows_bcast(C, st, 'ln_b%d' % ln_idx, C.din['ln_b'][ln_idx:ln_idx + 1, :], D)
        self.t2s = [sb(nc, st, 'ep_t2_%d' % i, [128, nsub, D], F32) for i in range(nbuf)]
        self.t2 = self.t2s[0]
        self.junk = sb(nc, st, 'ep_junk', [128, D], BF16)
        self.st = sb(nc, st, 'ep_st', [128, 6, nsub], F32)
        self.eps = sb(nc, st, 'ep_eps', [128, 1], F32)
        C.S.pool(lambda e: e.memset(self.eps[:], LN_EPS), writes=[self.eps])
        self.i = 0

    def sub(self, s_, ypair, xt):
        S, t2, stt = self.C.S, self.t2, self.st
        for hf in range(2):
            S.dve(lambda e, hf=hf: e.tensor_tensor(out=t2[:, s_, hf * 512:(hf + 1) * 512], in0=ypair[hf][:],
                                                   in1=self.gate[:, hf * 512:(hf + 1) * 512], op=ALU.mult),
                  reads=[ypair[hf], self.C.modrow], writes=[t2])
        S.dve(lambda e: e.scalar_tensor_tensor(out=t2[:, s_, :], in0=xt[:, s_, :], scalar=ALPHA, in1=t2[:, s_, :], op0=ALU.mult, op1=ALU.add),
              reads=[xt, t2], writes=[t2])
        S.act(lambda e: e.activation(out=self.junk[:], in_=t2[:, s_, :], func=AF.Copy, accum_out=stt[:, 0, s_:s_ + 1]),
              reads=[t2], writes=[self.junk, stt])
        S.act(lambda e: e.activation(out=self.junk[:], in_=t2[:, s_, :], func=AF.Square, accum_out=stt[:, 1, s_:s_ + 1]),
              reads=[t2], writes=[self.junk, stt])

    def finish(self, dst_rows, xt_unused=None):
        S, t2, stt, n = self.C.S, self.t2, self.st, self.nsub
        dst, r0 = dst_rows
        xo = t2
        self.i += 1
        self.t2 = self.t2s[self.i % len(self.t2s)]
        S.dve(lambda e: e.tensor_scalar(out=stt[:, 2, :], in0=stt[:, 0, :], scalar1=1.0 / D, scalar2=None, op0=ALU.mult), reads=[stt], writes=[stt])
        S.dve(lambda e: e.tensor_tensor(out=stt[:, 4, :], in0=stt[:, 2, :], in1=stt[:, 2, :], op=ALU.mult), reads=[stt], writes=[stt])
        S.dve(lambda e: e.scalar_tensor_tensor(out=stt[:, 3, :], in0=stt[:, 1, :], scalar=1.0 / D, in1=stt[:, 4, :], op0=ALU.mult, op1=ALU.subtract),
              reads=[stt], writes=[stt])
        S.act(lambda e: e.activation(out=stt[:, 3, :], in_=stt[:, 3, :], func=AF.Ln, bias=self.eps[:, 0:1]), reads=[stt, self.eps], writes=[stt])
        S.act(lambda e: e.activation(out=stt[:, 3, :], in_=stt[:, 3, :], func=AF.Exp, scale=-0.5), reads=[stt], writes=[stt])
        S.dve(lambda e: e.scalar_tensor_tensor(out=stt[:, 5, :], in0=stt[:, 2, :], scalar=-1.0, in1=stt[:, 3, :], op0=ALU.mult, op1=ALU.mult),
              reads=[stt], writes=[stt])
        for s_ in range(n):
            S.act(lambda e, s_=s_: e.activation(out=t2[:, s_, :], in_=t2[:, s_, :], func=AF.Identity, scale=stt[:, 3, s_:s_ + 1], bias=stt[:, 5, s_:s_ + 1]),
                  reads=[t2, stt], writes=[t2])
            S.pool(lambda e, s_=s_: e.tensor_tensor(out=xo[:, s_, :], in0=t2[:, s_, :], in1=self.g[:], op=ALU.mult), reads=[t2, self.g], writes=[xo])
            S.dve(lambda e, s_=s_: e.tensor_tensor(out=xo[:, s_, :], in0=xo[:, s_, :], in1=self.b[:], op=ALU.add), reads=[xo, self.b], writes=[xo])
        S.dma(dst[r0:r0 + n * 128, :].rearrange("(s p) d -> p s d", p=128), xo[:, :, :], reads=[xo], writes=[dst], q='sp')


def out_proj(C, mixT, wout, s_, ypair, nk):
    S = C.S
    for hf in range(2):
        for k in range(nk):
            S.pe(lambda e, hf=hf, k=k: e.matmul(ypair[hf][:], lhsT=mixT[:, k, s_ * 128:(s_ + 1) * 128], rhs=wout[:, k, hf * 512:(hf + 1) * 512],
                                                start=(k == 0), stop=(k == nk - 1)), reads=[mixT, wout], writes=[ypair[hf]], sig=(k == nk - 1))


def phase_mix0_out(C, X1):
    nc, S = C.nc, C.S
    st = ExitStack()
    wout = load_w_bf16(C, st, 'wout0', C.din['even_w_out'][:, :], 8, D)
    normw = load_rows_bcast(C, st, 'normw', C.din['gdn_norm_w'][0:1, :], 128)
    epi = Epi(C, st, 0, C.modrow[:, 2 * D:3 * D], 4)
    yps = [[ps(nc, st, 'yps%d_%d' % (i, h), [128, 512], F32) for h in range(2)] for i in range(2)]
    pT = [ps(nc, st, 'pTm%d' % i, [128, 512], BF16) for i in range(2)]
    ot = [sb(nc, st, 'ot%d' % i, [128, 4, 512], F32) for i in range(2)]
    gt_ = [sb(nc, st, 'gt%d' % i, [128, 4, 512], BF16) for i in range(2)]
    xt = [sb(nc, st, 'xtm%d' % i, [128, 4, D], F32) for i in range(2)]
    mixT = [sb(nc, st, 'mixT%d' % i, [128, 8, 512], BF16) for i in range(2)]
    osq = sb(nc, st, 'osq', [128, 4, 512], F32)
    ss = sb(nc, st, 'oss', [128, 16], F32)
    ogs = [sb(nc, st, 'og%d' % i, [128, 4, 512], BF16) for i in range(2)]
    epsr = sb(nc, st, 'epsr', [128, 1], F32)
    S.pool(lambda e: e.memset(epsr[:], RMS_EPS), writes=[epsr])

    def load_front(t):
        i = t % 2
        S.dma(ot[i][:, :, :], C.OACC[t * 512:(t + 1) * 512, :].rearrange("(s p) d -> p s d", p=128), reads=[C.OACC], writes=[ot[i]])
        S.dma(gt_[i][:, :, :], C.G0[t * 512:(t + 1) * 512, :].rearrange("(s p) d -> p s d", p=128), reads=[C.G0], writes=[gt_[i]])

    def load_back(t):
        i = t % 2
        S.dma(xt[i][:, :, :], C.din['x'][t * 512:(t + 1) * 512, :].rearrange("(s p) d -> p s d", p=128), writes=[xt[i]])
        S.dma(mixT[i][:, 4:8, :], C.YPT[:, t * 512:(t + 1) * 512].rearrange("(f p) n -> p f n", p=128), reads=[C.YPT], writes=[mixT[i]])

    def front(t):
        i = t % 2
        o_, g_, x_, m_, og = ot[i], gt_[i], xt[i], mixT[i], ogs[i]
        S.pool(lambda e, o_=o_: e.tensor_tensor(out=osq[:], in0=o_[:], in1=o_[:], op=ALU.mult), reads=[o_], writes=[osq])
        S.dve(lambda e: e.tensor_reduce(out=ss[:], in_=osq[:].rearrange("p s (h d) -> p (s h) d", d=128), axis=mybir.AxisListType.X, op=ALU.add),
              reads=[osq], writes=[ss])
        S.act(lambda e: e.activation(out=ss[:], in_=ss[:], func=AF.Ln, scale=1.0 / 128, bias=epsr[:, 0:1]), reads=[ss, epsr], writes=[ss])
        S.act(lambda e: e.activation(out=ss[:], in_=ss[:], func=AF.Exp, scale=-0.5), reads=[ss], writes=[ss])
        S.dve(lambda e, o_=o_: e.tensor_tensor(out=osq[:].rearrange("p s (h d) -> p (s h) d", d=128), in0=o_[:].rearrange("p s (h d) -> p (s h) d", d=128),
                                              in1=ss[:].unsqueeze(2).to_broadcast([128, 16, 128]), op=ALU.mult), reads=[o_, ss], writes=[osq])
        S.pool(lambda e: e.tensor_tensor(out=osq[:].rearrange("p s (h d) -> p (s h) d", d=128), in0=osq[:].rearrange("p s (h d) -> p (s h) d", d=128),
                                         in1=normw[:].unsqueeze(1).to_broadcast([128, 16, 128]), op=ALU.mult), reads=[osq, normw], writes=[osq])
        S.dve(lambda e, g_=g_: e.tensor_tensor(out=og[:], in0=osq[:], in1=g_[:], op=ALU.mult), reads=[osq, g_], writes=[og])
        for h in range(4):
            p = pT[h % 2]
            for s_ in range(4):
                S.pe(lambda e, p=p, h=h, s_=s_: e.transpose(out=p[:, s_ * 128:(s_ + 1) * 128], in_=og[:, s_, h * 128:(h + 1) * 128], identity=C.ident_b[:]),
                     reads=[og, C.ident_b], writes=[p], sig=(s_ == 3))
            if h % 2 == 0:
                S.act(lambda e, p=p, h=h, m_=m_: e.activation(out=m_[:, h, :], in_=p[:], func=AF.Copy), reads=[p], writes=[m_])
            else:
                S.dve(lambda e, p=p, h=h, m_=m_: e.tensor_copy(out=m_[:, h, :], in_=p[:]), reads=[p], writes=[m_])

    load_front(0)
    load_back(0)
    front(0)
    load_front(1)
    load_back(1)
    yi = 0
    for t in range(8):
        if t + 1 < 8:
            front(t + 1)
        if t + 2 < 8:
            load_front(t + 2)
        m_, x_ = mixT[t % 2], xt[t % 2]
        for s_ in range(4):
            yp = yps[yi % 2]; yi += 1
            out_proj(C, m_, wout, s_, yp, 8)
            epi.sub(s_, yp, x_)
        epi.finish((X1, t * 512))
        if t + 2 < 8:
            load_back(t + 2)
    dbg_dump(C, 'dbg_x1', X1, X1[0:128, :], [128, D], F32)
    S.barrier()
    S.flush()
    st.close()


def phase_ffn_up(C, layer, Xin):
    nc, S = C.nc, C.S
    st = ExitStack()
    if layer == 0:
        C.AT = C.scratch('AT', [DFF, T + 128], BF16)
        C.GTt = C.scratch('GTt', [DFF, T], BF16)
        for r0 in range(0, DFF, 128):
            S.dma(C.AT[r0:r0 + 128, 0:64], C.zeros_b[:, 0:64], reads=[C.zeros_b], writes=[C.AT])
            S.dma(C.AT[r0:r0 + 128, T + 64:T + 128], C.zeros_b[:, 0:64], reads=[C.zeros_b], writes=[C.AT])
    AT, GTt = C.AT, C.GTt
    w = load_w_bf16(C, st, 'wup', C.din['ffn_w_up'][layer, :, :], 8, 2 * DFF)
    xt = sb(nc, st, 'xtu', [128, 4, D], F32)
    ub = sb(nc, st, 'ubu', [128, 4, D], BF16)
    uT = [sb(nc, st, 'uTu%d' % i, [128, 8, 512], BF16) for i in range(2)]
    stg = [sb(nc, st, 'stgu%d' % i, [128, 4, 512], BF16) for i in range(2)]
    pT = [ps(nc, st, 'pTu%d' % i, [128, 512], BF16) for i in range(2)]
    pm = [ps(nc, st, 'pmu%d' % i, [128, 512], F32) for i in range(4)]
    pmi = 0
    for t in range(8):
        S.dma(xt[:, :, :], Xin[t * 512:(t + 1) * 512, :].rearrange("(s p) d -> p s d", p=128), reads=[Xin], writes=[xt])
        ut = uT[t % 2]
        modulate_transpose(C, xt, 4, C.modrow[:, 3 * D:4 * D], C.modrow[:, 4 * D:5 * D], ub, ut, pT, t)
        gi = 0
        for f0 in range(0, 44, 4):
            sg = stg[gi % 2]; gi += 1
            nf = min(4, 44 - f0)
            for j in range(nf):
                fc = f0 + j
                p = pm[pmi % 4]; pmi += 1
                for k in range(8):
                    S.pe(lambda e, p=p, k=k, fc=fc, ut=ut: e.matmul(p[:], lhsT=w[:, k, fc * 128:(fc + 1) * 128], rhs=ut[:, k, :],
                                                                 start=(k == 0), stop=(k == 7)), reads=[w, ut], writes=[p], sig=(k == 7))
                if pmi % 2 == 0:
                    S.act(lambda e, p=p, j=j, sg=sg: e.activation(out=sg[:, j, :], in_=p[:], func=AF.Copy), reads=[p], writes=[sg])
                else:
                    S.dve(lambda e, p=p, j=j, sg=sg: e.tensor_copy(out=sg[:, j, :], in_=p[:]), reads=[p], writes=[sg])
            if f0 < 22:
                na = min(nf, 22 - f0)
                S.dma(AT[f0 * 128:(f0 + na) * 128, 64 + t * 512:64 + (t + 1) * 512].rearrange("(j p) n -> p j n", p=128), sg[:, 0:na, :],
                      reads=[sg], writes=[AT], q='act')
                if na < nf:
                    S.dma(GTt[0:(nf - na) * 128, t * 512:(t + 1) * 512].rearrange("(j p) n -> p j n", p=128), sg[:, na:nf, :],
                          reads=[sg], writes=[GTt], q='act')
            else:
                g0 = f0 - 22
                S.dma(GTt[g0 * 128:(g0 + nf) * 128, t * 512:(t + 1) * 512].rearrange("(j p) n -> p j n", p=128), sg[:, 0:nf, :],
                      reads=[sg], writes=[GTt], q='act')
    S.barrier()
    S.flush()
    st.close()


def phase_ffn_down(C, layer, Xin, Xout):
    phase_ffn_conv(C, layer)
    phase_ffn_proj(C, layer, Xin, Xout)


def phase_ffn_conv(C, layer):
    nc, S = C.nc, C.S
    st = ExitStack()
    AT, GTt = C.AT, C.GTt
    if layer == 0:
        C.HGT = C.scratch('HGT', [DFF, T], BF16)
    HGT = C.HGT
    NTK, NH = 512, 11
    pconv = [ps(nc, st, 'pcv%d' % i, [128, 512], F32) for i in range(4)]
    dg = build_diag(C, st, pconv[0], C.din['ffn_conv_w'][layer, :, :], 9, 22, 'cw9')
    ad = [sb(nc, st, 'ad%d' % i, [128, NH, 640], BF16) for i in range(2)]
    gt_ = [sb(nc, st, 'gd%d' % i, [128, NH, 512], BF16) for i in range(2)]
    apad = [sb(nc, st, 'apad%d' % i, [128, NH, 10, 66], BF16) for i in range(2)]
    hgs = [sb(nc, st, 'hgs%d' % i, [128, NH, 512], BF16) for i in range(2)]
    hs = [sb(nc, st, 'hs%d' % i, [128, 512], BF16) for i in range(4)]
    for i in range(2):
        S.pool(lambda e, i=i: e.memset(apad[i][:], 0.0), writes=[apad[i]])
    items = [(t, hf) for t in range(T // NTK) for hf in range(2)]
    n = len(items)

    def load_a(w):
        t, hf = items[w]
        S.dma(ad[w % 2][:, :, :], AT[hf * NH * 128:(hf + 1) * NH * 128, t * NTK:t * NTK + 640].rearrange("(f p) n -> p f n", p=128),
              reads=[AT], writes=[ad[w % 2]])

    def load_g(w):
        t, hf = items[w]
        S.dma(gt_[w % 2][:, :, :], GTt[hf * NH * 128:(hf + 1) * NH * 128, t * NTK:(t + 1) * NTK].rearrange("(f p) n -> p f n", p=128),
              reads=[GTt], writes=[gt_[w % 2]])

    def pad(w):
        a_, ap_ = ad[w % 2], apad[w % 2]
        for j in range(NH):
            fn = lambda e, j=j, a_=a_, ap_=ap_: e.tensor_copy(out=ap_[:, j, :, 1:65], in_=a_[:, j, :].rearrange("p (r c) -> p r c", c=64))
            if j % 3 == 0:
                S.pool(fn, reads=[a_], writes=[ap_])
            else:
                S.dve(fn, reads=[a_], writes=[ap_])

    load_a(0)
    load_g(0)
    pad(0)
    load_a(1)
    ci = 0
    for w in range(n):
        t, hf = items[w]
        if w + 1 < n:
            pad(w + 1)
            load_g(w + 1)
        if w + 2 < n:
            load_a(w + 2)
        g_, ap_, hg = gt_[w % 2], apad[w % 2], hgs[w % 2]
        for j in range(NH):
            fc = hf * NH + j
            pc = pconv[ci % 4]; h_ = hs[ci % 4]; ci += 1
            for tap in range(9):
                dy, dx = tap // 3 - 1, tap % 3 - 1
                S.pe(lambda e, pc=pc, fc=fc, j=j, tap=tap, dy=dy, dx=dx, ap_=ap_: e.matmul(
                    pc[:], lhsT=dg[:, fc, tap, :], rhs=ap_[:, j, 1 + dy:9 + dy, 1 + dx:65 + dx], start=(tap == 0), stop=(tap == 8)),
                    reads=[dg, ap_], writes=[pc], sig=(tap == 8))
            S.act(lambda e, pc=pc, h_=h_: e.activation(out=h_[:], in_=pc[:], func=AF.Silu), reads=[pc], writes=[h_])
            S.dve(lambda e, j=j, h_=h_, g_=g_, hg=hg: e.tensor_tensor(out=hg[:, j, :], in0=h_[:], in1=g_[:, j, :], op=ALU.mult),
                  reads=[h_, g_], writes=[hg])
        S.dma(HGT[hf * NH * 128:(hf + 1) * NH * 128, t * NTK:(t + 1) * NTK].rearrange("(f p) n -> p f n", p=128), hg[:, :, :],
              reads=[hg], writes=[HGT], q='act')
    S.barrier()
    S.flush()
    st.close()


def phase_ffn_proj(C, layer, Xin, Xout):
    nc, S = C.nc, C.S
    st = ExitStack()
    HGT = C.HGT
    yps = [[ps(nc, st, 'ypd%d_%d' % (i, h), [128, 512], F32) for h in range(2)] for i in range(3)]
    wd = load_w_bf16(C, st, 'wdn', C.din['ffn_w_down'][layer, :, :], 22, D)
    epi = Epi(C, st, layer * 2 + 1, C.modrow[:, 5 * D:6 * D], 4, nbuf=2)
    hg = [sb(nc, st, 'hgp%d' % i, [128, 22, 512], BF16) for i in range(2)]
    xt = [sb(nc, st, 'xtd%d' % i, [128, 4, D], F32) for i in range(2)]

    def load(t):
        i = t % 2
        S.dma(hg[i][:, :, :], HGT[:, t * 512:(t + 1) * 512].rearrange("(f p) n -> p f n", p=128), reads=[HGT], writes=[hg[i]])
        S.dma(xt[i][:, :, :], Xin[t * 512:(t + 1) * 512, :].rearrange("(s p) d -> p s d", p=128), reads=[Xin], writes=[xt[i]])

    load(0)
    yi = 0
    for t in range(8):
        if t + 1 < 8:
            load(t + 1)
        h_, x_ = hg[t % 2], xt[t % 2]
        for s_ in range(4):
            yp = yps[yi % 3]; yi += 1
            out_proj(C, h_, wd, s_, yp, 22)
            epi.sub(s_, yp, x_)
        epi.finish((Xout, t * 512))
    S.barrier()
    S.flush()
    st.close()


def phase_inproj1(C, Xin):
    nc, S = C.nc, C.S
    st = ExitStack()
    GBT = C.scratch('GBT', [512, T], BF16); C.GBT = GBT
    M1T = C.scratch('M1T', [512, T + 32], BF16); C.M1T = M1T
    M2T = C.scratch('M2T', [512, T + 32], BF16); C.M2T = M2T
    for M in (M1T, M2T):
        for r0 in range(0, 512, 128):
            S.dma(M[r0:r0 + 128, 0:16], C.zeros_b[:, 0:16], reads=[C.zeros_b], writes=[M])
            S.dma(M[r0:r0 + 128, T + 16:T + 32], C.zeros_b[:, 0:16], reads=[C.zeros_b], writes=[M])
    w = load_w_bf16(C, st, 'w_in1', C.din['odd_w_in'][:, :], 8, 2560)
    xt = sb(nc, st, 'xt1', [128, 4, D], F32)
    ub = sb(nc, st, 'ub1', [128, 4, D], BF16)
    uT = [sb(nc, st, 'uT1%d' % i, [128, 8, 512], BF16) for i in range(2)]
    stg = [[sb(nc, st, 'stg1_%d_%d' % (g, i), [128, 4, 512], BF16) for i in range(2)] for g in range(3)]
    tmp = [sb(nc, st, 'tmp1_%d' % i, [128, 512], F32) for i in range(2)]
    pT = [ps(nc, st, 'pT1%d' % i, [128, 512], BF16) for i in range(2)]
    pm = [ps(nc, st, 'pm1%d' % i, [128, 512], F32) for i in range(4)]
    pmi = 0
    ti = 0

    def mm(fc, ut):
        nonlocal pmi
        p = pm[pmi % 4]; pmi += 1
        for k in range(8):
            S.pe(lambda e, p=p, k=k: e.matmul(p[:], lhsT=w[:, k, fc * 128:(fc + 1) * 128], rhs=ut[:, k, :], start=(k == 0), stop=(k == 7)),
                 reads=[w, ut], writes=[p], sig=(k == 7))
        return p

    for t in range(8):
        S.dma(xt[:, :, :], Xin[t * 512:(t + 1) * 512, :].rearrange("(s p) d -> p s d", p=128), reads=[Xin], writes=[xt])
        ut = uT[t % 2]
        modulate_transpose(C, xt, 4, C.modrow[:, 0:D], C.modrow[:, D:2 * D], ub, ut, pT, t)
        sgb, sm1, sm2 = stg[0][t % 2], stg[1][t % 2], stg[2][t % 2]
        for j in range(4):
            p = mm(j, ut)
            S.act(lambda e, p=p, j=j, sgb=sgb: e.activation(out=sgb[:, j, :], in_=p[:], func=AF.Copy), reads=[p], writes=[sgb])
            tm = tmp[ti % 2]; ti += 1
            p = mm(4 + j, ut)
            S.act(lambda e, p=p, tm=tm: e.activation(out=tm[:], in_=p[:], func=AF.Copy), reads=[p], writes=[tm])
            p = mm(8 + j, ut)
            S.dve(lambda e, p=p, tm=tm, j=j, sm1=sm1: e.tensor_tensor(out=sm1[:, j, :], in0=p[:], in1=tm[:], op=ALU.mult), reads=[p, tm], writes=[sm1])
            tm = tmp[ti % 2]; ti += 1
            p = mm(16 + j, ut)
            S.act(lambda e, p=p, tm=tm: e.activation(out=tm[:], in_=p[:], func=AF.Sigmoid), reads=[p], writes=[tm])
            p = mm(12 + j, ut)
            S.dve(lambda e, p=p, tm=tm, j=j, sm2=sm2: e.tensor_tensor(out=sm2[:, j, :], in0=p[:], in1=tm[:], op=ALU.mult), reads=[p, tm], writes=[sm2])
        c0 = t * 512
        S.dma(GBT[:, c0:c0 + 512].rearrange("(j p) n -> p j n", p=128), sgb[:, :, :], reads=[sgb], writes=[GBT], q='act')
        S.dma(M1T[:, 16 + c0:16 + c0 + 512].rearrange("(j p) n -> p j n", p=128), sm1[:, :, :], reads=[sm1], writes=[M1T], q='act')
        S.dma(M2T[:, 16 + c0:16 + c0 + 512].rearrange("(j p) n -> p j n", p=128), sm2[:, :, :], reads=[sm2], writes=[M2T], q='act')
    S.barrier()
    S.flush()
    st.close()


def phase_mix1_out(C, Xin, Xout):
    nc, S = C.nc, C.S
    st = ExitStack()
    pconv = [ps(nc, st, 'pc1%d' % i, [128, 512], F32) for i in range(2)]
    pstat = [ps(nc, st, 'pst1%d' % i, [128, 512], F32) for i in range(2)]
    yps = [[ps(nc, st, 'yp1%d_%d' % (i, h), [128, 512], F32) for h in range(2)] for i in range(2)]
    dg3 = build_diag(C, st, pconv[0], C.din['sconv_w'][:, :], 3, 4, 'cw3')
    dg31 = build_diag(C, st, pconv[1], C.din['conf_conv_w'][:, :], 31, 4, 'cw31')
    lng = to_col(C, st, pconv[0], C.din['conf_ln_g'][:, :], 1, 4, 'clng')
    lnb = to_col(C, st, pconv[1], C.din['conf_ln_b'][:, :], 1, 4, 'clnb')
    wout = load_w_bf16(C, st, 'wout1', C.din['odd_w_out'][:, :], 8, D)
    epi = Epi(C, st, 2, C.modrow[:, 2 * D:3 * D], 4)
    m1 = [sb(nc, st, 'm1_%d' % i, [128, 4, 514], BF16) for i in range(2)]
    m2 = [sb(nc, st, 'm2_%d' % i, [128, 4, 542], BF16) for i in range(2)]
    gb = [sb(nc, st, 'gb_%d' % i, [128, 4, 512], BF16) for i in range(2)]
    xt = sb(nc, st, 'xt1o', [128, 4, D], F32)
    mixTs = [sb(nc, st, 'mixT1_%d' % i, [128, 8, 512], BF16) for i in range(2)]
    z = sb(nc, st, 'z1', [128, 4, 512], F32)
    zsq = sb(nc, st, 'zsq1', [128, 4, 512], F32)
    mean = sb(nc, st, 'mean1', [128, 512], F32)
    rstd = sb(nc, st, 'rstd1', [128, 512], F32)
    msq = sb(nc, st, 'msq1', [128, 512], F32)
    epsl = sb(nc, st, 'epsl1', [128, 1], F32)
    S.pool(lambda e: e.memset(epsl[:], LN_EPS), writes=[epsl])

    def load_front(t):
        i = t % 2
        c0 = t * 512
        S.dma(m1[i][:, :, :], C.M1T[:, 15 + c0:15 + c0 + 514].rearrange("(f p) n -> p f n", p=128), reads=[C.M1T], writes=[m1[i]])
        S.dma(m2[i][:, :, :], C.M2T[:, 1 + c0:1 + c0 + 542].rearrange("(f p) n -> p f n", p=128), reads=[C.M2T], writes=[m2[i]])
        S.dma(gb[i][:, :, :], C.GBT[:, c0:c0 + 512].rearrange("(f p) n -> p f n", p=128), reads=[C.GBT], writes=[gb[i]])

    def load_x(t):
        S.dma(xt[:, :, :], Xin[t * 512:(t + 1) * 512, :].rearrange("(s p) d -> p s d", p=128), reads=[Xin], writes=[xt])

    cnt = [0]

    def front(t):
        i = t % 2
        a1, a2, g_, mixT = m1[i], m2[i], gb[i], mixTs[i]
        for j in range(4):
            pc = pconv[cnt[0] % 2]; cnt[0] += 1
            for tap in range(31):
                S.pe(lambda e, pc=pc, j=j, tap=tap: e.matmul(pc[:], lhsT=dg31[:, j, tap, :], rhs=a2[:, j, tap:tap + 512], start=(tap == 0), stop=(tap == 30)),
                     reads=[dg31, a2], writes=[pc], sig=(tap == 30))
            S.act(lambda e, pc=pc, j=j: e.activation(out=z[:, j, :], in_=pc[:], func=AF.Copy), reads=[pc], writes=[z])
            S.pool(lambda e, j=j: e.tensor_tensor(out=zsq[:, j, :], in0=z[:, j, :], in1=z[:, j, :], op=ALU.mult), reads=[z], writes=[zsq])
        for j in range(4):
            pc = pconv[cnt[0] % 2]; cnt[0] += 1
            for tap in range(3):
                S.pe(lambda e, pc=pc, j=j, tap=tap: e.matmul(pc[:], lhsT=dg3[:, j, tap, :], rhs=a1[:, j, tap:tap + 512], start=(tap == 0), stop=(tap == 2)),
                     reads=[dg3, a1], writes=[pc], sig=(tap == 2))
            S.dve(lambda e, pc=pc, j=j: e.tensor_tensor(out=mixT[:, j, :], in0=pc[:], in1=g_[:, j, :], op=ALU.mult), reads=[pc, g_], writes=[mixT])
        for j in range(4):
            S.pe(lambda e, j=j: e.matmul(pstat[0][:], lhsT=C.ones_f[:, 0:128], rhs=z[:, j, :], start=(j == 0), stop=(j == 3)),
                 reads=[C.ones_f, z], writes=[pstat[0]], sig=(j == 3))
        for j in range(4):
            S.pe(lambda e, j=j: e.matmul(pstat[1][:], lhsT=C.ones_f[:, 0:128], rhs=zsq[:, j, :], start=(j == 0), stop=(j == 3)),
                 reads=[C.ones_f, zsq], writes=[pstat[1]], sig=(j == 3))
        S.dve(lambda e: e.tensor_scalar(out=mean[:], in0=pstat[0][:], scalar1=1.0 / 512, scalar2=None, op0=ALU.mult), reads=[pstat[0]], writes=[mean])
        S.dve(lambda e: e.tensor_tensor(out=msq[:], in0=mean[:], in1=mean[:], op=ALU.mult), reads=[mean], writes=[msq])
        S.dve(lambda e: e.scalar_tensor_tensor(out=rstd[:], in0=pstat[1][:], scalar=1.0 / 512, in1=msq[:], op0=ALU.mult, op1=ALU.subtract),
              reads=[pstat[1], msq], writes=[rstd])
        S.act(lambda e: e.activation(out=rstd[:], in_=rstd[:], func=AF.Ln, bias=epsl[:, 0:1]), reads=[rstd, epsl], writes=[rstd])
        S.act(lambda e: e.activation(out=rstd[:], in_=rstd[:], func=AF.Exp, scale=-0.5), reads=[rstd], writes=[rstd])
        for j in range(4):
            S.dve(lambda e, j=j: e.tensor_tensor(out=z[:, j, :], in0=z[:, j, :], in1=mean[:], op=ALU.subtract), reads=[z, mean], writes=[z])
            S.pool(lambda e, j=j: e.tensor_tensor(out=z[:, j, :], in0=z[:, j, :], in1=rstd[:], op=ALU.mult), reads=[z, rstd], writes=[z])
            S.act(lambda e, j=j: e.activation(out=mixT[:, 4 + j, :], in_=z[:, j, :], func=AF.Silu, scale=lng[:, j, 0:1], bias=lnb[:, j, 0:1]),
                  reads=[z, lng, lnb], writes=[mixT])

    load_front(0)
    load_x(0)
    front(0)
    load_front(1)
    yi = 0
    for t in range(8):
        if t + 1 < 8:
            front(t + 1)
        if t + 2 < 8:
            load_front(t + 2)
        mixT = mixTs[t % 2]
        for s_ in range(4):
            yp = yps[yi % 2]; yi += 1
            out_proj(C, mixT, wout, s_, yp, 8)
            epi.sub(s_, yp, xt)
        epi.finish((Xout, t * 512))
        if t + 1 < 8:
            load_x(t + 1)
    S.barrier()
    S.flush()
    st.close()


_NC_CACHE = {}


def kernel(**inputs):
    if 'nc' not in _NC_CACHE:
        _NC_CACHE['nc'] = build()
    nc = _NC_CACHE['nc']
    f = lambda a: np.ascontiguousarray(np.asarray(a, dtype=np.float32))
    shared = {
        'c_ctx': f(inputs['c_ctx']).reshape(1, D),
        'ada_w': f(inputs['ada_w']), 'ada_b': f(inputs['ada_b']),
        'ln_g': f(inputs['ln_g']).reshape(4, D), 'ln_b': f(inputs['ln_b']).reshape(4, D),
        'even_w_in': f(inputs['even_w_in']), 'even_w_out': f(inputs['even_w_out']),
        'gdn_conv_w': f(inputs['gdn_conv_w']), 'gdn_a_log': f(inputs['gdn_a_log']).reshape(1, 8),
        'gdn_dt_bias': f(inputs['gdn_dt_bias']).reshape(1, 8), 'gdn_norm_w': f(inputs['gdn_norm_w']).reshape(1, 128),
        'pool_w': f(inputs['pool_w']), 'pool_scale': f(inputs['pool_scale']).reshape(1, 512),
        'odd_w_in': f(inputs['odd_w_in']), 'odd_w_out': f(inputs['odd_w_out']),
        'sconv_w': f(inputs['sconv_w']), 'conf_conv_w': f(inputs['conf_conv_w']),
        'conf_ln_g': f(inputs['conf_ln_g']).reshape(1, 512), 'conf_ln_b': f(inputs['conf_ln_b']).reshape(1, 512),
        'ffn_w_up': f(inputs['ffn_w_up']), 'ffn_conv_w': f(inputs['ffn_conv_w']).reshape(2, 9, DFF),
        'ffn_w_down': f(inputs['ffn_w_down']),
    }
    x = f(inputs['x']); c = f(inputs['c']); ctx = f(inputs['ctx'])
    in_maps = []
    for b in range(NCORES):
        m = dict(shared)
        m['x'] = x[b]; m['c'] = c[b:b + 1]; m['ctx'] = ctx[b]
        in_maps.append(m)
    res = run_bass_kernel_spmd(nc, in_maps, core_ids=list(range(NCORES)))
    return np.stack([r['out'] for r in res.results], axis=0)
```
